# Optimizing a Trainium2 kernel written in Bass

```python
import jax, jax.numpy as jnp
from jax import lax
import numpy as np

D_MODEL = 2048
BATCH = 2
SEQ = 8192
DEPTH = 4

N_MIXERS = 3
BLOCK = 128
EPS = 1e-6

FOX_HEADS = 16
FOX_HEAD_DIM = D_MODEL // FOX_HEADS
FOX_FORGET_BIAS = 2.0

SGU_WIDTH = D_MODEL
SGU_GROUPS = 16
SGU_GROUP_DIM = SGU_WIDTH // SGU_GROUPS
SGU_CHUNK = 128

SWA_HEAD_DIM = 64
SWA_Q_HEADS = D_MODEL // SWA_HEAD_DIM
SWA_KV_HEADS = 8
SWA_WINDOW = 128
ROPE_DIM = SWA_HEAD_DIM // 4
ROPE_THETA = 500000.0

D_FF = ((8 * D_MODEL + 3 * 256 - 1) // (3 * 256)) * 256

N_FOX = (DEPTH + 2) // 3
N_SGU = (DEPTH + 1) // 3
N_SWA = DEPTH // 3

kernel_name = "hybrid_fox_gmlp_swa_sink_adaln"

F32 = jnp.float32


def rmsnorm(x, g):
    xf = x.astype(F32)
    y = xf * lax.rsqrt(jnp.mean(xf * xf, axis=-1, keepdims=True) + EPS)
    return (y * g.astype(F32)).astype(x.dtype)


def layernorm(x, g, b):
    xf = x.astype(F32)
    mu = jnp.mean(xf, axis=-1, keepdims=True)
    var = jnp.mean(jnp.square(xf - mu), axis=-1, keepdims=True)
    y = (xf - mu) * lax.rsqrt(var + EPS)
    return (y * g.astype(F32) + b.astype(F32)).astype(x.dtype)


def modulate(h, shift, scale):
    return h * (1 + scale[:, None, :]) + shift[:, None, :]


def rope_tables(positions):
    inv = ROPE_THETA ** (-jnp.arange(0, ROPE_DIM, 2, dtype=F32) / ROPE_DIM)
    ang = positions.astype(F32)[..., None] * inv
    return jnp.cos(ang), jnp.sin(ang)


def apply_partial_rope(x, cos, sin):
    half = ROPE_DIM // 2
    x1 = x[..., :half]
    x2 = x[..., half:ROPE_DIM]
    rest = x[..., ROPE_DIM:]
    c = cos[:, :, None, :].astype(x.dtype)
    s = sin[:, :, None, :].astype(x.dtype)
    return jnp.concatenate([x1 * c - x2 * s, x2 * c + x1 * s, rest], axis=-1)


def fox_attention(h, w_in, b_f, w_out):
    B, S, _ = h.shape
    H, Dh = FOX_HEADS, FOX_HEAD_DIM
    proj = h @ w_in
    q, k, v, fg = jnp.split(proj, [H * Dh, 2 * H * Dh, 3 * H * Dh], axis=-1)
    q = q.reshape(B, S, H, Dh)
    k = k.reshape(B, S, H, Dh)
    v = v.reshape(B, S, H, Dh)
    log_f = jax.nn.log_sigmoid((fg + b_f).astype(F32))
    cum = lax.cumsum(log_f, axis=1)
    cum_t = cum.transpose(0, 2, 1)
    nb = S // BLOCK
    qb = q.reshape(B, nb, BLOCK, H, Dh).transpose(1, 0, 2, 3, 4)
    fq = cum.reshape(B, nb, BLOCK, H).transpose(1, 0, 3, 2)
    kpos = jnp.arange(S)
    scale = Dh ** -0.5

    def block_fn(args):
        i, q_i, f_i = args
        s = jnp.einsum('bqhd,bkhd->bhqk', q_i, k, preferred_element_type=F32) * scale
        s = s + f_i[..., None] - cum_t[:, :, None, :]
        qpos = i * BLOCK + jnp.arange(BLOCK)
        mask = kpos[None, :] <= qpos[:, None]
        s = jnp.where(mask, s, -jnp.inf)
        p = jax.nn.softmax(s, axis=-1)
        return jnp.einsum('bhqk,bkhd->bqhd', p.astype(v.dtype), v)

    out = lax.map(block_fn, (jnp.arange(nb), qb, fq))
    out = out.transpose(1, 0, 2, 3, 4).reshape(B, S, H * Dh)
    return out @ w_out


def gmlp_sgu(h, w_in, ln_g, ln_b, w_s, b_s, w_out):
    B, S, _ = h.shape
    z = jax.nn.gelu(h @ w_in)
    u, v = jnp.split(z, 2, axis=-1)
    v = layernorm(v, ln_g, ln_b)
    nc = S // SGU_CHUNK
    vg = v.reshape(B, nc, SGU_CHUNK, SGU_GROUPS, SGU_GROUP_DIM)
    causal = jnp.tril(jnp.ones((SGU_CHUNK, SGU_CHUNK), dtype=bool))
    ws = jnp.where(causal[None], w_s, jnp.zeros_like(w_s))
    f = jnp.einsum('gts,bcsgd->bctgd', ws, vg)
    f = f + b_s.T[None, None, :, :, None]
    gated = u * f.reshape(B, S, SGU_WIDTH)
    return gated @ w_out


def swa_sink_attention(h, w_in, sinks, w_out, cos, sin):
    B, S, _ = h.shape
    Hq, Hk, Dh = SWA_Q_HEADS, SWA_KV_HEADS, SWA_HEAD_DIM
    G = Hq // Hk
    proj = h @ w_in
    q, k, v = jnp.split(proj, [Hq * Dh, (Hq + Hk) * Dh], axis=-1)
    q = apply_partial_rope(q.reshape(B, S, Hq, Dh), cos, sin)
    k = apply_partial_rope(k.reshape(B, S, Hk, Dh), cos, sin)
    v = v.reshape(B, S, Hk, Dh)
    nb = S // BLOCK
    qb = q.reshape(B, nb, BLOCK, Hk, G, Dh)
    kb = k.reshape(B, nb, BLOCK, Hk, Dh)
    vb = v.reshape(B, nb, BLOCK, Hk, Dh)
    pad = ((0, 0), (1, 0), (0, 0), (0, 0), (0, 0))
    kband = jnp.concatenate([jnp.pad(kb[:, :-1], pad), kb], axis=2)
    vband = jnp.concatenate([jnp.pad(vb[:, :-1], pad), vb], axis=2)
    s = jnp.einsum('bnqhgd,bnkhd->bnhgqk', qb, kband, preferred_element_type=F32) * (Dh ** -0.5)
    qi = jnp.arange(BLOCK)[:, None]
    ki = jnp.arange(2 * BLOCK)[None, :] - BLOCK
    rel = qi - ki
    valid = (rel >= 0) & (rel < SWA_WINDOW)
    in_seq = (jnp.arange(nb)[:, None, None] * BLOCK + ki[None]) >= 0
    mask = valid[None] & in_seq
    s = jnp.where(mask[None, :, None, None], s, -jnp.inf)
    sink = sinks.astype(F32).reshape(Hk, G)[None, None, :, :, None, None]
    m = jnp.maximum(jnp.max(s, axis=-1, keepdims=True), sink)
    p = jnp.exp(s - m)
    p = p / (jnp.sum(p, axis=-1, keepdims=True) + jnp.exp(sink - m))
    o = jnp.einsum('bnhgqk,bnkhd->bnqhgd', p.astype(v.dtype), vband)
    return o.reshape(B, S, Hq * Dh) @ w_out


def swiglu(h, w_gu, w_down):
    g, u = jnp.split(h @ w_gu, 2, axis=-1)
    return (jax.nn.silu(g) * u) @ w_down


def setup_inputs(seed: int = 0) -> dict:
    key = jax.random.key(seed)
    ks = jax.random.split(key, 32)
    D = D_MODEL
    nrm = lambda k, shape, fan_in, mult=1.0: jax.random.normal(k, shape, F32) * (mult * fan_in ** -0.5)
    fox_in = 3 * FOX_HEADS * FOX_HEAD_DIM + FOX_HEADS
    swa_in = (SWA_Q_HEADS + 2 * SWA_KV_HEADS) * SWA_HEAD_DIM
    x = jax.random.normal(ks[0], (BATCH, SEQ, D), F32)
    c = jax.random.normal(ks[1], (BATCH, D), F32)
    offset = jax.random.randint(ks[2], (BATCH, 1), 0, 4096, dtype=jnp.int32)
    positions = offset + jnp.arange(SEQ, dtype=jnp.int32)[None, :]
    gain = lambda k, shape: 1.0 + 0.02 * jax.random.normal(k, shape, F32)
    return {
        "x": x,
        "c": c,
        "positions": positions,
        "ada_w": nrm(ks[3], (DEPTH, D, 6 * D), D, 0.5),
        "ada_b": 0.01 * jax.random.normal(ks[4], (DEPTH, 6 * D), F32),
        "mix_pre_g": gain(ks[5], (DEPTH, D)),
        "mix_post_g": gain(ks[6], (DEPTH, D)),
        "ffn_pre_g": gain(ks[7], (DEPTH, D)),
        "ffn_post_g": gain(ks[8], (DEPTH, D)),
        "ffn_w_gu": nrm(ks[9], (DEPTH, D, 2 * D_FF), D),
        "ffn_w_down": nrm(ks[10], (DEPTH, D_FF, D), D_FF),
        "fox_w_in": nrm(ks[11], (N_FOX, D, fox_in), D),
        "fox_b_f": FOX_FORGET_BIAS + 0.5 * jax.random.normal(ks[12], (N_FOX, FOX_HEADS), F32),
        "fox_w_out": nrm(ks[13], (N_FOX, FOX_HEADS * FOX_HEAD_DIM, D), FOX_HEADS * FOX_HEAD_DIM),
        "sgu_w_in": nrm(ks[14], (N_SGU, D, 2 * SGU_WIDTH), D),
        "sgu_ln_g": gain(ks[15], (N_SGU, SGU_WIDTH)),
        "sgu_ln_b": 0.01 * jax.random.normal(ks[16], (N_SGU, SGU_WIDTH), F32),
        "sgu_w_s": nrm(ks[17], (N_SGU, SGU_GROUPS, SGU_CHUNK, SGU_CHUNK), SGU_CHUNK),
        "sgu_b_s": 1.0 + 0.02 * jax.random.normal(ks[18], (N_SGU, SGU_GROUPS, SGU_CHUNK), F32),
        "sgu_w_out": nrm(ks[19], (N_SGU, SGU_WIDTH, D), SGU_WIDTH),
        "swa_w_in": nrm(ks[20], (N_SWA, D, swa_in), D),
        "swa_sinks": 0.5 * jax.random.normal(ks[21], (N_SWA, SWA_Q_HEADS), F32),
        "swa_w_out": nrm(ks[22], (N_SWA, SWA_Q_HEADS * SWA_HEAD_DIM, D), SWA_Q_HEADS * SWA_HEAD_DIM),
    }


def reference(x, c, positions, ada_w, ada_b, mix_pre_g, mix_post_g, ffn_pre_g, ffn_post_g,
              ffn_w_gu, ffn_w_down, fox_w_in, fox_b_f, fox_w_out, sgu_w_in, sgu_ln_g,
              sgu_ln_b, sgu_w_s, sgu_b_s, sgu_w_out, swa_w_in, swa_sinks, swa_w_out):
    cos, sin = rope_tables(positions)
    c_act = jax.nn.silu(c)
    for i in range(DEPTH):
        mod = c_act @ ada_w[i] + ada_b[i]
        sh_m, sc_m, g_m, sh_f, sc_f, g_f = jnp.split(mod, 6, axis=-1)
        h = modulate(rmsnorm(x, mix_pre_g[i]), sh_m, sc_m)
        kind, j = i % N_MIXERS, i // N_MIXERS
        if kind == 0:
            y = fox_attention(h, fox_w_in[j], fox_b_f[j], fox_w_out[j])
        elif kind == 1:
            y = gmlp_sgu(h, sgu_w_in[j], sgu_ln_g[j], sgu_ln_b[j], sgu_w_s[j], sgu_b_s[j], sgu_w_out[j])
        else:
            y = swa_sink_attention(h, swa_w_in[j], swa_sinks[j], swa_w_out[j], cos, sin)
        x = x + g_m[:, None, :] * rmsnorm(y, mix_post_g[i])
        h = modulate(rmsnorm(x, ffn_pre_g[i]), sh_f, sc_f)
        y = swiglu(h, ffn_w_gu[i], ffn_w_down[i])
        x = x + g_f[:, None, :] * rmsnorm(y, ffn_post_g[i])
    return x
```

```python
import numpy as np
import ml_dtypes
from contextlib import ExitStack
import concourse.bass as bass
import concourse.mybir as mybir
from concourse.bass_utils import run_bass_kernel_spmd

F32 = mybir.dt.float32
BF16 = mybir.dt.bfloat16
I32 = mybir.dt.int32
AF = mybir.ActivationFunctionType
ALU = mybir.AluOpType

D = 2048
NCH = 16
SEQ = 8192
TOK = 2048
NBLK = 16
TG = 512
NG = TOK // TG
DFF = 5632
NFC = DFF // 128
EPS = 1e-6
FOX_IN = 6160
SWA_IN = 3072
ENGS = ("pe", "act", "dve", "pool", "sp")
BLOCKNAME = {"pe": "tensor", "act": "scalar", "dve": "vector", "pool": "gpsimd", "sp": "sync"}


class Buf:
    __slots__ = ("name", "w", "rs", "dtotal")

    def __init__(self, name):
        self.name = name
        self.w = None
        self.rs = {}
        self.dtotal = 0


class Plan:
    def __init__(self):
        self.recs = {e: [] for e in ENGS}
        self.seen = {e: {} for e in ENGS}
        self.dbufs = {}

    def _deps(self, eng, reads, writes, skipkey=None):
        need = {}
        seen = self.seen[eng]

        def add(tok):
            key, val = tok
            if key == ("E", "pe") and eng == "pe":
                return
            if key == skipkey:
                return
            if seen.get(key, -1) >= val:
                return
            if need.get(key, -1) < val:
                need[key] = val

        for b in reads:
            if b.w is not None:
                add(b.w)
        for b in writes:
            if b.w is not None:
                add(b.w)
            for k, v in b.rs.items():
                add((k, v))
        for k, v in need.items():
            seen[k] = v
            if k[0] == "E":
                self.recs[k[1]][v][3] = True
        return list(need.items())

    def op(self, eng, fn, reads=(), writes=()):
        waits = self._deps(eng, reads, writes)
        idx = len(self.recs[eng])
        self.recs[eng].append([waits, fn, None, False, 0])
        key = ("E", eng)
        for b in reads:
            if b.rs.get(key, -1) < idx:
                b.rs[key] = idx
        for b in writes:
            b.w = (key, idx)
            b.rs = {}

    def dma(self, eng, fn, reads=(), writes=(), dbuf=None, inc=16):
        if dbuf is None:
            dbuf = writes[0]
        waits = self._deps(eng, reads, writes, skipkey=("D", id(dbuf)))
        self.dbufs[id(dbuf)] = dbuf
        dbuf.dtotal += inc
        key = ("D", id(dbuf))
        val = dbuf.dtotal
        self.recs[eng].append([waits, fn, id(dbuf), False, inc])
        for b in reads:
            if b.rs.get(key, -1) < val:
                b.rs[key] = val
        for b in writes:
            b.w = (key, val)
            b.rs = {}

    def barrier(self, exclude=()):
        excl = set(id(b) for b in exclude)
        for e in ENGS:
            need = []
            seen = self.seen[e]
            for e2 in ENGS:
                if e2 == e or not self.recs[e2]:
                    continue
                idx = None
                for j in range(len(self.recs[e2]) - 1, -1, -1):
                    r = self.recs[e2][j]
                    if r[1] is not None and r[2] is None:
                        idx = j
                        break
                if idx is None:
                    continue
                key = ("E", e2)
                if seen.get(key, -1) < idx:
                    seen[key] = idx
                    self.recs[e2][idx][3] = True
                    need.append((key, idx))
            for bid, b in self.dbufs.items():
                key = ("D", bid)
                if bid in excl:
                    continue
                if b.dtotal > 0 and seen.get(key, -1) < b.dtotal:
                    seen[key] = b.dtotal
                    need.append((key, b.dtotal))
            if need:
                self.recs[e].append([need, None, None, False, 0])

    def emit(self, nc):
        vals = {}
        for e in ENGS:
            cnt = 0
            v = []
            for rec in self.recs[e]:
                if rec[3]:
                    cnt += 1
                v.append(cnt)
            vals[e] = v
            assert cnt < 60000, (e, cnt)
        with ExitStack() as es:
            esem = {e: es.enter_context(nc.semaphore("sem_" + e)) for e in ENGS}
            dsem = {}
            for n, bid in enumerate(self.dbufs):
                dsem[bid] = es.enter_context(nc.semaphore("dsem%d" % n))
            block = es.enter_context(nc.Block())
            for e in ENGS:
                def body(h, e=e):
                    for waits, fn, dma, flagged, inc in self.recs[e]:
                        for key, val in waits:
                            if key[0] == "E":
                                h.wait_ge(esem[key[1]], vals[key[1]][val])
                            else:
                                h.wait_ge(dsem[key[1]], val)
                        if fn is None:
                            continue
                        ins = fn(h)
                        if dma is not None:
                            ins.then_inc(dsem[dma], inc)
                        elif flagged:
                            ins.then_inc(esem[e], 1)
                getattr(block, BLOCKNAME[e])(body)


class Rot:
    def __init__(self, items):
        self.items = items
        self.i = 0

    def next(self):
        it = self.items[self.i % len(self.items)]
        self.i += 1
        return it


def build_program(layers=(0, 1, 2, 3), do_mixer=True, do_ffn=True):
    NL = len(layers)
    LI = {l: i for i, l in enumerate(layers)}
    nc = bass.Bass("TRN2", target_bir_lowering=False)
    P = Plan()

    def din(name, shape, dt=F32):
        return nc.dram_tensor(name, list(shape), dt, kind="ExternalInput").ap()

    xs = din("xs", [TOK, D])
    cvec = din("cvec", [16, 128])
    ident = din("ident", [128, 128])
    ada_w = din("ada_w", [NL, D, 3072])
    ada_b = din("ada_b", [NL, 3072])
    gains = [din(n, [64, 128]) for n in ("mix_pre_g", "mix_post_g", "ffn_pre_g", "ffn_post_g")]
    w_gu = din("ffn_w_gu", [NL, D, 2 * DFF])
    w_dn = din("ffn_w_down", [NL, DFF, D])
    sgu_w_in = din("sgu_w_in", [1, D, 2 * D])
    sgu_ln_g = din("sgu_ln_g", [1, D])
    sgu_ln_b = din("sgu_ln_b", [1, D])
    sgu_w_s = din("sgu_w_s", [1, 16, 128, 128])
    sgu_b_s = din("sgu_b_s", [1, 16 * 128])
    sgu_w_out = din("sgu_w_out", [1, D, D])
    trimask = din("trimask", [128, 128])
    fox_w_in = din("fox_w_in", [2, D, FOX_IN])
    fox_b_f = din("fox_b_f", [2, 16])
    fox_w_out = din("fox_w_out", [2, D, D])
    swa_w_in = din("swa_w_in", [1, D, SWA_IN])
    swa_sinks = din("swa_sinks", [1, 32])
    swa_w_out = din("swa_w_out", [1, D, D])
    posin = din("posin", [1, TOK], I32)
    triu = din("triu", [128, 128])
    m_prev_in = din("m_prev", [128, 128])
    m_fox_in = din("m_fox", [128, 4 * 128])
    m_prev0_in = din("m_prev0", [128, 128])
    zo_in = din("zo", [128, 1024])
    oh4_in = din("oh4", [128, 4])
    sel5_in = din("sel5", [128, 5])
    pmat_in = din("pmat", [64, 16])
    invf_in = din("invf", [16, 1])
    yout = nc.dram_tensor("yout", [TOK, D], F32, kind="ExternalOutput").ap()

    xT_s = nc.dram_tensor("xT_s", [D, TOK], F32).ap()
    xT_v = xT_s.rearrange("(c p) t -> p c t", p=128)
    xT_b = [Buf("xT_s%d" % g) for g in range(NG)]
    xTw_b = [Buf("xTw_s%d" % g) for g in range(NG)]
    yout_b = Buf("yout")
    modin = [nc.dram_tensor("modin%d" % i, [128, 24], F32) for i in range(4)]
    modout = [nc.dram_tensor("modout%d" % i, [4 * 128, 24], F32) for i in range(4)]
    modin_b, modout_b = Buf("modin"), Buf("modout")
    q_s = nc.dram_tensor("q_s", [D, TOK], BF16).ap()
    q_b = Buf("q_s")
    att_s = nc.dram_tensor("att_s", [D, TOK], BF16).ap()
    att_b = Buf("att_s")
    RG = [[0, 1, 2, 3], [4, 5, 6, 7]]
    fdr = {}
    fdr["kin"] = [nc.dram_tensor("fkin%d" % i, [256, TOK], BF16) for i in range(8)]
    fdr["kout"] = [nc.dram_tensor("fkout%d" % i, [4 * 256, TOK], BF16) for i in range(8)]
    fdr["vin"] = [nc.dram_tensor("fvin%d" % i, [128, 2 * 16 * 128], BF16) for i in range(8)]
    fdr["vout"] = [nc.dram_tensor("fvout%d" % i, [4 * 128, 2 * 16 * 128], BF16) for i in range(8)]
    fdr["lin"] = nc.dram_tensor("flin", [TOK, 16], F32)
    fdr["lout"] = nc.dram_tensor("flout", [4 * TOK, 16], F32)
    fdr["b"] = {k: Buf("f" + k) for k in ("kin", "vin", "lin", "lout")}
    fdr["b"]["kout"] = [Buf("fkout%d" % i) for i in range(8)]
    fdr["b"]["vout"] = [Buf("fvout%d" % i) for i in range(8)]
    swa_dr = {}
    swa_dr["kin"] = [nc.dram_tensor("skin%d" % i, [256, TOK], BF16) for i in range(2)]
    swa_dr["kout"] = [nc.dram_tensor("skout%d" % i, [4 * 256, TOK], BF16) for i in range(2)]
    swa_dr["vin"] = [nc.dram_tensor("svin%d" % i, [128, 8 * 512], BF16) for i in range(2)]
    swa_dr["vout"] = [nc.dram_tensor("svout%d" % i, [4 * 128, 8 * 512], BF16) for i in range(2)]
    swa_dr["b"] = {k: Buf("s" + k) for k in ("kin", "kout", "vin", "vout")}

    arena = {"off": 16640}

    def sb(name, shape, dt):
        nbytes = int(np.prod(shape[1:])) * (4 if dt in (F32, I32) else 2)
        off = (arena["off"] + 31) // 32 * 32
        arena["off"] = off + nbytes
        assert arena["off"] <= 229344, (name, arena["off"])
        t = nc.alloc_sbuf_tensor_at(name, list(shape), dt, offset=off)
        return t

    ident_t = sb("ident_t", [128, 128], F32)
    ident_b = Buf("ident")
    ones_t = sb("ones_t", [128, 128], F32)
    ones_b = Buf("ones")
    eps_t = sb("eps_t", [128, 1], F32)
    one11 = ones_t
    cact_t = sb("cact_t", [128, 16], BF16)
    cact_b = Buf("cact")
    gains_t = sb("gains_t", [128, 4, 64], F32)
    gains_b = Buf("gains")
    modT = sb("modT", [128, 96], F32)
    modp_t = sb("modp_t", [128, 24], F32)
    modp_b = Buf("modp")
    modT_b = Buf("modT")
    der_t = sb("der_t", [128, 4, 16], F32)
    der_b = Buf("der")
    onesb_t = sb("onesb_t", [128, 128], BF16)
    cst_t = sb("cst_t", [128, 4], F32)
    xg_off = (arena["off"] + 31) // 32 * 32
    xg_t = sb("xg_t", [128, NCH, TG], F32)
    xg_b = Buf("xg")
    hT_t = sb("hT_t", [128, NCH, TG], BF16)
    hT_b = Buf("hT")
    yT_t = sb("yT_t", [128, NCH, TG], F32)
    yT_b = Buf("yT")
    big_off = (arena["off"] + 31) // 32 * 32
    big_t = sb("big_t", [128, NFC, TG], BF16)
    big_b = Buf("big")
    wsl = []
    for i in range(2):
        t = sb("wslot%d" % i, [128, 8192], BF16)
        wsl.append((t, Buf("wslot%d" % i)))
    wrot = Rot(wsl)
    attn_lim = arena["off"]
    sqs = Rot([(sb("sq%d" % i, [128, TG], F32), Buf("sq%d" % i)) for i in range(2)])
    tmps = Rot([(sb("tmp%d" % i, [128, TG], F32), Buf("tmp%d" % i)) for i in range(2)])
    rstd_t = sb("rstd_t", [128, TG], F32)
    rstd_b = Buf("rstd")
    rt_t = sb("rt_t", [128, TG], F32)
    rt_b = Buf("rt")
    row_t = sb("row_t", [1, 512], F32)
    row_b = Buf("row")
    brow_t = sb("brow_t", [1, 512], F32)
    brow_b = Buf("brow")
    small_t = sb("small_t", [128, 64], F32)
    small_b = Buf("small")
    phase_base = arena["off"]

    es = ExitStack()
    psum = []
    for i in range(8):
        t = es.enter_context(nc.psum_tensor("ps%d" % i, [128, 512], F32))
        psum.append((t, Buf("ps%d" % i)))
    PG = Rot([psum[0], psum[2]])
    PU = Rot([psum[1], psum[3]])
    PY = Rot([psum[4], psum[5]])
    PSSQ = psum[6]
    PM = psum[7]

    mm = lambda out, lhsT, rhs, st, sp: (lambda h: h.matmul(out, lhsT, rhs, start=st, stop=sp))

    def load_w_cols(W2d, col0, ncols, slot, slot_b, dst_col0=0, width=None, kch=NCH):
        width = width or ncols
        view = slot[:, 0:kch * width].rearrange("p (c n) -> p c n", n=width)
        src = W2d.rearrange("(c p) n -> p c n", p=128)[:, :, col0:col0 + ncols]
        P.dma("pool", lambda h: h.dma_start(out=view[:, :, dst_col0:dst_col0 + ncols], in_=src),
              writes=[slot_b])
        return view

    def ssq_rstd(src_t, src_b, src_fn=None):
        pst, psb = PSSQ
        if src_fn is None:
            src_fn = lambda c: src_t[:, c, :]
        for c in range(NCH):
            sq, sqb = sqs.next()
            P.op("act", lambda h, sq=sq, c=c: h.activation(out=sq[:, :], in_=src_fn(c), func=AF.Square),
                 reads=[src_b], writes=[sqb])
            P.op("pe", mm(pst[:, :], ones_t[:, :], sq[:, :], c == 0, c == NCH - 1),
                 reads=[ones_b, sqb], writes=[psb])
        P.op("act", lambda h: h.activation(out=rt_t[:, :], in_=pst[:, :], func=AF.Sqrt,
                                           bias=eps_t[:, 0:1], scale=1.0 / D),
             reads=[psb, ones_b], writes=[rt_b])
        P.op("dve", lambda h: h.reciprocal(out=rstd_t[:, :], in_=rt_t[:, :]), reads=[rt_b], writes=[rstd_b])

    def prenorm(acol, bcol):
        ssq_rstd(xg_t, xg_b)
        for c in range(NCH):
            tmp, tb = tmps.next()
            P.op("dve", lambda h, tmp=tmp, c=c: h.scalar_tensor_tensor(
                out=tmp[:, :], in0=xg_t[:, c, :], scalar=acol[:, c:c + 1], in1=rstd_t[:, :],
                op0=ALU.mult, op1=ALU.mult), reads=[xg_b, der_b, rstd_b], writes=[tb])
            P.op("act", lambda h, tmp=tmp, c=c: h.activation(
                out=hT_t[:, c, :], in_=tmp[:, :], func=AF.Identity, bias=bcol[:, c:c + 1], scale=1.0),
                reads=[tb, modT_b], writes=[hT_b])

    def postnorm_res(coef):
        ssq_rstd(yT_t, yT_b)
        for c in range(NCH):
            tmp, tb = tmps.next()
            P.op("dve", lambda h, tmp=tmp, c=c: h.scalar_tensor_tensor(
                out=tmp[:, :], in0=yT_t[:, c, :], scalar=coef[:, c:c + 1], in1=rstd_t[:, :],
                op0=ALU.mult, op1=ALU.mult), reads=[yT_b, der_b, rstd_b], writes=[tb])
            P.op("pool", lambda h, tmp=tmp, c=c: h.tensor_tensor(
                out=xg_t[:, c, :], in0=xg_t[:, c, :], in1=tmp[:, :], op=ALU.add),
                reads=[tb, xg_b], writes=[xg_b])

    def load_xg(g):
        P.dma("sp", lambda h: h.dma_start(out=xg_t[:, :, :], in_=xT_v[:, :, g * TG:(g + 1) * TG]),
              reads=[xT_b[g], xTw_b[g]], writes=[xg_b])

    def store_xg(g):
        P.dma("sp", lambda h: h.dma_start(out=xT_v[:, :, g * TG:(g + 1) * TG], in_=xg_t[:, :, :]),
              reads=[xg_b], writes=[xT_b[g]])

    def linear_fm(W2d, col0, n_oc, rhs_fn, rhs_bufs, kch, evac, ocw=128, per_load=4):
        oc = 0
        while oc < n_oc:
            nl = min(per_load, n_oc - oc)
            slot, slot_b = wrot.next()
            view = load_w_cols(W2d, col0 + oc * ocw, nl * ocw, slot, slot_b, kch=kch)
            for j in range(nl):
                pst, psb = PY.next()
                for c in range(kch):
                    P.op("pe", mm(pst[0:ocw, :], view[:, c, j * ocw:(j + 1) * ocw], rhs_fn(c), c == 0, c == kch - 1),
                         reads=[slot_b] + rhs_bufs, writes=[psb])
                evac(oc + j, pst, psb)
            oc += nl

    P.dma("sp", lambda h: h.dma_start(out=ident_t[:, :], in_=ident[:, :]), writes=[ident_b])
    P.op("dve", lambda h: h.memset(ones_t[:, :], 1.0), writes=[ones_b])
    P.op("dve", lambda h: h.memset(eps_t[:, :], EPS), writes=[ones_b])
    P.op("dve", lambda h: h.memset(onesb_t[:, :], 1.0), writes=[ones_b])
    P.op("dve", lambda h: h.memset(cst_t[:, 0:1], -float(np.pi)), writes=[ones_b])
    for k in range(4):
        tmp, tb = tmps.next()
        P.dma("sp", lambda h, tmp=tmp, k=k: h.dma_start(out=tmp[0:64, 0:128], in_=gains[k][:, :]), writes=[tb])
        pst, psb = PM
        P.op("pe", lambda h, tmp=tmp: h.transpose(pst[:, 0:64], tmp[0:64, 0:128], ident_t[0:64, 0:64]),
             reads=[tb, ident_b], writes=[psb])
        P.op("dve", lambda h, k=k: h.tensor_copy(out=gains_t[:, k, :], in_=pst[:, 0:64]), reads=[psb], writes=[gains_b])
    tmp, tb = tmps.next()
    P.dma("sp", lambda h, tmp=tmp: h.dma_start(out=tmp[0:16, 0:128], in_=cvec[:, :]), writes=[tb])
    pst, psb = PM
    P.op("pe", lambda h, tmp=tmp: h.transpose(pst[:, 0:16], tmp[0:16, 0:128], ident_t[0:16, 0:16]),
         reads=[tb, ident_b], writes=[psb])
    P.op("act", lambda h: h.activation(out=cact_t[:, :], in_=pst[:, 0:16], func=AF.Silu), reads=[psb], writes=[cact_b])

    xblk = sb("xblk", [128, 4, D], F32) if False else None
    for g in range(NG):
        stage = yT_t[:, :, :].rearrange("p c t -> p (c t)").rearrange("p (b d) -> p b d", b=4)
        P.dma("sp", lambda h, g=g: h.dma_start(
            out=stage, in_=xs[g * TG:(g + 1) * TG, :].rearrange("(b p) d -> p b d", p=128)), writes=[yT_b])
        for c in range(NCH):
            pst, psb = PY.next()
            for b in range(4):
                P.op("pe", lambda h, pst=pst, b=b, c=c: h.transpose(
                    pst[:, b * 128:(b + 1) * 128], stage[:, b, c * 128:(c + 1) * 128], ident_t[:, :]),
                    reads=[yT_b, ident_b], writes=[psb])
            eng = "act" if c % 2 == 0 else "dve"
            if eng == "act":
                P.op("act", lambda h, pst=pst, c=c: h.activation(out=xg_t[:, c, :], in_=pst[:, :], func=AF.Copy),
                     reads=[psb], writes=[xg_b])
            else:
                P.op("dve", lambda h, pst=pst, c=c: h.tensor_copy(out=xg_t[:, c, :], in_=pst[:, :]),
                     reads=[psb], writes=[xg_b])
        store_xg(g)

    def compute_mod(l):
        for cg in range(6):
            slot, slot_b = wrot.next()
            view = load_w_cols(ada_w[LI[l]], cg * 512, 512, slot, slot_b)
            P.dma("sp", lambda h, cg=cg: h.dma_start(out=brow_t[0:1, :], in_=ada_b[LI[l]:LI[l] + 1, cg * 512:(cg + 1) * 512]),
                  writes=[brow_b])
            pst, psb = PY.next()
            for c in range(NCH):
                P.op("pe", mm(pst[0:1, :], cact_t[:, c:c + 1], view[:, c, :], c == 0, c == NCH - 1),
                     reads=[cact_b, slot_b], writes=[psb])
            P.op("dve", lambda h, pst=pst: h.tensor_tensor(out=row_t[0:1, :], in0=pst[0:1, :], in1=brow_t[0:1, :],
                                                           op=ALU.add), reads=[psb, brow_b], writes=[row_b])
            pm, pmb = PM
            for j in range(4):
                P.op("pe", mm(pm[:, j:j + 1], row_t[0:1, j * 128:(j + 1) * 128], one11[0:1, 0:1], True, True),
                     reads=[row_b, ones_b], writes=[pmb])
            P.op("dve", lambda h, cg=cg: h.tensor_copy(out=modp_t[:, cg * 4:(cg + 1) * 4], in_=pm[:, 0:4]),
                 reads=[pmb], writes=[modp_b])
        P.dma("sp", lambda h: h.dma_start(out=modin[l].ap(), in_=modp_t[:, :]), reads=[modp_b], writes=[modin_b])
        P.dma("pool", lambda h: h.collective_compute("AllGather", ALU.bypass, replica_groups=RG,
                                                     ins=[modin[l].ap().opt()], outs=[modout[l].ap().opt()]),
              reads=[modin_b], writes=[modout_b], inc=1)
        P.dma("sp", lambda h: h.dma_start(out=modT[:, :].rearrange("p (r c) -> p r c", r=4),
                                          in_=modout[l].ap().rearrange("(r p) c -> p r c", r=4)),
              reads=[modout_b], writes=[modT_b])
        for which, (sc_i, gate_i, pre_k, post_k) in enumerate(((1, 2, 0, 1), (4, 5, 2, 3))):
            P.op("dve", lambda h, sc_i=sc_i: h.tensor_scalar_add(out=small_t[:, 0:16], in0=modT[:, sc_i * 16:(sc_i + 1) * 16],
                                                                 scalar1=1.0), reads=[modT_b], writes=[small_b])
            P.op("dve", lambda h, which=which, pre_k=pre_k: h.tensor_tensor(
                out=der_t[:, 2 * which, :], in0=small_t[:, 0:16], in1=gains_t[:, pre_k, l * 16:(l + 1) * 16], op=ALU.mult),
                reads=[small_b, gains_b], writes=[der_b])
            P.op("dve", lambda h, which=which, gate_i=gate_i, post_k=post_k: h.tensor_tensor(
                out=der_t[:, 2 * which + 1, :], in0=modT[:, gate_i * 16:(gate_i + 1) * 16],
                in1=gains_t[:, post_k, l * 16:(l + 1) * 16], op=ALU.mult),
                reads=[modT_b, gains_b], writes=[der_b])

    def ffn_group(l, g):
        load_xg(g)
        prenorm(der_t[:, 2, :], modT[:, 48:64])
        for fc in range(NFC):
            slot, slot_b = wrot.next()
            view = load_w_cols(w_gu[LI[l]], fc * 128, 128, slot, slot_b, dst_col0=0, width=256)
            load_w_cols(w_gu[LI[l]], DFF + fc * 128, 128, slot, slot_b, dst_col0=128, width=256)
            pg, pgb = PG.next()
            pu, pub = PU.next()
            for c in range(NCH):
                P.op("pe", mm(pg[:, :], view[:, c, 0:128], hT_t[:, c, :], c == 0, c == NCH - 1),
                     reads=[slot_b, hT_b], writes=[pgb])
            for c in range(NCH):
                P.op("pe", mm(pu[:, :], view[:, c, 128:256], hT_t[:, c, :], c == 0, c == NCH - 1),
                     reads=[slot_b, hT_b], writes=[pub])
            tmp, tb = tmps.next()
            P.op("act", lambda h, pg=pg, tmp=tmp: h.activation(out=tmp[:, :], in_=pg[:, :], func=AF.Silu),
                 reads=[pgb], writes=[tb])
            P.op("dve", lambda h, pu=pu, tmp=tmp, fc=fc: h.tensor_tensor(
                out=big_t[:, fc, :], in0=tmp[:, :], in1=pu[:, :], op=ALU.mult), reads=[tb, pub], writes=[big_b])
        for dc in range(NCH):
            slot, slot_b = wrot.next()
            view = slot[:, 0:NFC * 128].rearrange("p (j o) -> p j o", o=128)
            src = w_dn[LI[l]].rearrange("(j p) o -> p j o", p=128)[:, :, dc * 128:(dc + 1) * 128]
            P.dma("pool", lambda h, view=view, src=src: h.dma_start(out=view, in_=src), writes=[slot_b])
            py, pyb = PY.next()
            for j in range(NFC):
                P.op("pe", mm(py[:, :], view[:, j, :], big_t[:, j, :], j == 0, j == NFC - 1),
                     reads=[slot_b, big_b], writes=[pyb])
            P.op("act", lambda h, py=py, dc=dc: h.activation(out=yT_t[:, dc, :], in_=py[:, :], func=AF.Copy),
                 reads=[pyb], writes=[yT_b])
        postnorm_res(der_t[:, 3, :])
        store_xg(g)

    TG2 = 1024
    assert attn_lim - xg_off >= 159744, (attn_lim, xg_off)
    XY = nc.alloc_sbuf_tensor_at("f_xy", [128, NCH, TG2], F32, offset=xg_off)
    H2 = nc.alloc_sbuf_tensor_at("f_h2", [128, NCH, TG2], BF16, offset=xg_off + 65536)
    A2 = nc.alloc_sbuf_tensor_at("f_a2", [128, 22, TG2], BF16, offset=xg_off + 98304)
    fws = [(nc.alloc_sbuf_tensor_at("f_w%d" % i, [128, 4096], BF16, offset=xg_off + 143360 + i * 8192), Buf("f_w%d" % i))
           for i in range(2)]
    fws += [(nc.alloc_sbuf_tensor_at("f_w%d" % (2 + i), [128, 4096], BF16, offset=phase_base + i * 8192), Buf("f_w%d" % (2 + i)))
            for i in range(2)]
    fwrot = Rot(fws)
    xins = Rot([(nc.alloc_sbuf_tensor_at("f_xin%d" % i, [128, 512], F32, offset=phase_base + 16384 + i * 2048), Buf("f_xin%d" % i))
                for i in range(2)])
    xouts = Rot([(nc.alloc_sbuf_tensor_at("f_xo%d" % i, [128, 512], F32, offset=phase_base + 20480 + i * 2048), Buf("f_xo%d" % i))
                 for i in range(2)])
    XY_b, H2_b, A2_b = Buf("f_xy"), Buf("f_h2"), Buf("f_a2")

    def ffn_big(l):
        acol, bcol, coef = der_t[:, 2, :], modT[:, 48:64], der_t[:, 3, :]
        Wgu = w_gu[LI[l]]
        Wdn = w_dn[LI[l]].rearrange("(j p) o -> p j o", p=128)
        for g2 in range(2):
            t0 = g2 * TG2
            gb = [2 * g2, 2 * g2 + 1]
            P.dma("sp", lambda h, t0=t0: h.dma_start(out=XY[:, :, :], in_=xT_v[:, :, t0:t0 + TG2]),
                  reads=[xT_b[gb[0]], xT_b[gb[1]], xTw_b[gb[0]], xTw_b[gb[1]]], writes=[XY_b])
            for th in range(2):
                hs = slice(th * 512, (th + 1) * 512)
                ssq_rstd(None, XY_b, src_fn=lambda c, hs=hs: XY[:, c, hs])
                for c in range(NCH):
                    tmp, tb = tmps.next()
                    P.op("dve", lambda h, tmp=tmp, c=c, hs=hs: h.scalar_tensor_tensor(
                        out=tmp[:, :], in0=XY[:, c, hs], scalar=acol[:, c:c + 1], in1=rstd_t[:, :],
                        op0=ALU.mult, op1=ALU.mult), reads=[XY_b, der_b, rstd_b], writes=[tb])
                    P.op("act", lambda h, tmp=tmp, c=c, hs=hs: h.activation(
                        out=H2[:, c, hs], in_=tmp[:, :], func=AF.Identity, bias=bcol[:, c:c + 1], scale=1.0),
                        reads=[tb, modT_b], writes=[H2_b])
            for fh in range(2):
                for fcl in range(22):
                    fc = fh * 22 + fcl
                    slot, slot_b = fwrot.next()
                    view = load_w_cols(Wgu, fc * 128, 128, slot, slot_b, dst_col0=0, width=256)
                    load_w_cols(Wgu, DFF + fc * 128, 128, slot, slot_b, dst_col0=128, width=256)
                    for th in range(2):
                        hs = slice(th * 512, (th + 1) * 512)
                        pg, pgb = PG.next()
                        pu, pub = PU.next()
                        for c in range(NCH):
                            P.op("pe", mm(pg[:, :], view[:, c, 0:128], H2[:, c, hs], c == 0, c == NCH - 1),
                                 reads=[slot_b, H2_b], writes=[pgb])
                        for c in range(NCH):
                            P.op("pe", mm(pu[:, :], view[:, c, 128:256], H2[:, c, hs], c == 0, c == NCH - 1),
                                 reads=[slot_b, H2_b], writes=[pub])
                        tmp, tb = tmps.next()
                        P.op("act", lambda h, pg=pg, tmp=tmp: h.activation(out=tmp[:, :], in_=pg[:, :], func=AF.Silu),
                             reads=[pgb], writes=[tb])
                        P.op("dve", lambda h, pu=pu, tmp=tmp, fcl=fcl, hs=hs: h.tensor_tensor(
                            out=A2[:, fcl, hs], in0=tmp[:, :], in1=pu[:, :], op=ALU.mult), reads=[tb, pub], writes=[A2_b])
                for dc in range(NCH):
                    slot, slot_b = fwrot.next()
                    view = slot[:, 0:22 * 128].rearrange("p (j o) -> p j o", o=128)
                    src = Wdn[:, fh * 22:(fh + 1) * 22, dc * 128:(dc + 1) * 128]
                    P.dma("pool", lambda h, view=view, src=src: h.dma_start(out=view, in_=src), writes=[slot_b])
                    for th in range(2):
                        hs = slice(th * 512, (th + 1) * 512)
                        py, pyb = PY.next()
                        for j in range(22):
                            P.op("pe", mm(py[:, :], view[:, j, :], A2[:, j, hs], j == 0, j == 21),
                                 reads=[slot_b, A2_b], writes=[pyb])
                        if fh == 0:
                            P.op("act", lambda h, py=py, dc=dc, hs=hs: h.activation(out=XY[:, dc, hs], in_=py[:, :], func=AF.Copy),
                                 reads=[pyb], writes=[XY_b])
                        else:
                            P.op("dve", lambda h, py=py, dc=dc, hs=hs: h.tensor_tensor(out=XY[:, dc, hs], in0=py[:, :],
                                                                                       in1=XY[:, dc, hs], op=ALU.add),
                                 reads=[pyb, XY_b], writes=[XY_b])
            for th in range(2):
                hs = slice(th * 512, (th + 1) * 512)
                gg = gb[th]
                ssq_rstd(None, XY_b, src_fn=lambda c, hs=hs: XY[:, c, hs])
                for c in range(NCH):
                    xin, xinb = xins.next()
                    xo, xob = xouts.next()
                    P.dma("sp", lambda h, xin=xin, c=c, gg=gg: h.dma_start(out=xin[:, :], in_=xT_v[:, c, gg * 512:(gg + 1) * 512]),
                          reads=[xT_b[gg]], writes=[xinb])
                    tmp, tb = tmps.next()
                    P.op("dve", lambda h, tmp=tmp, c=c, hs=hs: h.scalar_tensor_tensor(
                        out=tmp[:, :], in0=XY[:, c, hs], scalar=coef[:, c:c + 1], in1=rstd_t[:, :],
                        op0=ALU.mult, op1=ALU.mult), reads=[XY_b, der_b, rstd_b], writes=[tb])
                    P.op("pool", lambda h, tmp=tmp, xin=xin, xo=xo: h.tensor_tensor(
                        out=xo[:, :], in0=xin[:, :], in1=tmp[:, :], op=ALU.add), reads=[tb, xinb], writes=[xob])
                    P.dma("sp", lambda h, xo=xo, c=c, gg=gg: h.dma_start(out=xT_v[:, c, gg * 512:(gg + 1) * 512], in_=xo[:, :]),
                          reads=[xob], writes=[xTw_b[gg]])

    def sgu_setup():
        st = {}
        st["wsT"] = sb("sgu_wsT", [128, 16, 128], BF16)
        st["bs"] = sb("sgu_bs", [128, 16, 128], F32)
        st["lng"] = sb("sgu_lng", [128, D], F32)
        st["lnb"] = sb("sgu_lnb", [128, D], F32)
        st["tri"] = sb("sgu_tri", [128, 128], F32)
        st["stat"] = sb("sgu_stat", [128, 8], F32)
        st["vtm"] = nc.alloc_sbuf_tensor_at("sgu_vtm", [128, 4, D], BF16, offset=big_off + 16 * TG * 2)
        st["b"] = {k: Buf("sgu_" + k) for k in ("wsT", "bs", "lng", "lnb", "vtm", "tri", "stat")}
        b = st["b"]
        P.dma("sp", lambda h: h.dma_start(out=st["tri"][:, :], in_=trimask[:, :]), writes=[b["tri"]])
        P.dma("sp", lambda h: h.dma_start(out=st["bs"][:, :, :].rearrange("p g t -> p (g t)"),
                                          in_=sgu_b_s[0, :].partition_broadcast(128)), writes=[b["bs"]])
        P.dma("sp", lambda h: h.dma_start(out=st["lng"][:, :], in_=sgu_ln_g[0, :].partition_broadcast(128)),
              writes=[b["lng"]])
        P.dma("sp", lambda h: h.dma_start(out=st["lnb"][:, :], in_=sgu_ln_b[0, :].partition_broadcast(128)),
              writes=[b["lnb"]])
        for gi in range(16):
            tmp, tb = tmps.next()
            P.dma("sp", lambda h, tmp=tmp, gi=gi: h.dma_start(out=tmp[:, 0:128], in_=sgu_w_s[0, gi, :, :]), writes=[tb])
            P.op("dve", lambda h, tmp=tmp: h.tensor_tensor(out=tmp[:, 128:256], in0=tmp[:, 0:128], in1=st["tri"][:, :],
                                                           op=ALU.mult), reads=[tb, b["tri"]], writes=[tb])
            pst, psb = PM
            P.op("pe", lambda h, tmp=tmp, pst=pst: h.transpose(pst[:, 0:128], tmp[:, 128:256], ident_t[:, :]),
                 reads=[tb, ident_b], writes=[psb])
            P.op("dve", lambda h, gi=gi, pst=pst: h.tensor_copy(out=st["wsT"][:, gi, :], in_=pst[:, 0:128]),
                 reads=[psb], writes=[b["wsT"]])
        return st

    def sgu_group(st, g):
        b = st["b"]
        load_xg(g)
        prenorm(der_t[:, 0, :], modT[:, 0:16])
        W = sgu_w_in[0]
        def evac_u(oc, pst, psb):
            P.op("act", lambda h: h.activation(out=big_t[:, oc, :], in_=pst[:, :], func=AF.Gelu),
                 reads=[psb], writes=[big_b])
        linear_fm(W, 0, 16, lambda c: hT_t[:, c, :], [hT_b], NCH, evac_u)
        zv4 = yT_t[:, :, :].rearrange("p c t -> p (c t)").rearrange("p (b d) -> p b d", b=4)
        for cg in range(4):
            slot, slot_b = wrot.next()
            view = load_w_cols(W, D + cg * 512, 512, slot, slot_b)
            for blk in range(4):
                pst, psb = PY.next()
                for c in range(NCH):
                    P.op("pe", mm(pst[:, :], hT_t[:, c, blk * 128:(blk + 1) * 128], view[:, c, :], c == 0, c == NCH - 1),
                         reads=[hT_b, slot_b], writes=[psb])
                P.op("act", lambda h, pst=pst, cg=cg, blk=blk: h.activation(
                    out=zv4[:, blk, cg * 512:(cg + 1) * 512], in_=pst[:, :], func=AF.Gelu), reads=[psb], writes=[yT_b])
        stat = st["stat"]
        for blk in range(4):
            zv = zv4[:, blk, :]
            P.op("dve", lambda h, zv=zv: h.tensor_reduce(out=stat[:, 0:1], in_=zv, axis=mybir.AxisListType.X, op=ALU.add),
                 reads=[yT_b], writes=[b["stat"]])
            P.op("act", lambda h, zv=zv, blk=blk: h.activation(out=st["vtm"][:, blk, :], in_=zv, func=AF.Square,
                                                               accum_out=stat[:, 1:2]),
                 reads=[yT_b], writes=[b["stat"], b["vtm"]])
            P.op("dve", lambda h: h.tensor_scalar_mul(out=stat[:, 2:3], in0=stat[:, 0:1], scalar1=1.0 / D),
                 reads=[b["stat"]], writes=[b["stat"]])
            P.op("dve", lambda h: h.tensor_tensor(out=stat[:, 3:4], in0=stat[:, 2:3], in1=stat[:, 2:3], op=ALU.mult),
                 reads=[b["stat"]], writes=[b["stat"]])
            P.op("dve", lambda h: h.scalar_tensor_tensor(out=stat[:, 4:5], in0=stat[:, 1:2], scalar=1.0 / D,
                                                         in1=stat[:, 3:4], op0=ALU.mult, op1=ALU.subtract),
                 reads=[b["stat"]], writes=[b["stat"]])
            P.op("act", lambda h: h.activation(out=stat[:, 5:6], in_=stat[:, 4:5], func=AF.Sqrt, bias=eps_t[:, 0:1],
                                               scale=1.0), reads=[b["stat"], ones_b], writes=[b["stat"]])
            P.op("dve", lambda h: h.reciprocal(out=stat[:, 6:7], in_=stat[:, 5:6]), reads=[b["stat"]], writes=[b["stat"]])
            P.op("dve", lambda h, zv=zv: h.tensor_scalar(out=zv, in0=zv, scalar1=stat[:, 2:3], scalar2=stat[:, 6:7],
                                                         op0=ALU.subtract, op1=ALU.mult), reads=[b["stat"], yT_b], writes=[yT_b])
            P.op("pool", lambda h, zv=zv: h.tensor_tensor(out=zv, in0=zv, in1=st["lng"][:, :], op=ALU.mult),
                 reads=[yT_b, b["lng"]], writes=[yT_b])
            P.op("dve", lambda h, zv=zv, blk=blk: h.tensor_tensor(out=st["vtm"][:, blk, :], in0=zv, in1=st["lnb"][:, :],
                                                                  op=ALU.add), reads=[yT_b, b["lnb"]], writes=[b["vtm"]])
        for gi in range(16):
            pst, psb = PY.next()
            for blk in range(4):
                P.op("pe", mm(pst[:, blk * 128:(blk + 1) * 128], st["vtm"][:, blk, gi * 128:(gi + 1) * 128],
                              st["wsT"][:, gi, :], True, True), reads=[b["vtm"], b["wsT"]], writes=[psb])
            tmp, tb = tmps.next()
            P.op("dve", lambda h, pst=pst, tmp=tmp, gi=gi: h.tensor_tensor(
                out=tmp[:, :].rearrange("p (b t) -> p b t", b=4), in0=pst[:, :].rearrange("p (b t) -> p b t", b=4),
                in1=st["bs"][:, gi:gi + 1, :].broadcast_to([128, 4, 128]), op=ALU.add),
                reads=[psb, b["bs"]], writes=[tb])
            P.op("pool", lambda h, tmp=tmp, gi=gi: h.tensor_tensor(out=big_t[:, gi, :], in0=big_t[:, gi, :], in1=tmp[:, :],
                                                                   op=ALU.mult), reads=[tb, big_b], writes=[big_b])
        def evac_y(oc, pst, psb):
            P.op("act", lambda h: h.activation(out=yT_t[:, oc, :], in_=pst[:, :], func=AF.Copy),
                 reads=[psb], writes=[yT_b])
        linear_fm(sgu_w_out[0], 0, 16, lambda c: big_t[:, c, :], [big_b], NCH, evac_y)
        postnorm_res(der_t[:, 1, :])
        store_xg(g)

    def at_alloc(state, name, shape, dt):
        nbytes = int(np.prod(shape[1:])) * (4 if dt in (F32, I32) else 2)
        off = (state["off"] + 31) // 32 * 32
        state["off"] = off + nbytes
        assert state["off"] <= attn_lim, (name, state["off"], attn_lim)
        state["n"] += 1
        return nc.alloc_sbuf_tensor_at("%s_%d" % (name, state["n"]), list(shape), dt, offset=off)

    def evac_to(dst_fn, dst_buf, func=None, eng="act"):
        def ev(oc, pst, psb):
            P.op("act", lambda h: h.activation(out=dst_fn(oc), in_=pst[0:dst_fn(oc).shape[0], :], func=AF.Copy),
                 reads=[psb], writes=[dst_buf])
        return ev

    def attn_out_phase(W2d):
        for g in range(NG):
            P.dma("sp", lambda h, g=g: h.dma_start(
                out=big_t[:, 0:16, :], in_=att_s.rearrange("(c p) t -> p c t", p=128)[:, :, g * TG:(g + 1) * TG]),
                reads=[att_b], writes=[big_b])
            load_xg(g)
            linear_fm(W2d, 0, 16, lambda c: big_t[:, c, :], [big_b], NCH,
                      evac_to(lambda oc: yT_t[:, oc, :], yT_b))
            postnorm_res(der_t[:, 1, :])
            store_xg(g)

    def all_gather(src, src_b, dst, dst_b):
        P.dma("pool", lambda h: h.collective_compute("AllGather", ALU.bypass, replica_groups=RG,
                                                     ins=[src.ap().opt()], outs=[dst.ap().opt()]),
              reads=[src_b], writes=[dst_b], inc=1)

    def fox_layer(l, j):
        dr = fdr
        db = dr["b"]
        W = fox_w_in[j]
        kin_v = [t.ap().rearrange("(c p) t -> p c t", p=128) for t in dr["kin"]]
        vin_v = [t.ap().rearrange("p (h i d) -> p h i d", h=2, i=16) for t in dr["vin"]]
        lin_v = dr["lin"].ap().rearrange("(i p) h -> p i h", p=128)
        ph = {"off": phase_base, "n": 100 * l}
        bf_t = sb("fox_bf%d" % l, [128, 16], F32)
        vst_t = sb("fox_vst%d" % l, [128, 4, 512], BF16)
        lf_t = sb("fox_lf%d" % l, [128, 4, 16], F32)
        bf_b, vst_b, lf_b = Buf("bf"), Buf("vst"), Buf("lf")
        P.dma("sp", lambda h: h.dma_start(out=bf_t[:, :], in_=fox_b_f[j, :].partition_broadcast(128)), writes=[bf_b])
        for g in range(NG):
            load_xg(g)
            prenorm(der_t[:, 0, :], modT[:, 0:16])
            linear_fm(W, 0, 16, lambda c: hT_t[:, c, :], [hT_b], NCH, evac_to(lambda oc: big_t[:, oc, :], big_b))
            P.dma("sp", lambda h, g=g: h.dma_start(
                out=q_s.rearrange("(c p) t -> p c t", p=128)[:, :, g * TG:(g + 1) * TG], in_=big_t[:, 0:16, :]),
                reads=[big_b], writes=[q_b])
            linear_fm(W, D, 16, lambda c: hT_t[:, c, :], [hT_b], NCH, evac_to(lambda oc: big_t[:, 16 + oc, :], big_b))
            for ck in range(8):
                P.dma("sp", lambda h, g=g, ck=ck: h.dma_start(out=kin_v[ck][:, :, g * TG:(g + 1) * TG],
                                                               in_=big_t[:, 16 + 2 * ck:18 + 2 * ck, :]),
                      reads=[big_b], writes=[db["kin"]])
            for cg in range(4):
                slot, slot_b = wrot.next()
                view = load_w_cols(W, 2 * D + cg * 512, 512, slot, slot_b)
                for blk in range(4):
                    pst, psb = PY.next()
                    for c in range(NCH):
                        P.op("pe", mm(pst[:, :], hT_t[:, c, blk * 128:(blk + 1) * 128], view[:, c, :], c == 0, c == NCH - 1),
                             reads=[hT_b, slot_b], writes=[psb])
                    P.op("act", lambda h, pst=pst, blk=blk: h.activation(out=vst_t[:, blk, :], in_=pst[:, :], func=AF.Copy),
                         reads=[psb], writes=[vst_b])
                for hh in range(4):
                    P.dma("sp", lambda h, g=g, cg=cg, hh=hh: h.dma_start(
                        out=vin_v[(cg * 4 + hh) // 2][:, (cg * 4 + hh) % 2, g * 4:(g + 1) * 4, :],
                        in_=vst_t[:, :, hh * 128:(hh + 1) * 128]), reads=[vst_b], writes=[db["vin"]])
            slot, slot_b = wrot.next()
            view = load_w_cols(W, 3 * D, 16, slot, slot_b)
            for blk in range(4):
                pst, psb = PY.next()
                for c in range(NCH):
                    P.op("pe", mm(pst[:, 0:16], hT_t[:, c, blk * 128:(blk + 1) * 128], view[:, c, :], c == 0, c == NCH - 1),
                         reads=[hT_b, slot_b], writes=[psb])
                P.op("dve", lambda h, pst=pst, blk=blk: h.tensor_tensor(out=lf_t[:, blk, :], in0=pst[:, 0:16], in1=bf_t[:, :],
                                                                        op=ALU.add), reads=[psb, bf_b], writes=[lf_b])
            lf2 = lf_t[:, :, :].rearrange("p b h -> p (b h)")
            P.op("act", lambda h: h.activation(out=lf2, in_=lf2, func=AF.Exp, scale=-1.0), reads=[lf_b], writes=[lf_b])
            P.op("act", lambda h: h.activation(out=lf2, in_=lf2, func=AF.Ln, bias=1.0, scale=1.0), reads=[lf_b], writes=[lf_b])
            P.op("dve", lambda h: h.tensor_scalar_mul(out=lf2, in0=lf2, scalar1=-1.0), reads=[lf_b], writes=[lf_b])
            P.dma("sp", lambda h, g=g: h.dma_start(out=lin_v[:, g * 4:(g + 1) * 4, :], in_=lf_t[:, :, :]),
                  reads=[lf_b], writes=[db["lin"]])
        all_gather(dr["lin"], db["lin"], dr["lout"], db["lout"])
        for ck in range(8):
            all_gather(dr["kin"][ck], db["kin"], dr["kout"][ck], db["kout"][ck])
            all_gather(dr["vin"][ck], db["vin"], dr["vout"][ck], db["vout"][ck])
        P.barrier(exclude=db["kout"] + db["vout"])
        A = {"off": xg_off, "n": 1000 * (l + 1)}
        lfa = at_alloc(A, "lfa", [128, 64, 16], F32)
        ftm = at_alloc(A, "ftm", [128, 64, 16], F32)
        tbc = at_alloc(A, "tbc", [128, 64, 16], F32)
        pfa = at_alloc(A, "pfa", [128, 64, 16], F32)
        pfb = at_alloc(A, "pfb", [128, 64, 16], F32)
        fref = at_alloc(A, "fref", [128, 16, 16], F32)
        triu_t = at_alloc(A, "triu", [128, 128], F32)
        oh4_t = at_alloc(A, "oh4", [128, 4], F32)
        mfox_t = at_alloc(A, "mfox", [128, 4, 128], BF16)
        zo_t = at_alloc(A, "zo", [128, 16, 64], F32)
        qh = [at_alloc(A, "qh%d" % i, [128, TOK], BF16) for i in range(2)]
        kh = [at_alloc(A, "kh%d" % i, [128, 4, TOK], BF16) for i in range(2)]
        vh = [at_alloc(A, "vh%d" % i, [128, 4, 16, 128], BF16) for i in range(2)]
        bh = [at_alloc(A, "bh%d" % i, [128, 16, 64], F32) for i in range(2)]
        ah = [at_alloc(A, "ah%d" % i, [128, TOK], BF16) for i in range(2)]
        pts = Rot([(at_alloc(A, "pt%d" % i, [128, 512], BF16), [Buf("pt%d_%d" % (i, m)) for m in range(4)]) for i in range(4)])
        rec_t = at_alloc(A, "rec", [128, 512], F32)
        B_ = {k: Buf("fx_" + k) for k in ("lfa", "ftm", "tbc", "pfa", "pfb", "fref", "triu", "oh4", "mfox", "rec",
                                          "qh0", "qh1", "kh0", "kh1", "vh0", "vh1", "bh0", "bh1", "ah0", "ah1")}
        P.dma("sp", lambda h: h.dma_start(out=triu_t[:, :], in_=triu[:, :]), writes=[B_["triu"]])
        P.dma("sp", lambda h: h.dma_start(out=oh4_t[:, :], in_=oh4_in[:, :]), writes=[B_["oh4"]])
        P.dma("sp", lambda h: h.dma_start(out=zo_t[:, :, :].rearrange("p a b -> p (a b)"), in_=zo_in[:, :]), writes=[B_["oh4"]])
        P.dma("pool", lambda h: h.dma_start(out=mfox_t[:, :, :].rearrange("p j q -> p (j q)"), in_=m_fox_in[:, :]),
              writes=[B_["mfox"]])
        lout_ap = dr["lout"].ap()
        for rr in range(4):
            P.dma("sp", lambda h, rr=rr: h.dma_start(
                out=lfa[:, :, :].rearrange("p (i r) h -> p r i h", r=4)[:, rr],
                in_=lout_ap[rr * TOK:(rr + 1) * TOK, :].rearrange("(i p) h -> p i h", p=128)),
                reads=[db["lout"]], writes=[B_["lfa"]])
        lfa2 = lfa[:, :, :].rearrange("p g h -> p (g h)")
        ftm2 = ftm[:, :, :].rearrange("p g h -> p (g h)")
        tbc2 = tbc[:, :, :].rearrange("p g h -> p (g h)")
        pfa2 = pfa[:, :, :].rearrange("p g h -> p (g h)")
        pfb2 = pfb[:, :, :].rearrange("p g h -> p (g h)")
        for half in range(2):
            cs = slice(half * 512, (half + 1) * 512)
            pst, psb = PY.next()
            P.op("pe", mm(pst[:, :], triu_t[:, :], lfa2[:, cs], True, True), reads=[B_["triu"], B_["lfa"]], writes=[psb])
            P.op("dve", lambda h, pst=pst, cs=cs: h.tensor_copy(out=ftm2[:, cs], in_=pst[:, :]), reads=[psb], writes=[B_["ftm"]])
            pst, psb = PY.next()
            P.op("pe", mm(pst[:, :], ones_t[:, :], lfa2[:, cs], True, True), reads=[ones_b, B_["lfa"]], writes=[psb])
            P.op("dve", lambda h, pst=pst, cs=cs: h.tensor_copy(out=tbc2[:, cs], in_=pst[:, :]), reads=[psb], writes=[B_["tbc"]])
        P.op("dve", lambda h: h.tensor_copy(out=pfa2, in_=tbc2), reads=[B_["tbc"]], writes=[B_["pfa"]])
        cur, curb, oth, othb = pfa2, B_["pfa"], pfb2, B_["pfb"]
        sh = 1
        while sh < 64:
            w = sh * 16
            P.op("dve", lambda h, cur=cur, oth=oth, w=w: h.tensor_copy(out=oth[:, 0:w], in_=cur[:, 0:w]),
                 reads=[curb], writes=[othb])
            P.op("dve", lambda h, cur=cur, oth=oth, w=w: h.tensor_tensor(out=oth[:, w:1024], in0=cur[:, w:1024],
                                                                         in1=cur[:, 0:1024 - w], op=ALU.add),
                 reads=[curb], writes=[othb])
            cur, curb, oth, othb = oth, othb, cur, curb
            sh *= 2
        P.op("dve", lambda h, cur=cur, oth=oth: h.tensor_tensor(out=oth, in0=cur, in1=tbc2, op=ALU.subtract),
             reads=[curb, B_["tbc"]], writes=[othb])
        P.op("dve", lambda h, oth=oth: h.tensor_tensor(out=ftm2, in0=ftm2, in1=oth, op=ALU.add),
             reads=[othb, B_["ftm"]], writes=[B_["ftm"]])
        P.op("dve", lambda h, cur=cur, oth=oth: h.scalar_tensor_tensor(out=oth, in0=tbc2, scalar=-0.5, in1=cur,
                                                                        op0=ALU.mult, op1=ALU.add),
             reads=[curb, B_["tbc"]], writes=[othb])
        fmid4 = (pfa if oth is pfa2 else pfb)[:, :, :].rearrange("p (i r) h -> p r i h", r=4)
        P.op("dve", lambda h: h.tensor_scalar(out=fref[:, :, :], in0=fmid4[:, 0], scalar1=oh4_t[:, 0:1], scalar2=None,
                                              op0=ALU.mult), reads=[othb, B_["oh4"]], writes=[B_["fref"]])
        for rr in range(1, 4):
            P.op("dve", lambda h, rr=rr: h.scalar_tensor_tensor(out=fref[:, :, :], in0=fmid4[:, rr], scalar=oh4_t[:, rr:rr + 1],
                                                                in1=fref[:, :, :], op0=ALU.mult, op1=ALU.add),
                 reads=[othb, B_["oh4"], B_["fref"]], writes=[B_["fref"]])
        kout_v = [t.ap().rearrange("(r c d) t -> d r c t", r=4, d=128) for t in dr["kout"]]
        vout_v = [t.ap().rearrange("(r p) (h i d) -> p r h i d", r=4, h=2, i=16) for t in dr["vout"]]
        PS_ST = Rot([psum[0], psum[1], psum[2], psum[7]])
        PS_O = Rot([psum[3], psum[4]])
        PS_D = Rot([psum[5], psum[6]])
        scale = 128.0 ** -0.5
        def head_res(hd):
            s2 = hd % 2
            return (qh[s2], kh[s2], vh[s2], bh[s2], ah[s2],
                    B_["qh%d" % s2], B_["kh%d" % s2], B_["vh%d" % s2], B_["bh%d" % s2], B_["ah%d" % s2])

        def emit_head_loads(hd):
            qt, kt, vt, bt, at, qb, kb, vb, bb, ab = head_res(hd)
            P.dma("sp", lambda h: h.dma_start(out=qt[:, :], in_=q_s[hd * 128:(hd + 1) * 128, :]), reads=[q_b], writes=[qb])
            P.dma("sp", lambda h: h.dma_start(out=kt[:, :, :], in_=kout_v[hd // 2][:, :, hd % 2, :]),
                  reads=[db["kout"][hd // 2]], writes=[kb])
            P.dma("sp", lambda h: h.dma_start(out=vt[:, :, :, :], in_=vout_v[hd // 2][:, :, hd % 2]),
                  reads=[db["vout"][hd // 2]], writes=[vb])
            P.op("dve", lambda h: h.tensor_tensor(
                out=bt[:, :, :], in0=fref[:, :, hd:hd + 1].broadcast_to([128, 16, 64]),
                in1=ftm[:, :, hd:hd + 1].rearrange("p g o -> p o g").broadcast_to([128, 16, 64]), op=ALU.subtract),
                reads=[B_["fref"], B_["ftm"]], writes=[bb])
            P.op("dve", lambda h: h.tensor_tensor(out=bt[:, :, :], in0=bt[:, :, :], in1=zo_t[:, :, :], op=ALU.add),
                 reads=[bb, B_["oh4"]], writes=[bb])

        steps = [(hd, jq, gk) for hd in range(16) for jq in range(4) for gk in range(16 * (jq + 1))]
        nst = len(steps)
        qk = {}
        acc = {}

        def emit_qk(idx):
            hd, jq, gk = steps[idx]
            qt, kt, vt, bt, at, qb, kb, vb, bb, ab = head_res(hd)
            rr, ii = gk % 4, gk // 4
            mmin = max(0, -(-(gk - 3 - 16 * jq) // 4))
            c0 = mmin * 128
            pst, psb = PS_ST.next()
            P.op("pe", mm(pst[:, c0:512], kt[:, rr, ii * 128:(ii + 1) * 128], qt[:, jq * 512 + c0:(jq + 1) * 512],
                          True, True), reads=[kb, qb], writes=[psb])
            qk[idx] = (pst, psb, mmin, c0)

        def emit_exp(idx):
            hd, jq, gk = steps[idx]
            qt, kt, vt, bt, at, qb, kb, vb, bb, ab = head_res(hd)
            pst, psb, mmin, c0 = qk[idx]
            pt, ptb = pts.next()
            for m in range(mmin, 4):
                li = 4 * jq + m
                P.op("act", lambda h, m=m, li=li: h.activation(
                    out=pt[:, m * 128:(m + 1) * 128], in_=pst[:, m * 128:(m + 1) * 128], func=AF.Exp,
                    bias=bt[:, li, gk:gk + 1], scale=scale), reads=[psb, bb], writes=[ptb[m]])
                jm = gk - 4 * li
                if 0 <= jm <= 3:
                    P.op("pool", lambda h, m=m, jm=jm: h.tensor_tensor(
                        out=pt[:, m * 128:(m + 1) * 128], in0=pt[:, m * 128:(m + 1) * 128], in1=mfox_t[:, jm, :],
                        op=ALU.mult), reads=[ptb[m], B_["mfox"]], writes=[ptb[m]])
            qk[idx] = (pst, psb, mmin, c0, pt, ptb)

        def emit_pv(idx):
            hd, jq, gk = steps[idx]
            qt, kt, vt, bt, at, qb, kb, vb, bb, ab = head_res(hd)
            pst, psb, mmin, c0, pt, ptb = qk.pop(idx)
            rr, ii = gk % 4, gk // 4
            ng = 16 * (jq + 1)
            if gk == 0:
                acc[(hd, jq)] = (PS_O.next(), PS_D.next())
            (po, pob), (pd, pdb) = acc[(hd, jq)]
            P.op("pe", mm(po[:, c0:512], vt[:, rr, ii, :], pt[:, c0:512], gk == 0, gk == ng - 1),
                 reads=[vb] + ptb[mmin:], writes=[pob])
            P.op("pe", mm(pd[:, c0:512], onesb_t[:, :], pt[:, c0:512], gk == 0, gk == ng - 1),
                 reads=[ones_b] + ptb[mmin:], writes=[pdb])
            if gk == ng - 1:
                del acc[(hd, jq)]
                P.op("dve", lambda h: h.reciprocal(out=rec_t[:, :], in_=pd[:, :]), reads=[pdb], writes=[B_["rec"]])
                P.op("dve", lambda h: h.tensor_tensor(out=at[:, jq * 512:(jq + 1) * 512], in0=po[:, :],
                                                      in1=rec_t[:, :], op=ALU.mult),
                     reads=[pob, B_["rec"]], writes=[ab])
                if jq == 3:
                    P.dma("sp", lambda h: h.dma_start(out=att_s[hd * 128:(hd + 1) * 128, :], in_=at[:, :]),
                          reads=[ab], writes=[att_b])
                    if hd + 2 < 16:
                        emit_head_loads(hd + 2)

        emit_head_loads(0)
        emit_head_loads(1)
        emit_qk(0)
        emit_qk(1)
        emit_qk(2)
        for idx in range(nst):
            emit_exp(idx)
            if idx + 3 < nst:
                emit_qk(idx + 3)
            emit_pv(idx)
        P.barrier()
        attn_out_phase(fox_w_out[j])
        P.barrier()

    def swa_layer(l):
        dr = swa_dr
        db = dr["b"]
        W = swa_w_in[0]
        kin_v = [t.ap().rearrange("(h d) t -> d h t", d=64) for t in dr["kin"]]
        vin_v = [t.ap().rearrange("p (i f) -> p i f", i=8) for t in dr["vin"]]
        q_v = q_s.rearrange("(h d) t -> d h t", d=64)
        att_v = att_s.rearrange("(h d) t -> d h t", d=64)
        c16 = sb("swa_c16", [16, TOK], F32)
        s16 = sb("swa_s16", [16, TOK], F32)
        pmat_t = sb("swa_pmat", [64, 16], BF16)
        invf_t = sb("swa_invf", [16, 1], F32)
        vst_t = sb("swa_vst", [128, 4, 512], BF16)
        SB_ = {k: Buf("sw_" + k) for k in ("c16", "s16", "pmat", "invf", "vst", "posi")}
        posi_t = sb("swa_posi", [16, TOK], I32)
        P.dma("sp", lambda h: h.dma_start(out=posi_t[:, :], in_=posin[0, :].partition_broadcast(16)), writes=[SB_["posi"]])
        P.dma("pool", lambda h: h.dma_start(out=pmat_t[:, :], in_=pmat_in[:, :]), writes=[SB_["pmat"]])
        P.dma("sp", lambda h: h.dma_start(out=invf_t[:, :], in_=invf_in[:, :]), writes=[SB_["invf"]])
        PI = float(np.pi)
        C1 = 6.28125
        C2 = float(2 * np.pi - 6.28125)
        ang = yT_t[0:16, 8:12, :].rearrange("p c t -> p (c t)")
        nf = yT_t[0:16, 0:4, :].rearrange("p c t -> p (c t)")
        mk = yT_t[0:16, 4:8, :].rearrange("p c t -> p (c t)")
        ys, yc = s16[:, :], c16[:, :]
        RW = dict(reads=[SB_["posi"], SB_["invf"], yT_b, SB_["s16"], SB_["c16"]],
                  writes=[yT_b, SB_["s16"], SB_["c16"], SB_["posi"]])
        P.op("dve", lambda h: h.tensor_copy(out=ang, in_=posi_t[:, :]), **RW)
        P.op("dve", lambda h: h.tensor_scalar_mul(out=ang, in0=ang, scalar1=invf_t[:, 0:1]), **RW)
        P.op("dve", lambda h: h.tensor_scalar_mul(out=nf, in0=ang, scalar1=float(1.0 / (2 * np.pi))), **RW)
        P.op("dve", lambda h: h.tensor_copy(out=posi_t[:, :], in_=nf), **RW)
        P.op("dve", lambda h: h.tensor_copy(out=nf, in_=posi_t[:, :]), **RW)
        P.op("dve", lambda h: h.scalar_tensor_tensor(out=ys, in0=nf, scalar=-C1, in1=ang, op0=ALU.mult, op1=ALU.add), **RW)
        P.op("dve", lambda h: h.scalar_tensor_tensor(out=ys, in0=nf, scalar=-C2, in1=ys, op0=ALU.mult, op1=ALU.add), **RW)
        P.op("dve", lambda h: h.tensor_single_scalar(out=mk, in_=ys, scalar=PI, op=ALU.is_gt), **RW)
        P.op("dve", lambda h: h.scalar_tensor_tensor(out=ys, in0=mk, scalar=-2 * PI, in1=ys, op0=ALU.mult, op1=ALU.add), **RW)
        P.op("dve", lambda h: h.tensor_single_scalar(out=mk, in_=ys, scalar=-PI, op=ALU.is_lt), **RW)
        P.op("dve", lambda h: h.scalar_tensor_tensor(out=ys, in0=mk, scalar=2 * PI, in1=ys, op0=ALU.mult, op1=ALU.add), **RW)
        P.op("dve", lambda h: h.tensor_scalar_add(out=yc, in0=ys, scalar1=PI / 2), **RW)
        P.op("dve", lambda h: h.tensor_single_scalar(out=mk, in_=yc, scalar=PI, op=ALU.is_gt), **RW)
        P.op("dve", lambda h: h.scalar_tensor_tensor(out=yc, in0=mk, scalar=-2 * PI, in1=yc, op0=ALU.mult, op1=ALU.add), **RW)
        P.op("act", lambda h: h.activation(out=ys, in_=ys, func=AF.Sin), reads=[SB_["s16"]], writes=[SB_["s16"]])
        P.op("act", lambda h: h.activation(out=yc, in_=yc, func=AF.Sin), reads=[SB_["c16"]], writes=[SB_["c16"]])

        def rope(tile_fn, nheads, g):
            for hh in range(nheads):
                pst, psb = PM
                P.op("pe", mm(pst[0:16, :], pmat_t[:, :], tile_fn(hh), True, True), reads=[SB_["pmat"], big_b], writes=[psb])
                t1, t1b = tmps.next()
                t2, t2b = tmps.next()
                P.op("dve", lambda h, t1=t1, hh=hh: h.tensor_tensor(out=t1[0:16, :], in0=tile_fn(hh)[0:16, :],
                                                                    in1=c16[:, g * TG:(g + 1) * TG], op=ALU.mult),
                     reads=[big_b, SB_["c16"]], writes=[t1b])
                P.op("dve", lambda h, t2=t2, pst=pst: h.tensor_tensor(out=t2[0:16, :], in0=pst[0:16, :],
                                                                      in1=s16[:, g * TG:(g + 1) * TG], op=ALU.mult),
                     reads=[psb, SB_["s16"]], writes=[t2b])
                P.op("pool", lambda h, t1=t1, t2=t2, hh=hh: h.tensor_tensor(out=tile_fn(hh)[0:16, :], in0=t1[0:16, :],
                                                                            in1=t2[0:16, :], op=ALU.add),
                     reads=[t1b, t2b], writes=[big_b])

        for g in range(NG):
            load_xg(g)
            prenorm(der_t[:, 0, :], modT[:, 0:16])
            linear_fm(W, 0, 32, lambda c: hT_t[:, c, :], [hT_b], NCH, evac_to(lambda oc: big_t[0:64, oc, :], big_b),
                      ocw=64, per_load=8)
            rope(lambda hh: big_t[0:64, hh, :], 32, g)
            P.dma("sp", lambda h, g=g: h.dma_start(out=q_v[:, :, g * TG:(g + 1) * TG], in_=big_t[0:64, 0:32, :]),
                  reads=[big_b], writes=[q_b])
            linear_fm(W, D, 8, lambda c: hT_t[:, c, :], [hT_b], NCH, evac_to(lambda oc: big_t[0:64, 32 + oc, :], big_b),
                      ocw=64, per_load=8)
            rope(lambda hh: big_t[0:64, 32 + hh, :], 8, g)
            for ck in range(2):
                P.dma("sp", lambda h, g=g, ck=ck: h.dma_start(out=kin_v[ck][:, :, g * TG:(g + 1) * TG],
                                                               in_=big_t[0:64, 32 + 4 * ck:36 + 4 * ck, :]),
                      reads=[big_b], writes=[db["kin"]])
            slot, slot_b = wrot.next()
            view = load_w_cols(W, D + 512, 512, slot, slot_b)
            for blk in range(4):
                pst, psb = PY.next()
                for c in range(NCH):
                    P.op("pe", mm(pst[:, :], hT_t[:, c, blk * 128:(blk + 1) * 128], view[:, c, :], c == 0, c == NCH - 1),
                         reads=[hT_b, slot_b], writes=[psb])
                P.op("act", lambda h, pst=pst, blk=blk: h.activation(out=vst_t[:, blk, :], in_=pst[:, :], func=AF.Copy),
                     reads=[psb], writes=[SB_["vst"]])
            P.dma("sp", lambda h, g=g: h.dma_start(out=vin_v[g // 2][:, (g % 2) * 4:(g % 2) * 4 + 4, :], in_=vst_t[:, :, :]),
                  reads=[SB_["vst"]], writes=[db["vin"]])
        for ck in range(2):
            all_gather(dr["kin"][ck], db["kin"], dr["kout"][ck], db["kout"])
            all_gather(dr["vin"][ck], db["vin"], dr["vout"][ck], db["vout"])
        P.barrier()
        A = {"off": xg_off, "n": 5000}
        kc2 = [at_alloc(A, "kc%d" % i, [64, 8, 128], BF16) for i in range(2)]
        vc2 = [at_alloc(A, "vc%d" % i, [128, 512], BF16) for i in range(2)]
        kp2_ = [at_alloc(A, "kp%d" % i, [64, 8, 128], BF16) for i in range(2)]
        vp2_ = [at_alloc(A, "vp%d" % i, [128, 512], BF16) for i in range(2)]
        qb2 = [at_alloc(A, "qblk%d" % i, [64, 32, 128], BF16) for i in range(2)]
        ab2 = [at_alloc(A, "ablk%d" % i, [64, 32, 128], BF16) for i in range(2)]
        kcand = at_alloc(A, "kcand", [64, 5, 8, 128], BF16)
        vcand = at_alloc(A, "vcand", [128, 5, 512], BF16)
        mtri = at_alloc(A, "mtri", [128, 128], BF16)
        mprev = at_alloc(A, "mprev", [128, 128], BF16)
        mprev0 = at_alloc(A, "mprev0", [128, 128], BF16)
        sel5 = at_alloc(A, "sel5", [128, 5], F32)
        sinke = at_alloc(A, "sinke", [64, 32], F32)
        den_t = at_alloc(A, "den", [64, 512], F32)
        ptc = Rot([(at_alloc(A, "ptc%d" % i, [128, 512], BF16), Buf("ptc%d" % i)) for i in range(2)])
        ptp = Rot([(at_alloc(A, "ptp%d" % i, [128, 512], BF16), Buf("ptp%d" % i)) for i in range(2)])
        B_ = {k: Buf("sw3_" + k) for k in ("kc0", "kc1", "vc0", "vc1", "kcand", "vcand", "kp0", "kp1", "vp0", "vp1",
                                           "qblk0", "qblk1", "ablk0", "ablk1", "mtri", "mprev",
                                           "mprev0", "sel5", "sinke", "den")}
        P.dma("pool", lambda h: h.dma_start(out=mtri[:, :], in_=triu[:, :]), writes=[B_["mtri"]])
        P.dma("pool", lambda h: h.dma_start(out=mprev[:, :], in_=m_prev_in[:, :]), writes=[B_["mprev"]])
        P.dma("pool", lambda h: h.dma_start(out=mprev0[:, :], in_=m_prev0_in[:, :]), writes=[B_["mprev0"]])
        P.dma("sp", lambda h: h.dma_start(out=sel5[:, :], in_=sel5_in[:, :]), writes=[B_["sel5"]])
        P.dma("sp", lambda h: h.dma_start(out=sinke[:, :], in_=swa_sinks[0, :].partition_broadcast(64)), writes=[B_["sinke"]])
        P.op("act", lambda h: h.activation(out=sinke[:, :], in_=sinke[:, :], func=AF.Exp), reads=[B_["sinke"]], writes=[B_["sinke"]])
        kout_v = [t.ap().rearrange("(r h d) t -> d r h t", r=4, d=64) for t in dr["kout"]]
        vout_v = [t.ap().rearrange("(r p) (i f) -> p r i f", r=4, i=8) for t in dr["vout"]]
        PS_C = Rot([psum[0], psum[1]])
        PS_P = Rot([psum[2], psum[3]])
        PS_O = Rot([psum[4], psum[5]])
        PS_D = Rot([psum[6], psum[7]])
        scale = 64.0 ** -0.5

        def emit_block_loads(i):
            par = i % 2
            kc, vc, kp, vp, qblk = kc2[par], vc2[par], kp2_[par], vp2_[par], qb2[par]
            kcb, vcb, kpb, vpb, qbb = (B_["kc%d" % par], B_["vc%d" % par], B_["kp%d" % par], B_["vp%d" % par],
                                       B_["qblk%d" % par])
            ip = max(i - 1, 0)
            for ck in range(2):
                P.dma("sp", lambda h, ck=ck: h.dma_start(out=kc[:, 4 * ck:4 * ck + 4, :],
                                                         in_=kin_v[ck][:, :, i * 128:(i + 1) * 128]),
                      reads=[db["kin"]], writes=[kcb])
                for rr in range(4):
                    P.dma("sp", lambda h, ck=ck, rr=rr: h.dma_start(
                        out=kcand[:, rr, 4 * ck:4 * ck + 4, :], in_=kout_v[ck][:, rr, :, i * 128:(i + 1) * 128]),
                        reads=[db["kout"]], writes=[B_["kcand"]])
                P.dma("sp", lambda h, ck=ck: h.dma_start(
                    out=kcand[:, 4, 4 * ck:4 * ck + 4, :], in_=kout_v[ck][:, 3, :, ip * 128:(ip + 1) * 128]),
                    reads=[db["kout"]], writes=[B_["kcand"]])
            P.dma("sp", lambda h: h.dma_start(out=vc[:, :], in_=vin_v[i // 8][:, i % 8, :]), reads=[db["vin"]], writes=[vcb])
            P.dma("sp", lambda h: h.dma_start(out=vcand[:, 0:4, :], in_=vout_v[i // 8][:, :, i % 8, :]),
                  reads=[db["vout"]], writes=[B_["vcand"]])
            P.dma("sp", lambda h: h.dma_start(out=vcand[:, 4, :], in_=vout_v[ip // 8][:, 3, ip % 8, :]),
                  reads=[db["vout"]], writes=[B_["vcand"]])
            P.dma("sp", lambda h: h.dma_start(out=qblk[:, :, :], in_=q_v[:, :, i * 128:(i + 1) * 128]),
                  reads=[q_b], writes=[qbb])
            kpf = kp[:, :, :].rearrange("d h t -> d (h t)")
            P.op("dve", lambda h: h.tensor_scalar(out=kpf, in0=kcand[:, 0, :, :].rearrange("d h t -> d (h t)"),
                                                  scalar1=sel5[0:64, 0:1], scalar2=None, op0=ALU.mult),
                 reads=[B_["kcand"], B_["sel5"]], writes=[kpb])
            P.op("dve", lambda h: h.tensor_scalar(out=vp[:, :], in0=vcand[:, 0, :], scalar1=sel5[:, 0:1], scalar2=None,
                                                  op0=ALU.mult), reads=[B_["vcand"], B_["sel5"]], writes=[vpb])
            for cnd in range(1, 5):
                P.op("dve", lambda h, cnd=cnd: h.scalar_tensor_tensor(
                    out=kpf, in0=kcand[:, cnd, :, :].rearrange("d h t -> d (h t)"), scalar=sel5[0:64, cnd:cnd + 1], in1=kpf,
                    op0=ALU.mult, op1=ALU.add), reads=[B_["kcand"], B_["sel5"], kpb], writes=[kpb])
                P.op("dve", lambda h, cnd=cnd: h.scalar_tensor_tensor(
                    out=vp[:, :], in0=vcand[:, cnd, :], scalar=sel5[:, cnd:cnd + 1], in1=vp[:, :],
                    op0=ALU.mult, op1=ALU.add), reads=[B_["vcand"], B_["sel5"], vpb], writes=[vpb])

        sw_steps = [(i, hk) for i in range(NBLK) for hk in range(8)]
        sA = {}

        def stageA(si):
            i, hk = sw_steps[si]
            par = i % 2
            qsl = qb2[par][:, hk * 4:(hk + 1) * 4, :]
            pc, pcb = PS_C.next()
            pp, ppb = PS_P.next()
            P.op("pe", mm(pc[:, :], kc2[par][:, hk, :], qsl, True, True), reads=[B_["kc%d" % par], B_["qblk%d" % par]], writes=[pcb])
            P.op("pe", mm(pp[:, :], kp2_[par][:, hk, :], qsl, True, True), reads=[B_["kp%d" % par], B_["qblk%d" % par]], writes=[ppb])
            sA[si] = (pc, pcb, pp, ppb)

        def stageB(si):
            i, hk = sw_steps[si]
            par = i % 2
            vc, vp, ablk = vc2[par], vp2_[par], ab2[par]
            vcb, vpb, abb = B_["vc%d" % par], B_["vp%d" % par], B_["ablk%d" % par]
            pc, pcb, pp, ppb = sA.pop(si)
            mpv, mpvb = (mprev0, B_["mprev0"]) if i == 0 else (mprev, B_["mprev"])
            tc_, tcb = ptc.next()
            tp_, tpb = ptp.next()
            P.op("act", lambda h: h.activation(out=tc_[:, :], in_=pc[:, :], func=AF.Exp, scale=scale), reads=[pcb], writes=[tcb])
            P.op("act", lambda h: h.activation(out=tp_[:, :], in_=pp[:, :], func=AF.Exp, scale=scale), reads=[ppb], writes=[tpb])
            P.op("pool", lambda h: h.tensor_tensor(
                out=tc_[:, :].rearrange("k (a q) -> k a q", a=4), in0=tc_[:, :].rearrange("k (a q) -> k a q", a=4),
                in1=mtri[:, :].rearrange("k (o q) -> k o q", o=1).broadcast_to([128, 4, 128]), op=ALU.mult),
                reads=[tcb, B_["mtri"]], writes=[tcb])
            P.op("dve", lambda h: h.tensor_tensor(
                out=tp_[:, :].rearrange("k (a q) -> k a q", a=4), in0=tp_[:, :].rearrange("k (a q) -> k a q", a=4),
                in1=mpv[:, :].rearrange("k (o q) -> k o q", o=1).broadcast_to([128, 4, 128]), op=ALU.mult),
                reads=[tpb, mpvb], writes=[tpb])
            po, pob = PS_O.next()
            pd, pdb = PS_D.next()
            P.op("pe", mm(po[0:64, :], vc[:, hk * 64:(hk + 1) * 64], tc_[:, :], True, False), reads=[vcb, tcb], writes=[pob])
            P.op("pe", mm(po[0:64, :], vp[:, hk * 64:(hk + 1) * 64], tp_[:, :], False, True), reads=[vpb, tpb], writes=[pob])
            P.op("pe", mm(pd[0:64, :], onesb_t[:, 0:64], tc_[:, :], True, False), reads=[ones_b, tcb], writes=[pdb])
            P.op("pe", mm(pd[0:64, :], onesb_t[:, 0:64], tp_[:, :], False, True), reads=[ones_b, tpb], writes=[pdb])
            P.op("dve", lambda h: h.tensor_tensor(
                out=den_t[:, :].rearrange("d (a q) -> d a q", a=4), in0=pd[0:64, :].rearrange("d (a q) -> d a q", a=4),
                in1=sinke[:, hk * 4:(hk + 1) * 4].rearrange("d (a o) -> d a o", o=1).broadcast_to([64, 4, 128]), op=ALU.add),
                reads=[pdb, B_["sinke"]], writes=[B_["den"]])
            P.op("dve", lambda h: h.reciprocal(out=den_t[:, :], in_=den_t[:, :]), reads=[B_["den"]], writes=[B_["den"]])
            P.op("dve", lambda h: h.tensor_tensor(
                out=ablk[:, hk * 4:(hk + 1) * 4, :], in0=po[0:64, :].rearrange("d (a q) -> d a q", a=4),
                in1=den_t[:, :].rearrange("d (a q) -> d a q", a=4), op=ALU.mult),
                reads=[pob, B_["den"]], writes=[abb])
            if hk == 7:
                P.dma("sp", lambda h: h.dma_start(out=att_v[:, :, i * 128:(i + 1) * 128], in_=ablk[:, :, :]),
                      reads=[abb], writes=[att_b])

        emit_block_loads(0)
        stageA(0)
        for si in range(len(sw_steps)):
            if si + 1 < len(sw_steps):
                if sw_steps[si + 1][1] == 0:
                    emit_block_loads(sw_steps[si + 1][0])
                stageA(si + 1)
            stageB(si)
        P.barrier()
        attn_out_phase(swa_w_out[0])
        P.barrier()

    for l in layers:
        compute_mod(l)
        kind = l % 3
        if do_mixer:
            if kind == 0:
                arena["off"] = phase_base
                fox_layer(l, l // 3)
            if kind == 2:
                arena["off"] = phase_base
                swa_layer(l)
            if kind == 1:
                arena["off"] = phase_base
                st = sgu_setup()
                for g in range(NG):
                    sgu_group(st, g)
                P.barrier()
        if do_ffn:
            P.barrier()
            ffn_big(l)
            P.barrier()

    for g in range(NG):
        load_xg(g)
        stage = yT_t[:, :, :].rearrange("p c t -> p (c t)").rearrange("p (b d) -> p b d", b=4)
        for b in range(4):
            for q in range(4):
                pst, psb = PY.next()
                for j in range(4):
                    c = q * 4 + j
                    P.op("pe", lambda h, pst=pst, b=b, c=c, j=j: h.transpose(
                        pst[:, j * 128:(j + 1) * 128], xg_t[:, c, b * 128:(b + 1) * 128], ident_t[:, :]),
                        reads=[xg_b, ident_b], writes=[psb])
                if q % 2 == 0:
                    P.op("act", lambda h, pst=pst, b=b, q=q: h.activation(out=stage[:, b, q * 512:(q + 1) * 512],
                                                                          in_=pst[:, :], func=AF.Copy),
                         reads=[psb], writes=[yT_b])
                else:
                    P.op("dve", lambda h, pst=pst, b=b, q=q: h.tensor_copy(out=stage[:, b, q * 512:(q + 1) * 512],
                                                                           in_=pst[:, :]), reads=[psb], writes=[yT_b])
        P.dma("sp", lambda h, g=g: h.dma_start(
            out=yout[g * TG:(g + 1) * TG, :].rearrange("(b p) d -> p b d", p=128), in_=stage),
            reads=[yT_b], writes=[yout_b])
    P.barrier()
    P.emit(nc)
    es.close()
    return nc


_TRI = np.tril(np.ones((128, 128), np.float32))


def _prep_inputs(inp, layers):
    x = np.asarray(inp["x"], np.float32)
    maps = []
    shared = {
        "ident": np.eye(128, dtype=np.float32),
        "trimask": _TRI,
        "ffn_w_gu": np.ascontiguousarray(np.asarray(inp["ffn_w_gu"], np.float32)[list(layers)]),
        "ffn_w_down": np.ascontiguousarray(np.asarray(inp["ffn_w_down"], np.float32)[list(layers)]),
        "sgu_w_in": np.ascontiguousarray(inp["sgu_w_in"], np.float32),
        "sgu_ln_g": np.ascontiguousarray(inp["sgu_ln_g"], np.float32),
        "sgu_ln_b": np.ascontiguousarray(inp["sgu_ln_b"], np.float32),
        "sgu_w_s": np.ascontiguousarray(inp["sgu_w_s"], np.float32),
        "sgu_b_s": np.ascontiguousarray(inp["sgu_b_s"], np.float32).reshape(1, 2048),
        "sgu_w_out": np.ascontiguousarray(inp["sgu_w_out"], np.float32),
    }
    for n in ("fox_w_in", "fox_b_f", "fox_w_out", "swa_w_in", "swa_sinks", "swa_w_out"):
        shared[n] = np.ascontiguousarray(inp[n], np.float32)
    shared["triu"] = np.ascontiguousarray(_TRI.T)
    shared["m_prev"] = np.ascontiguousarray(1.0 - _TRI.T)
    pm = np.zeros((64, 16), np.float32)
    for m_ in range(8):
        pm[m_ + 8, m_] = -1.0
        pm[m_, m_ + 8] = 1.0
    shared["pmat"] = pm
    inv = (500000.0 ** (-np.arange(0, 16, 2, dtype=np.float32) / np.float32(16))).astype(np.float32)
    shared["invf"] = np.concatenate([inv, inv]).reshape(16, 1).astype(np.float32)
    for n in ("mix_pre_g", "mix_post_g", "ffn_pre_g", "ffn_post_g"):
        shared[n] = np.ascontiguousarray(inp[n], np.float32).reshape(64, 128)
    for core in range(8):
        b, r = core // 4, core % 4
        xb = x[b].reshape(16, 4, 128, D)[:, r].reshape(TOK, D)
        m = dict(shared)
        m["xs"] = np.ascontiguousarray(xb)
        m["ada_w"] = np.ascontiguousarray(np.asarray(inp["ada_w"], np.float32)[list(layers)][:, :, r * 3072:(r + 1) * 3072])
        m["ada_b"] = np.ascontiguousarray(np.asarray(inp["ada_b"], np.float32)[list(layers)][:, r * 3072:(r + 1) * 3072])
        m["cvec"] = np.ascontiguousarray(inp["c"][b], np.float32).reshape(16, 128)
        pos = np.asarray(inp["positions"])[b].astype(np.int32)
        m["posin"] = np.ascontiguousarray(pos.reshape(16, 4, 128)[:, r].reshape(1, TOK))
        mf = np.zeros((128, 4, 128), np.float32)
        for j_ in range(4):
            if j_ < r:
                mf[:, j_, :] = 1.0
            elif j_ == r:
                mf[:, j_, :] = _TRI.T
        m["m_fox"] = mf.reshape(128, 512)
        m["m_prev0"] = np.zeros((128, 128), np.float32) if r == 0 else np.ascontiguousarray(1.0 - _TRI.T)
        oh = np.zeros((128, 4), np.float32)
        oh[:, r] = 1.0
        m["oh4"] = oh
        zo = np.zeros((128, 16, 64), np.float32)
        for li_ in range(16):
            zo[:, li_, 4 * li_ + r + 1:] = -30000.0
        m["zo"] = zo.reshape(128, 1024)
        s5 = np.zeros((128, 5), np.float32)
        s5[:, (r - 1) if r > 0 else 4] = 1.0
        m["sel5"] = s5
        maps.append(m)
    return maps


def run(inp, layers=(0, 1, 2, 3), **kw):
    nc = build_program(layers=layers, **kw)
    maps = _prep_inputs(inp, layers)
    res = run_bass_kernel_spmd(nc, maps, core_ids=list(range(8)))
    out = np.empty((2, SEQ, D), np.float32)
    for core in range(8):
        b, r = core // 4, core % 4
        out[b].reshape(16, 4, 128, D)[:, r] = res.results[core]["yout"].reshape(16, 128, D)
    return out


def kernel(**inputs):
    return run(inputs)
```

```python
import numpy as np
import ml_dtypes
from contextlib import ExitStack
import concourse.bass as bass
import concourse.mybir as mybir
from concourse.bass_utils import run_bass_kernel_spmd

F32 = mybir.dt.float32
BF16 = mybir.dt.bfloat16
I32 = mybir.dt.int32
AF = mybir.ActivationFunctionType
ALU = mybir.AluOpType

D = 2048
NCH = 16
SEQ = 8192
TOK = 2048
NBLK = 16
TG = 512
NG = TOK // TG
DFF = 5632
NFC = DFF // 128
EPS = 1e-6
FOX_IN = 6160
SWA_IN = 3072
ENGS = ("pe", "act", "dve", "pool", "sp")
BLOCKNAME = {"pe": "tensor", "act": "scalar", "dve": "vector", "pool": "gpsimd", "sp": "sync"}


class Buf:
    __slots__ = ("name", "w", "rs", "dtotal")

    def __init__(self, name):
        self.name = name
        self.w = None
        self.rs = {}
        self.dtotal = 0


class Plan:
    def __init__(self):
        self.recs = {e: [] for e in ENGS}
        self.seen = {e: {} for e in ENGS}
        self.dbufs = {}

    def _deps(self, eng, reads, writes, skipkey=None):
        need = {}
        seen = self.seen[eng]

        def add(tok):
            key, val = tok
            if key == ("E", "pe") and eng == "pe":
                return
            if key == skipkey:
                return
            if seen.get(key, -1) >= val:
                return
            if need.get(key, -1) < val:
                need[key] = val

        for b in reads:
            if b.w is not None:
                add(b.w)
        for b in writes:
            if b.w is not None:
                add(b.w)
            for k, v in b.rs.items():
                add((k, v))
        for k, v in need.items():
            seen[k] = v
            if k[0] == "E":
                self.recs[k[1]][v][3] = True
        return list(need.items())

    def op(self, eng, fn, reads=(), writes=()):
        waits = self._deps(eng, reads, writes)
        idx = len(self.recs[eng])
        self.recs[eng].append([waits, fn, None, False, 0])
        key = ("E", eng)
        for b in reads:
            if b.rs.get(key, -1) < idx:
                b.rs[key] = idx
        for b in writes:
            b.w = (key, idx)
            b.rs = {}

    def dma(self, eng, fn, reads=(), writes=(), dbuf=None, inc=16):
        if dbuf is None:
            dbuf = writes[0]
        waits = self._deps(eng, reads, writes, skipkey=("D", id(dbuf)))
        self.dbufs[id(dbuf)] = dbuf
        dbuf.dtotal += inc
        key = ("D", id(dbuf))
        val = dbuf.dtotal
        self.recs[eng].append([waits, fn, id(dbuf), False, inc])
        for b in reads:
            if b.rs.get(key, -1) < val:
                b.rs[key] = val
        for b in writes:
            b.w = (key, val)
            b.rs = {}

    def barrier(self, exclude=()):
        excl = set(id(b) for b in exclude)
        for e in ENGS:
            need = []
            seen = self.seen[e]
            for e2 in ENGS:
                if e2 == e or not self.recs[e2]:
                    continue
                idx = None
                for j in range(len(self.recs[e2]) - 1, -1, -1):
                    r = self.recs[e2][j]
                    if r[1] is not None and r[2] is None:
                        idx = j
                        break
                if idx is None:
                    continue
                key = ("E", e2)
                if seen.get(key, -1) < idx:
                    seen[key] = idx
                    self.recs[e2][idx][3] = True
                    need.append((key, idx))
            for bid, b in self.dbufs.items():
                key = ("D", bid)
                if bid in excl:
                    continue
                if b.dtotal > 0 and seen.get(key, -1) < b.dtotal:
                    seen[key] = b.dtotal
                    need.append((key, b.dtotal))
            if need:
                self.recs[e].append([need, None, None, False, 0])

    def emit(self, nc):
        vals = {}
        for e in ENGS:
            cnt = 0
            v = []
            for rec in self.recs[e]:
                if rec[3]:
                    cnt += 1
                v.append(cnt)
            vals[e] = v
            assert cnt < 60000, (e, cnt)
        with ExitStack() as es:
            esem = {e: es.enter_context(nc.semaphore("sem_" + e)) for e in ENGS}
            dsem = {}
            for n, bid in enumerate(self.dbufs):
                dsem[bid] = es.enter_context(nc.semaphore("dsem%d" % n))
            block = es.enter_context(nc.Block())
            for e in ENGS:
                def body(h, e=e):
                    for waits, fn, dma, flagged, inc in self.recs[e]:
                        for key, val in waits:
                            if key[0] == "E":
                                h.wait_ge(esem[key[1]], vals[key[1]][val])
                            else:
                                h.wait_ge(dsem[key[1]], val)
                        if fn is None:
                            continue
                        ins = fn(h)
                        if dma is not None:
                            ins.then_inc(dsem[dma], inc)
                        elif flagged:
                            ins.then_inc(esem[e], 1)
                getattr(block, BLOCKNAME[e])(body)


class Rot:
    def __init__(self, items):
        self.items = items
        self.i = 0

    def next(self):
        it = self.items[self.i % len(self.items)]
        self.i += 1
        return it


def build_program(layers=(0, 1, 2, 3), do_mixer=True, do_ffn=True):
    NL = len(layers)
    LI = {l: i for i, l in enumerate(layers)}
    nc = bass.Bass("TRN2", target_bir_lowering=False)
    P = Plan()

    def din(name, shape, dt=F32):
        return nc.dram_tensor(name, list(shape), dt, kind="ExternalInput").ap()

    xs = din("xs", [TOK, D])
    cvec = din("cvec", [16, 128])
    ident = din("ident", [128, 128])
    ada_w = din("ada_w", [NL, D, 3072])
    ada_b = din("ada_b", [NL, 3072])
    gains = [din(n, [64, 128]) for n in ("mix_pre_g", "mix_post_g", "ffn_pre_g", "ffn_post_g")]
    w_gu = din("ffn_w_gu", [NL, D, 2 * DFF])
    w_dn = din("ffn_w_down", [NL, DFF, D])
    sgu_w_in = din("sgu_w_in", [1, D, 2 * D])
    sgu_ln_g = din("sgu_ln_g", [1, D])
    sgu_ln_b = din("sgu_ln_b", [1, D])
    sgu_w_s = din("sgu_w_s", [1, 16, 128, 128])
    sgu_b_s = din("sgu_b_s", [1, 16 * 128])
    sgu_w_out = din("sgu_w_out", [1, D, D])
    trimask = din("trimask", [128, 128])
    fox_w_in = din("fox_w_in", [2, D, FOX_IN])
    fox_b_f = din("fox_b_f", [2, 16])
    fox_w_out = din("fox_w_out", [2, D, D])
    swa_w_in = din("swa_w_in", [1, D, SWA_IN])
    swa_sinks = din("swa_sinks", [1, 32])
    swa_w_out = din("swa_w_out", [1, D, D])
    posin = din("posin", [1, TOK], I32)
    triu = din("triu", [128, 128])
    m_prev_in = din("m_prev", [128, 128])
    m_fox_in = din("m_fox", [128, 4 * 128])
    m_prev0_in = din("m_prev0", [128, 128])
    zo_in = din("zo", [128, 1024])
    oh4_in = din("oh4", [128, 4])
    sel5_in = din("sel5", [128, 5])
    pmat_in = din("pmat", [64, 16])
    invf_in = din("invf", [16, 1])
    yout = nc.dram_tensor("yout", [TOK, D], F32, kind="ExternalOutput").ap()

    xT_s = nc.dram_tensor("xT_s", [D, TOK], F32).ap()
    xT_v = xT_s.rearrange("(c p) t -> p c t", p=128)
    xT_b = [Buf("xT_s%d" % g) for g in range(NG)]
    xTw_b = [Buf("xTw_s%d" % g) for g in range(NG)]
    yout_b = Buf("yout")
    modin = [nc.dram_tensor("modin%d" % i, [128, 24], F32) for i in range(4)]
    modout = [nc.dram_tensor("modout%d" % i, [4 * 128, 24], F32) for i in range(4)]
    modin_b, modout_b = Buf("modin"), Buf("modout")
    q_s = nc.dram_tensor("q_s", [D, TOK], BF16).ap()
    q_b = Buf("q_s")
    att_s = nc.dram_tensor("att_s", [D, TOK], BF16).ap()
    att_b = Buf("att_s")
    RG = [[0, 1, 2, 3], [4, 5, 6, 7]]
    fdr = {}
    fdr["kin"] = [nc.dram_tensor("fkin%d" % i, [256, TOK], BF16) for i in range(8)]
    fdr["kout"] = [nc.dram_tensor("fkout%d" % i, [4 * 256, TOK], BF16) for i in range(8)]
    fdr["vin"] = [nc.dram_tensor("fvin%d" % i, [128, 2 * 16 * 128], BF16) for i in range(8)]
    fdr["vout"] = [nc.dram_tensor("fvout%d" % i, [4 * 128, 2 * 16 * 128], BF16) for i in range(8)]
    fdr["lin"] = nc.dram_tensor("flin", [TOK, 16], F32)
    fdr["lout"] = nc.dram_tensor("flout", [4 * TOK, 16], F32)
    fdr["b"] = {k: Buf("f" + k) for k in ("kin", "vin", "lin", "lout")}
    fdr["b"]["kout"] = [Buf("fkout%d" % i) for i in range(8)]
    fdr["b"]["vout"] = [Buf("fvout%d" % i) for i in range(8)]
    swa_dr = {}
    swa_dr["kin"] = [nc.dram_tensor("skin%d" % i, [256, TOK], BF16) for i in range(2)]
    swa_dr["kout"] = [nc.dram_tensor("skout%d" % i, [4 * 256, TOK], BF16) for i in range(2)]
    swa_dr["vin"] = [nc.dram_tensor("svin%d" % i, [128, 8 * 512], BF16) for i in range(2)]
    swa_dr["vout"] = [nc.dram_tensor("svout%d" % i, [4 * 128, 8 * 512], BF16) for i in range(2)]
    swa_dr["b"] = {k: Buf("s" + k) for k in ("kin", "kout", "vin", "vout")}

    arena = {"off": 16640}

    def sb(name, shape, dt):
        nbytes = int(np.prod(shape[1:])) * (4 if dt in (F32, I32) else 2)
        off = (arena["off"] + 31) // 32 * 32
        arena["off"] = off + nbytes
        assert arena["off"] <= 229344, (name, arena["off"])
        t = nc.alloc_sbuf_tensor_at(name, list(shape), dt, offset=off)
        return t

    ident_t = sb("ident_t", [128, 128], F32)
    ident_b = Buf("ident")
    ones_t = sb("ones_t", [128, 128], F32)
    ones_b = Buf("ones")
    eps_t = sb("eps_t", [128, 1], F32)
    one11 = ones_t
    cact_t = sb("cact_t", [128, 16], BF16)
    cact_b = Buf("cact")
    gains_t = sb("gains_t", [128, 4, 64], F32)
    gains_b = Buf("gains")
    modT = sb("modT", [128, 96], F32)
    modp_t = sb("modp_t", [128, 24], F32)
    modp_b = Buf("modp")
    modT_b = Buf("modT")
    der_t = sb("der_t", [128, 4, 16], F32)
    der_b = Buf("der")
    onesb_t = sb("onesb_t", [128, 128], BF16)
    cst_t = sb("cst_t", [128, 4], F32)
    xg_off = (arena["off"] + 31) // 32 * 32
    xg_t = sb("xg_t", [128, NCH, TG], F32)
    xg_b = Buf("xg")
    hT_t = sb("hT_t", [128, NCH, TG], BF16)
    hT_b = Buf("hT")
    yT_t = sb("yT_t", [128, NCH, TG], F32)
    yT_b = Buf("yT")
    big_off = (arena["off"] + 31) // 32 * 32
    big_t = sb("big_t", [128, NFC, TG], BF16)
    big_b = Buf("big")
    wsl = []
    for i in range(2):
        t = sb("wslot%d" % i, [128, 8192], BF16)
        wsl.append((t, Buf("wslot%d" % i)))
    wrot = Rot(wsl)
    attn_lim = arena["off"]
    sqs = Rot([(sb("sq%d" % i, [128, TG], BF16), Buf("sq%d" % i)) for i in range(4)])
    tmps = Rot([(sb("tmp%d" % i, [128, TG], F32), Buf("tmp%d" % i)) for i in range(2)])
    rstd_t = sb("rstd_t", [128, TG], F32)
    rstd_b = Buf("rstd")
    rt_t = sb("rt_t", [128, TG], F32)
    rt_b = Buf("rt")
    row_t = sb("row_t", [1, 512], F32)
    row_b = Buf("row")
    brow_t = sb("brow_t", [1, 512], F32)
    brow_b = Buf("brow")
    small_t = sb("small_t", [128, 64], F32)
    small_b = Buf("small")
    phase_base = arena["off"]

    es = ExitStack()
    psum = []
    for i in range(8):
        t = es.enter_context(nc.psum_tensor("ps%d" % i, [128, 512], F32))
        psum.append((t, Buf("ps%d" % i)))
    PG = Rot([psum[0], psum[2]])
    PU = Rot([psum[1], psum[3]])
    PY = Rot([psum[4], psum[5]])
    PSSQ = psum[6]
    PM = psum[7]

    mm = lambda out, lhsT, rhs, st, sp: (lambda h: h.matmul(out, lhsT, rhs, start=st, stop=sp))

    def load_w_cols(W2d, col0, ncols, slot, slot_b, dst_col0=0, width=None, kch=NCH):
        width = width or ncols
        view = slot[:, 0:kch * width].rearrange("p (c n) -> p c n", n=width)
        src = W2d.rearrange("(c p) n -> p c n", p=128)[:, :, col0:col0 + ncols]
        P.dma("pool", lambda h: h.dma_start(out=view[:, :, dst_col0:dst_col0 + ncols], in_=src),
              writes=[slot_b])
        return view

    def ssq_rstd(src_t, src_b, src_fn=None):
        pst, psb = PSSQ
        if src_fn is None:
            src_fn = lambda c: src_t[:, c, :]
        for c in range(NCH):
            sq, sqb = sqs.next()
            P.op("act", lambda h, sq=sq, c=c: h.activation(out=sq[:, :], in_=src_fn(c), func=AF.Square),
                 reads=[src_b], writes=[sqb])
            P.op("pe", mm(pst[:, :], onesb_t[:, :], sq[:, :], c == 0, c == NCH - 1),
                 reads=[ones_b, sqb], writes=[psb])
        P.op("act", lambda h: h.activation(out=rt_t[:, :], in_=pst[:, :], func=AF.Sqrt,
                                           bias=eps_t[:, 0:1], scale=1.0 / D),
             reads=[psb, ones_b], writes=[rt_b])
        P.op("dve", lambda h: h.reciprocal(out=rstd_t[:, :], in_=rt_t[:, :]), reads=[rt_b], writes=[rstd_b])

    def prenorm(acol, bcol):
        ssq_rstd(xg_t, xg_b)
        for c in range(NCH):
            tmp, tb = tmps.next()
            P.op("dve", lambda h, tmp=tmp, c=c: h.scalar_tensor_tensor(
                out=tmp[:, :], in0=xg_t[:, c, :], scalar=acol[:, c:c + 1], in1=rstd_t[:, :],
                op0=ALU.mult, op1=ALU.mult), reads=[xg_b, der_b, rstd_b], writes=[tb])
            P.op("act", lambda h, tmp=tmp, c=c: h.activation(
                out=hT_t[:, c, :], in_=tmp[:, :], func=AF.Identity, bias=bcol[:, c:c + 1], scale=1.0),
                reads=[tb, modT_b], writes=[hT_b])

    def postnorm_res(coef):
        ssq_rstd(yT_t, yT_b)
        for c in range(NCH):
            tmp, tb = tmps.next()
            P.op("dve", lambda h, tmp=tmp, c=c: h.scalar_tensor_tensor(
                out=tmp[:, :], in0=yT_t[:, c, :], scalar=coef[:, c:c + 1], in1=rstd_t[:, :],
                op0=ALU.mult, op1=ALU.mult), reads=[yT_b, der_b, rstd_b], writes=[tb])
            P.op("pool", lambda h, tmp=tmp, c=c: h.tensor_tensor(
                out=xg_t[:, c, :], in0=xg_t[:, c, :], in1=tmp[:, :], op=ALU.add),
                reads=[tb, xg_b], writes=[xg_b])

    def load_xg(g):
        P.dma("sp", lambda h: h.dma_start(out=xg_t[:, :, :], in_=xT_v[:, :, g * TG:(g + 1) * TG]),
              reads=[xT_b[g], xTw_b[g]], writes=[xg_b])

    def store_xg(g):
        P.dma("sp", lambda h: h.dma_start(out=xT_v[:, :, g * TG:(g + 1) * TG], in_=xg_t[:, :, :]),
              reads=[xg_b], writes=[xT_b[g]])

    def linear_fm(W2d, col0, n_oc, rhs_fn, rhs_bufs, kch, evac, ocw=128, per_load=4):
        oc = 0
        while oc < n_oc:
            nl = min(per_load, n_oc - oc)
            slot, slot_b = wrot.next()
            view = load_w_cols(W2d, col0 + oc * ocw, nl * ocw, slot, slot_b, kch=kch)
            for j in range(nl):
                pst, psb = PY.next()
                for c in range(kch):
                    P.op("pe", mm(pst[0:ocw, :], view[:, c, j * ocw:(j + 1) * ocw], rhs_fn(c), c == 0, c == kch - 1),
                         reads=[slot_b] + rhs_bufs, writes=[psb])
                evac(oc + j, pst, psb)
            oc += nl

    P.dma("sp", lambda h: h.dma_start(out=ident_t[:, :], in_=ident[:, :]), writes=[ident_b])
    P.op("dve", lambda h: h.memset(ones_t[:, :], 1.0), writes=[ones_b])
    P.op("dve", lambda h: h.memset(eps_t[:, :], EPS), writes=[ones_b])
    P.op("dve", lambda h: h.memset(onesb_t[:, :], 1.0), writes=[ones_b])
    P.op("dve", lambda h: h.memset(cst_t[:, 0:1], -float(np.pi)), writes=[ones_b])
    for k in range(4):
        tmp, tb = tmps.next()
        P.dma("sp", lambda h, tmp=tmp, k=k: h.dma_start(out=tmp[0:64, 0:128], in_=gains[k][:, :]), writes=[tb])
        pst, psb = PM
        P.op("pe", lambda h, tmp=tmp: h.transpose(pst[:, 0:64], tmp[0:64, 0:128], ident_t[0:64, 0:64]),
             reads=[tb, ident_b], writes=[psb])
        P.op("dve", lambda h, k=k: h.tensor_copy(out=gains_t[:, k, :], in_=pst[:, 0:64]), reads=[psb], writes=[gains_b])
    tmp, tb = tmps.next()
    P.dma("sp", lambda h, tmp=tmp: h.dma_start(out=tmp[0:16, 0:128], in_=cvec[:, :]), writes=[tb])
    pst, psb = PM
    P.op("pe", lambda h, tmp=tmp: h.transpose(pst[:, 0:16], tmp[0:16, 0:128], ident_t[0:16, 0:16]),
         reads=[tb, ident_b], writes=[psb])
    P.op("act", lambda h: h.activation(out=cact_t[:, :], in_=pst[:, 0:16], func=AF.Silu), reads=[psb], writes=[cact_b])

    xblk = sb("xblk", [128, 4, D], F32) if False else None
    for g in range(NG):
        stage = yT_t[:, :, :].rearrange("p c t -> p (c t)").rearrange("p (b d) -> p b d", b=4)
        P.dma("sp", lambda h, g=g: h.dma_start(
            out=stage, in_=xs[g * TG:(g + 1) * TG, :].rearrange("(b p) d -> p b d", p=128)), writes=[yT_b])
        for c in range(NCH):
            pst, psb = PY.next()
            for b in range(4):
                P.op("pe", lambda h, pst=pst, b=b, c=c: h.transpose(
                    pst[:, b * 128:(b + 1) * 128], stage[:, b, c * 128:(c + 1) * 128], ident_t[:, :]),
                    reads=[yT_b, ident_b], writes=[psb])
            eng = "act" if c % 2 == 0 else "dve"
            if eng == "act":
                P.op("act", lambda h, pst=pst, c=c: h.activation(out=xg_t[:, c, :], in_=pst[:, :], func=AF.Copy),
                     reads=[psb], writes=[xg_b])
            else:
                P.op("dve", lambda h, pst=pst, c=c: h.tensor_copy(out=xg_t[:, c, :], in_=pst[:, :]),
                     reads=[psb], writes=[xg_b])
        store_xg(g)

    def compute_mod(l):
        for cg in range(6):
            slot, slot_b = wrot.next()
            view = load_w_cols(ada_w[LI[l]], cg * 512, 512, slot, slot_b)
            P.dma("sp", lambda h, cg=cg: h.dma_start(out=brow_t[0:1, :], in_=ada_b[LI[l]:LI[l] + 1, cg * 512:(cg + 1) * 512]),
                  writes=[brow_b])
            pst, psb = PY.next()
            for c in range(NCH):
                P.op("pe", mm(pst[0:1, :], cact_t[:, c:c + 1], view[:, c, :], c == 0, c == NCH - 1),
                     reads=[cact_b, slot_b], writes=[psb])
            P.op("dve", lambda h, pst=pst: h.tensor_tensor(out=row_t[0:1, :], in0=pst[0:1, :], in1=brow_t[0:1, :],
                                                           op=ALU.add), reads=[psb, brow_b], writes=[row_b])
            pm, pmb = PM
            for j in range(4):
                P.op("pe", mm(pm[:, j:j + 1], row_t[0:1, j * 128:(j + 1) * 128], one11[0:1, 0:1], True, True),
                     reads=[row_b, ones_b], writes=[pmb])
            P.op("dve", lambda h, cg=cg: h.tensor_copy(out=modp_t[:, cg * 4:(cg + 1) * 4], in_=pm[:, 0:4]),
                 reads=[pmb], writes=[modp_b])
        P.dma("sp", lambda h: h.dma_start(out=modin[l].ap(), in_=modp_t[:, :]), reads=[modp_b], writes=[modin_b])
        P.dma("pool", lambda h: h.collective_compute("AllGather", ALU.bypass, replica_groups=RG,
                                                     ins=[modin[l].ap().opt()], outs=[modout[l].ap().opt()]),
              reads=[modin_b], writes=[modout_b], inc=1)
        P.dma("sp", lambda h: h.dma_start(out=modT[:, :].rearrange("p (r c) -> p r c", r=4),
                                          in_=modout[l].ap().rearrange("(r p) c -> p r c", r=4)),
              reads=[modout_b], writes=[modT_b])
        for which, (sc_i, gate_i, pre_k, post_k) in enumerate(((1, 2, 0, 1), (4, 5, 2, 3))):
            P.op("dve", lambda h, sc_i=sc_i: h.tensor_scalar_add(out=small_t[:, 0:16], in0=modT[:, sc_i * 16:(sc_i + 1) * 16],
                                                                 scalar1=1.0), reads=[modT_b], writes=[small_b])
            P.op("dve", lambda h, which=which, pre_k=pre_k: h.tensor_tensor(
                out=der_t[:, 2 * which, :], in0=small_t[:, 0:16], in1=gains_t[:, pre_k, l * 16:(l + 1) * 16], op=ALU.mult),
                reads=[small_b, gains_b], writes=[der_b])
            P.op("dve", lambda h, which=which, gate_i=gate_i, post_k=post_k: h.tensor_tensor(
                out=der_t[:, 2 * which + 1, :], in0=modT[:, gate_i * 16:(gate_i + 1) * 16],
                in1=gains_t[:, post_k, l * 16:(l + 1) * 16], op=ALU.mult),
                reads=[modT_b, gains_b], writes=[der_b])

    def ffn_group(l, g):
        load_xg(g)
        prenorm(der_t[:, 2, :], modT[:, 48:64])
        for fc in range(NFC):
            slot, slot_b = wrot.next()
            view = load_w_cols(w_gu[LI[l]], fc * 128, 128, slot, slot_b, dst_col0=0, width=256)
            load_w_cols(w_gu[LI[l]], DFF + fc * 128, 128, slot, slot_b, dst_col0=128, width=256)
            pg, pgb = PG.next()
            pu, pub = PU.next()
            for c in range(NCH):
                P.op("pe", mm(pg[:, :], view[:, c, 0:128], hT_t[:, c, :], c == 0, c == NCH - 1),
                     reads=[slot_b, hT_b], writes=[pgb])
            for c in range(NCH):
                P.op("pe", mm(pu[:, :], view[:, c, 128:256], hT_t[:, c, :], c == 0, c == NCH - 1),
                     reads=[slot_b, hT_b], writes=[pub])
            tmp, tb = tmps.next()
            P.op("act", lambda h, pg=pg, tmp=tmp: h.activation(out=tmp[:, :], in_=pg[:, :], func=AF.Silu),
                 reads=[pgb], writes=[tb])
            P.op("dve", lambda h, pu=pu, tmp=tmp, fc=fc: h.tensor_tensor(
                out=big_t[:, fc, :], in0=tmp[:, :], in1=pu[:, :], op=ALU.mult), reads=[tb, pub], writes=[big_b])
        for dc in range(NCH):
            slot, slot_b = wrot.next()
            view = slot[:, 0:NFC * 128].rearrange("p (j o) -> p j o", o=128)
            src = w_dn[LI[l]].rearrange("(j p) o -> p j o", p=128)[:, :, dc * 128:(dc + 1) * 128]
            P.dma("pool", lambda h, view=view, src=src: h.dma_start(out=view, in_=src), writes=[slot_b])
            py, pyb = PY.next()
            for j in range(NFC):
                P.op("pe", mm(py[:, :], view[:, j, :], big_t[:, j, :], j == 0, j == NFC - 1),
                     reads=[slot_b, big_b], writes=[pyb])
            P.op("act", lambda h, py=py, dc=dc: h.activation(out=yT_t[:, dc, :], in_=py[:, :], func=AF.Copy),
                 reads=[pyb], writes=[yT_b])
        postnorm_res(der_t[:, 3, :])
        store_xg(g)

    TG2 = 1024
    assert attn_lim - xg_off >= 159744, (attn_lim, xg_off)
    XY = nc.alloc_sbuf_tensor_at("f_xy", [128, NCH, TG2], F32, offset=xg_off)
    H2 = nc.alloc_sbuf_tensor_at("f_h2", [128, NCH, TG2], BF16, offset=xg_off + 65536)
    A2 = nc.alloc_sbuf_tensor_at("f_a2", [128, 22, TG2], BF16, offset=xg_off + 98304)
    fws = [(nc.alloc_sbuf_tensor_at("f_w%d" % i, [128, 4096], BF16, offset=xg_off + 143360 + i * 8192), Buf("f_w%d" % i))
           for i in range(2)]
    fws += [(nc.alloc_sbuf_tensor_at("f_w%d" % (2 + i), [128, 4096], BF16, offset=phase_base + i * 8192), Buf("f_w%d" % (2 + i)))
            for i in range(2)]
    fwrot = Rot(fws)
    xins = Rot([(nc.alloc_sbuf_tensor_at("f_xin%d" % i, [128, 512], F32, offset=phase_base + 16384 + i * 2048), Buf("f_xin%d" % i))
                for i in range(4)])
    xouts = Rot([(nc.alloc_sbuf_tensor_at("f_xo%d" % i, [128, 512], F32, offset=phase_base + 24576 + i * 2048), Buf("f_xo%d" % i))
                 for i in range(4)])
    XY_b, H2_b, A2_b = Buf("f_xy"), Buf("f_h2"), Buf("f_a2")

    def ffn_big(l):
        acol, bcol, coef = der_t[:, 2, :], modT[:, 48:64], der_t[:, 3, :]
        Wgu = w_gu[LI[l]]
        Wdn = w_dn[LI[l]].rearrange("(j p) o -> p j o", p=128)
        for g2 in range(2):
            t0 = g2 * TG2
            gb = [2 * g2, 2 * g2 + 1]
            P.dma("sp", lambda h, t0=t0: h.dma_start(out=XY[:, :, :], in_=xT_v[:, :, t0:t0 + TG2]),
                  reads=[xT_b[gb[0]], xT_b[gb[1]], xTw_b[gb[0]], xTw_b[gb[1]]], writes=[XY_b])
            for th in range(2):
                hs = slice(th * 512, (th + 1) * 512)
                ssq_rstd(None, XY_b, src_fn=lambda c, hs=hs: XY[:, c, hs])
                for c in range(NCH):
                    tmp, tb = tmps.next()
                    P.op("dve", lambda h, tmp=tmp, c=c, hs=hs: h.scalar_tensor_tensor(
                        out=tmp[:, :], in0=XY[:, c, hs], scalar=acol[:, c:c + 1], in1=rstd_t[:, :],
                        op0=ALU.mult, op1=ALU.mult), reads=[XY_b, der_b, rstd_b], writes=[tb])
                    P.op("act", lambda h, tmp=tmp, c=c, hs=hs: h.activation(
                        out=H2[:, c, hs], in_=tmp[:, :], func=AF.Identity, bias=bcol[:, c:c + 1], scale=1.0),
                        reads=[tb, modT_b], writes=[H2_b])
            for fh in range(2):
                for fcl in range(22):
                    fc = fh * 22 + fcl
                    slot, slot_b = fwrot.next()
                    view = load_w_cols(Wgu, fc * 128, 128, slot, slot_b, dst_col0=0, width=256)
                    load_w_cols(Wgu, DFF + fc * 128, 128, slot, slot_b, dst_col0=128, width=256)
                    for th in range(2):
                        hs = slice(th * 512, (th + 1) * 512)
                        pg, pgb = PG.next()
                        pu, pub = PU.next()
                        for c in range(NCH):
                            P.op("pe", mm(pg[:, :], view[:, c, 0:128], H2[:, c, hs], c == 0, c == NCH - 1),
                                 reads=[slot_b, H2_b], writes=[pgb])
                        for c in range(NCH):
                            P.op("pe", mm(pu[:, :], view[:, c, 128:256], H2[:, c, hs], c == 0, c == NCH - 1),
                                 reads=[slot_b, H2_b], writes=[pub])
                        tmp, tb = tmps.next()
                        P.op("act", lambda h, pg=pg, tmp=tmp: h.activation(out=tmp[:, :], in_=pg[:, :], func=AF.Silu),
                             reads=[pgb], writes=[tb])
                        P.op("dve", lambda h, pu=pu, tmp=tmp, fcl=fcl, hs=hs: h.tensor_tensor(
                            out=A2[:, fcl, hs], in0=tmp[:, :], in1=pu[:, :], op=ALU.mult), reads=[tb, pub], writes=[A2_b])
                for dc in range(NCH):
                    slot, slot_b = fwrot.next()
                    view = slot[:, 0:22 * 128].rearrange("p (j o) -> p j o", o=128)
                    src = Wdn[:, fh * 22:(fh + 1) * 22, dc * 128:(dc + 1) * 128]
                    P.dma("pool", lambda h, view=view, src=src: h.dma_start(out=view, in_=src), writes=[slot_b])
                    for th in range(2):
                        hs = slice(th * 512, (th + 1) * 512)
                        py, pyb = PY.next()
                        for j in range(22):
                            P.op("pe", mm(py[:, :], view[:, j, :], A2[:, j, hs], j == 0, j == 21),
                                 reads=[slot_b, A2_b], writes=[pyb])
                        if fh == 0:
                            P.op("act", lambda h, py=py, dc=dc, hs=hs: h.activation(out=XY[:, dc, hs], in_=py[:, :], func=AF.Copy),
                                 reads=[pyb], writes=[XY_b])
                        else:
                            P.op("dve", lambda h, py=py, dc=dc, hs=hs: h.tensor_tensor(out=XY[:, dc, hs], in0=py[:, :],
                                                                                       in1=XY[:, dc, hs], op=ALU.add),
                                 reads=[pyb, XY_b], writes=[XY_b])
            for th in range(2):
                hs = slice(th * 512, (th + 1) * 512)
                gg = gb[th]
                ssq_rstd(None, XY_b, src_fn=lambda c, hs=hs: XY[:, c, hs])
                for c in range(NCH):
                    xin, xinb = xins.next()
                    xo, xob = xouts.next()
                    P.dma("sp", lambda h, xin=xin, c=c, gg=gg: h.dma_start(out=xin[:, :], in_=xT_v[:, c, gg * 512:(gg + 1) * 512]),
                          reads=[xT_b[gg]], writes=[xinb])
                    tmp, tb = tmps.next()
                    P.op("dve", lambda h, tmp=tmp, c=c, hs=hs: h.scalar_tensor_tensor(
                        out=tmp[:, :], in0=XY[:, c, hs], scalar=coef[:, c:c + 1], in1=rstd_t[:, :],
                        op0=ALU.mult, op1=ALU.mult), reads=[XY_b, der_b, rstd_b], writes=[tb])
                    P.op("pool", lambda h, tmp=tmp, xin=xin, xo=xo: h.tensor_tensor(
                        out=xo[:, :], in0=xin[:, :], in1=tmp[:, :], op=ALU.add), reads=[tb, xinb], writes=[xob])
                    P.dma("sp", lambda h, xo=xo, c=c, gg=gg: h.dma_start(out=xT_v[:, c, gg * 512:(gg + 1) * 512], in_=xo[:, :]),
                          reads=[xob], writes=[xTw_b[gg]])

    def sgu_setup():
        st = {}
        st["wsT"] = sb("sgu_wsT", [128, 16, 128], BF16)
        st["bs"] = sb("sgu_bs", [128, 16, 128], F32)
        st["lng"] = sb("sgu_lng", [128, D], F32)
        st["lnb"] = sb("sgu_lnb", [128, D], F32)
        st["tri"] = sb("sgu_tri", [128, 128], F32)
        st["stat"] = sb("sgu_stat", [128, 8], F32)
        st["vtm"] = nc.alloc_sbuf_tensor_at("sgu_vtm", [128, 4, D], BF16, offset=big_off + 16 * TG * 2)
        st["b"] = {k: Buf("sgu_" + k) for k in ("wsT", "bs", "lng", "lnb", "vtm", "tri", "stat")}
        b = st["b"]
        P.dma("sp", lambda h: h.dma_start(out=st["tri"][:, :], in_=trimask[:, :]), writes=[b["tri"]])
        P.dma("sp", lambda h: h.dma_start(out=st["bs"][:, :, :].rearrange("p g t -> p (g t)"),
                                          in_=sgu_b_s[0, :].partition_broadcast(128)), writes=[b["bs"]])
        P.dma("sp", lambda h: h.dma_start(out=st["lng"][:, :], in_=sgu_ln_g[0, :].partition_broadcast(128)),
              writes=[b["lng"]])
        P.dma("sp", lambda h: h.dma_start(out=st["lnb"][:, :], in_=sgu_ln_b[0, :].partition_broadcast(128)),
              writes=[b["lnb"]])
        for gi in range(16):
            tmp, tb = tmps.next()
            P.dma("sp", lambda h, tmp=tmp, gi=gi: h.dma_start(out=tmp[:, 0:128], in_=sgu_w_s[0, gi, :, :]), writes=[tb])
            P.op("dve", lambda h, tmp=tmp: h.tensor_tensor(out=tmp[:, 128:256], in0=tmp[:, 0:128], in1=st["tri"][:, :],
                                                           op=ALU.mult), reads=[tb, b["tri"]], writes=[tb])
            pst, psb = PM
            P.op("pe", lambda h, tmp=tmp, pst=pst: h.transpose(pst[:, 0:128], tmp[:, 128:256], ident_t[:, :]),
                 reads=[tb, ident_b], writes=[psb])
            P.op("dve", lambda h, gi=gi, pst=pst: h.tensor_copy(out=st["wsT"][:, gi, :], in_=pst[:, 0:128]),
                 reads=[psb], writes=[b["wsT"]])
        return st

    def sgu_group(st, g):
        b = st["b"]
        load_xg(g)
        prenorm(der_t[:, 0, :], modT[:, 0:16])
        W = sgu_w_in[0]
        def evac_u(oc, pst, psb):
            P.op("act", lambda h: h.activation(out=big_t[:, oc, :], in_=pst[:, :], func=AF.Gelu),
                 reads=[psb], writes=[big_b])
        linear_fm(W, 0, 16, lambda c: hT_t[:, c, :], [hT_b], NCH, evac_u)
        zv4 = yT_t[:, :, :].rearrange("p c t -> p (c t)").rearrange("p (b d) -> p b d", b=4)
        for cg in range(4):
            slot, slot_b = wrot.next()
            view = load_w_cols(W, D + cg * 512, 512, slot, slot_b)
            for blk in range(4):
                pst, psb = PY.next()
                for c in range(NCH):
                    P.op("pe", mm(pst[:, :], hT_t[:, c, blk * 128:(blk + 1) * 128], view[:, c, :], c == 0, c == NCH - 1),
                         reads=[hT_b, slot_b], writes=[psb])
                P.op("act", lambda h, pst=pst, cg=cg, blk=blk: h.activation(
                    out=zv4[:, blk, cg * 512:(cg + 1) * 512], in_=pst[:, :], func=AF.Gelu), reads=[psb], writes=[yT_b])
        stat = st["stat"]
        for blk in range(4):
            zv = zv4[:, blk, :]
            P.op("dve", lambda h, zv=zv: h.tensor_reduce(out=stat[:, 0:1], in_=zv, axis=mybir.AxisListType.X, op=ALU.add),
                 reads=[yT_b], writes=[b["stat"]])
            P.op("act", lambda h, zv=zv, blk=blk: h.activation(out=st["vtm"][:, blk, :], in_=zv, func=AF.Square,
                                                               accum_out=stat[:, 1:2]),
                 reads=[yT_b], writes=[b["stat"], b["vtm"]])
            P.op("dve", lambda h: h.tensor_scalar_mul(out=stat[:, 2:3], in0=stat[:, 0:1], scalar1=1.0 / D),
                 reads=[b["stat"]], writes=[b["stat"]])
            P.op("dve", lambda h: h.tensor_tensor(out=stat[:, 3:4], in0=stat[:, 2:3], in1=stat[:, 2:3], op=ALU.mult),
                 reads=[b["stat"]], writes=[b["stat"]])
            P.op("dve", lambda h: h.scalar_tensor_tensor(out=stat[:, 4:5], in0=stat[:, 1:2], scalar=1.0 / D,
                                                         in1=stat[:, 3:4], op0=ALU.mult, op1=ALU.subtract),
                 reads=[b["stat"]], writes=[b["stat"]])
            P.op("act", lambda h: h.activation(out=stat[:, 5:6], in_=stat[:, 4:5], func=AF.Sqrt, bias=eps_t[:, 0:1],
                                               scale=1.0), reads=[b["stat"], ones_b], writes=[b["stat"]])
            P.op("dve", lambda h: h.reciprocal(out=stat[:, 6:7], in_=stat[:, 5:6]), reads=[b["stat"]], writes=[b["stat"]])
            P.op("dve", lambda h, zv=zv: h.tensor_scalar(out=zv, in0=zv, scalar1=stat[:, 2:3], scalar2=stat[:, 6:7],
                                                         op0=ALU.subtract, op1=ALU.mult), reads=[b["stat"], yT_b], writes=[yT_b])
            P.op("pool", lambda h, zv=zv: h.tensor_tensor(out=zv, in0=zv, in1=st["lng"][:, :], op=ALU.mult),
                 reads=[yT_b, b["lng"]], writes=[yT_b])
            P.op("dve", lambda h, zv=zv, blk=blk: h.tensor_tensor(out=st["vtm"][:, blk, :], in0=zv, in1=st["lnb"][:, :],
                                                                  op=ALU.add), reads=[yT_b, b["lnb"]], writes=[b["vtm"]])
        for gi in range(16):
            pst, psb = PY.next()
            for blk in range(4):
                P.op("pe", mm(pst[:, blk * 128:(blk + 1) * 128], st["vtm"][:, blk, gi * 128:(gi + 1) * 128],
                              st["wsT"][:, gi, :], True, True), reads=[b["vtm"], b["wsT"]], writes=[psb])
            tmp, tb = tmps.next()
            P.op("dve", lambda h, pst=pst, tmp=tmp, gi=gi: h.tensor_tensor(
                out=tmp[:, :].rearrange("p (b t) -> p b t", b=4), in0=pst[:, :].rearrange("p (b t) -> p b t", b=4),
                in1=st["bs"][:, gi:gi + 1, :].broadcast_to([128, 4, 128]), op=ALU.add),
                reads=[psb, b["bs"]], writes=[tb])
            P.op("pool", lambda h, tmp=tmp, gi=gi: h.tensor_tensor(out=big_t[:, gi, :], in0=big_t[:, gi, :], in1=tmp[:, :],
                                                                   op=ALU.mult), reads=[tb, big_b], writes=[big_b])
        def evac_y(oc, pst, psb):
            P.op("act", lambda h: h.activation(out=yT_t[:, oc, :], in_=pst[:, :], func=AF.Copy),
                 reads=[psb], writes=[yT_b])
        linear_fm(sgu_w_out[0], 0, 16, lambda c: big_t[:, c, :], [big_b], NCH, evac_y)
        postnorm_res(der_t[:, 1, :])
        store_xg(g)

    def at_alloc(state, name, shape, dt):
        nbytes = int(np.prod(shape[1:])) * (4 if dt in (F32, I32) else 2)
        off = (state["off"] + 31) // 32 * 32
        state["off"] = off + nbytes
        assert state["off"] <= attn_lim, (name, state["off"], attn_lim)
        state["n"] += 1
        return nc.alloc_sbuf_tensor_at("%s_%d" % (name, state["n"]), list(shape), dt, offset=off)

    def evac_to(dst_fn, dst_buf, func=None, eng="act"):
        def ev(oc, pst, psb):
            P.op("act", lambda h: h.activation(out=dst_fn(oc), in_=pst[0:dst_fn(oc).shape[0], :], func=AF.Copy),
                 reads=[psb], writes=[dst_buf])
        return ev

    def attn_out_phase(W2d):
        for g in range(NG):
            P.dma("sp", lambda h, g=g: h.dma_start(
                out=big_t[:, 0:16, :], in_=att_s.rearrange("(c p) t -> p c t", p=128)[:, :, g * TG:(g + 1) * TG]),
                reads=[att_b], writes=[big_b])
            load_xg(g)
            linear_fm(W2d, 0, 16, lambda c: big_t[:, c, :], [big_b], NCH,
                      evac_to(lambda oc: yT_t[:, oc, :], yT_b))
            postnorm_res(der_t[:, 1, :])
            store_xg(g)

    def all_gather(src, src_b, dst, dst_b):
        P.dma("pool", lambda h: h.collective_compute("AllGather", ALU.bypass, replica_groups=RG,
                                                     ins=[src.ap().opt()], outs=[dst.ap().opt()]),
              reads=[src_b], writes=[dst_b], inc=1)

    fox_cache = {}

    def fox_layer(l, j):
        dr = fdr
        db = dr["b"]
        W = fox_w_in[j]
        kin_v = [t.ap().rearrange("(c p) t -> p c t", p=128) for t in dr["kin"]]
        vin_v = [t.ap().rearrange("p (h i d) -> p h i d", h=2, i=16) for t in dr["vin"]]
        lin_v = dr["lin"].ap().rearrange("(i p) h -> p i h", p=128)
        ph = {"off": phase_base, "n": 100 * l}
        bf_t = sb("fox_bf%d" % l, [128, 16], F32)
        vst_t = sb("fox_vst%d" % l, [128, 4, 512], BF16)
        lf_t = sb("fox_lf%d" % l, [128, 4, 16], F32)
        bf_b, vst_b, lf_b = fox_cache.setdefault("p1", (Buf("bf"), Buf("vst"), Buf("lf")))
        P.dma("sp", lambda h: h.dma_start(out=bf_t[:, :], in_=fox_b_f[j, :].partition_broadcast(128)), writes=[bf_b])
        for g in range(NG):
            load_xg(g)
            prenorm(der_t[:, 0, :], modT[:, 0:16])
            linear_fm(W, 0, 16, lambda c: hT_t[:, c, :], [hT_b], NCH, evac_to(lambda oc: big_t[:, oc, :], big_b))
            P.dma("sp", lambda h, g=g: h.dma_start(
                out=q_s.rearrange("(c p) t -> p c t", p=128)[:, :, g * TG:(g + 1) * TG], in_=big_t[:, 0:16, :]),
                reads=[big_b], writes=[q_b])
            linear_fm(W, D, 16, lambda c: hT_t[:, c, :], [hT_b], NCH, evac_to(lambda oc: big_t[:, 16 + oc, :], big_b))
            for ck in range(8):
                P.dma("sp", lambda h, g=g, ck=ck: h.dma_start(out=kin_v[ck][:, :, g * TG:(g + 1) * TG],
                                                               in_=big_t[:, 16 + 2 * ck:18 + 2 * ck, :]),
                      reads=[big_b], writes=[db["kin"]])
            for cg in range(4):
                slot, slot_b = wrot.next()
                view = load_w_cols(W, 2 * D + cg * 512, 512, slot, slot_b)
                for blk in range(4):
                    pst, psb = PY.next()
                    for c in range(NCH):
                        P.op("pe", mm(pst[:, :], hT_t[:, c, blk * 128:(blk + 1) * 128], view[:, c, :], c == 0, c == NCH - 1),
                             reads=[hT_b, slot_b], writes=[psb])
                    P.op("act", lambda h, pst=pst, blk=blk: h.activation(out=vst_t[:, blk, :], in_=pst[:, :], func=AF.Copy),
                         reads=[psb], writes=[vst_b])
                for hh in range(4):
                    P.dma("sp", lambda h, g=g, cg=cg, hh=hh: h.dma_start(
                        out=vin_v[(cg * 4 + hh) // 2][:, (cg * 4 + hh) % 2, g * 4:(g + 1) * 4, :],
                        in_=vst_t[:, :, hh * 128:(hh + 1) * 128]), reads=[vst_b], writes=[db["vin"]])
            slot, slot_b = wrot.next()
            view = load_w_cols(W, 3 * D, 16, slot, slot_b)
            for blk in range(4):
                pst, psb = PY.next()
                for c in range(NCH):
                    P.op("pe", mm(pst[:, 0:16], hT_t[:, c, blk * 128:(blk + 1) * 128], view[:, c, :], c == 0, c == NCH - 1),
                         reads=[hT_b, slot_b], writes=[psb])
                P.op("dve", lambda h, pst=pst, blk=blk: h.tensor_tensor(out=lf_t[:, blk, :], in0=pst[:, 0:16], in1=bf_t[:, :],
                                                                        op=ALU.add), reads=[psb, bf_b], writes=[lf_b])
            lf2 = lf_t[:, :, :].rearrange("p b h -> p (b h)")
            P.op("act", lambda h: h.activation(out=lf2, in_=lf2, func=AF.Exp, scale=-1.0), reads=[lf_b], writes=[lf_b])
            P.op("act", lambda h: h.activation(out=lf2, in_=lf2, func=AF.Ln, bias=1.0, scale=1.0), reads=[lf_b], writes=[lf_b])
            P.op("dve", lambda h: h.tensor_scalar_mul(out=lf2, in0=lf2, scalar1=-1.0), reads=[lf_b], writes=[lf_b])
            P.dma("sp", lambda h, g=g: h.dma_start(out=lin_v[:, g * 4:(g + 1) * 4, :], in_=lf_t[:, :, :]),
                  reads=[lf_b], writes=[db["lin"]])
        all_gather(dr["lin"], db["lin"], dr["lout"], db["lout"])
        for ck in range(8):
            all_gather(dr["kin"][ck], db["kin"], dr["kout"][ck], db["kout"][ck])
            all_gather(dr["vin"][ck], db["vin"], dr["vout"][ck], db["vout"][ck])
        P.barrier(exclude=db["kout"] + db["vout"])
        A = {"off": xg_off, "n": 1000 * (l + 1)}
        lfa = at_alloc(A, "lfa", [128, 64, 16], F32)
        ftm = at_alloc(A, "ftm", [128, 64, 16], F32)
        tbc = at_alloc(A, "tbc", [128, 64, 16], F32)
        pfa = at_alloc(A, "pfa", [128, 64, 16], F32)
        pfb = at_alloc(A, "pfb", [128, 64, 16], F32)
        fref = at_alloc(A, "fref", [128, 16, 16], F32)
        triu_t = at_alloc(A, "triu", [128, 128], F32)
        oh4_t = at_alloc(A, "oh4", [128, 4], F32)
        mfox_t = at_alloc(A, "mfox", [128, 4, 128], BF16)
        zo_t = at_alloc(A, "zo", [128, 16, 64], F32)
        qh = [at_alloc(A, "qh%d" % i, [128, TOK], BF16) for i in range(2)]
        kh = [at_alloc(A, "kh%d" % i, [128, 4, TOK], BF16) for i in range(2)]
        vh = [at_alloc(A, "vh%d" % i, [128, 4, 16, 128], BF16) for i in range(2)]
        bh = [at_alloc(A, "bh%d" % i, [128, 16, 64], F32) for i in range(2)]
        ah = [at_alloc(A, "ah%d" % i, [128, TOK], BF16) for i in range(2)]
        pts = Rot([(at_alloc(A, "pt%d" % i, [128, 512], BF16), [Buf("pt%d_%d" % (i, m)) for m in range(4)]) for i in range(4)])
        rec_t = at_alloc(A, "rec", [128, 512], F32)
        B_ = fox_cache.setdefault("B_", {k: Buf("fx_" + k) for k in (
            "lfa", "ftm", "tbc", "pfa", "pfb", "fref", "triu", "oh4", "mfox", "rec",
            "qh0", "qh1", "kh0", "kh1", "vh0", "vh1", "bh0", "bh1", "ah0", "ah1")})
        P.dma("sp", lambda h: h.dma_start(out=triu_t[:, :], in_=triu[:, :]), writes=[B_["triu"]])
        P.dma("sp", lambda h: h.dma_start(out=oh4_t[:, :], in_=oh4_in[:, :]), writes=[B_["oh4"]])
        P.dma("sp", lambda h: h.dma_start(out=zo_t[:, :, :].rearrange("p a b -> p (a b)"), in_=zo_in[:, :]), writes=[B_["oh4"]])
        P.dma("pool", lambda h: h.dma_start(out=mfox_t[:, :, :].rearrange("p j q -> p (j q)"), in_=m_fox_in[:, :]),
              writes=[B_["mfox"]])
        lout_ap = dr["lout"].ap()
        for rr in range(4):
            P.dma("sp", lambda h, rr=rr: h.dma_start(
                out=lfa[:, :, :].rearrange("p (i r) h -> p r i h", r=4)[:, rr],
                in_=lout_ap[rr * TOK:(rr + 1) * TOK, :].rearrange("(i p) h -> p i h", p=128)),
                reads=[db["lout"]], writes=[B_["lfa"]])
        lfa2 = lfa[:, :, :].rearrange("p g h -> p (g h)")
        ftm2 = ftm[:, :, :].rearrange("p g h -> p (g h)")
        tbc2 = tbc[:, :, :].rearrange("p g h -> p (g h)")
        pfa2 = pfa[:, :, :].rearrange("p g h -> p (g h)")
        pfb2 = pfb[:, :, :].rearrange("p g h -> p (g h)")
        for half in range(2):
            cs = slice(half * 512, (half + 1) * 512)
            pst, psb = PY.next()
            P.op("pe", mm(pst[:, :], triu_t[:, :], lfa2[:, cs], True, True), reads=[B_["triu"], B_["lfa"]], writes=[psb])
            P.op("dve", lambda h, pst=pst, cs=cs: h.tensor_copy(out=ftm2[:, cs], in_=pst[:, :]), reads=[psb], writes=[B_["ftm"]])
            pst, psb = PY.next()
            P.op("pe", mm(pst[:, :], ones_t[:, :], lfa2[:, cs], True, True), reads=[ones_b, B_["lfa"]], writes=[psb])
            P.op("dve", lambda h, pst=pst, cs=cs: h.tensor_copy(out=tbc2[:, cs], in_=pst[:, :]), reads=[psb], writes=[B_["tbc"]])
        P.op("dve", lambda h: h.tensor_copy(out=pfa2, in_=tbc2), reads=[B_["tbc"]], writes=[B_["pfa"]])
        cur, curb, oth, othb = pfa2, B_["pfa"], pfb2, B_["pfb"]
        sh = 1
        while sh < 64:
            w = sh * 16
            P.op("dve", lambda h, cur=cur, oth=oth, w=w: h.tensor_copy(out=oth[:, 0:w], in_=cur[:, 0:w]),
                 reads=[curb], writes=[othb])
            P.op("dve", lambda h, cur=cur, oth=oth, w=w: h.tensor_tensor(out=oth[:, w:1024], in0=cur[:, w:1024],
                                                                         in1=cur[:, 0:1024 - w], op=ALU.add),
                 reads=[curb], writes=[othb])
            cur, curb, oth, othb = oth, othb, cur, curb
            sh *= 2
        P.op("dve", lambda h, cur=cur, oth=oth: h.tensor_tensor(out=oth, in0=cur, in1=tbc2, op=ALU.subtract),
             reads=[curb, B_["tbc"]], writes=[othb])
        P.op("dve", lambda h, oth=oth: h.tensor_tensor(out=ftm2, in0=ftm2, in1=oth, op=ALU.add),
             reads=[othb, B_["ftm"]], writes=[B_["ftm"]])
        P.op("dve", lambda h, cur=cur, oth=oth: h.scalar_tensor_tensor(out=oth, in0=tbc2, scalar=-0.5, in1=cur,
                                                                        op0=ALU.mult, op1=ALU.add),
             reads=[curb, B_["tbc"]], writes=[othb])
        fmid4 = (pfa if oth is pfa2 else pfb)[:, :, :].rearrange("p (i r) h -> p r i h", r=4)
        P.op("dve", lambda h: h.tensor_scalar(out=fref[:, :, :], in0=fmid4[:, 0], scalar1=oh4_t[:, 0:1], scalar2=None,
                                              op0=ALU.mult), reads=[othb, B_["oh4"]], writes=[B_["fref"]])
        for rr in range(1, 4):
            P.op("dve", lambda h, rr=rr: h.scalar_tensor_tensor(out=fref[:, :, :], in0=fmid4[:, rr], scalar=oh4_t[:, rr:rr + 1],
                                                                in1=fref[:, :, :], op0=ALU.mult, op1=ALU.add),
                 reads=[othb, B_["oh4"], B_["fref"]], writes=[B_["fref"]])
        kout_v = [t.ap().rearrange("(r c d) t -> d r c t", r=4, d=128) for t in dr["kout"]]
        vout_v = [t.ap().rearrange("(r p) (h i d) -> p r h i d", r=4, h=2, i=16) for t in dr["vout"]]
        PS_ST = Rot([psum[0], psum[1], psum[2], psum[7]])
        PS_O = Rot([psum[3], psum[4]])
        PS_D = Rot([psum[5], psum[6]])
        scale = 128.0 ** -0.5
        def head_res(hd):
            s2 = hd % 2
            return (qh[s2], kh[s2], vh[s2], bh[s2], ah[s2],
                    B_["qh%d" % s2], B_["kh%d" % s2], B_["vh%d" % s2], B_["bh%d" % s2], B_["ah%d" % s2])

        def emit_head_loads(hd):
            qt, kt, vt, bt, at, qb, kb, vb, bb, ab = head_res(hd)
            P.dma("sp", lambda h: h.dma_start(out=qt[:, :], in_=q_s[hd * 128:(hd + 1) * 128, :]), reads=[q_b], writes=[qb])
            P.dma("sp", lambda h: h.dma_start(out=kt[:, :, :], in_=kout_v[hd // 2][:, :, hd % 2, :]),
                  reads=[db["kout"][hd // 2]], writes=[kb])
            P.dma("sp", lambda h: h.dma_start(out=vt[:, :, :, :], in_=vout_v[hd // 2][:, :, hd % 2]),
                  reads=[db["vout"][hd // 2]], writes=[vb])
            P.op("dve", lambda h: h.tensor_tensor(
                out=bt[:, :, :], in0=fref[:, :, hd:hd + 1].broadcast_to([128, 16, 64]),
                in1=ftm[:, :, hd:hd + 1].rearrange("p g o -> p o g").broadcast_to([128, 16, 64]), op=ALU.subtract),
                reads=[B_["fref"], B_["ftm"]], writes=[bb])
            P.op("dve", lambda h: h.tensor_tensor(out=bt[:, :, :], in0=bt[:, :, :], in1=zo_t[:, :, :], op=ALU.add),
                 reads=[bb, B_["oh4"]], writes=[bb])

        steps = [(hd, jq, gk) for hd in range(16) for jq in range(4) for gk in range(16 * (jq + 1))]
        nst = len(steps)
        qk = {}
        acc = {}

        def emit_qk(idx):
            hd, jq, gk = steps[idx]
            qt, kt, vt, bt, at, qb, kb, vb, bb, ab = head_res(hd)
            rr, ii = gk % 4, gk // 4
            mmin = max(0, -(-(gk - 3 - 16 * jq) // 4))
            c0 = mmin * 128
            pst, psb = PS_ST.next()
            P.op("pe", mm(pst[:, c0:512], kt[:, rr, ii * 128:(ii + 1) * 128], qt[:, jq * 512 + c0:(jq + 1) * 512],
                          True, True), reads=[kb, qb], writes=[psb])
            qk[idx] = (pst, psb, mmin, c0)

        def emit_exp(idx):
            hd, jq, gk = steps[idx]
            qt, kt, vt, bt, at, qb, kb, vb, bb, ab = head_res(hd)
            pst, psb, mmin, c0 = qk[idx]
            pt, ptb = pts.next()
            for m in range(mmin, 4):
                li = 4 * jq + m
                P.op("act", lambda h, m=m, li=li: h.activation(
                    out=pt[:, m * 128:(m + 1) * 128], in_=pst[:, m * 128:(m + 1) * 128], func=AF.Exp,
                    bias=bt[:, li, gk:gk + 1], scale=scale), reads=[psb, bb], writes=[ptb[m]])
                jm = gk - 4 * li
                if 0 <= jm <= 3:
                    P.op("pool", lambda h, m=m, jm=jm: h.tensor_tensor(
                        out=pt[:, m * 128:(m + 1) * 128], in0=pt[:, m * 128:(m + 1) * 128], in1=mfox_t[:, jm, :],
                        op=ALU.mult), reads=[ptb[m], B_["mfox"]], writes=[ptb[m]])
            qk[idx] = (pst, psb, mmin, c0, pt, ptb)

        def emit_pv(idx):
            hd, jq, gk = steps[idx]
            qt, kt, vt, bt, at, qb, kb, vb, bb, ab = head_res(hd)
            pst, psb, mmin, c0, pt, ptb = qk.pop(idx)
            rr, ii = gk % 4, gk // 4
            ng = 16 * (jq + 1)
            if gk == 0:
                acc[(hd, jq)] = (PS_O.next(), PS_D.next())
            (po, pob), (pd, pdb) = acc[(hd, jq)]
            P.op("pe", mm(po[:, c0:512], vt[:, rr, ii, :], pt[:, c0:512], gk == 0, gk == ng - 1),
                 reads=[vb] + ptb[mmin:], writes=[pob])
            P.op("pe", mm(pd[:, c0:512], onesb_t[:, :], pt[:, c0:512], gk == 0, gk == ng - 1),
                 reads=[ones_b] + ptb[mmin:], writes=[pdb])
            if gk == ng - 1:
                del acc[(hd, jq)]
                P.op("dve", lambda h: h.reciprocal(out=rec_t[:, :], in_=pd[:, :]), reads=[pdb], writes=[B_["rec"]])
                P.op("dve", lambda h: h.tensor_tensor(out=at[:, jq * 512:(jq + 1) * 512], in0=po[:, :],
                                                      in1=rec_t[:, :], op=ALU.mult),
                     reads=[pob, B_["rec"]], writes=[ab])
                if jq == 3:
                    P.dma("sp", lambda h: h.dma_start(out=att_s[hd * 128:(hd + 1) * 128, :], in_=at[:, :]),
                          reads=[ab], writes=[att_b])
                    if hd + 2 < 16:
                        emit_head_loads(hd + 2)

        emit_head_loads(0)
        emit_head_loads(1)
        emit_qk(0)
        emit_qk(1)
        emit_qk(2)
        for idx in range(nst):
            emit_exp(idx)
            if idx + 3 < nst:
                emit_qk(idx + 3)
            emit_pv(idx)
        P.barrier()
        attn_out_phase(fox_w_out[j])
        P.barrier()

    def swa_layer(l):
        dr = swa_dr
        db = dr["b"]
        W = swa_w_in[0]
        kin_v = [t.ap().rearrange("(h d) t -> d h t", d=64) for t in dr["kin"]]
        vin_v = [t.ap().rearrange("p (i f) -> p i f", i=8) for t in dr["vin"]]
        q_v = q_s.rearrange("(h d) t -> d h t", d=64)
        att_v = att_s.rearrange("(h d) t -> d h t", d=64)
        c16 = sb("swa_c16", [16, TOK], F32)
        s16 = sb("swa_s16", [16, TOK], F32)
        pmat_t = sb("swa_pmat", [64, 16], BF16)
        invf_t = sb("swa_invf", [16, 1], F32)
        vst_t = sb("swa_vst", [128, 4, 512], BF16)
        SB_ = {k: Buf("sw_" + k) for k in ("c16", "s16", "pmat", "invf", "vst", "posi")}
        posi_t = sb("swa_posi", [16, TOK], I32)
        P.dma("sp", lambda h: h.dma_start(out=posi_t[:, :], in_=posin[0, :].partition_broadcast(16)), writes=[SB_["posi"]])
        P.dma("pool", lambda h: h.dma_start(out=pmat_t[:, :], in_=pmat_in[:, :]), writes=[SB_["pmat"]])
        P.dma("sp", lambda h: h.dma_start(out=invf_t[:, :], in_=invf_in[:, :]), writes=[SB_["invf"]])
        PI = float(np.pi)
        C1 = 6.28125
        C2 = float(2 * np.pi - 6.28125)
        ang = yT_t[0:16, 8:12, :].rearrange("p c t -> p (c t)")
        nf = yT_t[0:16, 0:4, :].rearrange("p c t -> p (c t)")
        mk = yT_t[0:16, 4:8, :].rearrange("p c t -> p (c t)")
        ys, yc = s16[:, :], c16[:, :]
        RW = dict(reads=[SB_["posi"], SB_["invf"], yT_b, SB_["s16"], SB_["c16"]],
                  writes=[yT_b, SB_["s16"], SB_["c16"], SB_["posi"]])
        P.op("dve", lambda h: h.tensor_copy(out=ang, in_=posi_t[:, :]), **RW)
        P.op("dve", lambda h: h.tensor_scalar_mul(out=ang, in0=ang, scalar1=invf_t[:, 0:1]), **RW)
        P.op("dve", lambda h: h.tensor_scalar_mul(out=nf, in0=ang, scalar1=float(1.0 / (2 * np.pi))), **RW)
        P.op("dve", lambda h: h.tensor_copy(out=posi_t[:, :], in_=nf), **RW)
        P.op("dve", lambda h: h.tensor_copy(out=nf, in_=posi_t[:, :]), **RW)
        P.op("dve", lambda h: h.scalar_tensor_tensor(out=ys, in0=nf, scalar=-C1, in1=ang, op0=ALU.mult, op1=ALU.add), **RW)
        P.op("dve", lambda h: h.scalar_tensor_tensor(out=ys, in0=nf, scalar=-C2, in1=ys, op0=ALU.mult, op1=ALU.add), **RW)
        P.op("dve", lambda h: h.tensor_single_scalar(out=mk, in_=ys, scalar=PI, op=ALU.is_gt), **RW)
        P.op("dve", lambda h: h.scalar_tensor_tensor(out=ys, in0=mk, scalar=-2 * PI, in1=ys, op0=ALU.mult, op1=ALU.add), **RW)
        P.op("dve", lambda h: h.tensor_single_scalar(out=mk, in_=ys, scalar=-PI, op=ALU.is_lt), **RW)
        P.op("dve", lambda h: h.scalar_tensor_tensor(out=ys, in0=mk, scalar=2 * PI, in1=ys, op0=ALU.mult, op1=ALU.add), **RW)
        P.op("dve", lambda h: h.tensor_scalar_add(out=yc, in0=ys, scalar1=PI / 2), **RW)
        P.op("dve", lambda h: h.tensor_single_scalar(out=mk, in_=yc, scalar=PI, op=ALU.is_gt), **RW)
        P.op("dve", lambda h: h.scalar_tensor_tensor(out=yc, in0=mk, scalar=-2 * PI, in1=yc, op0=ALU.mult, op1=ALU.add), **RW)
        P.op("act", lambda h: h.activation(out=ys, in_=ys, func=AF.Sin), reads=[SB_["s16"]], writes=[SB_["s16"]])
        P.op("act", lambda h: h.activation(out=yc, in_=yc, func=AF.Sin), reads=[SB_["c16"]], writes=[SB_["c16"]])

        def rope(tile_fn, nheads, g):
            for hh in range(nheads):
                pst, psb = PM
                P.op("pe", mm(pst[0:16, :], pmat_t[:, :], tile_fn(hh), True, True), reads=[SB_["pmat"], big_b], writes=[psb])
                t1, t1b = tmps.next()
                t2, t2b = tmps.next()
                P.op("dve", lambda h, t1=t1, hh=hh: h.tensor_tensor(out=t1[0:16, :], in0=tile_fn(hh)[0:16, :],
                                                                    in1=c16[:, g * TG:(g + 1) * TG], op=ALU.mult),
                     reads=[big_b, SB_["c16"]], writes=[t1b])
                P.op("dve", lambda h, t2=t2, pst=pst: h.tensor_tensor(out=t2[0:16, :], in0=pst[0:16, :],
                                                                      in1=s16[:, g * TG:(g + 1) * TG], op=ALU.mult),
                     reads=[psb, SB_["s16"]], writes=[t2b])
                P.op("pool", lambda h, t1=t1, t2=t2, hh=hh: h.tensor_tensor(out=tile_fn(hh)[0:16, :], in0=t1[0:16, :],
                                                                            in1=t2[0:16, :], op=ALU.add),
                     reads=[t1b, t2b], writes=[big_b])

        for g in range(NG):
            load_xg(g)
            prenorm(der_t[:, 0, :], modT[:, 0:16])
            linear_fm(W, 0, 32, lambda c: hT_t[:, c, :], [hT_b], NCH, evac_to(lambda oc: big_t[0:64, oc, :], big_b),
                      ocw=64, per_load=8)
            rope(lambda hh: big_t[0:64, hh, :], 32, g)
            P.dma("sp", lambda h, g=g: h.dma_start(out=q_v[:, :, g * TG:(g + 1) * TG], in_=big_t[0:64, 0:32, :]),
                  reads=[big_b], writes=[q_b])
            linear_fm(W, D, 8, lambda c: hT_t[:, c, :], [hT_b], NCH, evac_to(lambda oc: big_t[0:64, 32 + oc, :], big_b),
                      ocw=64, per_load=8)
            rope(lambda hh: big_t[0:64, 32 + hh, :], 8, g)
            for ck in range(2):
                P.dma("sp", lambda h, g=g, ck=ck: h.dma_start(out=kin_v[ck][:, :, g * TG:(g + 1) * TG],
                                                               in_=big_t[0:64, 32 + 4 * ck:36 + 4 * ck, :]),
                      reads=[big_b], writes=[db["kin"]])
            slot, slot_b = wrot.next()
            view = load_w_cols(W, D + 512, 512, slot, slot_b)
            for blk in range(4):
                pst, psb = PY.next()
                for c in range(NCH):
                    P.op("pe", mm(pst[:, :], hT_t[:, c, blk * 128:(blk + 1) * 128], view[:, c, :], c == 0, c == NCH - 1),
                         reads=[hT_b, slot_b], writes=[psb])
                P.op("act", lambda h, pst=pst, blk=blk: h.activation(out=vst_t[:, blk, :], in_=pst[:, :], func=AF.Copy),
                     reads=[psb], writes=[SB_["vst"]])
            P.dma("sp", lambda h, g=g: h.dma_start(out=vin_v[g // 2][:, (g % 2) * 4:(g % 2) * 4 + 4, :], in_=vst_t[:, :, :]),
                  reads=[SB_["vst"]], writes=[db["vin"]])
        for ck in range(2):
            all_gather(dr["kin"][ck], db["kin"], dr["kout"][ck], db["kout"])
            all_gather(dr["vin"][ck], db["vin"], dr["vout"][ck], db["vout"])
        P.barrier()
        A = {"off": xg_off, "n": 5000}
        kc2 = [at_alloc(A, "kc%d" % i, [64, 8, 128], BF16) for i in range(2)]
        vc2 = [at_alloc(A, "vc%d" % i, [128, 512], BF16) for i in range(2)]
        kp2_ = [at_alloc(A, "kp%d" % i, [64, 8, 128], BF16) for i in range(2)]
        vp2_ = [at_alloc(A, "vp%d" % i, [128, 512], BF16) for i in range(2)]
        qb2 = [at_alloc(A, "qblk%d" % i, [64, 32, 128], BF16) for i in range(2)]
        ab2 = [at_alloc(A, "ablk%d" % i, [64, 32, 128], BF16) for i in range(2)]
        kcand = at_alloc(A, "kcand", [64, 5, 8, 128], BF16)
        vcand = at_alloc(A, "vcand", [128, 5, 512], BF16)
        mtri = at_alloc(A, "mtri", [128, 128], BF16)
        mprev = at_alloc(A, "mprev", [128, 128], BF16)
        mprev0 = at_alloc(A, "mprev0", [128, 128], BF16)
        sel5 = at_alloc(A, "sel5", [128, 5], F32)
        sinke = at_alloc(A, "sinke", [64, 32], F32)
        den_t = at_alloc(A, "den", [64, 512], F32)
        ptc = Rot([(at_alloc(A, "ptc%d" % i, [128, 512], BF16), Buf("ptc%d" % i)) for i in range(2)])
        ptp = Rot([(at_alloc(A, "ptp%d" % i, [128, 512], BF16), Buf("ptp%d" % i)) for i in range(2)])
        B_ = {k: Buf("sw3_" + k) for k in ("kc0", "kc1", "vc0", "vc1", "kcand", "vcand", "kp0", "kp1", "vp0", "vp1",
                                           "qblk0", "qblk1", "ablk0", "ablk1", "mtri", "mprev",
                                           "mprev0", "sel5", "sinke", "den")}
        P.dma("pool", lambda h: h.dma_start(out=mtri[:, :], in_=triu[:, :]), writes=[B_["mtri"]])
        P.dma("pool", lambda h: h.dma_start(out=mprev[:, :], in_=m_prev_in[:, :]), writes=[B_["mprev"]])
        P.dma("pool", lambda h: h.dma_start(out=mprev0[:, :], in_=m_prev0_in[:, :]), writes=[B_["mprev0"]])
        P.dma("sp", lambda h: h.dma_start(out=sel5[:, :], in_=sel5_in[:, :]), writes=[B_["sel5"]])
        P.dma("sp", lambda h: h.dma_start(out=sinke[:, :], in_=swa_sinks[0, :].partition_broadcast(64)), writes=[B_["sinke"]])
        P.op("act", lambda h: h.activation(out=sinke[:, :], in_=sinke[:, :], func=AF.Exp), reads=[B_["sinke"]], writes=[B_["sinke"]])
        kout_v = [t.ap().rearrange("(r h d) t -> d r h t", r=4, d=64) for t in dr["kout"]]
        vout_v = [t.ap().rearrange("(r p) (i f) -> p r i f", r=4, i=8) for t in dr["vout"]]
        PS_C = Rot([psum[0], psum[1]])
        PS_P = Rot([psum[2], psum[3]])
        PS_O = Rot([psum[4], psum[5]])
        PS_D = Rot([psum[6], psum[7]])
        scale = 64.0 ** -0.5

        def emit_block_loads(i):
            par = i % 2
            kc, vc, kp, vp, qblk = kc2[par], vc2[par], kp2_[par], vp2_[par], qb2[par]
            kcb, vcb, kpb, vpb, qbb = (B_["kc%d" % par], B_["vc%d" % par], B_["kp%d" % par], B_["vp%d" % par],
                                       B_["qblk%d" % par])
            ip = max(i - 1, 0)
            for ck in range(2):
                P.dma("sp", lambda h, ck=ck: h.dma_start(out=kc[:, 4 * ck:4 * ck + 4, :],
                                                         in_=kin_v[ck][:, :, i * 128:(i + 1) * 128]),
                      reads=[db["kin"]], writes=[kcb])
                for rr in range(4):
                    P.dma("sp", lambda h, ck=ck, rr=rr: h.dma_start(
                        out=kcand[:, rr, 4 * ck:4 * ck + 4, :], in_=kout_v[ck][:, rr, :, i * 128:(i + 1) * 128]),
                        reads=[db["kout"]], writes=[B_["kcand"]])
                P.dma("sp", lambda h, ck=ck: h.dma_start(
                    out=kcand[:, 4, 4 * ck:4 * ck + 4, :], in_=kout_v[ck][:, 3, :, ip * 128:(ip + 1) * 128]),
                    reads=[db["kout"]], writes=[B_["kcand"]])
            P.dma("sp", lambda h: h.dma_start(out=vc[:, :], in_=vin_v[i // 8][:, i % 8, :]), reads=[db["vin"]], writes=[vcb])
            P.dma("sp", lambda h: h.dma_start(out=vcand[:, 0:4, :], in_=vout_v[i // 8][:, :, i % 8, :]),
                  reads=[db["vout"]], writes=[B_["vcand"]])
            P.dma("sp", lambda h: h.dma_start(out=vcand[:, 4, :], in_=vout_v[ip // 8][:, 3, ip % 8, :]),
                  reads=[db["vout"]], writes=[B_["vcand"]])
            P.dma("sp", lambda h: h.dma_start(out=qblk[:, :, :], in_=q_v[:, :, i * 128:(i + 1) * 128]),
                  reads=[q_b], writes=[qbb])

        def emit_block_select(i):
            par = i % 2
            kp, vp = kp2_[par], vp2_[par]
            kpb, vpb = B_["kp%d" % par], B_["vp%d" % par]
            kpf = kp[:, :, :].rearrange("d h t -> d (h t)")
            P.op("dve", lambda h: h.tensor_scalar(out=kpf, in0=kcand[:, 0, :, :].rearrange("d h t -> d (h t)"),
                                                  scalar1=sel5[0:64, 0:1], scalar2=None, op0=ALU.mult),
                 reads=[B_["kcand"], B_["sel5"]], writes=[kpb])
            P.op("dve", lambda h: h.tensor_scalar(out=vp[:, :], in0=vcand[:, 0, :], scalar1=sel5[:, 0:1], scalar2=None,
                                                  op0=ALU.mult), reads=[B_["vcand"], B_["sel5"]], writes=[vpb])
            for cnd in range(1, 5):
                P.op("dve", lambda h, cnd=cnd: h.scalar_tensor_tensor(
                    out=kpf, in0=kcand[:, cnd, :, :].rearrange("d h t -> d (h t)"), scalar=sel5[0:64, cnd:cnd + 1], in1=kpf,
                    op0=ALU.mult, op1=ALU.add), reads=[B_["kcand"], B_["sel5"], kpb], writes=[kpb])
                P.op("dve", lambda h, cnd=cnd: h.scalar_tensor_tensor(
                    out=vp[:, :], in0=vcand[:, cnd, :], scalar=sel5[:, cnd:cnd + 1], in1=vp[:, :],
                    op0=ALU.mult, op1=ALU.add), reads=[B_["vcand"], B_["sel5"], vpb], writes=[vpb])

        sw_steps = [(i, hk) for i in range(NBLK) for hk in range(8)]
        sA = {}

        def stageA(si):
            i, hk = sw_steps[si]
            par = i % 2
            qsl = qb2[par][:, hk * 4:(hk + 1) * 4, :]
            pc, pcb = PS_C.next()
            pp, ppb = PS_P.next()
            P.op("pe", mm(pc[:, :], kc2[par][:, hk, :], qsl, True, True), reads=[B_["kc%d" % par], B_["qblk%d" % par]], writes=[pcb])
            P.op("pe", mm(pp[:, :], kp2_[par][:, hk, :], qsl, True, True), reads=[B_["kp%d" % par], B_["qblk%d" % par]], writes=[ppb])
            sA[si] = (pc, pcb, pp, ppb)

        def stageB(si):
            i, hk = sw_steps[si]
            par = i % 2
            vc, vp, ablk = vc2[par], vp2_[par], ab2[par]
            vcb, vpb, abb = B_["vc%d" % par], B_["vp%d" % par], B_["ablk%d" % par]
            pc, pcb, pp, ppb = sA.pop(si)
            mpv, mpvb = (mprev0, B_["mprev0"]) if i == 0 else (mprev, B_["mprev"])
            tc_, tcb = ptc.next()
            tp_, tpb = ptp.next()
            P.op("act", lambda h: h.activation(out=tc_[:, :], in_=pc[:, :], func=AF.Exp, scale=scale), reads=[pcb], writes=[tcb])
            P.op("act", lambda h: h.activation(out=tp_[:, :], in_=pp[:, :], func=AF.Exp, scale=scale), reads=[ppb], writes=[tpb])
            P.op("pool", lambda h: h.tensor_tensor(
                out=tc_[:, :].rearrange("k (a q) -> k a q", a=4), in0=tc_[:, :].rearrange("k (a q) -> k a q", a=4),
                in1=mtri[:, :].rearrange("k (o q) -> k o q", o=1).broadcast_to([128, 4, 128]), op=ALU.mult),
                reads=[tcb, B_["mtri"]], writes=[tcb])
            P.op("dve", lambda h: h.tensor_tensor(
                out=tp_[:, :].rearrange("k (a q) -> k a q", a=4), in0=tp_[:, :].rearrange("k (a q) -> k a q", a=4),
                in1=mpv[:, :].rearrange("k (o q) -> k o q", o=1).broadcast_to([128, 4, 128]), op=ALU.mult),
                reads=[tpb, mpvb], writes=[tpb])
            po, pob = PS_O.next()
            pd, pdb = PS_D.next()
            P.op("pe", mm(po[0:64, :], vc[:, hk * 64:(hk + 1) * 64], tc_[:, :], True, False), reads=[vcb, tcb], writes=[pob])
            P.op("pe", mm(po[0:64, :], vp[:, hk * 64:(hk + 1) * 64], tp_[:, :], False, True), reads=[vpb, tpb], writes=[pob])
            P.op("pe", mm(pd[0:64, :], onesb_t[:, 0:64], tc_[:, :], True, False), reads=[ones_b, tcb], writes=[pdb])
            P.op("pe", mm(pd[0:64, :], onesb_t[:, 0:64], tp_[:, :], False, True), reads=[ones_b, tpb], writes=[pdb])
            P.op("dve", lambda h: h.tensor_tensor(
                out=den_t[:, :].rearrange("d (a q) -> d a q", a=4), in0=pd[0:64, :].rearrange("d (a q) -> d a q", a=4),
                in1=sinke[:, hk * 4:(hk + 1) * 4].rearrange("d (a o) -> d a o", o=1).broadcast_to([64, 4, 128]), op=ALU.add),
                reads=[pdb, B_["sinke"]], writes=[B_["den"]])
            P.op("dve", lambda h: h.reciprocal(out=den_t[:, :], in_=den_t[:, :]), reads=[B_["den"]], writes=[B_["den"]])
            P.op("dve", lambda h: h.tensor_tensor(
                out=ablk[:, hk * 4:(hk + 1) * 4, :], in0=po[0:64, :].rearrange("d (a q) -> d a q", a=4),
                in1=den_t[:, :].rearrange("d (a q) -> d a q", a=4), op=ALU.mult),
                reads=[pob, B_["den"]], writes=[abb])
            if hk == 7:
                P.dma("sp", lambda h: h.dma_start(out=att_v[:, :, i * 128:(i + 1) * 128], in_=ablk[:, :, :]),
                      reads=[abb], writes=[att_b])

        emit_block_loads(0)
        emit_block_select(0)
        stageA(0)
        for si in range(len(sw_steps)):
            bi, bh = sw_steps[si]
            if bh == 0 and bi + 1 < NBLK:
                emit_block_loads(bi + 1)
            if si + 1 < len(sw_steps):
                if sw_steps[si + 1][1] == 0:
                    emit_block_select(sw_steps[si + 1][0])
                stageA(si + 1)
            stageB(si)
        P.barrier()
        attn_out_phase(swa_w_out[0])
        P.barrier()

    for l in layers:
        compute_mod(l)
        kind = l % 3
        if do_mixer:
            if kind == 0:
                arena["off"] = phase_base
                fox_layer(l, l // 3)
            if kind == 2:
                arena["off"] = phase_base
                swa_layer(l)
            if kind == 1:
                arena["off"] = phase_base
                st = sgu_setup()
                for g in range(NG):
                    sgu_group(st, g)
                P.barrier()
        if do_ffn:
            P.barrier()
            ffn_big(l)
            P.barrier()

    for g in range(NG):
        load_xg(g)
        stage = yT_t[:, :, :].rearrange("p c t -> p (c t)").rearrange("p (b d) -> p b d", b=4)
        for b in range(4):
            for q in range(4):
                pst, psb = PY.next()
                for j in range(4):
                    c = q * 4 + j
                    P.op("pe", lambda h, pst=pst, b=b, c=c, j=j: h.transpose(
                        pst[:, j * 128:(j + 1) * 128], xg_t[:, c, b * 128:(b + 1) * 128], ident_t[:, :]),
                        reads=[xg_b, ident_b], writes=[psb])
                if q % 2 == 0:
                    P.op("act", lambda h, pst=pst, b=b, q=q: h.activation(out=stage[:, b, q * 512:(q + 1) * 512],
                                                                          in_=pst[:, :], func=AF.Copy),
                         reads=[psb], writes=[yT_b])
                else:
                    P.op("dve", lambda h, pst=pst, b=b, q=q: h.tensor_copy(out=stage[:, b, q * 512:(q + 1) * 512],
                                                                           in_=pst[:, :]), reads=[psb], writes=[yT_b])
        P.dma("sp", lambda h, g=g: h.dma_start(
            out=yout[g * TG:(g + 1) * TG, :].rearrange("(b p) d -> p b d", p=128), in_=stage),
            reads=[yT_b], writes=[yout_b])
    P.barrier()
    P.emit(nc)
    es.close()
    return nc


_TRI = np.tril(np.ones((128, 128), np.float32))


def _prep_inputs(inp, layers):
    x = np.asarray(inp["x"], np.float32)
    maps = []
    shared = {
        "ident": np.eye(128, dtype=np.float32),
        "trimask": _TRI,
        "ffn_w_gu": np.ascontiguousarray(np.asarray(inp["ffn_w_gu"], np.float32)[list(layers)]),
        "ffn_w_down": np.ascontiguousarray(np.asarray(inp["ffn_w_down"], np.float32)[list(layers)]),
        "sgu_w_in": np.ascontiguousarray(inp["sgu_w_in"], np.float32),
        "sgu_ln_g": np.ascontiguousarray(inp["sgu_ln_g"], np.float32),
        "sgu_ln_b": np.ascontiguousarray(inp["sgu_ln_b"], np.float32),
        "sgu_w_s": np.ascontiguousarray(inp["sgu_w_s"], np.float32),
        "sgu_b_s": np.ascontiguousarray(inp["sgu_b_s"], np.float32).reshape(1, 2048),
        "sgu_w_out": np.ascontiguousarray(inp["sgu_w_out"], np.float32),
    }
    for n in ("fox_w_in", "fox_b_f", "fox_w_out", "swa_w_in", "swa_sinks", "swa_w_out"):
        shared[n] = np.ascontiguousarray(inp[n], np.float32)
    shared["triu"] = np.ascontiguousarray(_TRI.T)
    shared["m_prev"] = np.ascontiguousarray(1.0 - _TRI.T)
    pm = np.zeros((64, 16), np.float32)
    for m_ in range(8):
        pm[m_ + 8, m_] = -1.0
        pm[m_, m_ + 8] = 1.0
    shared["pmat"] = pm
    inv = (500000.0 ** (-np.arange(0, 16, 2, dtype=np.float32) / np.float32(16))).astype(np.float32)
    shared["invf"] = np.concatenate([inv, inv]).reshape(16, 1).astype(np.float32)
    for n in ("mix_pre_g", "mix_post_g", "ffn_pre_g", "ffn_post_g"):
        shared[n] = np.ascontiguousarray(inp[n], np.float32).reshape(64, 128)
    for core in range(8):
        b, r = core // 4, core % 4
        xb = x[b].reshape(16, 4, 128, D)[:, r].reshape(TOK, D)
        m = dict(shared)
        m["xs"] = np.ascontiguousarray(xb)
        m["ada_w"] = np.ascontiguousarray(np.asarray(inp["ada_w"], np.float32)[list(layers)][:, :, r * 3072:(r + 1) * 3072])
        m["ada_b"] = np.ascontiguousarray(np.asarray(inp["ada_b"], np.float32)[list(layers)][:, r * 3072:(r + 1) * 3072])
        m["cvec"] = np.ascontiguousarray(inp["c"][b], np.float32).reshape(16, 128)
        pos = np.asarray(inp["positions"])[b].astype(np.int32)
        m["posin"] = np.ascontiguousarray(pos.reshape(16, 4, 128)[:, r].reshape(1, TOK))
        mf = np.zeros((128, 4, 128), np.float32)
        for j_ in range(4):
            if j_ < r:
                mf[:, j_, :] = 1.0
            elif j_ == r:
                mf[:, j_, :] = _TRI.T
        m["m_fox"] = mf.reshape(128, 512)
        m["m_prev0"] = np.zeros((128, 128), np.float32) if r == 0 else np.ascontiguousarray(1.0 - _TRI.T)
        oh = np.zeros((128, 4), np.float32)
        oh[:, r] = 1.0
        m["oh4"] = oh
        zo = np.zeros((128, 16, 64), np.float32)
        for li_ in range(16):
            zo[:, li_, 4 * li_ + r + 1:] = -30000.0
        m["zo"] = zo.reshape(128, 1024)
        s5 = np.zeros((128, 5), np.float32)
        s5[:, (r - 1) if r > 0 else 4] = 1.0
        m["sel5"] = s5
        maps.append(m)
    return maps


def run(inp, layers=(0, 1, 2, 3), **kw):
    nc = build_program(layers=layers, **kw)
    maps = _prep_inputs(inp, layers)
    res = run_bass_kernel_spmd(nc, maps, core_ids=list(range(8)))
    out = np.empty((2, SEQ, D), np.float32)
    for core in range(8):
        b, r = core // 4, core % 4
        out[b].reshape(16, 4, 128, D)[:, r] = res.results[core]["yout"].reshape(16, 128, D)
    return out


def kernel(**inputs):
    return run(inputs)
```

```python
import numpy as np
import ml_dtypes
from contextlib import ExitStack
import concourse.bass as bass
import concourse.mybir as mybir
from concourse.bass_utils import run_bass_kernel_spmd

F32 = mybir.dt.float32
BF16 = mybir.dt.bfloat16
I32 = mybir.dt.int32
AF = mybir.ActivationFunctionType
ALU = mybir.AluOpType

D = 2048
NCH = 16
SEQ = 8192
TOK = 2048
NBLK = 16
TG = 512
NG = TOK // TG
DFF = 5632
NFC = DFF // 128
EPS = 1e-6
FOX_IN = 6160
SWA_IN = 3072
ENGS = ("pe", "act", "dve", "pool", "sp")
BLOCKNAME = {"pe": "tensor", "act": "scalar", "dve": "vector", "pool": "gpsimd", "sp": "sync"}


class Buf:
    __slots__ = ("name", "w", "rs", "dtotal")

    def __init__(self, name):
        self.name = name
        self.w = None
        self.rs = {}
        self.dtotal = 0


class Plan:
    def __init__(self):
        self.recs = {e: [] for e in ENGS}
        self.seen = {e: {} for e in ENGS}
        self.dbufs = {}

    def _deps(self, eng, reads, writes, skipkey=None):
        need = {}
        seen = self.seen[eng]

        def add(tok):
            key, val = tok
            if key == ("E", "pe") and eng == "pe":
                return
            if key == skipkey:
                return
            if seen.get(key, -1) >= val:
                return
            if need.get(key, -1) < val:
                need[key] = val

        for b in reads:
            if b.w is not None:
                add(b.w)
        for b in writes:
            if b.w is not None:
                add(b.w)
            for k, v in b.rs.items():
                add((k, v))
        for k, v in need.items():
            seen[k] = v
            if k[0] == "E":
                self.recs[k[1]][v][3] = True
        return list(need.items())

    def op(self, eng, fn, reads=(), writes=()):
        waits = self._deps(eng, reads, writes)
        idx = len(self.recs[eng])
        self.recs[eng].append([waits, fn, None, False, 0])
        key = ("E", eng)
        for b in reads:
            if b.rs.get(key, -1) < idx:
                b.rs[key] = idx
        for b in writes:
            b.w = (key, idx)
            b.rs = {}

    def dma(self, eng, fn, reads=(), writes=(), dbuf=None, inc=16):
        if dbuf is None:
            dbuf = writes[0]
        waits = self._deps(eng, reads, writes, skipkey=("D", id(dbuf)))
        self.dbufs[id(dbuf)] = dbuf
        dbuf.dtotal += inc
        key = ("D", id(dbuf))
        val = dbuf.dtotal
        self.recs[eng].append([waits, fn, id(dbuf), False, inc])
        for b in reads:
            if b.rs.get(key, -1) < val:
                b.rs[key] = val
        for b in writes:
            b.w = (key, val)
            b.rs = {}

    def barrier(self, exclude=()):
        excl = set(id(b) for b in exclude)
        for e in ENGS:
            need = []
            seen = self.seen[e]
            for e2 in ENGS:
                if e2 == e or not self.recs[e2]:
                    continue
                idx = None
                for j in range(len(self.recs[e2]) - 1, -1, -1):
                    r = self.recs[e2][j]
                    if r[1] is not None and r[2] is None:
                        idx = j
                        break
                if idx is None:
                    continue
                key = ("E", e2)
                if seen.get(key, -1) < idx:
                    seen[key] = idx
                    self.recs[e2][idx][3] = True
                    need.append((key, idx))
            for bid, b in self.dbufs.items():
                key = ("D", bid)
                if bid in excl:
                    continue
                if b.dtotal > 0 and seen.get(key, -1) < b.dtotal:
                    seen[key] = b.dtotal
                    need.append((key, b.dtotal))
            if need:
                self.recs[e].append([need, None, None, False, 0])

    def emit(self, nc):
        vals = {}
        for e in ENGS:
            cnt = 0
            v = []
            for rec in self.recs[e]:
                if rec[3]:
                    cnt += 1
                v.append(cnt)
            vals[e] = v
            assert cnt < 60000, (e, cnt)
        with ExitStack() as es:
            esem = {e: es.enter_context(nc.semaphore("sem_" + e)) for e in ENGS}
            dsem = {}
            for n, bid in enumerate(self.dbufs):
                dsem[bid] = es.enter_context(nc.semaphore("dsem%d" % n))
            block = es.enter_context(nc.Block())
            for e in ENGS:
                def body(h, e=e):
                    for waits, fn, dma, flagged, inc in self.recs[e]:
                        for key, val in waits:
                            if key[0] == "E":
                                h.wait_ge(esem[key[1]], vals[key[1]][val])
                            else:
                                h.wait_ge(dsem[key[1]], val)
                        if fn is None:
                            continue
                        ins = fn(h)
                        if dma is not None:
                            ins.then_inc(dsem[dma], inc)
                        elif flagged:
                            ins.then_inc(esem[e], 1)
                getattr(block, BLOCKNAME[e])(body)


class Rot:
    def __init__(self, items):
        self.items = items
        self.i = 0

    def next(self):
        it = self.items[self.i % len(self.items)]
        self.i += 1
        return it


def build_program(layers=(0, 1, 2, 3), do_mixer=True, do_ffn=True):
    NL = len(layers)
    LI = {l: i for i, l in enumerate(layers)}
    nc = bass.Bass("TRN2", target_bir_lowering=False)
    P = Plan()

    def din(name, shape, dt=F32):
        return nc.dram_tensor(name, list(shape), dt, kind="ExternalInput").ap()

    xs = din("xs", [TOK, D])
    cvec = din("cvec", [16, 128])
    ident = din("ident", [128, 128])
    ada_w = din("ada_w", [NL, D, 3072])
    ada_b = din("ada_b", [NL, 3072])
    gains = [din(n, [64, 128]) for n in ("mix_pre_g", "mix_post_g", "ffn_pre_g", "ffn_post_g")]
    w_gu = din("ffn_w_gu", [NL, D, 2 * DFF])
    w_dn = din("ffn_w_down", [NL, DFF, D])
    sgu_w_in = din("sgu_w_in", [1, D, 2 * D])
    sgu_ln_g = din("sgu_ln_g", [1, D])
    sgu_ln_b = din("sgu_ln_b", [1, D])
    sgu_w_s = din("sgu_w_s", [1, 16, 128, 128])
    sgu_b_s = din("sgu_b_s", [1, 16 * 128])
    sgu_w_out = din("sgu_w_out", [1, D, D])
    trimask = din("trimask", [128, 128])
    fox_w_in = din("fox_w_in", [2, D, FOX_IN])
    fox_b_f = din("fox_b_f", [2, 16])
    fox_w_out = din("fox_w_out", [2, D, D])
    swa_w_in = din("swa_w_in", [1, D, SWA_IN])
    swa_sinks = din("swa_sinks", [1, 32])
    swa_w_out = din("swa_w_out", [1, D, D])
    posin = din("posin", [1, TOK], I32)
    triu = din("triu", [128, 128])
    m_prev_in = din("m_prev", [128, 128])
    m_fox_in = din("m_fox", [128, 4 * 128])
    m_prev0_in = din("m_prev0", [128, 128])
    zo_in = din("zo", [128, 1024])
    oh4_in = din("oh4", [128, 4])
    sel5_in = din("sel5", [128, 5])
    pmat_in = din("pmat", [64, 16])
    invf_in = din("invf", [16, 1])
    yout = nc.dram_tensor("yout", [TOK, D], F32, kind="ExternalOutput").ap()

    xT_s = nc.dram_tensor("xT_s", [D, TOK], F32).ap()
    xT_v = xT_s.rearrange("(c p) t -> p c t", p=128)
    xT_b = [Buf("xT_s%d" % g) for g in range(NG)]
    xTw_b = [Buf("xTw_s%d" % g) for g in range(NG)]
    yout_b = Buf("yout")
    modin = [nc.dram_tensor("modin%d" % i, [128, 24], F32) for i in range(4)]
    modout = [nc.dram_tensor("modout%d" % i, [4 * 128, 24], F32) for i in range(4)]
    modin_b, modout_b = Buf("modin"), Buf("modout")
    q_s = nc.dram_tensor("q_s", [D, TOK], BF16).ap()
    q_b = Buf("q_s")
    att_s = nc.dram_tensor("att_s", [D, TOK], BF16).ap()
    att_b = Buf("att_s")
    RG = [[0, 1, 2, 3], [4, 5, 6, 7]]
    fdr = {}
    fdr["kin"] = [nc.dram_tensor("fkin%d" % i, [256, TOK], BF16) for i in range(8)]
    fdr["kout"] = [nc.dram_tensor("fkout%d" % i, [4 * 256, TOK], BF16) for i in range(8)]
    fdr["vin"] = [nc.dram_tensor("fvin%d" % i, [128, 2 * 16 * 128], BF16) for i in range(8)]
    fdr["vout"] = [nc.dram_tensor("fvout%d" % i, [4 * 128, 2 * 16 * 128], BF16) for i in range(8)]
    fdr["lin"] = nc.dram_tensor("flin", [TOK, 16], F32)
    fdr["lout"] = nc.dram_tensor("flout", [4 * TOK, 16], F32)
    fdr["b"] = {k: Buf("f" + k) for k in ("kin", "vin", "lin", "lout")}
    fdr["b"]["kout"] = [Buf("fkout%d" % i) for i in range(8)]
    fdr["b"]["vout"] = [Buf("fvout%d" % i) for i in range(8)]
    swa_dr = {}
    swa_dr["kin"] = [nc.dram_tensor("skin%d" % i, [256, TOK], BF16) for i in range(2)]
    swa_dr["kout"] = [nc.dram_tensor("skout%d" % i, [4 * 256, TOK], BF16) for i in range(2)]
    swa_dr["vin"] = [nc.dram_tensor("svin%d" % i, [128, 8 * 512], BF16) for i in range(2)]
    swa_dr["vout"] = [nc.dram_tensor("svout%d" % i, [4 * 128, 8 * 512], BF16) for i in range(2)]
    swa_dr["b"] = {k: Buf("s" + k) for k in ("kin", "kout", "vin", "vout")}

    arena = {"off": 16640}

    def sb(name, shape, dt):
        nbytes = int(np.prod(shape[1:])) * (4 if dt in (F32, I32) else 2)
        off = (arena["off"] + 31) // 32 * 32
        arena["off"] = off + nbytes
        assert arena["off"] <= 229344, (name, arena["off"])
        t = nc.alloc_sbuf_tensor_at(name, list(shape), dt, offset=off)
        return t

    ident_t = sb("ident_t", [128, 128], F32)
    ident_b = Buf("ident")
    ones_t = sb("ones_t", [128, 128], F32)
    ones_b = Buf("ones")
    eps_t = sb("eps_t", [128, 1], F32)
    one11 = ones_t
    cact_t = sb("cact_t", [128, 16], BF16)
    cact_b = Buf("cact")
    gains_t = sb("gains_t", [128, 4, 64], F32)
    gains_b = Buf("gains")
    modT = sb("modT", [128, 96], F32)
    modp_t = sb("modp_t", [128, 24], F32)
    modp_b = Buf("modp")
    modT_b = Buf("modT")
    der_t = sb("der_t", [128, 4, 16], F32)
    der_b = Buf("der")
    onesb_t = sb("onesb_t", [128, 128], BF16)
    cst_t = sb("cst_t", [128, 4], F32)
    xg_off = (arena["off"] + 31) // 32 * 32
    xg_t = sb("xg_t", [128, NCH, TG], F32)
    xg_b = Buf("xg")
    hT_t = sb("hT_t", [128, NCH, TG], BF16)
    hT_b = Buf("hT")
    yT_t = sb("yT_t", [128, NCH, TG], F32)
    yT_b = Buf("yT")
    big_off = (arena["off"] + 31) // 32 * 32
    big_t = sb("big_t", [128, NFC, TG], BF16)
    big_b = Buf("big")
    wsl = []
    for i in range(2):
        t = sb("wslot%d" % i, [128, 8192], BF16)
        wsl.append((t, Buf("wslot%d" % i)))
    wrot = Rot(wsl)
    attn_lim = arena["off"]
    sqs = Rot([(sb("sq%d" % i, [128, TG], BF16), Buf("sq%d" % i)) for i in range(4)])
    tmps = Rot([(sb("tmp%d" % i, [128, TG], F32), Buf("tmp%d" % i)) for i in range(2)])
    rstd_t = sb("rstd_t", [128, TG], F32)
    rstd_b = Buf("rstd")
    rt_t = sb("rt_t", [128, TG], F32)
    rt_b = Buf("rt")
    row_t = sb("row_t", [1, 512], F32)
    row_b = Buf("row")
    brow_t = sb("brow_t", [1, 512], F32)
    brow_b = Buf("brow")
    small_t = sb("small_t", [128, 64], F32)
    small_b = Buf("small")
    phase_base = arena["off"]

    es = ExitStack()
    psum = []
    for i in range(8):
        t = es.enter_context(nc.psum_tensor("ps%d" % i, [128, 512], F32))
        psum.append((t, Buf("ps%d" % i)))
    PG = Rot([psum[0], psum[2]])
    PU = Rot([psum[1], psum[3]])
    PY = Rot([psum[4], psum[5]])
    PSSQ = psum[6]
    PM = psum[7]

    mm = lambda out, lhsT, rhs, st, sp: (lambda h: h.matmul(out, lhsT, rhs, start=st, stop=sp))

    def load_w_cols(W2d, col0, ncols, slot, slot_b, dst_col0=0, width=None, kch=NCH):
        width = width or ncols
        view = slot[:, 0:kch * width].rearrange("p (c n) -> p c n", n=width)
        src = W2d.rearrange("(c p) n -> p c n", p=128)[:, :, col0:col0 + ncols]
        P.dma("pool", lambda h: h.dma_start(out=view[:, :, dst_col0:dst_col0 + ncols], in_=src),
              writes=[slot_b])
        return view

    def ssq_rstd(src_t, src_b, src_fn=None):
        pst, psb = PSSQ
        if src_fn is None:
            src_fn = lambda c: src_t[:, c, :]
        for c in range(NCH):
            sq, sqb = sqs.next()
            P.op("act", lambda h, sq=sq, c=c: h.activation(out=sq[:, :], in_=src_fn(c), func=AF.Square),
                 reads=[src_b], writes=[sqb])
            P.op("pe", mm(pst[:, :], onesb_t[:, :], sq[:, :], c == 0, c == NCH - 1),
                 reads=[ones_b, sqb], writes=[psb])
        P.op("act", lambda h: h.activation(out=rt_t[:, :], in_=pst[:, :], func=AF.Ln,
                                           bias=eps_t[:, 0:1], scale=1.0 / D),
             reads=[psb, ones_b], writes=[rt_b])
        P.op("act", lambda h: h.activation(out=rstd_t[:, :], in_=rt_t[:, :], func=AF.Exp, scale=-0.5),
             reads=[rt_b], writes=[rstd_b])

    def prenorm(acol, bcol):
        ssq_rstd(xg_t, xg_b)
        for c in range(NCH):
            tmp, tb = tmps.next()
            P.op("dve", lambda h, tmp=tmp, c=c: h.scalar_tensor_tensor(
                out=tmp[:, :], in0=xg_t[:, c, :], scalar=acol[:, c:c + 1], in1=rstd_t[:, :],
                op0=ALU.mult, op1=ALU.mult), reads=[xg_b, der_b, rstd_b], writes=[tb])
            P.op("act", lambda h, tmp=tmp, c=c: h.activation(
                out=hT_t[:, c, :], in_=tmp[:, :], func=AF.Identity, bias=bcol[:, c:c + 1], scale=1.0),
                reads=[tb, modT_b], writes=[hT_b])

    def postnorm_res(coef):
        ssq_rstd(yT_t, yT_b)
        for c in range(NCH):
            tmp, tb = tmps.next()
            P.op("dve", lambda h, tmp=tmp, c=c: h.scalar_tensor_tensor(
                out=tmp[:, :], in0=yT_t[:, c, :], scalar=coef[:, c:c + 1], in1=rstd_t[:, :],
                op0=ALU.mult, op1=ALU.mult), reads=[yT_b, der_b, rstd_b], writes=[tb])
            P.op("pool", lambda h, tmp=tmp, c=c: h.tensor_tensor(
                out=xg_t[:, c, :], in0=xg_t[:, c, :], in1=tmp[:, :], op=ALU.add),
                reads=[tb, xg_b], writes=[xg_b])

    def load_xg(g):
        P.dma("sp", lambda h: h.dma_start(out=xg_t[:, :, :], in_=xT_v[:, :, g * TG:(g + 1) * TG]),
              reads=[xT_b[g], xTw_b[g]], writes=[xg_b])

    def store_xg(g):
        P.dma("sp", lambda h: h.dma_start(out=xT_v[:, :, g * TG:(g + 1) * TG], in_=xg_t[:, :, :]),
              reads=[xg_b], writes=[xT_b[g]])

    def linear_fm(W2d, col0, n_oc, rhs_fn, rhs_bufs, kch, evac, ocw=128, per_load=4):
        oc = 0
        while oc < n_oc:
            nl = min(per_load, n_oc - oc)
            slot, slot_b = wrot.next()
            view = load_w_cols(W2d, col0 + oc * ocw, nl * ocw, slot, slot_b, kch=kch)
            for j in range(nl):
                pst, psb = PY.next()
                for c in range(kch):
                    P.op("pe", mm(pst[0:ocw, :], view[:, c, j * ocw:(j + 1) * ocw], rhs_fn(c), c == 0, c == kch - 1),
                         reads=[slot_b] + rhs_bufs, writes=[psb])
                evac(oc + j, pst, psb)
            oc += nl

    P.dma("sp", lambda h: h.dma_start(out=ident_t[:, :], in_=ident[:, :]), writes=[ident_b])
    P.op("dve", lambda h: h.memset(ones_t[:, :], 1.0), writes=[ones_b])
    P.op("dve", lambda h: h.memset(eps_t[:, :], EPS), writes=[ones_b])
    P.op("dve", lambda h: h.memset(onesb_t[:, :], 1.0), writes=[ones_b])
    P.op("dve", lambda h: h.memset(cst_t[:, 0:1], -float(np.pi)), writes=[ones_b])
    for k in range(4):
        tmp, tb = tmps.next()
        P.dma("sp", lambda h, tmp=tmp, k=k: h.dma_start(out=tmp[0:64, 0:128], in_=gains[k][:, :]), writes=[tb])
        pst, psb = PM
        P.op("pe", lambda h, tmp=tmp: h.transpose(pst[:, 0:64], tmp[0:64, 0:128], ident_t[0:64, 0:64]),
             reads=[tb, ident_b], writes=[psb])
        P.op("dve", lambda h, k=k: h.tensor_copy(out=gains_t[:, k, :], in_=pst[:, 0:64]), reads=[psb], writes=[gains_b])
    tmp, tb = tmps.next()
    P.dma("sp", lambda h, tmp=tmp: h.dma_start(out=tmp[0:16, 0:128], in_=cvec[:, :]), writes=[tb])
    pst, psb = PM
    P.op("pe", lambda h, tmp=tmp: h.transpose(pst[:, 0:16], tmp[0:16, 0:128], ident_t[0:16, 0:16]),
         reads=[tb, ident_b], writes=[psb])
    P.op("act", lambda h: h.activation(out=cact_t[:, :], in_=pst[:, 0:16], func=AF.Silu), reads=[psb], writes=[cact_b])

    xblk = sb("xblk", [128, 4, D], F32) if False else None
    for g in range(NG):
        stage = yT_t[:, :, :].rearrange("p c t -> p (c t)").rearrange("p (b d) -> p b d", b=4)
        P.dma("sp", lambda h, g=g: h.dma_start(
            out=stage, in_=xs[g * TG:(g + 1) * TG, :].rearrange("(b p) d -> p b d", p=128)), writes=[yT_b])
        for c in range(NCH):
            pst, psb = PY.next()
            for b in range(4):
                P.op("pe", lambda h, pst=pst, b=b, c=c: h.transpose(
                    pst[:, b * 128:(b + 1) * 128], stage[:, b, c * 128:(c + 1) * 128], ident_t[:, :]),
                    reads=[yT_b, ident_b], writes=[psb])
            eng = "act" if c % 2 == 0 else "dve"
            if eng == "act":
                P.op("act", lambda h, pst=pst, c=c: h.activation(out=xg_t[:, c, :], in_=pst[:, :], func=AF.Copy),
                     reads=[psb], writes=[xg_b])
            else:
                P.op("dve", lambda h, pst=pst, c=c: h.tensor_copy(out=xg_t[:, c, :], in_=pst[:, :]),
                     reads=[psb], writes=[xg_b])
        store_xg(g)

    def compute_mod(l):
        for cg in range(6):
            slot, slot_b = wrot.next()
            view = load_w_cols(ada_w[LI[l]], cg * 512, 512, slot, slot_b)
            P.dma("sp", lambda h, cg=cg: h.dma_start(out=brow_t[0:1, :], in_=ada_b[LI[l]:LI[l] + 1, cg * 512:(cg + 1) * 512]),
                  writes=[brow_b])
            pst, psb = PY.next()
            for c in range(NCH):
                P.op("pe", mm(pst[0:1, :], cact_t[:, c:c + 1], view[:, c, :], c == 0, c == NCH - 1),
                     reads=[cact_b, slot_b], writes=[psb])
            P.op("dve", lambda h, pst=pst: h.tensor_tensor(out=row_t[0:1, :], in0=pst[0:1, :], in1=brow_t[0:1, :],
                                                           op=ALU.add), reads=[psb, brow_b], writes=[row_b])
            pm, pmb = PM
            for j in range(4):
                P.op("pe", mm(pm[:, j:j + 1], row_t[0:1, j * 128:(j + 1) * 128], one11[0:1, 0:1], True, True),
                     reads=[row_b, ones_b], writes=[pmb])
            P.op("dve", lambda h, cg=cg: h.tensor_copy(out=modp_t[:, cg * 4:(cg + 1) * 4], in_=pm[:, 0:4]),
                 reads=[pmb], writes=[modp_b])
        P.dma("sp", lambda h: h.dma_start(out=modin[l].ap(), in_=modp_t[:, :]), reads=[modp_b], writes=[modin_b])
        P.dma("pool", lambda h: h.collective_compute("AllGather", ALU.bypass, replica_groups=RG,
                                                     ins=[modin[l].ap().opt()], outs=[modout[l].ap().opt()]),
              reads=[modin_b], writes=[modout_b], inc=1)
        P.dma("sp", lambda h: h.dma_start(out=modT[:, :].rearrange("p (r c) -> p r c", r=4),
                                          in_=modout[l].ap().rearrange("(r p) c -> p r c", r=4)),
              reads=[modout_b], writes=[modT_b])
        for which, (sc_i, gate_i, pre_k, post_k) in enumerate(((1, 2, 0, 1), (4, 5, 2, 3))):
            P.op("dve", lambda h, sc_i=sc_i: h.tensor_scalar_add(out=small_t[:, 0:16], in0=modT[:, sc_i * 16:(sc_i + 1) * 16],
                                                                 scalar1=1.0), reads=[modT_b], writes=[small_b])
            P.op("dve", lambda h, which=which, pre_k=pre_k: h.tensor_tensor(
                out=der_t[:, 2 * which, :], in0=small_t[:, 0:16], in1=gains_t[:, pre_k, l * 16:(l + 1) * 16], op=ALU.mult),
                reads=[small_b, gains_b], writes=[der_b])
            P.op("dve", lambda h, which=which, gate_i=gate_i, post_k=post_k: h.tensor_tensor(
                out=der_t[:, 2 * which + 1, :], in0=modT[:, gate_i * 16:(gate_i + 1) * 16],
                in1=gains_t[:, post_k, l * 16:(l + 1) * 16], op=ALU.mult),
                reads=[modT_b, gains_b], writes=[der_b])

    def ffn_group(l, g):
        load_xg(g)
        prenorm(der_t[:, 2, :], modT[:, 48:64])
        for fc in range(NFC):
            slot, slot_b = wrot.next()
            view = load_w_cols(w_gu[LI[l]], fc * 128, 128, slot, slot_b, dst_col0=0, width=256)
            load_w_cols(w_gu[LI[l]], DFF + fc * 128, 128, slot, slot_b, dst_col0=128, width=256)
            pg, pgb = PG.next()
            pu, pub = PU.next()
            for c in range(NCH):
                P.op("pe", mm(pg[:, :], view[:, c, 0:128], hT_t[:, c, :], c == 0, c == NCH - 1),
                     reads=[slot_b, hT_b], writes=[pgb])
            for c in range(NCH):
                P.op("pe", mm(pu[:, :], view[:, c, 128:256], hT_t[:, c, :], c == 0, c == NCH - 1),
                     reads=[slot_b, hT_b], writes=[pub])
            tmp, tb = tmps.next()
            P.op("act", lambda h, pg=pg, tmp=tmp: h.activation(out=tmp[:, :], in_=pg[:, :], func=AF.Silu),
                 reads=[pgb], writes=[tb])
            P.op("dve", lambda h, pu=pu, tmp=tmp, fc=fc: h.tensor_tensor(
                out=big_t[:, fc, :], in0=tmp[:, :], in1=pu[:, :], op=ALU.mult), reads=[tb, pub], writes=[big_b])
        for dc in range(NCH):
            slot, slot_b = wrot.next()
            view = slot[:, 0:NFC * 128].rearrange("p (j o) -> p j o", o=128)
            src = w_dn[LI[l]].rearrange("(j p) o -> p j o", p=128)[:, :, dc * 128:(dc + 1) * 128]
            P.dma("pool", lambda h, view=view, src=src: h.dma_start(out=view, in_=src), writes=[slot_b])
            py, pyb = PY.next()
            for j in range(NFC):
                P.op("pe", mm(py[:, :], view[:, j, :], big_t[:, j, :], j == 0, j == NFC - 1),
                     reads=[slot_b, big_b], writes=[pyb])
            P.op("act", lambda h, py=py, dc=dc: h.activation(out=yT_t[:, dc, :], in_=py[:, :], func=AF.Copy),
                 reads=[pyb], writes=[yT_b])
        postnorm_res(der_t[:, 3, :])
        store_xg(g)

    TG2 = 1024
    assert attn_lim - xg_off >= 159744, (attn_lim, xg_off)
    XY = nc.alloc_sbuf_tensor_at("f_xy", [128, NCH, TG2], F32, offset=xg_off)
    H2 = nc.alloc_sbuf_tensor_at("f_h2", [128, NCH, TG2], BF16, offset=xg_off + 65536)
    A2 = nc.alloc_sbuf_tensor_at("f_a2", [128, 22, TG2], BF16, offset=xg_off + 98304)
    fws = [(nc.alloc_sbuf_tensor_at("f_w%d" % i, [128, 4096], BF16, offset=xg_off + 143360 + i * 8192), Buf("f_w%d" % i))
           for i in range(2)]
    fws += [(nc.alloc_sbuf_tensor_at("f_w%d" % (2 + i), [128, 4096], BF16, offset=phase_base + i * 8192), Buf("f_w%d" % (2 + i)))
            for i in range(2)]
    fwrot = Rot(fws)
    xins = Rot([(nc.alloc_sbuf_tensor_at("f_xin%d" % i, [128, 512], F32, offset=phase_base + 16384 + i * 2048), Buf("f_xin%d" % i))
                for i in range(4)])
    xouts = Rot([(nc.alloc_sbuf_tensor_at("f_xo%d" % i, [128, 512], F32, offset=phase_base + 24576 + i * 2048), Buf("f_xo%d" % i))
                 for i in range(4)])
    XY_b, H2_b, A2_b = Buf("f_xy"), Buf("f_h2"), Buf("f_a2")

    def ffn_big(l):
        acol, bcol, coef = der_t[:, 2, :], modT[:, 48:64], der_t[:, 3, :]
        Wgu = w_gu[LI[l]]
        Wdn = w_dn[LI[l]].rearrange("(j p) o -> p j o", p=128)
        for g2 in range(2):
            t0 = g2 * TG2
            gb = [2 * g2, 2 * g2 + 1]
            P.dma("sp", lambda h, t0=t0: h.dma_start(out=XY[:, :, :], in_=xT_v[:, :, t0:t0 + TG2]),
                  reads=[xT_b[gb[0]], xT_b[gb[1]], xTw_b[gb[0]], xTw_b[gb[1]]], writes=[XY_b])
            for th in range(2):
                hs = slice(th * 512, (th + 1) * 512)
                ssq_rstd(None, XY_b, src_fn=lambda c, hs=hs: XY[:, c, hs])
                for c in range(NCH):
                    tmp, tb = tmps.next()
                    P.op("dve", lambda h, tmp=tmp, c=c, hs=hs: h.scalar_tensor_tensor(
                        out=tmp[:, :], in0=XY[:, c, hs], scalar=acol[:, c:c + 1], in1=rstd_t[:, :],
                        op0=ALU.mult, op1=ALU.mult), reads=[XY_b, der_b, rstd_b], writes=[tb])
                    P.op("act", lambda h, tmp=tmp, c=c, hs=hs: h.activation(
                        out=H2[:, c, hs], in_=tmp[:, :], func=AF.Identity, bias=bcol[:, c:c + 1], scale=1.0),
                        reads=[tb, modT_b], writes=[H2_b])
            for fh in range(2):
                for fcl in range(22):
                    fc = fh * 22 + fcl
                    slot, slot_b = fwrot.next()
                    view = load_w_cols(Wgu, fc * 128, 128, slot, slot_b, dst_col0=0, width=256)
                    load_w_cols(Wgu, DFF + fc * 128, 128, slot, slot_b, dst_col0=128, width=256)
                    for th in range(2):
                        hs = slice(th * 512, (th + 1) * 512)
                        pg, pgb = PG.next()
                        pu, pub = PU.next()
                        for c in range(NCH):
                            P.op("pe", mm(pg[:, :], view[:, c, 0:128], H2[:, c, hs], c == 0, c == NCH - 1),
                                 reads=[slot_b, H2_b], writes=[pgb])
                        for c in range(NCH):
                            P.op("pe", mm(pu[:, :], view[:, c, 128:256], H2[:, c, hs], c == 0, c == NCH - 1),
                                 reads=[slot_b, H2_b], writes=[pub])
                        tmp, tb = tmps.next()
                        P.op("act", lambda h, pg=pg, tmp=tmp: h.activation(out=tmp[:, :], in_=pg[:, :], func=AF.Silu),
                             reads=[pgb], writes=[tb])
                        P.op("dve", lambda h, pu=pu, tmp=tmp, fcl=fcl, hs=hs: h.tensor_tensor(
                            out=A2[:, fcl, hs], in0=tmp[:, :], in1=pu[:, :], op=ALU.mult), reads=[tb, pub], writes=[A2_b])
                for dc in range(NCH):
                    slot, slot_b = fwrot.next()
                    view = slot[:, 0:22 * 128].rearrange("p (j o) -> p j o", o=128)
                    src = Wdn[:, fh * 22:(fh + 1) * 22, dc * 128:(dc + 1) * 128]
                    P.dma("pool", lambda h, view=view, src=src: h.dma_start(out=view, in_=src), writes=[slot_b])
                    for th in range(2):
                        hs = slice(th * 512, (th + 1) * 512)
                        py, pyb = PY.next()
                        for j in range(22):
                            P.op("pe", mm(py[:, :], view[:, j, :], A2[:, j, hs], j == 0, j == 21),
                                 reads=[slot_b, A2_b], writes=[pyb])
                        if fh == 0:
                            P.op("act", lambda h, py=py, dc=dc, hs=hs: h.activation(out=XY[:, dc, hs], in_=py[:, :], func=AF.Copy),
                                 reads=[pyb], writes=[XY_b])
                        else:
                            P.op("dve", lambda h, py=py, dc=dc, hs=hs: h.tensor_tensor(out=XY[:, dc, hs], in0=py[:, :],
                                                                                       in1=XY[:, dc, hs], op=ALU.add),
                                 reads=[pyb, XY_b], writes=[XY_b])
            for th in range(2):
                hs = slice(th * 512, (th + 1) * 512)
                gg = gb[th]
                ssq_rstd(None, XY_b, src_fn=lambda c, hs=hs: XY[:, c, hs])
                for c in range(NCH):
                    xin, xinb = xins.next()
                    xo, xob = xouts.next()
                    P.dma("sp", lambda h, xin=xin, c=c, gg=gg: h.dma_start(out=xin[:, :], in_=xT_v[:, c, gg * 512:(gg + 1) * 512]),
                          reads=[xT_b[gg]], writes=[xinb])
                    tmp, tb = tmps.next()
                    P.op("dve", lambda h, tmp=tmp, c=c, hs=hs: h.scalar_tensor_tensor(
                        out=tmp[:, :], in0=XY[:, c, hs], scalar=coef[:, c:c + 1], in1=rstd_t[:, :],
                        op0=ALU.mult, op1=ALU.mult), reads=[XY_b, der_b, rstd_b], writes=[tb])
                    P.op("pool", lambda h, tmp=tmp, xin=xin, xo=xo: h.tensor_tensor(
                        out=xo[:, :], in0=xin[:, :], in1=tmp[:, :], op=ALU.add), reads=[tb, xinb], writes=[xob])
                    P.dma("sp", lambda h, xo=xo, c=c, gg=gg: h.dma_start(out=xT_v[:, c, gg * 512:(gg + 1) * 512], in_=xo[:, :]),
                          reads=[xob], writes=[xTw_b[gg]])

    def sgu_setup():
        st = {}
        st["wsT"] = sb("sgu_wsT", [128, 16, 128], BF16)
        st["bs"] = sb("sgu_bs", [128, 16, 128], F32)
        st["lng"] = sb("sgu_lng", [128, D], F32)
        st["lnb"] = sb("sgu_lnb", [128, D], F32)
        st["tri"] = sb("sgu_tri", [128, 128], F32)
        st["stat"] = sb("sgu_stat", [128, 8], F32)
        st["vtm"] = nc.alloc_sbuf_tensor_at("sgu_vtm", [128, 4, D], BF16, offset=big_off + 16 * TG * 2)
        st["b"] = {k: Buf("sgu_" + k) for k in ("wsT", "bs", "lng", "lnb", "vtm", "tri", "stat")}
        b = st["b"]
        P.dma("sp", lambda h: h.dma_start(out=st["tri"][:, :], in_=trimask[:, :]), writes=[b["tri"]])
        P.dma("sp", lambda h: h.dma_start(out=st["bs"][:, :, :].rearrange("p g t -> p (g t)"),
                                          in_=sgu_b_s[0, :].partition_broadcast(128)), writes=[b["bs"]])
        P.dma("sp", lambda h: h.dma_start(out=st["lng"][:, :], in_=sgu_ln_g[0, :].partition_broadcast(128)),
              writes=[b["lng"]])
        P.dma("sp", lambda h: h.dma_start(out=st["lnb"][:, :], in_=sgu_ln_b[0, :].partition_broadcast(128)),
              writes=[b["lnb"]])
        for gi in range(16):
            tmp, tb = tmps.next()
            P.dma("sp", lambda h, tmp=tmp, gi=gi: h.dma_start(out=tmp[:, 0:128], in_=sgu_w_s[0, gi, :, :]), writes=[tb])
            P.op("dve", lambda h, tmp=tmp: h.tensor_tensor(out=tmp[:, 128:256], in0=tmp[:, 0:128], in1=st["tri"][:, :],
                                                           op=ALU.mult), reads=[tb, b["tri"]], writes=[tb])
            pst, psb = PM
            P.op("pe", lambda h, tmp=tmp, pst=pst: h.transpose(pst[:, 0:128], tmp[:, 128:256], ident_t[:, :]),
                 reads=[tb, ident_b], writes=[psb])
            P.op("dve", lambda h, gi=gi, pst=pst: h.tensor_copy(out=st["wsT"][:, gi, :], in_=pst[:, 0:128]),
                 reads=[psb], writes=[b["wsT"]])
        return st

    def sgu_group(st, g):
        b = st["b"]
        load_xg(g)
        prenorm(der_t[:, 0, :], modT[:, 0:16])
        W = sgu_w_in[0]
        def evac_u(oc, pst, psb):
            P.op("act", lambda h: h.activation(out=big_t[:, oc, :], in_=pst[:, :], func=AF.Gelu),
                 reads=[psb], writes=[big_b])
        linear_fm(W, 0, 16, lambda c: hT_t[:, c, :], [hT_b], NCH, evac_u)
        zv4 = yT_t[:, :, :].rearrange("p c t -> p (c t)").rearrange("p (b d) -> p b d", b=4)
        for cg in range(4):
            slot, slot_b = wrot.next()
            view = load_w_cols(W, D + cg * 512, 512, slot, slot_b)
            for blk in range(4):
                pst, psb = PY.next()
                for c in range(NCH):
                    P.op("pe", mm(pst[:, :], hT_t[:, c, blk * 128:(blk + 1) * 128], view[:, c, :], c == 0, c == NCH - 1),
                         reads=[hT_b, slot_b], writes=[psb])
                P.op("act", lambda h, pst=pst, cg=cg, blk=blk: h.activation(
                    out=zv4[:, blk, cg * 512:(cg + 1) * 512], in_=pst[:, :], func=AF.Gelu), reads=[psb], writes=[yT_b])
        stat = st["stat"]
        for blk in range(4):
            zv = zv4[:, blk, :]
            P.op("dve", lambda h, zv=zv: h.tensor_reduce(out=stat[:, 0:1], in_=zv, axis=mybir.AxisListType.X, op=ALU.add),
                 reads=[yT_b], writes=[b["stat"]])
            P.op("act", lambda h, zv=zv, blk=blk: h.activation(out=st["vtm"][:, blk, :], in_=zv, func=AF.Square,
                                                               accum_out=stat[:, 1:2]),
                 reads=[yT_b], writes=[b["stat"], b["vtm"]])
            P.op("dve", lambda h: h.tensor_scalar_mul(out=stat[:, 2:3], in0=stat[:, 0:1], scalar1=1.0 / D),
                 reads=[b["stat"]], writes=[b["stat"]])
            P.op("dve", lambda h: h.tensor_tensor(out=stat[:, 3:4], in0=stat[:, 2:3], in1=stat[:, 2:3], op=ALU.mult),
                 reads=[b["stat"]], writes=[b["stat"]])
            P.op("dve", lambda h: h.scalar_tensor_tensor(out=stat[:, 4:5], in0=stat[:, 1:2], scalar=1.0 / D,
                                                         in1=stat[:, 3:4], op0=ALU.mult, op1=ALU.subtract),
                 reads=[b["stat"]], writes=[b["stat"]])
            P.op("act", lambda h: h.activation(out=stat[:, 5:6], in_=stat[:, 4:5], func=AF.Sqrt, bias=eps_t[:, 0:1],
                                               scale=1.0), reads=[b["stat"], ones_b], writes=[b["stat"]])
            P.op("dve", lambda h: h.reciprocal(out=stat[:, 6:7], in_=stat[:, 5:6]), reads=[b["stat"]], writes=[b["stat"]])
            P.op("dve", lambda h, zv=zv: h.tensor_scalar(out=zv, in0=zv, scalar1=stat[:, 2:3], scalar2=stat[:, 6:7],
                                                         op0=ALU.subtract, op1=ALU.mult), reads=[b["stat"], yT_b], writes=[yT_b])
            P.op("pool", lambda h, zv=zv: h.tensor_tensor(out=zv, in0=zv, in1=st["lng"][:, :], op=ALU.mult),
                 reads=[yT_b, b["lng"]], writes=[yT_b])
            P.op("dve", lambda h, zv=zv, blk=blk: h.tensor_tensor(out=st["vtm"][:, blk, :], in0=zv, in1=st["lnb"][:, :],
                                                                  op=ALU.add), reads=[yT_b, b["lnb"]], writes=[b["vtm"]])
        for gi in range(16):
            pst, psb = PY.next()
            for blk in range(4):
                P.op("pe", mm(pst[:, blk * 128:(blk + 1) * 128], st["vtm"][:, blk, gi * 128:(gi + 1) * 128],
                              st["wsT"][:, gi, :], True, True), reads=[b["vtm"], b["wsT"]], writes=[psb])
            tmp, tb = tmps.next()
            P.op("dve", lambda h, pst=pst, tmp=tmp, gi=gi: h.tensor_tensor(
                out=tmp[:, :].rearrange("p (b t) -> p b t", b=4), in0=pst[:, :].rearrange("p (b t) -> p b t", b=4),
                in1=st["bs"][:, gi:gi + 1, :].broadcast_to([128, 4, 128]), op=ALU.add),
                reads=[psb, b["bs"]], writes=[tb])
            P.op("pool", lambda h, tmp=tmp, gi=gi: h.tensor_tensor(out=big_t[:, gi, :], in0=big_t[:, gi, :], in1=tmp[:, :],
                                                                   op=ALU.mult), reads=[tb, big_b], writes=[big_b])
        def evac_y(oc, pst, psb):
            P.op("act", lambda h: h.activation(out=yT_t[:, oc, :], in_=pst[:, :], func=AF.Copy),
                 reads=[psb], writes=[yT_b])
        linear_fm(sgu_w_out[0], 0, 16, lambda c: big_t[:, c, :], [big_b], NCH, evac_y)
        postnorm_res(der_t[:, 1, :])
        store_xg(g)

    def at_alloc(state, name, shape, dt):
        nbytes = int(np.prod(shape[1:])) * (4 if dt in (F32, I32) else 2)
        off = (state["off"] + 31) // 32 * 32
        state["off"] = off + nbytes
        assert state["off"] <= attn_lim, (name, state["off"], attn_lim)
        state["n"] += 1
        return nc.alloc_sbuf_tensor_at("%s_%d" % (name, state["n"]), list(shape), dt, offset=off)

    def evac_to(dst_fn, dst_buf, func=None, eng="act"):
        def ev(oc, pst, psb):
            P.op("act", lambda h: h.activation(out=dst_fn(oc), in_=pst[0:dst_fn(oc).shape[0], :], func=AF.Copy),
                 reads=[psb], writes=[dst_buf])
        return ev

    def attn_out_phase(W2d):
        for g in range(NG):
            P.dma("sp", lambda h, g=g: h.dma_start(
                out=big_t[:, 0:16, :], in_=att_s.rearrange("(c p) t -> p c t", p=128)[:, :, g * TG:(g + 1) * TG]),
                reads=[att_b], writes=[big_b])
            load_xg(g)
            linear_fm(W2d, 0, 16, lambda c: big_t[:, c, :], [big_b], NCH,
                      evac_to(lambda oc: yT_t[:, oc, :], yT_b))
            postnorm_res(der_t[:, 1, :])
            store_xg(g)

    def all_gather(src, src_b, dst, dst_b):
        P.dma("pool", lambda h: h.collective_compute("AllGather", ALU.bypass, replica_groups=RG,
                                                     ins=[src.ap().opt()], outs=[dst.ap().opt()]),
              reads=[src_b], writes=[dst_b], inc=1)

    fox_cache = {}

    def fox_layer(l, j):
        dr = fdr
        db = dr["b"]
        W = fox_w_in[j]
        kin_v = [t.ap().rearrange("(c p) t -> p c t", p=128) for t in dr["kin"]]
        vin_v = [t.ap().rearrange("p (h i d) -> p h i d", h=2, i=16) for t in dr["vin"]]
        lin_v = dr["lin"].ap().rearrange("(i p) h -> p i h", p=128)
        ph = {"off": phase_base, "n": 100 * l}
        bf_t = sb("fox_bf%d" % l, [128, 16], F32)
        vst_t = sb("fox_vst%d" % l, [128, 4, 512], BF16)
        lf_t = sb("fox_lf%d" % l, [128, 4, 16], F32)
        bf_b, vst_b, lf_b = fox_cache.setdefault("p1", (Buf("bf"), Buf("vst"), Buf("lf")))
        P.dma("sp", lambda h: h.dma_start(out=bf_t[:, :], in_=fox_b_f[j, :].partition_broadcast(128)), writes=[bf_b])
        for g in range(NG):
            load_xg(g)
            prenorm(der_t[:, 0, :], modT[:, 0:16])
            linear_fm(W, 0, 16, lambda c: hT_t[:, c, :], [hT_b], NCH, evac_to(lambda oc: big_t[:, oc, :], big_b))
            P.dma("sp", lambda h, g=g: h.dma_start(
                out=q_s.rearrange("(c p) t -> p c t", p=128)[:, :, g * TG:(g + 1) * TG], in_=big_t[:, 0:16, :]),
                reads=[big_b], writes=[q_b])
            linear_fm(W, D, 16, lambda c: hT_t[:, c, :], [hT_b], NCH, evac_to(lambda oc: big_t[:, 16 + oc, :], big_b))
            for ck in range(8):
                P.dma("sp", lambda h, g=g, ck=ck: h.dma_start(out=kin_v[ck][:, :, g * TG:(g + 1) * TG],
                                                               in_=big_t[:, 16 + 2 * ck:18 + 2 * ck, :]),
                      reads=[big_b], writes=[db["kin"]])
            for cg in range(4):
                slot, slot_b = wrot.next()
                view = load_w_cols(W, 2 * D + cg * 512, 512, slot, slot_b)
                for blk in range(4):
                    pst, psb = PY.next()
                    for c in range(NCH):
                        P.op("pe", mm(pst[:, :], hT_t[:, c, blk * 128:(blk + 1) * 128], view[:, c, :], c == 0, c == NCH - 1),
                             reads=[hT_b, slot_b], writes=[psb])
                    P.op("act", lambda h, pst=pst, blk=blk: h.activation(out=vst_t[:, blk, :], in_=pst[:, :], func=AF.Copy),
                         reads=[psb], writes=[vst_b])
                for hh in range(4):
                    P.dma("sp", lambda h, g=g, cg=cg, hh=hh: h.dma_start(
                        out=vin_v[(cg * 4 + hh) // 2][:, (cg * 4 + hh) % 2, g * 4:(g + 1) * 4, :],
                        in_=vst_t[:, :, hh * 128:(hh + 1) * 128]), reads=[vst_b], writes=[db["vin"]])
            slot, slot_b = wrot.next()
            view = load_w_cols(W, 3 * D, 16, slot, slot_b)
            for blk in range(4):
                pst, psb = PY.next()
                for c in range(NCH):
                    P.op("pe", mm(pst[:, 0:16], hT_t[:, c, blk * 128:(blk + 1) * 128], view[:, c, :], c == 0, c == NCH - 1),
                         reads=[hT_b, slot_b], writes=[psb])
                P.op("dve", lambda h, pst=pst, blk=blk: h.tensor_tensor(out=lf_t[:, blk, :], in0=pst[:, 0:16], in1=bf_t[:, :],
                                                                        op=ALU.add), reads=[psb, bf_b], writes=[lf_b])
            lf2 = lf_t[:, :, :].rearrange("p b h -> p (b h)")
            P.op("act", lambda h: h.activation(out=lf2, in_=lf2, func=AF.Exp, scale=-1.0), reads=[lf_b], writes=[lf_b])
            P.op("act", lambda h: h.activation(out=lf2, in_=lf2, func=AF.Ln, bias=1.0, scale=1.0), reads=[lf_b], writes=[lf_b])
            P.op("dve", lambda h: h.tensor_scalar_mul(out=lf2, in0=lf2, scalar1=-1.0), reads=[lf_b], writes=[lf_b])
            P.dma("sp", lambda h, g=g: h.dma_start(out=lin_v[:, g * 4:(g + 1) * 4, :], in_=lf_t[:, :, :]),
                  reads=[lf_b], writes=[db["lin"]])
        all_gather(dr["lin"], db["lin"], dr["lout"], db["lout"])
        for ck in range(8):
            all_gather(dr["kin"][ck], db["kin"], dr["kout"][ck], db["kout"][ck])
            all_gather(dr["vin"][ck], db["vin"], dr["vout"][ck], db["vout"][ck])
        P.barrier(exclude=db["kout"] + db["vout"])
        A = {"off": xg_off, "n": 1000 * (l + 1)}
        lfa = at_alloc(A, "lfa", [128, 64, 16], F32)
        ftm = at_alloc(A, "ftm", [128, 64, 16], F32)
        tbc = at_alloc(A, "tbc", [128, 64, 16], F32)
        pfa = at_alloc(A, "pfa", [128, 64, 16], F32)
        pfb = at_alloc(A, "pfb", [128, 64, 16], F32)
        fref = at_alloc(A, "fref", [128, 16, 16], F32)
        triu_t = at_alloc(A, "triu", [128, 128], F32)
        oh4_t = at_alloc(A, "oh4", [128, 4], F32)
        mfox_t = at_alloc(A, "mfox", [128, 4, 128], BF16)
        zo_t = at_alloc(A, "zo", [128, 16, 64], F32)
        qh = [at_alloc(A, "qh%d" % i, [128, TOK], BF16) for i in range(2)]
        kh = [at_alloc(A, "kh%d" % i, [128, 4, TOK], BF16) for i in range(2)]
        vh = [at_alloc(A, "vh%d" % i, [128, 4, 16, 128], BF16) for i in range(2)]
        bh = [at_alloc(A, "bh%d" % i, [128, 16, 64], F32) for i in range(2)]
        ah = [at_alloc(A, "ah%d" % i, [128, TOK], BF16) for i in range(2)]
        pts = Rot([(at_alloc(A, "pt%d" % i, [128, 512], BF16), [Buf("pt%d_%d" % (i, m)) for m in range(4)]) for i in range(4)])
        rec_t = at_alloc(A, "rec", [128, 512], F32)
        B_ = fox_cache.setdefault("B_", {k: Buf("fx_" + k) for k in (
            "lfa", "ftm", "tbc", "pfa", "pfb", "fref", "triu", "oh4", "mfox", "rec",
            "qh0", "qh1", "kh0", "kh1", "vh0", "vh1", "bh0", "bh1", "ah0", "ah1")})
        P.dma("sp", lambda h: h.dma_start(out=triu_t[:, :], in_=triu[:, :]), writes=[B_["triu"]])
        P.dma("sp", lambda h: h.dma_start(out=oh4_t[:, :], in_=oh4_in[:, :]), writes=[B_["oh4"]])
        P.dma("sp", lambda h: h.dma_start(out=zo_t[:, :, :].rearrange("p a b -> p (a b)"), in_=zo_in[:, :]), writes=[B_["oh4"]])
        P.dma("pool", lambda h: h.dma_start(out=mfox_t[:, :, :].rearrange("p j q -> p (j q)"), in_=m_fox_in[:, :]),
              writes=[B_["mfox"]])
        lout_ap = dr["lout"].ap()
        for rr in range(4):
            P.dma("sp", lambda h, rr=rr: h.dma_start(
                out=lfa[:, :, :].rearrange("p (i r) h -> p r i h", r=4)[:, rr],
                in_=lout_ap[rr * TOK:(rr + 1) * TOK, :].rearrange("(i p) h -> p i h", p=128)),
                reads=[db["lout"]], writes=[B_["lfa"]])
        lfa2 = lfa[:, :, :].rearrange("p g h -> p (g h)")
        ftm2 = ftm[:, :, :].rearrange("p g h -> p (g h)")
        tbc2 = tbc[:, :, :].rearrange("p g h -> p (g h)")
        pfa2 = pfa[:, :, :].rearrange("p g h -> p (g h)")
        pfb2 = pfb[:, :, :].rearrange("p g h -> p (g h)")
        for half in range(2):
            cs = slice(half * 512, (half + 1) * 512)
            pst, psb = PY.next()
            P.op("pe", mm(pst[:, :], triu_t[:, :], lfa2[:, cs], True, True), reads=[B_["triu"], B_["lfa"]], writes=[psb])
            P.op("dve", lambda h, pst=pst, cs=cs: h.tensor_copy(out=ftm2[:, cs], in_=pst[:, :]), reads=[psb], writes=[B_["ftm"]])
            pst, psb = PY.next()
            P.op("pe", mm(pst[:, :], ones_t[:, :], lfa2[:, cs], True, True), reads=[ones_b, B_["lfa"]], writes=[psb])
            P.op("dve", lambda h, pst=pst, cs=cs: h.tensor_copy(out=tbc2[:, cs], in_=pst[:, :]), reads=[psb], writes=[B_["tbc"]])
        P.op("dve", lambda h: h.tensor_copy(out=pfa2, in_=tbc2), reads=[B_["tbc"]], writes=[B_["pfa"]])
        cur, curb, oth, othb = pfa2, B_["pfa"], pfb2, B_["pfb"]
        sh = 1
        while sh < 64:
            w = sh * 16
            P.op("dve", lambda h, cur=cur, oth=oth, w=w: h.tensor_copy(out=oth[:, 0:w], in_=cur[:, 0:w]),
                 reads=[curb], writes=[othb])
            P.op("dve", lambda h, cur=cur, oth=oth, w=w: h.tensor_tensor(out=oth[:, w:1024], in0=cur[:, w:1024],
                                                                         in1=cur[:, 0:1024 - w], op=ALU.add),
                 reads=[curb], writes=[othb])
            cur, curb, oth, othb = oth, othb, cur, curb
            sh *= 2
        P.op("dve", lambda h, cur=cur, oth=oth: h.tensor_tensor(out=oth, in0=cur, in1=tbc2, op=ALU.subtract),
             reads=[curb, B_["tbc"]], writes=[othb])
        P.op("dve", lambda h, oth=oth: h.tensor_tensor(out=ftm2, in0=ftm2, in1=oth, op=ALU.add),
             reads=[othb, B_["ftm"]], writes=[B_["ftm"]])
        P.op("dve", lambda h, cur=cur, oth=oth: h.scalar_tensor_tensor(out=oth, in0=tbc2, scalar=-0.5, in1=cur,
                                                                        op0=ALU.mult, op1=ALU.add),
             reads=[curb, B_["tbc"]], writes=[othb])
        fmid4 = (pfa if oth is pfa2 else pfb)[:, :, :].rearrange("p (i r) h -> p r i h", r=4)
        P.op("dve", lambda h: h.tensor_scalar(out=fref[:, :, :], in0=fmid4[:, 0], scalar1=oh4_t[:, 0:1], scalar2=None,
                                              op0=ALU.mult), reads=[othb, B_["oh4"]], writes=[B_["fref"]])
        for rr in range(1, 4):
            P.op("dve", lambda h, rr=rr: h.scalar_tensor_tensor(out=fref[:, :, :], in0=fmid4[:, rr], scalar=oh4_t[:, rr:rr + 1],
                                                                in1=fref[:, :, :], op0=ALU.mult, op1=ALU.add),
                 reads=[othb, B_["oh4"], B_["fref"]], writes=[B_["fref"]])
        kout_v = [t.ap().rearrange("(r c d) t -> d r c t", r=4, d=128) for t in dr["kout"]]
        vout_v = [t.ap().rearrange("(r p) (h i d) -> p r h i d", r=4, h=2, i=16) for t in dr["vout"]]
        PS_ST = Rot([psum[0], psum[1], psum[2], psum[7]])
        PS_O = Rot([psum[3], psum[4]])
        PS_D = Rot([psum[5], psum[6]])
        scale = 128.0 ** -0.5
        def head_res(hd):
            s2 = hd % 2
            return (qh[s2], kh[s2], vh[s2], bh[s2], ah[s2],
                    B_["qh%d" % s2], B_["kh%d" % s2], B_["vh%d" % s2], B_["bh%d" % s2], B_["ah%d" % s2])

        def emit_head_loads(hd):
            qt, kt, vt, bt, at, qb, kb, vb, bb, ab = head_res(hd)
            P.dma("sp", lambda h: h.dma_start(out=qt[:, :], in_=q_s[hd * 128:(hd + 1) * 128, :]), reads=[q_b], writes=[qb])
            P.dma("sp", lambda h: h.dma_start(out=kt[:, :, :], in_=kout_v[hd // 2][:, :, hd % 2, :]),
                  reads=[db["kout"][hd // 2]], writes=[kb])
            P.dma("sp", lambda h: h.dma_start(out=vt[:, :, :, :], in_=vout_v[hd // 2][:, :, hd % 2]),
                  reads=[db["vout"][hd // 2]], writes=[vb])
            P.op("dve", lambda h: h.tensor_tensor(
                out=bt[:, :, :], in0=fref[:, :, hd:hd + 1].broadcast_to([128, 16, 64]),
                in1=ftm[:, :, hd:hd + 1].rearrange("p g o -> p o g").broadcast_to([128, 16, 64]), op=ALU.subtract),
                reads=[B_["fref"], B_["ftm"]], writes=[bb])
            P.op("dve", lambda h: h.tensor_tensor(out=bt[:, :, :], in0=bt[:, :, :], in1=zo_t[:, :, :], op=ALU.add),
                 reads=[bb, B_["oh4"]], writes=[bb])

        steps = [(hd, jq, gk) for hd in range(16) for jq in range(4) for gk in range(16 * (jq + 1))]
        nst = len(steps)
        qk = {}
        acc = {}

        def emit_qk(idx):
            hd, jq, gk = steps[idx]
            qt, kt, vt, bt, at, qb, kb, vb, bb, ab = head_res(hd)
            rr, ii = gk % 4, gk // 4
            mmin = max(0, -(-(gk - 3 - 16 * jq) // 4))
            c0 = mmin * 128
            pst, psb = PS_ST.next()
            P.op("pe", mm(pst[:, c0:512], kt[:, rr, ii * 128:(ii + 1) * 128], qt[:, jq * 512 + c0:(jq + 1) * 512],
                          True, True), reads=[kb, qb], writes=[psb])
            qk[idx] = (pst, psb, mmin, c0)

        def emit_exp(idx):
            hd, jq, gk = steps[idx]
            qt, kt, vt, bt, at, qb, kb, vb, bb, ab = head_res(hd)
            pst, psb, mmin, c0 = qk[idx]
            pt, ptb = pts.next()
            for m in range(mmin, 4):
                li = 4 * jq + m
                P.op("act", lambda h, m=m, li=li: h.activation(
                    out=pt[:, m * 128:(m + 1) * 128], in_=pst[:, m * 128:(m + 1) * 128], func=AF.Exp,
                    bias=bt[:, li, gk:gk + 1], scale=scale), reads=[psb, bb], writes=[ptb[m]])
                jm = gk - 4 * li
                if 0 <= jm <= 3:
                    P.op("pool", lambda h, m=m, jm=jm: h.tensor_tensor(
                        out=pt[:, m * 128:(m + 1) * 128], in0=pt[:, m * 128:(m + 1) * 128], in1=mfox_t[:, jm, :],
                        op=ALU.mult), reads=[ptb[m], B_["mfox"]], writes=[ptb[m]])
            qk[idx] = (pst, psb, mmin, c0, pt, ptb)

        def emit_pv(idx):
            hd, jq, gk = steps[idx]
            qt, kt, vt, bt, at, qb, kb, vb, bb, ab = head_res(hd)
            pst, psb, mmin, c0, pt, ptb = qk.pop(idx)
            rr, ii = gk % 4, gk // 4
            ng = 16 * (jq + 1)
            if gk == 0:
                acc[(hd, jq)] = (PS_O.next(), PS_D.next())
            (po, pob), (pd, pdb) = acc[(hd, jq)]
            P.op("pe", mm(po[:, c0:512], vt[:, rr, ii, :], pt[:, c0:512], gk == 0, gk == ng - 1),
                 reads=[vb] + ptb[mmin:], writes=[pob])
            P.op("pe", mm(pd[:, c0:512], onesb_t[:, :], pt[:, c0:512], gk == 0, gk == ng - 1),
                 reads=[ones_b] + ptb[mmin:], writes=[pdb])
            if gk == ng - 1:
                del acc[(hd, jq)]
                P.op("dve", lambda h: h.reciprocal(out=rec_t[:, :], in_=pd[:, :]), reads=[pdb], writes=[B_["rec"]])
                P.op("dve", lambda h: h.tensor_tensor(out=at[:, jq * 512:(jq + 1) * 512], in0=po[:, :],
                                                      in1=rec_t[:, :], op=ALU.mult),
                     reads=[pob, B_["rec"]], writes=[ab])
                if jq == 3:
                    P.dma("sp", lambda h: h.dma_start(out=att_s[hd * 128:(hd + 1) * 128, :], in_=at[:, :]),
                          reads=[ab], writes=[att_b])
                    if hd + 2 < 16:
                        emit_head_loads(hd + 2)

        emit_head_loads(0)
        emit_head_loads(1)
        emit_qk(0)
        emit_qk(1)
        emit_qk(2)
        for idx in range(nst):
            emit_exp(idx)
            if idx + 3 < nst:
                emit_qk(idx + 3)
            emit_pv(idx)
        P.barrier()
        attn_out_phase(fox_w_out[j])
        P.barrier()

    def swa_layer(l):
        dr = swa_dr
        db = dr["b"]
        W = swa_w_in[0]
        kin_v = [t.ap().rearrange("(h d) t -> d h t", d=64) for t in dr["kin"]]
        vin_v = [t.ap().rearrange("p (i f) -> p i f", i=8) for t in dr["vin"]]
        q_v = q_s.rearrange("(h d) t -> d h t", d=64)
        att_v = att_s.rearrange("(h d) t -> d h t", d=64)
        c16 = sb("swa_c16", [16, TOK], F32)
        s16 = sb("swa_s16", [16, TOK], F32)
        pmat_t = sb("swa_pmat", [64, 16], BF16)
        invf_t = sb("swa_invf", [16, 1], F32)
        vst_t = sb("swa_vst", [128, 4, 512], BF16)
        SB_ = {k: Buf("sw_" + k) for k in ("c16", "s16", "pmat", "invf", "vst", "posi")}
        posi_t = sb("swa_posi", [16, TOK], I32)
        P.dma("sp", lambda h: h.dma_start(out=posi_t[:, :], in_=posin[0, :].partition_broadcast(16)), writes=[SB_["posi"]])
        P.dma("pool", lambda h: h.dma_start(out=pmat_t[:, :], in_=pmat_in[:, :]), writes=[SB_["pmat"]])
        P.dma("sp", lambda h: h.dma_start(out=invf_t[:, :], in_=invf_in[:, :]), writes=[SB_["invf"]])
        PI = float(np.pi)
        C1 = 6.28125
        C2 = float(2 * np.pi - 6.28125)
        ang = yT_t[0:16, 8:12, :].rearrange("p c t -> p (c t)")
        nf = yT_t[0:16, 0:4, :].rearrange("p c t -> p (c t)")
        mk = yT_t[0:16, 4:8, :].rearrange("p c t -> p (c t)")
        ys, yc = s16[:, :], c16[:, :]
        RW = dict(reads=[SB_["posi"], SB_["invf"], yT_b, SB_["s16"], SB_["c16"]],
                  writes=[yT_b, SB_["s16"], SB_["c16"], SB_["posi"]])
        P.op("dve", lambda h: h.tensor_copy(out=ang, in_=posi_t[:, :]), **RW)
        P.op("dve", lambda h: h.tensor_scalar_mul(out=ang, in0=ang, scalar1=invf_t[:, 0:1]), **RW)
        P.op("dve", lambda h: h.tensor_scalar_mul(out=nf, in0=ang, scalar1=float(1.0 / (2 * np.pi))), **RW)
        P.op("dve", lambda h: h.tensor_copy(out=posi_t[:, :], in_=nf), **RW)
        P.op("dve", lambda h: h.tensor_copy(out=nf, in_=posi_t[:, :]), **RW)
        P.op("dve", lambda h: h.scalar_tensor_tensor(out=ys, in0=nf, scalar=-C1, in1=ang, op0=ALU.mult, op1=ALU.add), **RW)
        P.op("dve", lambda h: h.scalar_tensor_tensor(out=ys, in0=nf, scalar=-C2, in1=ys, op0=ALU.mult, op1=ALU.add), **RW)
        P.op("dve", lambda h: h.tensor_single_scalar(out=mk, in_=ys, scalar=PI, op=ALU.is_gt), **RW)
        P.op("dve", lambda h: h.scalar_tensor_tensor(out=ys, in0=mk, scalar=-2 * PI, in1=ys, op0=ALU.mult, op1=ALU.add), **RW)
        P.op("dve", lambda h: h.tensor_single_scalar(out=mk, in_=ys, scalar=-PI, op=ALU.is_lt), **RW)
        P.op("dve", lambda h: h.scalar_tensor_tensor(out=ys, in0=mk, scalar=2 * PI, in1=ys, op0=ALU.mult, op1=ALU.add), **RW)
        P.op("dve", lambda h: h.tensor_scalar_add(out=yc, in0=ys, scalar1=PI / 2), **RW)
        P.op("dve", lambda h: h.tensor_single_scalar(out=mk, in_=yc, scalar=PI, op=ALU.is_gt), **RW)
        P.op("dve", lambda h: h.scalar_tensor_tensor(out=yc, in0=mk, scalar=-2 * PI, in1=yc, op0=ALU.mult, op1=ALU.add), **RW)
        P.op("act", lambda h: h.activation(out=ys, in_=ys, func=AF.Sin), reads=[SB_["s16"]], writes=[SB_["s16"]])
        P.op("act", lambda h: h.activation(out=yc, in_=yc, func=AF.Sin), reads=[SB_["c16"]], writes=[SB_["c16"]])

        def rope(tile_fn, nheads, g):
            for hh in range(nheads):
                pst, psb = PM
                P.op("pe", mm(pst[0:16, :], pmat_t[:, :], tile_fn(hh), True, True), reads=[SB_["pmat"], big_b], writes=[psb])
                t1, t1b = tmps.next()
                t2, t2b = tmps.next()
                P.op("dve", lambda h, t1=t1, hh=hh: h.tensor_tensor(out=t1[0:16, :], in0=tile_fn(hh)[0:16, :],
                                                                    in1=c16[:, g * TG:(g + 1) * TG], op=ALU.mult),
                     reads=[big_b, SB_["c16"]], writes=[t1b])
                P.op("dve", lambda h, t2=t2, pst=pst: h.tensor_tensor(out=t2[0:16, :], in0=pst[0:16, :],
                                                                      in1=s16[:, g * TG:(g + 1) * TG], op=ALU.mult),
                     reads=[psb, SB_["s16"]], writes=[t2b])
                P.op("pool", lambda h, t1=t1, t2=t2, hh=hh: h.tensor_tensor(out=tile_fn(hh)[0:16, :], in0=t1[0:16, :],
                                                                            in1=t2[0:16, :], op=ALU.add),
                     reads=[t1b, t2b], writes=[big_b])

        for g in range(NG):
            load_xg(g)
            prenorm(der_t[:, 0, :], modT[:, 0:16])
            linear_fm(W, 0, 32, lambda c: hT_t[:, c, :], [hT_b], NCH, evac_to(lambda oc: big_t[0:64, oc, :], big_b),
                      ocw=64, per_load=8)
            rope(lambda hh: big_t[0:64, hh, :], 32, g)
            P.dma("sp", lambda h, g=g: h.dma_start(out=q_v[:, :, g * TG:(g + 1) * TG], in_=big_t[0:64, 0:32, :]),
                  reads=[big_b], writes=[q_b])
            linear_fm(W, D, 8, lambda c: hT_t[:, c, :], [hT_b], NCH, evac_to(lambda oc: big_t[0:64, 32 + oc, :], big_b),
                      ocw=64, per_load=8)
            rope(lambda hh: big_t[0:64, 32 + hh, :], 8, g)
            for ck in range(2):
                P.dma("sp", lambda h, g=g, ck=ck: h.dma_start(out=kin_v[ck][:, :, g * TG:(g + 1) * TG],
                                                               in_=big_t[0:64, 32 + 4 * ck:36 + 4 * ck, :]),
                      reads=[big_b], writes=[db["kin"]])
            slot, slot_b = wrot.next()
            view = load_w_cols(W, D + 512, 512, slot, slot_b)
            for blk in range(4):
                pst, psb = PY.next()
                for c in range(NCH):
                    P.op("pe", mm(pst[:, :], hT_t[:, c, blk * 128:(blk + 1) * 128], view[:, c, :], c == 0, c == NCH - 1),
                         reads=[hT_b, slot_b], writes=[psb])
                P.op("act", lambda h, pst=pst, blk=blk: h.activation(out=vst_t[:, blk, :], in_=pst[:, :], func=AF.Copy),
                     reads=[psb], writes=[SB_["vst"]])
            P.dma("sp", lambda h, g=g: h.dma_start(out=vin_v[g // 2][:, (g % 2) * 4:(g % 2) * 4 + 4, :], in_=vst_t[:, :, :]),
                  reads=[SB_["vst"]], writes=[db["vin"]])
        for ck in range(2):
            all_gather(dr["kin"][ck], db["kin"], dr["kout"][ck], db["kout"])
            all_gather(dr["vin"][ck], db["vin"], dr["vout"][ck], db["vout"])
        P.barrier()
        A = {"off": xg_off, "n": 5000}
        kc2 = [at_alloc(A, "kc%d" % i, [64, 8, 128], BF16) for i in range(2)]
        vc2 = [at_alloc(A, "vc%d" % i, [128, 512], BF16) for i in range(2)]
        kp2_ = [at_alloc(A, "kp%d" % i, [64, 8, 128], BF16) for i in range(2)]
        vp2_ = [at_alloc(A, "vp%d" % i, [128, 512], BF16) for i in range(2)]
        qb2 = [at_alloc(A, "qblk%d" % i, [64, 32, 128], BF16) for i in range(2)]
        ab2 = [at_alloc(A, "ablk%d" % i, [64, 32, 128], BF16) for i in range(2)]
        kcand = at_alloc(A, "kcand", [64, 5, 8, 128], BF16)
        vcand = at_alloc(A, "vcand", [128, 5, 512], BF16)
        mtri = at_alloc(A, "mtri", [128, 128], BF16)
        mprev = at_alloc(A, "mprev", [128, 128], BF16)
        mprev0 = at_alloc(A, "mprev0", [128, 128], BF16)
        sel5 = at_alloc(A, "sel5", [128, 5], F32)
        sinke = at_alloc(A, "sinke", [64, 32], F32)
        den_t = at_alloc(A, "den", [64, 512], F32)
        ptc = Rot([(at_alloc(A, "ptc%d" % i, [128, 512], BF16), Buf("ptc%d" % i)) for i in range(2)])
        ptp = Rot([(at_alloc(A, "ptp%d" % i, [128, 512], BF16), Buf("ptp%d" % i)) for i in range(2)])
        B_ = {k: Buf("sw3_" + k) for k in ("kc0", "kc1", "vc0", "vc1", "kcand", "vcand", "kp0", "kp1", "vp0", "vp1",
                                           "qblk0", "qblk1", "ablk0", "ablk1", "mtri", "mprev",
                                           "mprev0", "sel5", "sinke", "den")}
        P.dma("pool", lambda h: h.dma_start(out=mtri[:, :], in_=triu[:, :]), writes=[B_["mtri"]])
        P.dma("pool", lambda h: h.dma_start(out=mprev[:, :], in_=m_prev_in[:, :]), writes=[B_["mprev"]])
        P.dma("pool", lambda h: h.dma_start(out=mprev0[:, :], in_=m_prev0_in[:, :]), writes=[B_["mprev0"]])
        P.dma("sp", lambda h: h.dma_start(out=sel5[:, :], in_=sel5_in[:, :]), writes=[B_["sel5"]])
        P.dma("sp", lambda h: h.dma_start(out=sinke[:, :], in_=swa_sinks[0, :].partition_broadcast(64)), writes=[B_["sinke"]])
        P.op("act", lambda h: h.activation(out=sinke[:, :], in_=sinke[:, :], func=AF.Exp), reads=[B_["sinke"]], writes=[B_["sinke"]])
        kout_v = [t.ap().rearrange("(r h d) t -> d r h t", r=4, d=64) for t in dr["kout"]]
        vout_v = [t.ap().rearrange("(r p) (i f) -> p r i f", r=4, i=8) for t in dr["vout"]]
        PS_C = Rot([psum[0], psum[1]])
        PS_P = Rot([psum[2], psum[3]])
        PS_O = Rot([psum[4], psum[5]])
        PS_D = Rot([psum[6], psum[7]])
        scale = 64.0 ** -0.5

        def emit_block_loads(i):
            par = i % 2
            kc, vc, kp, vp, qblk = kc2[par], vc2[par], kp2_[par], vp2_[par], qb2[par]
            kcb, vcb, kpb, vpb, qbb = (B_["kc%d" % par], B_["vc%d" % par], B_["kp%d" % par], B_["vp%d" % par],
                                       B_["qblk%d" % par])
            ip = max(i - 1, 0)
            for ck in range(2):
                P.dma("sp", lambda h, ck=ck: h.dma_start(out=kc[:, 4 * ck:4 * ck + 4, :],
                                                         in_=kin_v[ck][:, :, i * 128:(i + 1) * 128]),
                      reads=[db["kin"]], writes=[kcb])
                for rr in range(4):
                    P.dma("sp", lambda h, ck=ck, rr=rr: h.dma_start(
                        out=kcand[:, rr, 4 * ck:4 * ck + 4, :], in_=kout_v[ck][:, rr, :, i * 128:(i + 1) * 128]),
                        reads=[db["kout"]], writes=[B_["kcand"]])
                P.dma("sp", lambda h, ck=ck: h.dma_start(
                    out=kcand[:, 4, 4 * ck:4 * ck + 4, :], in_=kout_v[ck][:, 3, :, ip * 128:(ip + 1) * 128]),
                    reads=[db["kout"]], writes=[B_["kcand"]])
            P.dma("sp", lambda h: h.dma_start(out=vc[:, :], in_=vin_v[i // 8][:, i % 8, :]), reads=[db["vin"]], writes=[vcb])
            P.dma("sp", lambda h: h.dma_start(out=vcand[:, 0:4, :], in_=vout_v[i // 8][:, :, i % 8, :]),
                  reads=[db["vout"]], writes=[B_["vcand"]])
            P.dma("sp", lambda h: h.dma_start(out=vcand[:, 4, :], in_=vout_v[ip // 8][:, 3, ip % 8, :]),
                  reads=[db["vout"]], writes=[B_["vcand"]])
            P.dma("sp", lambda h: h.dma_start(out=qblk[:, :, :], in_=q_v[:, :, i * 128:(i + 1) * 128]),
                  reads=[q_b], writes=[qbb])

        def emit_block_select(i):
            par = i % 2
            kp, vp = kp2_[par], vp2_[par]
            kpb, vpb = B_["kp%d" % par], B_["vp%d" % par]
            kpf = kp[:, :, :].rearrange("d h t -> d (h t)")
            P.op("dve", lambda h: h.tensor_scalar(out=kpf, in0=kcand[:, 0, :, :].rearrange("d h t -> d (h t)"),
                                                  scalar1=sel5[0:64, 0:1], scalar2=None, op0=ALU.mult),
                 reads=[B_["kcand"], B_["sel5"]], writes=[kpb])
            P.op("dve", lambda h: h.tensor_scalar(out=vp[:, :], in0=vcand[:, 0, :], scalar1=sel5[:, 0:1], scalar2=None,
                                                  op0=ALU.mult), reads=[B_["vcand"], B_["sel5"]], writes=[vpb])
            for cnd in range(1, 5):
                P.op("dve", lambda h, cnd=cnd: h.scalar_tensor_tensor(
                    out=kpf, in0=kcand[:, cnd, :, :].rearrange("d h t -> d (h t)"), scalar=sel5[0:64, cnd:cnd + 1], in1=kpf,
                    op0=ALU.mult, op1=ALU.add), reads=[B_["kcand"], B_["sel5"], kpb], writes=[kpb])
                P.op("dve", lambda h, cnd=cnd: h.scalar_tensor_tensor(
                    out=vp[:, :], in0=vcand[:, cnd, :], scalar=sel5[:, cnd:cnd + 1], in1=vp[:, :],
                    op0=ALU.mult, op1=ALU.add), reads=[B_["vcand"], B_["sel5"], vpb], writes=[vpb])

        sw_steps = [(i, hk) for i in range(NBLK) for hk in range(8)]
        sA = {}

        def stageA(si):
            i, hk = sw_steps[si]
            par = i % 2
            qsl = qb2[par][:, hk * 4:(hk + 1) * 4, :]
            pc, pcb = PS_C.next()
            pp, ppb = PS_P.next()
            P.op("pe", mm(pc[:, :], kc2[par][:, hk, :], qsl, True, True), reads=[B_["kc%d" % par], B_["qblk%d" % par]], writes=[pcb])
            P.op("pe", mm(pp[:, :], kp2_[par][:, hk, :], qsl, True, True), reads=[B_["kp%d" % par], B_["qblk%d" % par]], writes=[ppb])
            sA[si] = (pc, pcb, pp, ppb)

        def stageB(si):
            i, hk = sw_steps[si]
            par = i % 2
            vc, vp, ablk = vc2[par], vp2_[par], ab2[par]
            vcb, vpb, abb = B_["vc%d" % par], B_["vp%d" % par], B_["ablk%d" % par]
            pc, pcb, pp, ppb = sA.pop(si)
            mpv, mpvb = (mprev0, B_["mprev0"]) if i == 0 else (mprev, B_["mprev"])
            tc_, tcb = ptc.next()
            tp_, tpb = ptp.next()
            P.op("act", lambda h: h.activation(out=tc_[:, :], in_=pc[:, :], func=AF.Exp, scale=scale), reads=[pcb], writes=[tcb])
            P.op("act", lambda h: h.activation(out=tp_[:, :], in_=pp[:, :], func=AF.Exp, scale=scale), reads=[ppb], writes=[tpb])
            P.op("pool", lambda h: h.tensor_tensor(
                out=tc_[:, :].rearrange("k (a q) -> k a q", a=4), in0=tc_[:, :].rearrange("k (a q) -> k a q", a=4),
                in1=mtri[:, :].rearrange("k (o q) -> k o q", o=1).broadcast_to([128, 4, 128]), op=ALU.mult),
                reads=[tcb, B_["mtri"]], writes=[tcb])
            P.op("dve", lambda h: h.tensor_tensor(
                out=tp_[:, :].rearrange("k (a q) -> k a q", a=4), in0=tp_[:, :].rearrange("k (a q) -> k a q", a=4),
                in1=mpv[:, :].rearrange("k (o q) -> k o q", o=1).broadcast_to([128, 4, 128]), op=ALU.mult),
                reads=[tpb, mpvb], writes=[tpb])
            po, pob = PS_O.next()
            pd, pdb = PS_D.next()
            P.op("pe", mm(po[0:64, :], vc[:, hk * 64:(hk + 1) * 64], tc_[:, :], True, False), reads=[vcb, tcb], writes=[pob])
            P.op("pe", mm(po[0:64, :], vp[:, hk * 64:(hk + 1) * 64], tp_[:, :], False, True), reads=[vpb, tpb], writes=[pob])
            P.op("pe", mm(pd[0:64, :], onesb_t[:, 0:64], tc_[:, :], True, False), reads=[ones_b, tcb], writes=[pdb])
            P.op("pe", mm(pd[0:64, :], onesb_t[:, 0:64], tp_[:, :], False, True), reads=[ones_b, tpb], writes=[pdb])
            P.op("dve", lambda h: h.tensor_tensor(
                out=den_t[:, :].rearrange("d (a q) -> d a q", a=4), in0=pd[0:64, :].rearrange("d (a q) -> d a q", a=4),
                in1=sinke[:, hk * 4:(hk + 1) * 4].rearrange("d (a o) -> d a o", o=1).broadcast_to([64, 4, 128]), op=ALU.add),
                reads=[pdb, B_["sinke"]], writes=[B_["den"]])
            P.op("act", lambda h: h.activation(out=den_t[:, :], in_=den_t[:, :], func=AF.Ln), reads=[B_["den"]], writes=[B_["den"]])
            P.op("act", lambda h: h.activation(out=den_t[:, :], in_=den_t[:, :], func=AF.Exp, scale=-1.0),
                 reads=[B_["den"]], writes=[B_["den"]])
            P.op("dve", lambda h: h.tensor_tensor(
                out=ablk[:, hk * 4:(hk + 1) * 4, :], in0=po[0:64, :].rearrange("d (a q) -> d a q", a=4),
                in1=den_t[:, :].rearrange("d (a q) -> d a q", a=4), op=ALU.mult),
                reads=[pob, B_["den"]], writes=[abb])
            if hk == 7:
                P.dma("sp", lambda h: h.dma_start(out=att_v[:, :, i * 128:(i + 1) * 128], in_=ablk[:, :, :]),
                      reads=[abb], writes=[att_b])

        emit_block_loads(0)
        emit_block_select(0)
        stageA(0)
        for si in range(len(sw_steps)):
            bi, bh = sw_steps[si]
            if bh == 0 and bi + 1 < NBLK:
                emit_block_loads(bi + 1)
            if si + 1 < len(sw_steps):
                if sw_steps[si + 1][1] == 0:
                    emit_block_select(sw_steps[si + 1][0])
                stageA(si + 1)
            stageB(si)
        P.barrier()
        attn_out_phase(swa_w_out[0])
        P.barrier()

    for l in layers:
        compute_mod(l)
        kind = l % 3
        if do_mixer:
            if kind == 0:
                arena["off"] = phase_base
                fox_layer(l, l // 3)
            if kind == 2:
                arena["off"] = phase_base
                swa_layer(l)
            if kind == 1:
                arena["off"] = phase_base
                st = sgu_setup()
                for g in range(NG):
                    sgu_group(st, g)
                P.barrier()
        if do_ffn:
            P.barrier()
            ffn_big(l)
            P.barrier()

    for g in range(NG):
        load_xg(g)
        stage = yT_t[:, :, :].rearrange("p c t -> p (c t)").rearrange("p (b d) -> p b d", b=4)
        for b in range(4):
            for q in range(4):
                pst, psb = PY.next()
                for j in range(4):
                    c = q * 4 + j
                    P.op("pe", lambda h, pst=pst, b=b, c=c, j=j: h.transpose(
                        pst[:, j * 128:(j + 1) * 128], xg_t[:, c, b * 128:(b + 1) * 128], ident_t[:, :]),
                        reads=[xg_b, ident_b], writes=[psb])
                if q % 2 == 0:
                    P.op("act", lambda h, pst=pst, b=b, q=q: h.activation(out=stage[:, b, q * 512:(q + 1) * 512],
                                                                          in_=pst[:, :], func=AF.Copy),
                         reads=[psb], writes=[yT_b])
                else:
                    P.op("dve", lambda h, pst=pst, b=b, q=q: h.tensor_copy(out=stage[:, b, q * 512:(q + 1) * 512],
                                                                           in_=pst[:, :]), reads=[psb], writes=[yT_b])
        P.dma("sp", lambda h, g=g: h.dma_start(
            out=yout[g * TG:(g + 1) * TG, :].rearrange("(b p) d -> p b d", p=128), in_=stage),
            reads=[yT_b], writes=[yout_b])
    P.barrier()
    P.emit(nc)
    es.close()
    return nc


_TRI = np.tril(np.ones((128, 128), np.float32))


def _prep_inputs(inp, layers):
    x = np.asarray(inp["x"], np.float32)
    maps = []
    shared = {
        "ident": np.eye(128, dtype=np.float32),
        "trimask": _TRI,
        "ffn_w_gu": np.ascontiguousarray(np.asarray(inp["ffn_w_gu"], np.float32)[list(layers)]),
        "ffn_w_down": np.ascontiguousarray(np.asarray(inp["ffn_w_down"], np.float32)[list(layers)]),
        "sgu_w_in": np.ascontiguousarray(inp["sgu_w_in"], np.float32),
        "sgu_ln_g": np.ascontiguousarray(inp["sgu_ln_g"], np.float32),
        "sgu_ln_b": np.ascontiguousarray(inp["sgu_ln_b"], np.float32),
        "sgu_w_s": np.ascontiguousarray(inp["sgu_w_s"], np.float32),
        "sgu_b_s": np.ascontiguousarray(inp["sgu_b_s"], np.float32).reshape(1, 2048),
        "sgu_w_out": np.ascontiguousarray(inp["sgu_w_out"], np.float32),
    }
    for n in ("fox_w_in", "fox_b_f", "fox_w_out", "swa_w_in", "swa_sinks", "swa_w_out"):
        shared[n] = np.ascontiguousarray(inp[n], np.float32)
    shared["triu"] = np.ascontiguousarray(_TRI.T)
    shared["m_prev"] = np.ascontiguousarray(1.0 - _TRI.T)
    pm = np.zeros((64, 16), np.float32)
    for m_ in range(8):
        pm[m_ + 8, m_] = -1.0
        pm[m_, m_ + 8] = 1.0
    shared["pmat"] = pm
    inv = (500000.0 ** (-np.arange(0, 16, 2, dtype=np.float32) / np.float32(16))).astype(np.float32)
    shared["invf"] = np.concatenate([inv, inv]).reshape(16, 1).astype(np.float32)
    for n in ("mix_pre_g", "mix_post_g", "ffn_pre_g", "ffn_post_g"):
        shared[n] = np.ascontiguousarray(inp[n], np.float32).reshape(64, 128)
    for core in range(8):
        b, r = core // 4, core % 4
        xb = x[b].reshape(16, 4, 128, D)[:, r].reshape(TOK, D)
        m = dict(shared)
        m["xs"] = np.ascontiguousarray(xb)
        m["ada_w"] = np.ascontiguousarray(np.asarray(inp["ada_w"], np.float32)[list(layers)][:, :, r * 3072:(r + 1) * 3072])
        m["ada_b"] = np.ascontiguousarray(np.asarray(inp["ada_b"], np.float32)[list(layers)][:, r * 3072:(r + 1) * 3072])
        m["cvec"] = np.ascontiguousarray(inp["c"][b], np.float32).reshape(16, 128)
        pos = np.asarray(inp["positions"])[b].astype(np.int32)
        m["posin"] = np.ascontiguousarray(pos.reshape(16, 4, 128)[:, r].reshape(1, TOK))
        mf = np.zeros((128, 4, 128), np.float32)
        for j_ in range(4):
            if j_ < r:
                mf[:, j_, :] = 1.0
            elif j_ == r:
                mf[:, j_, :] = _TRI.T
        m["m_fox"] = mf.reshape(128, 512)
        m["m_prev0"] = np.zeros((128, 128), np.float32) if r == 0 else np.ascontiguousarray(1.0 - _TRI.T)
        oh = np.zeros((128, 4), np.float32)
        oh[:, r] = 1.0
        m["oh4"] = oh
        zo = np.zeros((128, 16, 64), np.float32)
        for li_ in range(16):
            zo[:, li_, 4 * li_ + r + 1:] = -30000.0
        m["zo"] = zo.reshape(128, 1024)
        s5 = np.zeros((128, 5), np.float32)
        s5[:, (r - 1) if r > 0 else 4] = 1.0
        m["sel5"] = s5
        maps.append(m)
    return maps


def run(inp, layers=(0, 1, 2, 3), **kw):
    nc = build_program(layers=layers, **kw)
    maps = _prep_inputs(inp, layers)
    res = run_bass_kernel_spmd(nc, maps, core_ids=list(range(8)))
    out = np.empty((2, SEQ, D), np.float32)
    for core in range(8):
        b, r = core // 4, core % 4
        out[b].reshape(16, 4, 128, D)[:, r] = res.results[core]["yout"].reshape(16, 128, D)
    return out


def kernel(**inputs):
    return run(inputs)
```

```python
import numpy as np
import ml_dtypes
from contextlib import ExitStack
import concourse.bass as bass
import concourse.mybir as mybir
from concourse.bass_utils import run_bass_kernel_spmd

F32 = mybir.dt.float32
BF16 = mybir.dt.bfloat16
I32 = mybir.dt.int32
AF = mybir.ActivationFunctionType
ALU = mybir.AluOpType

D = 2048
NCH = 16
SEQ = 8192
TOK = 2048
NBLK = 16
TG = 512
NG = TOK // TG
DFF = 5632
NFC = DFF // 128
EPS = 1e-6
FOX_IN = 6160
SWA_IN = 3072
ENGS = ("pe", "act", "dve", "pool", "sp")
BLOCKNAME = {"pe": "tensor", "act": "scalar", "dve": "vector", "pool": "gpsimd", "sp": "sync"}


class Buf:
    __slots__ = ("name", "w", "rs", "dtotal")

    def __init__(self, name):
        self.name = name
        self.w = None
        self.rs = {}
        self.dtotal = 0


class Plan:
    def __init__(self):
        self.recs = {e: [] for e in ENGS}
        self.seen = {e: {} for e in ENGS}
        self.dbufs = {}

    def _deps(self, eng, reads, writes, skipkey=None):
        need = {}
        seen = self.seen[eng]

        def add(tok):
            key, val = tok
            if key == ("E", "pe") and eng == "pe":
                return
            if key == skipkey:
                return
            if seen.get(key, -1) >= val:
                return
            if need.get(key, -1) < val:
                need[key] = val

        for b in reads:
            if b.w is not None:
                add(b.w)
        for b in writes:
            if b.w is not None:
                add(b.w)
            for k, v in b.rs.items():
                add((k, v))
        for k, v in need.items():
            seen[k] = v
            if k[0] == "E":
                self.recs[k[1]][v][3] = True
        return list(need.items())

    def op(self, eng, fn, reads=(), writes=()):
        waits = self._deps(eng, reads, writes)
        idx = len(self.recs[eng])
        self.recs[eng].append([waits, fn, None, False, 0])
        key = ("E", eng)
        for b in reads:
            if b.rs.get(key, -1) < idx:
                b.rs[key] = idx
        for b in writes:
            b.w = (key, idx)
            b.rs = {}

    def dma(self, eng, fn, reads=(), writes=(), dbuf=None, inc=16):
        if dbuf is None:
            dbuf = writes[0]
        waits = self._deps(eng, reads, writes, skipkey=("D", id(dbuf)))
        self.dbufs[id(dbuf)] = dbuf
        dbuf.dtotal += inc
        key = ("D", id(dbuf))
        val = dbuf.dtotal
        self.recs[eng].append([waits, fn, id(dbuf), False, inc])
        for b in reads:
            if b.rs.get(key, -1) < val:
                b.rs[key] = val
        for b in writes:
            b.w = (key, val)
            b.rs = {}

    def barrier(self, exclude=()):
        excl = set(id(b) for b in exclude)
        for e in ENGS:
            need = []
            seen = self.seen[e]
            for e2 in ENGS:
                if e2 == e or not self.recs[e2]:
                    continue
                idx = None
                for j in range(len(self.recs[e2]) - 1, -1, -1):
                    r = self.recs[e2][j]
                    if r[1] is not None and r[2] is None:
                        idx = j
                        break
                if idx is None:
                    continue
                key = ("E", e2)
                if seen.get(key, -1) < idx:
                    seen[key] = idx
                    self.recs[e2][idx][3] = True
                    need.append((key, idx))
            for bid, b in self.dbufs.items():
                key = ("D", bid)
                if bid in excl:
                    continue
                if b.dtotal > 0 and seen.get(key, -1) < b.dtotal:
                    seen[key] = b.dtotal
                    need.append((key, b.dtotal))
            if need:
                self.recs[e].append([need, None, None, False, 0])

    def emit(self, nc):
        vals = {}
        for e in ENGS:
            cnt = 0
            v = []
            for rec in self.recs[e]:
                if rec[3]:
                    cnt += 1
                v.append(cnt)
            vals[e] = v
            assert cnt < 60000, (e, cnt)
        with ExitStack() as es:
            esem = {e: es.enter_context(nc.semaphore("sem_" + e)) for e in ENGS}
            dsem = {}
            for n, bid in enumerate(self.dbufs):
                dsem[bid] = es.enter_context(nc.semaphore("dsem%d" % n))
            block = es.enter_context(nc.Block())
            for e in ENGS:
                def body(h, e=e):
                    for waits, fn, dma, flagged, inc in self.recs[e]:
                        for key, val in waits:
                            if key[0] == "E":
                                h.wait_ge(esem[key[1]], vals[key[1]][val])
                            else:
                                h.wait_ge(dsem[key[1]], val)
                        if fn is None:
                            continue
                        ins = fn(h)
                        if dma is not None:
                            ins.then_inc(dsem[dma], inc)
                        elif flagged:
                            ins.then_inc(esem[e], 1)
                getattr(block, BLOCKNAME[e])(body)


class Rot:
    def __init__(self, items):
        self.items = items
        self.i = 0

    def next(self):
        it = self.items[self.i % len(self.items)]
        self.i += 1
        return it


def build_program(layers=(0, 1, 2, 3), do_mixer=True, do_ffn=True):
    NL = len(layers)
    LI = {l: i for i, l in enumerate(layers)}
    nc = bass.Bass("TRN2", target_bir_lowering=False)
    P = Plan()

    def din(name, shape, dt=F32):
        return nc.dram_tensor(name, list(shape), dt, kind="ExternalInput").ap()

    xs = din("xs", [TOK, D])
    cvec = din("cvec", [16, 128])
    ident = din("ident", [128, 128])
    ada_w = din("ada_w", [NL, D, 3072])
    ada_b = din("ada_b", [NL, 3072])
    gains = [din(n, [64, 128]) for n in ("mix_pre_g", "mix_post_g", "ffn_pre_g", "ffn_post_g")]
    w_gu = din("ffn_w_gu", [NL, D, 2 * DFF])
    w_dn = din("ffn_w_down", [NL, DFF, D])
    sgu_w_in = din("sgu_w_in", [1, D, 2 * D])
    sgu_ln_g = din("sgu_ln_g", [1, D])
    sgu_ln_b = din("sgu_ln_b", [1, D])
    sgu_w_s = din("sgu_w_s", [1, 16, 128, 128])
    sgu_b_s = din("sgu_b_s", [1, 16 * 128])
    sgu_w_out = din("sgu_w_out", [1, D, D])
    trimask = din("trimask", [128, 128])
    fox_w_in = din("fox_w_in", [2, D, FOX_IN])
    fox_b_f = din("fox_b_f", [2, 16])
    fox_w_out = din("fox_w_out", [2, D, D])
    swa_w_in = din("swa_w_in", [1, D, SWA_IN])
    swa_sinks = din("swa_sinks", [1, 32])
    swa_w_out = din("swa_w_out", [1, D, D])
    posin = din("posin", [1, TOK], I32)
    triu = din("triu", [128, 128])
    m_prev_in = din("m_prev", [128, 128])
    m_fox_in = din("m_fox", [128, 4 * 128])
    m_prev0_in = din("m_prev0", [128, 128])
    zo_in = din("zo", [128, 1024])
    oh4_in = din("oh4", [128, 4])
    sel5_in = din("sel5", [128, 5])
    pmat_in = din("pmat", [64, 16])
    invf_in = din("invf", [16, 1])
    yout = nc.dram_tensor("yout", [TOK, D], F32, kind="ExternalOutput").ap()

    xT_s = nc.dram_tensor("xT_s", [D, TOK], F32).ap()
    xT_v = xT_s.rearrange("(c p) t -> p c t", p=128)
    xT_b = [Buf("xT_s%d" % g) for g in range(NG)]
    xTw_b = [Buf("xTw_s%d" % g) for g in range(NG)]
    yout_b = Buf("yout")
    modin = [nc.dram_tensor("modin%d" % i, [128, 24], F32) for i in range(4)]
    modout = [nc.dram_tensor("modout%d" % i, [4 * 128, 24], F32) for i in range(4)]
    modin_b, modout_b = Buf("modin"), Buf("modout")
    q_s = nc.dram_tensor("q_s", [D, TOK], BF16).ap()
    q_b = Buf("q_s")
    att_s = nc.dram_tensor("att_s", [D, TOK], BF16).ap()
    att_b = Buf("att_s")
    RG = [[0, 1, 2, 3], [4, 5, 6, 7]]
    fdr = {}
    fdr["kin"] = [nc.dram_tensor("fkin%d" % i, [256, TOK], BF16) for i in range(8)]
    fdr["kout"] = [nc.dram_tensor("fkout%d" % i, [4 * 256, TOK], BF16) for i in range(8)]
    fdr["vin"] = [nc.dram_tensor("fvin%d" % i, [128, 2 * 16 * 128], BF16) for i in range(8)]
    fdr["vout"] = [nc.dram_tensor("fvout%d" % i, [4 * 128, 2 * 16 * 128], BF16) for i in range(8)]
    fdr["lin"] = nc.dram_tensor("flin", [TOK, 16], F32)
    fdr["lout"] = nc.dram_tensor("flout", [4 * TOK, 16], F32)
    fdr["b"] = {k: Buf("f" + k) for k in ("kin", "vin", "lin", "lout")}
    fdr["b"]["kout"] = [Buf("fkout%d" % i) for i in range(8)]
    fdr["b"]["vout"] = [Buf("fvout%d" % i) for i in range(8)]
    swa_dr = {}
    swa_dr["kin"] = [nc.dram_tensor("skin%d" % i, [256, TOK], BF16) for i in range(2)]
    swa_dr["kout"] = [nc.dram_tensor("skout%d" % i, [4 * 256, TOK], BF16) for i in range(2)]
    swa_dr["vin"] = [nc.dram_tensor("svin%d" % i, [128, 8 * 512], BF16) for i in range(2)]
    swa_dr["vout"] = [nc.dram_tensor("svout%d" % i, [4 * 128, 8 * 512], BF16) for i in range(2)]
    swa_dr["b"] = {k: Buf("s" + k) for k in ("kin", "kout", "vin", "vout")}

    arena = {"off": 16640}

    def sb(name, shape, dt):
        nbytes = int(np.prod(shape[1:])) * (4 if dt in (F32, I32) else 2)
        off = (arena["off"] + 31) // 32 * 32
        arena["off"] = off + nbytes
        assert arena["off"] <= 229344, (name, arena["off"])
        t = nc.alloc_sbuf_tensor_at(name, list(shape), dt, offset=off)
        return t

    ident_t = sb("ident_t", [128, 128], F32)
    ident_b = Buf("ident")
    ones_t = sb("ones_t", [128, 128], F32)
    ones_b = Buf("ones")
    eps_t = sb("eps_t", [128, 1], F32)
    one11 = ones_t
    cact_t = sb("cact_t", [128, 16], BF16)
    cact_b = Buf("cact")
    gains_t = sb("gains_t", [128, 4, 64], F32)
    gains_b = Buf("gains")
    modT = sb("modT", [128, 96], F32)
    modp_t = sb("modp_t", [128, 24], F32)
    modp_b = Buf("modp")
    modT_b = Buf("modT")
    der_t = sb("der_t", [128, 4, 16], F32)
    der_b = Buf("der")
    onesb_t = sb("onesb_t", [128, 128], BF16)
    cst_t = sb("cst_t", [128, 4], F32)
    xg_off = (arena["off"] + 31) // 32 * 32
    xg_t = sb("xg_t", [128, NCH, TG], F32)
    xg_b = Buf("xg")
    hT_t = sb("hT_t", [128, NCH, TG], BF16)
    hT_b = Buf("hT")
    yT_t = sb("yT_t", [128, NCH, TG], F32)
    yT_b = Buf("yT")
    big_off = (arena["off"] + 31) // 32 * 32
    big_t = sb("big_t", [128, NFC, TG], BF16)
    big_b = Buf("big")
    wsl = []
    for i in range(2):
        t = sb("wslot%d" % i, [128, 8192], BF16)
        wsl.append((t, Buf("wslot%d" % i)))
    wrot = Rot(wsl)
    attn_lim = arena["off"]
    sqs = Rot([(sb("sq%d" % i, [128, TG], BF16), Buf("sq%d" % i)) for i in range(4)])
    tmps = Rot([(sb("tmp%d" % i, [128, TG], F32), Buf("tmp%d" % i)) for i in range(2)])
    rstd_t = sb("rstd_t", [128, TG], F32)
    rstd_b = Buf("rstd")
    rt_t = sb("rt_t", [128, TG], F32)
    rt_b = Buf("rt")
    row_t = sb("row_t", [1, 512], F32)
    row_b = Buf("row")
    brow_t = sb("brow_t", [1, 512], F32)
    brow_b = Buf("brow")
    small_t = sb("small_t", [128, 64], F32)
    small_b = Buf("small")
    phase_base = arena["off"]

    es = ExitStack()
    psum = []
    for i in range(8):
        t = es.enter_context(nc.psum_tensor("ps%d" % i, [128, 512], F32))
        psum.append((t, Buf("ps%d" % i)))
    PG = Rot([psum[0], psum[2]])
    PU = Rot([psum[1], psum[3]])
    PY = Rot([psum[4], psum[5]])
    PSSQ = psum[6]
    PM = psum[7]

    mm = lambda out, lhsT, rhs, st, sp: (lambda h: h.matmul(out, lhsT, rhs, start=st, stop=sp))

    def load_w_cols(W2d, col0, ncols, slot, slot_b, dst_col0=0, width=None, kch=NCH):
        width = width or ncols
        view = slot[:, 0:kch * width].rearrange("p (c n) -> p c n", n=width)
        src = W2d.rearrange("(c p) n -> p c n", p=128)[:, :, col0:col0 + ncols]
        P.dma("pool", lambda h: h.dma_start(out=view[:, :, dst_col0:dst_col0 + ncols], in_=src),
              writes=[slot_b])
        return view

    def ssq_rstd(src_t, src_b, src_fn=None):
        pst, psb = PSSQ
        if src_fn is None:
            src_fn = lambda c: src_t[:, c, :]
        for c in range(NCH):
            sq, sqb = sqs.next()
            P.op("act", lambda h, sq=sq, c=c: h.activation(out=sq[:, :], in_=src_fn(c), func=AF.Square),
                 reads=[src_b], writes=[sqb])
            P.op("pe", mm(pst[:, :], onesb_t[:, :], sq[:, :], c == 0, c == NCH - 1),
                 reads=[ones_b, sqb], writes=[psb])
        P.op("act", lambda h: h.activation(out=rt_t[:, :], in_=pst[:, :], func=AF.Ln,
                                           bias=eps_t[:, 0:1], scale=1.0 / D),
             reads=[psb, ones_b], writes=[rt_b])
        P.op("act", lambda h: h.activation(out=rstd_t[:, :], in_=rt_t[:, :], func=AF.Exp, scale=-0.5),
             reads=[rt_b], writes=[rstd_b])

    def prenorm(acol, bcol):
        ssq_rstd(xg_t, xg_b)
        for c in range(NCH):
            tmp, tb = tmps.next()
            P.op("dve", lambda h, tmp=tmp, c=c: h.scalar_tensor_tensor(
                out=tmp[:, :], in0=xg_t[:, c, :], scalar=acol[:, c:c + 1], in1=rstd_t[:, :],
                op0=ALU.mult, op1=ALU.mult), reads=[xg_b, der_b, rstd_b], writes=[tb])
            P.op("act", lambda h, tmp=tmp, c=c: h.activation(
                out=hT_t[:, c, :], in_=tmp[:, :], func=AF.Identity, bias=bcol[:, c:c + 1], scale=1.0),
                reads=[tb, modT_b], writes=[hT_b])

    def postnorm_res(coef):
        ssq_rstd(yT_t, yT_b)
        for c in range(NCH):
            tmp, tb = tmps.next()
            P.op("dve", lambda h, tmp=tmp, c=c: h.scalar_tensor_tensor(
                out=tmp[:, :], in0=yT_t[:, c, :], scalar=coef[:, c:c + 1], in1=rstd_t[:, :],
                op0=ALU.mult, op1=ALU.mult), reads=[yT_b, der_b, rstd_b], writes=[tb])
            P.op("pool", lambda h, tmp=tmp, c=c: h.tensor_tensor(
                out=xg_t[:, c, :], in0=xg_t[:, c, :], in1=tmp[:, :], op=ALU.add),
                reads=[tb, xg_b], writes=[xg_b])

    def load_xg(g):
        P.dma("sp", lambda h: h.dma_start(out=xg_t[:, :, :], in_=xT_v[:, :, g * TG:(g + 1) * TG]),
              reads=[xT_b[g], xTw_b[g]], writes=[xg_b])

    def store_xg(g):
        P.dma("sp", lambda h: h.dma_start(out=xT_v[:, :, g * TG:(g + 1) * TG], in_=xg_t[:, :, :]),
              reads=[xg_b], writes=[xT_b[g]])

    def linear_fm(W2d, col0, n_oc, rhs_fn, rhs_bufs, kch, evac, ocw=128, per_load=4):
        oc = 0
        while oc < n_oc:
            nl = min(per_load, n_oc - oc)
            slot, slot_b = wrot.next()
            view = load_w_cols(W2d, col0 + oc * ocw, nl * ocw, slot, slot_b, kch=kch)
            for j in range(nl):
                pst, psb = PY.next()
                for c in range(kch):
                    P.op("pe", mm(pst[0:ocw, :], view[:, c, j * ocw:(j + 1) * ocw], rhs_fn(c), c == 0, c == kch - 1),
                         reads=[slot_b] + rhs_bufs, writes=[psb])
                evac(oc + j, pst, psb)
            oc += nl

    P.dma("sp", lambda h: h.dma_start(out=ident_t[:, :], in_=ident[:, :]), writes=[ident_b])
    P.op("dve", lambda h: h.memset(ones_t[:, :], 1.0), writes=[ones_b])
    P.op("dve", lambda h: h.memset(eps_t[:, :], EPS), writes=[ones_b])
    P.op("dve", lambda h: h.memset(onesb_t[:, :], 1.0), writes=[ones_b])
    P.op("dve", lambda h: h.memset(cst_t[:, 0:1], -float(np.pi)), writes=[ones_b])
    for k in range(4):
        tmp, tb = tmps.next()
        P.dma("sp", lambda h, tmp=tmp, k=k: h.dma_start(out=tmp[0:64, 0:128], in_=gains[k][:, :]), writes=[tb])
        pst, psb = PM
        P.op("pe", lambda h, tmp=tmp: h.transpose(pst[:, 0:64], tmp[0:64, 0:128], ident_t[0:64, 0:64]),
             reads=[tb, ident_b], writes=[psb])
        P.op("dve", lambda h, k=k: h.tensor_copy(out=gains_t[:, k, :], in_=pst[:, 0:64]), reads=[psb], writes=[gains_b])
    tmp, tb = tmps.next()
    P.dma("sp", lambda h, tmp=tmp: h.dma_start(out=tmp[0:16, 0:128], in_=cvec[:, :]), writes=[tb])
    pst, psb = PM
    P.op("pe", lambda h, tmp=tmp: h.transpose(pst[:, 0:16], tmp[0:16, 0:128], ident_t[0:16, 0:16]),
         reads=[tb, ident_b], writes=[psb])
    P.op("act", lambda h: h.activation(out=cact_t[:, :], in_=pst[:, 0:16], func=AF.Silu), reads=[psb], writes=[cact_b])

    xblk = sb("xblk", [128, 4, D], F32) if False else None
    for g in range(NG):
        stage = yT_t[:, :, :].rearrange("p c t -> p (c t)").rearrange("p (b d) -> p b d", b=4)
        P.dma("sp", lambda h, g=g: h.dma_start(
            out=stage, in_=xs[g * TG:(g + 1) * TG, :].rearrange("(b p) d -> p b d", p=128)), writes=[yT_b])
        for c in range(NCH):
            pst, psb = PY.next()
            for b in range(4):
                P.op("pe", lambda h, pst=pst, b=b, c=c: h.transpose(
                    pst[:, b * 128:(b + 1) * 128], stage[:, b, c * 128:(c + 1) * 128], ident_t[:, :]),
                    reads=[yT_b, ident_b], writes=[psb])
            eng = "act" if c % 2 == 0 else "dve"
            if eng == "act":
                P.op("act", lambda h, pst=pst, c=c: h.activation(out=xg_t[:, c, :], in_=pst[:, :], func=AF.Copy),
                     reads=[psb], writes=[xg_b])
            else:
                P.op("dve", lambda h, pst=pst, c=c: h.tensor_copy(out=xg_t[:, c, :], in_=pst[:, :]),
                     reads=[psb], writes=[xg_b])
        store_xg(g)

    def compute_mod(l):
        for cg in range(6):
            slot, slot_b = wrot.next()
            view = load_w_cols(ada_w[LI[l]], cg * 512, 512, slot, slot_b)
            P.dma("sp", lambda h, cg=cg: h.dma_start(out=brow_t[0:1, :], in_=ada_b[LI[l]:LI[l] + 1, cg * 512:(cg + 1) * 512]),
                  writes=[brow_b])
            pst, psb = PY.next()
            for c in range(NCH):
                P.op("pe", mm(pst[0:1, :], cact_t[:, c:c + 1], view[:, c, :], c == 0, c == NCH - 1),
                     reads=[cact_b, slot_b], writes=[psb])
            P.op("dve", lambda h, pst=pst: h.tensor_tensor(out=row_t[0:1, :], in0=pst[0:1, :], in1=brow_t[0:1, :],
                                                           op=ALU.add), reads=[psb, brow_b], writes=[row_b])
            pm, pmb = PM
            for j in range(4):
                P.op("pe", mm(pm[:, j:j + 1], row_t[0:1, j * 128:(j + 1) * 128], one11[0:1, 0:1], True, True),
                     reads=[row_b, ones_b], writes=[pmb])
            P.op("dve", lambda h, cg=cg: h.tensor_copy(out=modp_t[:, cg * 4:(cg + 1) * 4], in_=pm[:, 0:4]),
                 reads=[pmb], writes=[modp_b])
        P.dma("sp", lambda h: h.dma_start(out=modin[l].ap(), in_=modp_t[:, :]), reads=[modp_b], writes=[modin_b])
        P.dma("pool", lambda h: h.collective_compute("AllGather", ALU.bypass, replica_groups=RG,
                                                     ins=[modin[l].ap().opt()], outs=[modout[l].ap().opt()]),
              reads=[modin_b], writes=[modout_b], inc=1)
        P.dma("sp", lambda h: h.dma_start(out=modT[:, :].rearrange("p (r c) -> p r c", r=4),
                                          in_=modout[l].ap().rearrange("(r p) c -> p r c", r=4)),
              reads=[modout_b], writes=[modT_b])
        for which, (sc_i, gate_i, pre_k, post_k) in enumerate(((1, 2, 0, 1), (4, 5, 2, 3))):
            P.op("dve", lambda h, sc_i=sc_i: h.tensor_scalar_add(out=small_t[:, 0:16], in0=modT[:, sc_i * 16:(sc_i + 1) * 16],
                                                                 scalar1=1.0), reads=[modT_b], writes=[small_b])
            P.op("dve", lambda h, which=which, pre_k=pre_k: h.tensor_tensor(
                out=der_t[:, 2 * which, :], in0=small_t[:, 0:16], in1=gains_t[:, pre_k, l * 16:(l + 1) * 16], op=ALU.mult),
                reads=[small_b, gains_b], writes=[der_b])
            P.op("dve", lambda h, which=which, gate_i=gate_i, post_k=post_k: h.tensor_tensor(
                out=der_t[:, 2 * which + 1, :], in0=modT[:, gate_i * 16:(gate_i + 1) * 16],
                in1=gains_t[:, post_k, l * 16:(l + 1) * 16], op=ALU.mult),
                reads=[modT_b, gains_b], writes=[der_b])

    def ffn_group(l, g):
        load_xg(g)
        prenorm(der_t[:, 2, :], modT[:, 48:64])
        for fc in range(NFC):
            slot, slot_b = wrot.next()
            view = load_w_cols(w_gu[LI[l]], fc * 128, 128, slot, slot_b, dst_col0=0, width=256)
            load_w_cols(w_gu[LI[l]], DFF + fc * 128, 128, slot, slot_b, dst_col0=128, width=256)
            pg, pgb = PG.next()
            pu, pub = PU.next()
            for c in range(NCH):
                P.op("pe", mm(pg[:, :], view[:, c, 0:128], hT_t[:, c, :], c == 0, c == NCH - 1),
                     reads=[slot_b, hT_b], writes=[pgb])
            for c in range(NCH):
                P.op("pe", mm(pu[:, :], view[:, c, 128:256], hT_t[:, c, :], c == 0, c == NCH - 1),
                     reads=[slot_b, hT_b], writes=[pub])
            tmp, tb = tmps.next()
            P.op("act", lambda h, pg=pg, tmp=tmp: h.activation(out=tmp[:, :], in_=pg[:, :], func=AF.Silu),
                 reads=[pgb], writes=[tb])
            P.op("dve", lambda h, pu=pu, tmp=tmp, fc=fc: h.tensor_tensor(
                out=big_t[:, fc, :], in0=tmp[:, :], in1=pu[:, :], op=ALU.mult), reads=[tb, pub], writes=[big_b])
        for dc in range(NCH):
            slot, slot_b = wrot.next()
            view = slot[:, 0:NFC * 128].rearrange("p (j o) -> p j o", o=128)
            src = w_dn[LI[l]].rearrange("(j p) o -> p j o", p=128)[:, :, dc * 128:(dc + 1) * 128]
            P.dma("pool", lambda h, view=view, src=src: h.dma_start(out=view, in_=src), writes=[slot_b])
            py, pyb = PY.next()
            for j in range(NFC):
                P.op("pe", mm(py[:, :], view[:, j, :], big_t[:, j, :], j == 0, j == NFC - 1),
                     reads=[slot_b, big_b], writes=[pyb])
            P.op("act", lambda h, py=py, dc=dc: h.activation(out=yT_t[:, dc, :], in_=py[:, :], func=AF.Copy),
                 reads=[pyb], writes=[yT_b])
        postnorm_res(der_t[:, 3, :])
        store_xg(g)

    TG2 = 1024
    assert attn_lim - xg_off >= 159744, (attn_lim, xg_off)
    XY = nc.alloc_sbuf_tensor_at("f_xy", [128, NCH, TG2], F32, offset=xg_off)
    H2 = nc.alloc_sbuf_tensor_at("f_h2", [128, NCH, TG2], BF16, offset=xg_off + 65536)
    A2 = nc.alloc_sbuf_tensor_at("f_a2", [128, 22, TG2], BF16, offset=xg_off + 98304)
    fws = [(nc.alloc_sbuf_tensor_at("f_w%d" % i, [128, 4096], BF16, offset=xg_off + 143360 + i * 8192), Buf("f_w%d" % i))
           for i in range(2)]
    fws += [(nc.alloc_sbuf_tensor_at("f_w%d" % (2 + i), [128, 4096], BF16, offset=phase_base + i * 8192), Buf("f_w%d" % (2 + i)))
            for i in range(2)]
    fwrot = Rot(fws)
    xins = Rot([(nc.alloc_sbuf_tensor_at("f_xin%d" % i, [128, 512], F32, offset=phase_base + 16384 + i * 2048), Buf("f_xin%d" % i))
                for i in range(4)])
    xouts = Rot([(nc.alloc_sbuf_tensor_at("f_xo%d" % i, [128, 512], F32, offset=phase_base + 24576 + i * 2048), Buf("f_xo%d" % i))
                 for i in range(4)])
    XY_b, H2_b, A2_b = Buf("f_xy"), Buf("f_h2"), Buf("f_a2")

    def ffn_big(l):
        acol, bcol, coef = der_t[:, 2, :], modT[:, 48:64], der_t[:, 3, :]
        Wgu = w_gu[LI[l]]
        Wdn = w_dn[LI[l]].rearrange("(j p) o -> p j o", p=128)
        for g2 in range(2):
            t0 = g2 * TG2
            gb = [2 * g2, 2 * g2 + 1]
            P.dma("sp", lambda h, t0=t0: h.dma_start(out=XY[:, :, :], in_=xT_v[:, :, t0:t0 + TG2]),
                  reads=[xT_b[gb[0]], xT_b[gb[1]], xTw_b[gb[0]], xTw_b[gb[1]]], writes=[XY_b])
            for th in range(2):
                hs = slice(th * 512, (th + 1) * 512)
                ssq_rstd(None, XY_b, src_fn=lambda c, hs=hs: XY[:, c, hs])
                for c in range(NCH):
                    tmp, tb = tmps.next()
                    P.op("dve", lambda h, tmp=tmp, c=c, hs=hs: h.scalar_tensor_tensor(
                        out=tmp[:, :], in0=XY[:, c, hs], scalar=acol[:, c:c + 1], in1=rstd_t[:, :],
                        op0=ALU.mult, op1=ALU.mult), reads=[XY_b, der_b, rstd_b], writes=[tb])
                    P.op("act", lambda h, tmp=tmp, c=c, hs=hs: h.activation(
                        out=H2[:, c, hs], in_=tmp[:, :], func=AF.Identity, bias=bcol[:, c:c + 1], scale=1.0),
                        reads=[tb, modT_b], writes=[H2_b])
            for fh in range(2):
                for fcl in range(22):
                    fc = fh * 22 + fcl
                    slot, slot_b = fwrot.next()
                    view = load_w_cols(Wgu, fc * 128, 128, slot, slot_b, dst_col0=0, width=256)
                    load_w_cols(Wgu, DFF + fc * 128, 128, slot, slot_b, dst_col0=128, width=256)
                    for th in range(2):
                        hs = slice(th * 512, (th + 1) * 512)
                        pg, pgb = PG.next()
                        pu, pub = PU.next()
                        for c in range(NCH):
                            P.op("pe", mm(pg[:, :], view[:, c, 0:128], H2[:, c, hs], c == 0, c == NCH - 1),
                                 reads=[slot_b, H2_b], writes=[pgb])
                        for c in range(NCH):
                            P.op("pe", mm(pu[:, :], view[:, c, 128:256], H2[:, c, hs], c == 0, c == NCH - 1),
                                 reads=[slot_b, H2_b], writes=[pub])
                        tmp, tb = tmps.next()
                        P.op("act", lambda h, pg=pg, tmp=tmp: h.activation(out=tmp[:, :], in_=pg[:, :], func=AF.Silu),
                             reads=[pgb], writes=[tb])
                        P.op("dve", lambda h, pu=pu, tmp=tmp, fcl=fcl, hs=hs: h.tensor_tensor(
                            out=A2[:, fcl, hs], in0=tmp[:, :], in1=pu[:, :], op=ALU.mult), reads=[tb, pub], writes=[A2_b])
                for dc in range(NCH):
                    slot, slot_b = fwrot.next()
                    view = slot[:, 0:22 * 128].rearrange("p (j o) -> p j o", o=128)
                    src = Wdn[:, fh * 22:(fh + 1) * 22, dc * 128:(dc + 1) * 128]
                    P.dma("pool", lambda h, view=view, src=src: h.dma_start(out=view, in_=src), writes=[slot_b])
                    for th in range(2):
                        hs = slice(th * 512, (th + 1) * 512)
                        py, pyb = PY.next()
                        for j in range(22):
                            P.op("pe", mm(py[:, :], view[:, j, :], A2[:, j, hs], j == 0, j == 21),
                                 reads=[slot_b, A2_b], writes=[pyb])
                        if fh == 0:
                            P.op("act", lambda h, py=py, dc=dc, hs=hs: h.activation(out=XY[:, dc, hs], in_=py[:, :], func=AF.Copy),
                                 reads=[pyb], writes=[XY_b])
                        else:
                            P.op("dve", lambda h, py=py, dc=dc, hs=hs: h.tensor_tensor(out=XY[:, dc, hs], in0=py[:, :],
                                                                                       in1=XY[:, dc, hs], op=ALU.add),
                                 reads=[pyb, XY_b], writes=[XY_b])
            for th in range(2):
                hs = slice(th * 512, (th + 1) * 512)
                gg = gb[th]
                ssq_rstd(None, XY_b, src_fn=lambda c, hs=hs: XY[:, c, hs])
                for c in range(NCH):
                    xin, xinb = xins.next()
                    xo, xob = xouts.next()
                    P.dma("sp", lambda h, xin=xin, c=c, gg=gg: h.dma_start(out=xin[:, :], in_=xT_v[:, c, gg * 512:(gg + 1) * 512]),
                          reads=[xT_b[gg]], writes=[xinb])
                    tmp, tb = tmps.next()
                    P.op("dve", lambda h, tmp=tmp, c=c, hs=hs: h.scalar_tensor_tensor(
                        out=tmp[:, :], in0=XY[:, c, hs], scalar=coef[:, c:c + 1], in1=rstd_t[:, :],
                        op0=ALU.mult, op1=ALU.mult), reads=[XY_b, der_b, rstd_b], writes=[tb])
                    P.op("pool", lambda h, tmp=tmp, xin=xin, xo=xo: h.tensor_tensor(
                        out=xo[:, :], in0=xin[:, :], in1=tmp[:, :], op=ALU.add), reads=[tb, xinb], writes=[xob])
                    P.dma("sp", lambda h, xo=xo, c=c, gg=gg: h.dma_start(out=xT_v[:, c, gg * 512:(gg + 1) * 512], in_=xo[:, :]),
                          reads=[xob], writes=[xTw_b[gg]])

    def sgu_setup():
        st = {}
        st["wsT"] = sb("sgu_wsT", [128, 16, 128], BF16)
        st["bs"] = sb("sgu_bs", [128, 16, 128], F32)
        st["lng"] = sb("sgu_lng", [128, D], F32)
        st["lnb"] = sb("sgu_lnb", [128, D], F32)
        st["tri"] = sb("sgu_tri", [128, 128], F32)
        st["stat"] = sb("sgu_stat", [128, 8], F32)
        st["vtm"] = nc.alloc_sbuf_tensor_at("sgu_vtm", [128, 4, D], BF16, offset=big_off + 16 * TG * 2)
        st["b"] = {k: Buf("sgu_" + k) for k in ("wsT", "bs", "lng", "lnb", "vtm", "tri", "stat")}
        b = st["b"]
        P.dma("sp", lambda h: h.dma_start(out=st["tri"][:, :], in_=trimask[:, :]), writes=[b["tri"]])
        P.dma("sp", lambda h: h.dma_start(out=st["bs"][:, :, :].rearrange("p g t -> p (g t)"),
                                          in_=sgu_b_s[0, :].partition_broadcast(128)), writes=[b["bs"]])
        P.dma("sp", lambda h: h.dma_start(out=st["lng"][:, :], in_=sgu_ln_g[0, :].partition_broadcast(128)),
              writes=[b["lng"]])
        P.dma("sp", lambda h: h.dma_start(out=st["lnb"][:, :], in_=sgu_ln_b[0, :].partition_broadcast(128)),
              writes=[b["lnb"]])
        for gi in range(16):
            tmp, tb = tmps.next()
            P.dma("sp", lambda h, tmp=tmp, gi=gi: h.dma_start(out=tmp[:, 0:128], in_=sgu_w_s[0, gi, :, :]), writes=[tb])
            P.op("dve", lambda h, tmp=tmp: h.tensor_tensor(out=tmp[:, 128:256], in0=tmp[:, 0:128], in1=st["tri"][:, :],
                                                           op=ALU.mult), reads=[tb, b["tri"]], writes=[tb])
            pst, psb = PM
            P.op("pe", lambda h, tmp=tmp, pst=pst: h.transpose(pst[:, 0:128], tmp[:, 128:256], ident_t[:, :]),
                 reads=[tb, ident_b], writes=[psb])
            P.op("dve", lambda h, gi=gi, pst=pst: h.tensor_copy(out=st["wsT"][:, gi, :], in_=pst[:, 0:128]),
                 reads=[psb], writes=[b["wsT"]])
        return st

    def sgu_group(st, g):
        b = st["b"]
        load_xg(g)
        prenorm(der_t[:, 0, :], modT[:, 0:16])
        W = sgu_w_in[0]
        def evac_u(oc, pst, psb):
            P.op("act", lambda h: h.activation(out=big_t[:, oc, :], in_=pst[:, :], func=AF.Gelu),
                 reads=[psb], writes=[big_b])
        linear_fm(W, 0, 16, lambda c: hT_t[:, c, :], [hT_b], NCH, evac_u)
        zv4 = yT_t[:, :, :].rearrange("p c t -> p (c t)").rearrange("p (b d) -> p b d", b=4)
        for cg in range(4):
            slot, slot_b = wrot.next()
            view = load_w_cols(W, D + cg * 512, 512, slot, slot_b)
            for blk in range(4):
                pst, psb = PY.next()
                for c in range(NCH):
                    P.op("pe", mm(pst[:, :], hT_t[:, c, blk * 128:(blk + 1) * 128], view[:, c, :], c == 0, c == NCH - 1),
                         reads=[hT_b, slot_b], writes=[psb])
                P.op("act", lambda h, pst=pst, cg=cg, blk=blk: h.activation(
                    out=zv4[:, blk, cg * 512:(cg + 1) * 512], in_=pst[:, :], func=AF.Gelu), reads=[psb], writes=[yT_b])
        stat = st["stat"]
        for blk in range(4):
            zv = zv4[:, blk, :]
            P.op("dve", lambda h, zv=zv: h.tensor_reduce(out=stat[:, 0:1], in_=zv, axis=mybir.AxisListType.X, op=ALU.add),
                 reads=[yT_b], writes=[b["stat"]])
            P.op("act", lambda h, zv=zv, blk=blk: h.activation(out=st["vtm"][:, blk, :], in_=zv, func=AF.Square,
                                                               accum_out=stat[:, 1:2]),
                 reads=[yT_b], writes=[b["stat"], b["vtm"]])
            P.op("dve", lambda h: h.tensor_scalar_mul(out=stat[:, 2:3], in0=stat[:, 0:1], scalar1=1.0 / D),
                 reads=[b["stat"]], writes=[b["stat"]])
            P.op("dve", lambda h: h.tensor_tensor(out=stat[:, 3:4], in0=stat[:, 2:3], in1=stat[:, 2:3], op=ALU.mult),
                 reads=[b["stat"]], writes=[b["stat"]])
            P.op("dve", lambda h: h.scalar_tensor_tensor(out=stat[:, 4:5], in0=stat[:, 1:2], scalar=1.0 / D,
                                                         in1=stat[:, 3:4], op0=ALU.mult, op1=ALU.subtract),
                 reads=[b["stat"]], writes=[b["stat"]])
            P.op("act", lambda h: h.activation(out=stat[:, 5:6], in_=stat[:, 4:5], func=AF.Sqrt, bias=eps_t[:, 0:1],
                                               scale=1.0), reads=[b["stat"], ones_b], writes=[b["stat"]])
            P.op("dve", lambda h: h.reciprocal(out=stat[:, 6:7], in_=stat[:, 5:6]), reads=[b["stat"]], writes=[b["stat"]])
            P.op("dve", lambda h, zv=zv: h.tensor_scalar(out=zv, in0=zv, scalar1=stat[:, 2:3], scalar2=stat[:, 6:7],
                                                         op0=ALU.subtract, op1=ALU.mult), reads=[b["stat"], yT_b], writes=[yT_b])
            P.op("pool", lambda h, zv=zv: h.tensor_tensor(out=zv, in0=zv, in1=st["lng"][:, :], op=ALU.mult),
                 reads=[yT_b, b["lng"]], writes=[yT_b])
            P.op("dve", lambda h, zv=zv, blk=blk: h.tensor_tensor(out=st["vtm"][:, blk, :], in0=zv, in1=st["lnb"][:, :],
                                                                  op=ALU.add), reads=[yT_b, b["lnb"]], writes=[b["vtm"]])
        for gi in range(16):
            pst, psb = PY.next()
            for blk in range(4):
                P.op("pe", mm(pst[:, blk * 128:(blk + 1) * 128], st["vtm"][:, blk, gi * 128:(gi + 1) * 128],
                              st["wsT"][:, gi, :], True, True), reads=[b["vtm"], b["wsT"]], writes=[psb])
            tmp, tb = tmps.next()
            P.op("dve", lambda h, pst=pst, tmp=tmp, gi=gi: h.tensor_tensor(
                out=tmp[:, :].rearrange("p (b t) -> p b t", b=4), in0=pst[:, :].rearrange("p (b t) -> p b t", b=4),
                in1=st["bs"][:, gi:gi + 1, :].broadcast_to([128, 4, 128]), op=ALU.add),
                reads=[psb, b["bs"]], writes=[tb])
            P.op("pool", lambda h, tmp=tmp, gi=gi: h.tensor_tensor(out=big_t[:, gi, :], in0=big_t[:, gi, :], in1=tmp[:, :],
                                                                   op=ALU.mult), reads=[tb, big_b], writes=[big_b])
        def evac_y(oc, pst, psb):
            P.op("act", lambda h: h.activation(out=yT_t[:, oc, :], in_=pst[:, :], func=AF.Copy),
                 reads=[psb], writes=[yT_b])
        linear_fm(sgu_w_out[0], 0, 16, lambda c: big_t[:, c, :], [big_b], NCH, evac_y)
        postnorm_res(der_t[:, 1, :])
        store_xg(g)

    def at_alloc(state, name, shape, dt):
        nbytes = int(np.prod(shape[1:])) * (4 if dt in (F32, I32) else 2)
        off = (state["off"] + 31) // 32 * 32
        state["off"] = off + nbytes
        assert state["off"] <= attn_lim, (name, state["off"], attn_lim)
        state["n"] += 1
        return nc.alloc_sbuf_tensor_at("%s_%d" % (name, state["n"]), list(shape), dt, offset=off)

    def evac_to(dst_fn, dst_buf, func=None, eng="act"):
        def ev(oc, pst, psb):
            P.op("act", lambda h: h.activation(out=dst_fn(oc), in_=pst[0:dst_fn(oc).shape[0], :], func=AF.Copy),
                 reads=[psb], writes=[dst_buf])
        return ev

    def attn_out_phase(W2d):
        for g in range(NG):
            P.dma("sp", lambda h, g=g: h.dma_start(
                out=big_t[:, 0:16, :], in_=att_s.rearrange("(c p) t -> p c t", p=128)[:, :, g * TG:(g + 1) * TG]),
                reads=[att_b], writes=[big_b])
            load_xg(g)
            linear_fm(W2d, 0, 16, lambda c: big_t[:, c, :], [big_b], NCH,
                      evac_to(lambda oc: yT_t[:, oc, :], yT_b))
            postnorm_res(der_t[:, 1, :])
            store_xg(g)

    def all_gather(src, src_b, dst, dst_b):
        P.dma("pool", lambda h: h.collective_compute("AllGather", ALU.bypass, replica_groups=RG,
                                                     ins=[src.ap().opt()], outs=[dst.ap().opt()]),
              reads=[src_b], writes=[dst_b], inc=1)

    fox_cache = {}

    def fox_layer(l, j):
        dr = fdr
        db = dr["b"]
        W = fox_w_in[j]
        kin_v = [t.ap().rearrange("(c p) t -> p c t", p=128) for t in dr["kin"]]
        vin_v = [t.ap().rearrange("p (h i d) -> p h i d", h=2, i=16) for t in dr["vin"]]
        lin_v = dr["lin"].ap().rearrange("(i p) h -> p i h", p=128)
        ph = {"off": phase_base, "n": 100 * l}
        bf_t = sb("fox_bf%d" % l, [128, 16], F32)
        vst_t = sb("fox_vst%d" % l, [128, 4, 512], BF16)
        lf_t = sb("fox_lf%d" % l, [128, 4, 16], F32)
        bf_b, vst_b, lf_b = fox_cache.setdefault("p1", (Buf("bf"), Buf("vst"), Buf("lf")))
        P.dma("sp", lambda h: h.dma_start(out=bf_t[:, :], in_=fox_b_f[j, :].partition_broadcast(128)), writes=[bf_b])
        for g in range(NG):
            load_xg(g)
            prenorm(der_t[:, 0, :], modT[:, 0:16])
            linear_fm(W, 0, 16, lambda c: hT_t[:, c, :], [hT_b], NCH, evac_to(lambda oc: big_t[:, oc, :], big_b))
            P.dma("sp", lambda h, g=g: h.dma_start(
                out=q_s.rearrange("(c p) t -> p c t", p=128)[:, :, g * TG:(g + 1) * TG], in_=big_t[:, 0:16, :]),
                reads=[big_b], writes=[q_b])
            linear_fm(W, D, 16, lambda c: hT_t[:, c, :], [hT_b], NCH, evac_to(lambda oc: big_t[:, 16 + oc, :], big_b))
            for ck in range(8):
                P.dma("sp", lambda h, g=g, ck=ck: h.dma_start(out=kin_v[ck][:, :, g * TG:(g + 1) * TG],
                                                               in_=big_t[:, 16 + 2 * ck:18 + 2 * ck, :]),
                      reads=[big_b], writes=[db["kin"]])
            for cg in range(4):
                slot, slot_b = wrot.next()
                view = load_w_cols(W, 2 * D + cg * 512, 512, slot, slot_b)
                for blk in range(4):
                    pst, psb = PY.next()
                    for c in range(NCH):
                        P.op("pe", mm(pst[:, :], hT_t[:, c, blk * 128:(blk + 1) * 128], view[:, c, :], c == 0, c == NCH - 1),
                             reads=[hT_b, slot_b], writes=[psb])
                    P.op("act", lambda h, pst=pst, blk=blk: h.activation(out=vst_t[:, blk, :], in_=pst[:, :], func=AF.Copy),
                         reads=[psb], writes=[vst_b])
                for hh in range(4):
                    P.dma("sp", lambda h, g=g, cg=cg, hh=hh: h.dma_start(
                        out=vin_v[(cg * 4 + hh) // 2][:, (cg * 4 + hh) % 2, g * 4:(g + 1) * 4, :],
                        in_=vst_t[:, :, hh * 128:(hh + 1) * 128]), reads=[vst_b], writes=[db["vin"]])
            slot, slot_b = wrot.next()
            view = load_w_cols(W, 3 * D, 16, slot, slot_b)
            for blk in range(4):
                pst, psb = PY.next()
                for c in range(NCH):
                    P.op("pe", mm(pst[:, 0:16], hT_t[:, c, blk * 128:(blk + 1) * 128], view[:, c, :], c == 0, c == NCH - 1),
                         reads=[hT_b, slot_b], writes=[psb])
                P.op("dve", lambda h, pst=pst, blk=blk: h.tensor_tensor(out=lf_t[:, blk, :], in0=pst[:, 0:16], in1=bf_t[:, :],
                                                                        op=ALU.add), reads=[psb, bf_b], writes=[lf_b])
            lf2 = lf_t[:, :, :].rearrange("p b h -> p (b h)")
            P.op("act", lambda h: h.activation(out=lf2, in_=lf2, func=AF.Exp, scale=-1.0), reads=[lf_b], writes=[lf_b])
            P.op("act", lambda h: h.activation(out=lf2, in_=lf2, func=AF.Ln, bias=1.0, scale=1.0), reads=[lf_b], writes=[lf_b])
            P.op("dve", lambda h: h.tensor_scalar_mul(out=lf2, in0=lf2, scalar1=-1.0), reads=[lf_b], writes=[lf_b])
            P.dma("sp", lambda h, g=g: h.dma_start(out=lin_v[:, g * 4:(g + 1) * 4, :], in_=lf_t[:, :, :]),
                  reads=[lf_b], writes=[db["lin"]])
        all_gather(dr["lin"], db["lin"], dr["lout"], db["lout"])
        for ck in range(8):
            all_gather(dr["kin"][ck], db["kin"], dr["kout"][ck], db["kout"][ck])
            all_gather(dr["vin"][ck], db["vin"], dr["vout"][ck], db["vout"][ck])
        P.barrier(exclude=db["kout"] + db["vout"])
        A = {"off": xg_off, "n": 1000 * (l + 1)}
        lfa = at_alloc(A, "lfa", [128, 64, 16], F32)
        ftm = at_alloc(A, "ftm", [128, 64, 16], F32)
        tbc = at_alloc(A, "tbc", [128, 64, 16], F32)
        pfa = at_alloc(A, "pfa", [128, 64, 16], F32)
        pfb = at_alloc(A, "pfb", [128, 64, 16], F32)
        fref = at_alloc(A, "fref", [128, 16, 16], F32)
        triu_t = at_alloc(A, "triu", [128, 128], F32)
        oh4_t = at_alloc(A, "oh4", [128, 4], F32)
        mfox_t = at_alloc(A, "mfox", [128, 4, 128], BF16)
        zo_t = at_alloc(A, "zo", [128, 16, 64], F32)
        qh = [at_alloc(A, "qh%d" % i, [128, TOK], BF16) for i in range(2)]
        kh = [at_alloc(A, "kh%d" % i, [128, 4, TOK], BF16) for i in range(2)]
        vh = [at_alloc(A, "vh%d" % i, [128, 4, 16, 128], BF16) for i in range(2)]
        bh = [at_alloc(A, "bh%d" % i, [128, 16, 64], F32) for i in range(2)]
        ah = [at_alloc(A, "ah%d" % i, [128, TOK], BF16) for i in range(2)]
        pts = Rot([(at_alloc(A, "pt%d" % i, [128, 512], BF16), [Buf("pt%d_%d" % (i, m)) for m in range(4)]) for i in range(4)])
        rec_t = at_alloc(A, "rec", [128, 512], F32)
        B_ = fox_cache.setdefault("B_", {k: Buf("fx_" + k) for k in (
            "lfa", "ftm", "tbc", "pfa", "pfb", "fref", "triu", "oh4", "mfox", "rec",
            "qh0", "qh1", "kh0", "kh1", "vh0", "vh1", "bh0", "bh1", "ah0", "ah1")})
        P.dma("sp", lambda h: h.dma_start(out=triu_t[:, :], in_=triu[:, :]), writes=[B_["triu"]])
        P.dma("sp", lambda h: h.dma_start(out=oh4_t[:, :], in_=oh4_in[:, :]), writes=[B_["oh4"]])
        P.dma("sp", lambda h: h.dma_start(out=zo_t[:, :, :].rearrange("p a b -> p (a b)"), in_=zo_in[:, :]), writes=[B_["oh4"]])
        P.dma("pool", lambda h: h.dma_start(out=mfox_t[:, :, :].rearrange("p j q -> p (j q)"), in_=m_fox_in[:, :]),
              writes=[B_["mfox"]])
        lout_ap = dr["lout"].ap()
        for rr in range(4):
            P.dma("sp", lambda h, rr=rr: h.dma_start(
                out=lfa[:, :, :].rearrange("p (i r) h -> p r i h", r=4)[:, rr],
                in_=lout_ap[rr * TOK:(rr + 1) * TOK, :].rearrange("(i p) h -> p i h", p=128)),
                reads=[db["lout"]], writes=[B_["lfa"]])
        lfa2 = lfa[:, :, :].rearrange("p g h -> p (g h)")
        ftm2 = ftm[:, :, :].rearrange("p g h -> p (g h)")
        tbc2 = tbc[:, :, :].rearrange("p g h -> p (g h)")
        pfa2 = pfa[:, :, :].rearrange("p g h -> p (g h)")
        pfb2 = pfb[:, :, :].rearrange("p g h -> p (g h)")
        for half in range(2):
            cs = slice(half * 512, (half + 1) * 512)
            pst, psb = PY.next()
            P.op("pe", mm(pst[:, :], triu_t[:, :], lfa2[:, cs], True, True), reads=[B_["triu"], B_["lfa"]], writes=[psb])
            P.op("dve", lambda h, pst=pst, cs=cs: h.tensor_copy(out=ftm2[:, cs], in_=pst[:, :]), reads=[psb], writes=[B_["ftm"]])
            pst, psb = PY.next()
            P.op("pe", mm(pst[:, :], ones_t[:, :], lfa2[:, cs], True, True), reads=[ones_b, B_["lfa"]], writes=[psb])
            P.op("dve", lambda h, pst=pst, cs=cs: h.tensor_copy(out=tbc2[:, cs], in_=pst[:, :]), reads=[psb], writes=[B_["tbc"]])
        P.op("dve", lambda h: h.tensor_copy(out=pfa2, in_=tbc2), reads=[B_["tbc"]], writes=[B_["pfa"]])
        cur, curb, oth, othb = pfa2, B_["pfa"], pfb2, B_["pfb"]
        sh = 1
        while sh < 64:
            w = sh * 16
            P.op("dve", lambda h, cur=cur, oth=oth, w=w: h.tensor_copy(out=oth[:, 0:w], in_=cur[:, 0:w]),
                 reads=[curb], writes=[othb])
            P.op("dve", lambda h, cur=cur, oth=oth, w=w: h.tensor_tensor(out=oth[:, w:1024], in0=cur[:, w:1024],
                                                                         in1=cur[:, 0:1024 - w], op=ALU.add),
                 reads=[curb], writes=[othb])
            cur, curb, oth, othb = oth, othb, cur, curb
            sh *= 2
        P.op("dve", lambda h, cur=cur, oth=oth: h.tensor_tensor(out=oth, in0=cur, in1=tbc2, op=ALU.subtract),
             reads=[curb, B_["tbc"]], writes=[othb])
        P.op("dve", lambda h, oth=oth: h.tensor_tensor(out=ftm2, in0=ftm2, in1=oth, op=ALU.add),
             reads=[othb, B_["ftm"]], writes=[B_["ftm"]])
        P.op("dve", lambda h, cur=cur, oth=oth: h.scalar_tensor_tensor(out=oth, in0=tbc2, scalar=-0.5, in1=cur,
                                                                        op0=ALU.mult, op1=ALU.add),
             reads=[curb, B_["tbc"]], writes=[othb])
        fmid4 = (pfa if oth is pfa2 else pfb)[:, :, :].rearrange("p (i r) h -> p r i h", r=4)
        P.op("dve", lambda h: h.tensor_scalar(out=fref[:, :, :], in0=fmid4[:, 0], scalar1=oh4_t[:, 0:1], scalar2=None,
                                              op0=ALU.mult), reads=[othb, B_["oh4"]], writes=[B_["fref"]])
        for rr in range(1, 4):
            P.op("dve", lambda h, rr=rr: h.scalar_tensor_tensor(out=fref[:, :, :], in0=fmid4[:, rr], scalar=oh4_t[:, rr:rr + 1],
                                                                in1=fref[:, :, :], op0=ALU.mult, op1=ALU.add),
                 reads=[othb, B_["oh4"], B_["fref"]], writes=[B_["fref"]])
        kout_v = [t.ap().rearrange("(r c d) t -> d r c t", r=4, d=128) for t in dr["kout"]]
        vout_v = [t.ap().rearrange("(r p) (h i d) -> p r h i d", r=4, h=2, i=16) for t in dr["vout"]]
        PS_ST = Rot([psum[0], psum[1], psum[2], psum[7]])
        PS_O = Rot([psum[3], psum[4]])
        PS_D = Rot([psum[5], psum[6]])
        scale = 128.0 ** -0.5
        def head_res(hd):
            s2 = hd % 2
            return (qh[s2], kh[s2], vh[s2], bh[s2], ah[s2],
                    B_["qh%d" % s2], B_["kh%d" % s2], B_["vh%d" % s2], B_["bh%d" % s2], B_["ah%d" % s2])

        def emit_head_loads(hd):
            qt, kt, vt, bt, at, qb, kb, vb, bb, ab = head_res(hd)
            P.dma("sp", lambda h: h.dma_start(out=qt[:, :], in_=q_s[hd * 128:(hd + 1) * 128, :]), reads=[q_b], writes=[qb])
            P.dma("sp", lambda h: h.dma_start(out=kt[:, :, :], in_=kout_v[hd // 2][:, :, hd % 2, :]),
                  reads=[db["kout"][hd // 2]], writes=[kb])
            P.dma("sp", lambda h: h.dma_start(out=vt[:, :, :, :], in_=vout_v[hd // 2][:, :, hd % 2]),
                  reads=[db["vout"][hd // 2]], writes=[vb])
            P.op("dve", lambda h: h.tensor_tensor(
                out=bt[:, :, :], in0=fref[:, :, hd:hd + 1].broadcast_to([128, 16, 64]),
                in1=ftm[:, :, hd:hd + 1].rearrange("p g o -> p o g").broadcast_to([128, 16, 64]), op=ALU.subtract),
                reads=[B_["fref"], B_["ftm"]], writes=[bb])
            P.op("dve", lambda h: h.tensor_tensor(out=bt[:, :, :], in0=bt[:, :, :], in1=zo_t[:, :, :], op=ALU.add),
                 reads=[bb, B_["oh4"]], writes=[bb])

        steps = [(hd, jq, gk) for hd in range(16) for jq in range(4) for gk in range(16 * (jq + 1))]
        nst = len(steps)
        qk = {}
        acc = {}

        def emit_qk(idx):
            hd, jq, gk = steps[idx]
            qt, kt, vt, bt, at, qb, kb, vb, bb, ab = head_res(hd)
            rr, ii = gk % 4, gk // 4
            mmin = max(0, -(-(gk - 3 - 16 * jq) // 4))
            c0 = mmin * 128
            pst, psb = PS_ST.next()
            P.op("pe", mm(pst[:, c0:512], kt[:, rr, ii * 128:(ii + 1) * 128], qt[:, jq * 512 + c0:(jq + 1) * 512],
                          True, True), reads=[kb, qb], writes=[psb])
            qk[idx] = (pst, psb, mmin, c0)

        def emit_exp(idx):
            hd, jq, gk = steps[idx]
            qt, kt, vt, bt, at, qb, kb, vb, bb, ab = head_res(hd)
            pst, psb, mmin, c0 = qk[idx]
            pt, ptb = pts.next()
            for m in range(mmin, 4):
                li = 4 * jq + m
                P.op("act", lambda h, m=m, li=li: h.activation(
                    out=pt[:, m * 128:(m + 1) * 128], in_=pst[:, m * 128:(m + 1) * 128], func=AF.Exp,
                    bias=bt[:, li, gk:gk + 1], scale=scale), reads=[psb, bb], writes=[ptb[m]])
                jm = gk - 4 * li
                if 0 <= jm <= 3:
                    P.op("pool", lambda h, m=m, jm=jm: h.tensor_tensor(
                        out=pt[:, m * 128:(m + 1) * 128], in0=pt[:, m * 128:(m + 1) * 128], in1=mfox_t[:, jm, :],
                        op=ALU.mult), reads=[ptb[m], B_["mfox"]], writes=[ptb[m]])
            qk[idx] = (pst, psb, mmin, c0, pt, ptb)

        def emit_pv(idx):
            hd, jq, gk = steps[idx]
            qt, kt, vt, bt, at, qb, kb, vb, bb, ab = head_res(hd)
            pst, psb, mmin, c0, pt, ptb = qk.pop(idx)
            rr, ii = gk % 4, gk // 4
            ng = 16 * (jq + 1)
            if gk == 0:
                acc[(hd, jq)] = (PS_O.next(), PS_D.next())
            (po, pob), (pd, pdb) = acc[(hd, jq)]
            P.op("pe", mm(po[:, c0:512], vt[:, rr, ii, :], pt[:, c0:512], gk == 0, gk == ng - 1),
                 reads=[vb] + ptb[mmin:], writes=[pob])
            P.op("pe", mm(pd[:, c0:512], onesb_t[:, :], pt[:, c0:512], gk == 0, gk == ng - 1),
                 reads=[ones_b] + ptb[mmin:], writes=[pdb])
            if gk == ng - 1:
                del acc[(hd, jq)]
                P.op("dve", lambda h: h.reciprocal(out=rec_t[:, :], in_=pd[:, :]), reads=[pdb], writes=[B_["rec"]])
                P.op("dve", lambda h: h.tensor_tensor(out=at[:, jq * 512:(jq + 1) * 512], in0=po[:, :],
                                                      in1=rec_t[:, :], op=ALU.mult),
                     reads=[pob, B_["rec"]], writes=[ab])
                if jq == 3:
                    P.dma("sp", lambda h: h.dma_start(out=att_s[hd * 128:(hd + 1) * 128, :], in_=at[:, :]),
                          reads=[ab], writes=[att_b])
                    if hd + 2 < 16:
                        emit_head_loads(hd + 2)

        emit_head_loads(0)
        emit_head_loads(1)
        emit_qk(0)
        emit_qk(1)
        emit_qk(2)
        for idx in range(nst):
            emit_exp(idx)
            if idx + 3 < nst:
                emit_qk(idx + 3)
            emit_pv(idx)
        P.barrier()
        attn_out_phase(fox_w_out[j])
        P.barrier()

    def swa_layer(l):
        dr = swa_dr
        db = dr["b"]
        W = swa_w_in[0]
        kin_v = [t.ap().rearrange("(h d) t -> d h t", d=64) for t in dr["kin"]]
        vin_v = [t.ap().rearrange("p (i f) -> p i f", i=8) for t in dr["vin"]]
        q_v = q_s.rearrange("(h d) t -> d h t", d=64)
        att_v = att_s.rearrange("(h d) t -> d h t", d=64)
        c16 = sb("swa_c16", [16, TOK], F32)
        s16 = sb("swa_s16", [16, TOK], F32)
        pmat_t = sb("swa_pmat", [64, 16], BF16)
        invf_t = sb("swa_invf", [16, 1], F32)
        vst_t = sb("swa_vst", [128, 4, 512], BF16)
        SB_ = {k: Buf("sw_" + k) for k in ("c16", "s16", "pmat", "invf", "vst", "posi")}
        posi_t = sb("swa_posi", [16, TOK], I32)
        P.dma("sp", lambda h: h.dma_start(out=posi_t[:, :], in_=posin[0, :].partition_broadcast(16)), writes=[SB_["posi"]])
        P.dma("pool", lambda h: h.dma_start(out=pmat_t[:, :], in_=pmat_in[:, :]), writes=[SB_["pmat"]])
        P.dma("sp", lambda h: h.dma_start(out=invf_t[:, :], in_=invf_in[:, :]), writes=[SB_["invf"]])
        PI = float(np.pi)
        C1 = 6.28125
        C2 = float(2 * np.pi - 6.28125)
        ang = yT_t[0:16, 8:12, :].rearrange("p c t -> p (c t)")
        nf = yT_t[0:16, 0:4, :].rearrange("p c t -> p (c t)")
        mk = yT_t[0:16, 4:8, :].rearrange("p c t -> p (c t)")
        ys, yc = s16[:, :], c16[:, :]
        RW = dict(reads=[SB_["posi"], SB_["invf"], yT_b, SB_["s16"], SB_["c16"]],
                  writes=[yT_b, SB_["s16"], SB_["c16"], SB_["posi"]])
        P.op("dve", lambda h: h.tensor_copy(out=ang, in_=posi_t[:, :]), **RW)
        P.op("dve", lambda h: h.tensor_scalar_mul(out=ang, in0=ang, scalar1=invf_t[:, 0:1]), **RW)
        P.op("dve", lambda h: h.tensor_scalar_mul(out=nf, in0=ang, scalar1=float(1.0 / (2 * np.pi))), **RW)
        P.op("dve", lambda h: h.tensor_copy(out=posi_t[:, :], in_=nf), **RW)
        P.op("dve", lambda h: h.tensor_copy(out=nf, in_=posi_t[:, :]), **RW)
        P.op("dve", lambda h: h.scalar_tensor_tensor(out=ys, in0=nf, scalar=-C1, in1=ang, op0=ALU.mult, op1=ALU.add), **RW)
        P.op("dve", lambda h: h.scalar_tensor_tensor(out=ys, in0=nf, scalar=-C2, in1=ys, op0=ALU.mult, op1=ALU.add), **RW)
        P.op("dve", lambda h: h.tensor_single_scalar(out=mk, in_=ys, scalar=PI, op=ALU.is_gt), **RW)
        P.op("dve", lambda h: h.scalar_tensor_tensor(out=ys, in0=mk, scalar=-2 * PI, in1=ys, op0=ALU.mult, op1=ALU.add), **RW)
        P.op("dve", lambda h: h.tensor_single_scalar(out=mk, in_=ys, scalar=-PI, op=ALU.is_lt), **RW)
        P.op("dve", lambda h: h.scalar_tensor_tensor(out=ys, in0=mk, scalar=2 * PI, in1=ys, op0=ALU.mult, op1=ALU.add), **RW)
        P.op("dve", lambda h: h.tensor_scalar_add(out=yc, in0=ys, scalar1=PI / 2), **RW)
        P.op("dve", lambda h: h.tensor_single_scalar(out=mk, in_=yc, scalar=PI, op=ALU.is_gt), **RW)
        P.op("dve", lambda h: h.scalar_tensor_tensor(out=yc, in0=mk, scalar=-2 * PI, in1=yc, op0=ALU.mult, op1=ALU.add), **RW)
        P.op("act", lambda h: h.activation(out=ys, in_=ys, func=AF.Sin), reads=[SB_["s16"]], writes=[SB_["s16"]])
        P.op("act", lambda h: h.activation(out=yc, in_=yc, func=AF.Sin), reads=[SB_["c16"]], writes=[SB_["c16"]])

        def rope(tile_fn, nheads, g):
            for hh in range(nheads):
                pst, psb = PM
                P.op("pe", mm(pst[0:16, :], pmat_t[:, :], tile_fn(hh), True, True), reads=[SB_["pmat"], big_b], writes=[psb])
                t1, t1b = tmps.next()
                t2, t2b = tmps.next()
                P.op("dve", lambda h, t1=t1, hh=hh: h.tensor_tensor(out=t1[0:16, :], in0=tile_fn(hh)[0:16, :],
                                                                    in1=c16[:, g * TG:(g + 1) * TG], op=ALU.mult),
                     reads=[big_b, SB_["c16"]], writes=[t1b])
                P.op("dve", lambda h, t2=t2, pst=pst: h.tensor_tensor(out=t2[0:16, :], in0=pst[0:16, :],
                                                                      in1=s16[:, g * TG:(g + 1) * TG], op=ALU.mult),
                     reads=[psb, SB_["s16"]], writes=[t2b])
                P.op("pool", lambda h, t1=t1, t2=t2, hh=hh: h.tensor_tensor(out=tile_fn(hh)[0:16, :], in0=t1[0:16, :],
                                                                            in1=t2[0:16, :], op=ALU.add),
                     reads=[t1b, t2b], writes=[big_b])

        for g in range(NG):
            load_xg(g)
            prenorm(der_t[:, 0, :], modT[:, 0:16])
            linear_fm(W, 0, 32, lambda c: hT_t[:, c, :], [hT_b], NCH, evac_to(lambda oc: big_t[0:64, oc, :], big_b),
                      ocw=64, per_load=8)
            rope(lambda hh: big_t[0:64, hh, :], 32, g)
            P.dma("sp", lambda h, g=g: h.dma_start(out=q_v[:, :, g * TG:(g + 1) * TG], in_=big_t[0:64, 0:32, :]),
                  reads=[big_b], writes=[q_b])
            linear_fm(W, D, 8, lambda c: hT_t[:, c, :], [hT_b], NCH, evac_to(lambda oc: big_t[0:64, 32 + oc, :], big_b),
                      ocw=64, per_load=8)
            rope(lambda hh: big_t[0:64, 32 + hh, :], 8, g)
            for ck in range(2):
                P.dma("sp", lambda h, g=g, ck=ck: h.dma_start(out=kin_v[ck][:, :, g * TG:(g + 1) * TG],
                                                               in_=big_t[0:64, 32 + 4 * ck:36 + 4 * ck, :]),
                      reads=[big_b], writes=[db["kin"]])
            slot, slot_b = wrot.next()
            view = load_w_cols(W, D + 512, 512, slot, slot_b)
            for blk in range(4):
                pst, psb = PY.next()
                for c in range(NCH):
                    P.op("pe", mm(pst[:, :], hT_t[:, c, blk * 128:(blk + 1) * 128], view[:, c, :], c == 0, c == NCH - 1),
                         reads=[hT_b, slot_b], writes=[psb])
                P.op("act", lambda h, pst=pst, blk=blk: h.activation(out=vst_t[:, blk, :], in_=pst[:, :], func=AF.Copy),
                     reads=[psb], writes=[SB_["vst"]])
            P.dma("sp", lambda h, g=g: h.dma_start(out=vin_v[g // 2][:, (g % 2) * 4:(g % 2) * 4 + 4, :], in_=vst_t[:, :, :]),
                  reads=[SB_["vst"]], writes=[db["vin"]])
        for ck in range(2):
            all_gather(dr["kin"][ck], db["kin"], dr["kout"][ck], db["kout"])
            all_gather(dr["vin"][ck], db["vin"], dr["vout"][ck], db["vout"])
        P.barrier()
        A = {"off": xg_off, "n": 5000}
        kc2 = [at_alloc(A, "kc%d" % i, [64, 8, 128], BF16) for i in range(2)]
        vc2 = [at_alloc(A, "vc%d" % i, [128, 512], BF16) for i in range(2)]
        kp2_ = [at_alloc(A, "kp%d" % i, [64, 8, 128], BF16) for i in range(2)]
        vp2_ = [at_alloc(A, "vp%d" % i, [128, 512], BF16) for i in range(2)]
        qb2 = [at_alloc(A, "qblk%d" % i, [64, 32, 128], BF16) for i in range(2)]
        ab2 = [at_alloc(A, "ablk%d" % i, [64, 32, 128], BF16) for i in range(2)]
        kcand = at_alloc(A, "kcand", [64, 5, 8, 128], BF16)
        vcand = at_alloc(A, "vcand", [128, 5, 512], BF16)
        mtri = at_alloc(A, "mtri", [128, 128], BF16)
        mprev = at_alloc(A, "mprev", [128, 128], BF16)
        mprev0 = at_alloc(A, "mprev0", [128, 128], BF16)
        sel5 = at_alloc(A, "sel5", [128, 5], F32)
        sinke = at_alloc(A, "sinke", [64, 32], F32)
        den_t = at_alloc(A, "den", [64, 512], F32)
        ptc = Rot([(at_alloc(A, "ptc%d" % i, [128, 512], BF16), Buf("ptc%d" % i)) for i in range(2)])
        ptp = Rot([(at_alloc(A, "ptp%d" % i, [128, 512], BF16), Buf("ptp%d" % i)) for i in range(2)])
        B_ = {k: Buf("sw3_" + k) for k in ("kc0", "kc1", "vc0", "vc1", "kcand", "vcand", "kp0", "kp1", "vp0", "vp1",
                                           "qblk0", "qblk1", "ablk0", "ablk1", "mtri", "mprev",
                                           "mprev0", "sel5", "sinke", "den")}
        P.dma("pool", lambda h: h.dma_start(out=mtri[:, :], in_=triu[:, :]), writes=[B_["mtri"]])
        P.dma("pool", lambda h: h.dma_start(out=mprev[:, :], in_=m_prev_in[:, :]), writes=[B_["mprev"]])
        P.dma("pool", lambda h: h.dma_start(out=mprev0[:, :], in_=m_prev0_in[:, :]), writes=[B_["mprev0"]])
        P.dma("sp", lambda h: h.dma_start(out=sel5[:, :], in_=sel5_in[:, :]), writes=[B_["sel5"]])
        P.dma("sp", lambda h: h.dma_start(out=sinke[:, :], in_=swa_sinks[0, :].partition_broadcast(64)), writes=[B_["sinke"]])
        P.op("act", lambda h: h.activation(out=sinke[:, :], in_=sinke[:, :], func=AF.Exp), reads=[B_["sinke"]], writes=[B_["sinke"]])
        kout_v = [t.ap().rearrange("(r h d) t -> d r h t", r=4, d=64) for t in dr["kout"]]
        vout_v = [t.ap().rearrange("(r p) (i f) -> p r i f", r=4, i=8) for t in dr["vout"]]
        PS_C = Rot([psum[0], psum[1]])
        PS_P = Rot([psum[2], psum[3]])
        PS_O = Rot([psum[4], psum[5]])
        PS_D = Rot([psum[6], psum[7]])
        scale = 64.0 ** -0.5

        def emit_block_loads(i):
            par = i % 2
            kc, vc, kp, vp, qblk = kc2[par], vc2[par], kp2_[par], vp2_[par], qb2[par]
            kcb, vcb, kpb, vpb, qbb = (B_["kc%d" % par], B_["vc%d" % par], B_["kp%d" % par], B_["vp%d" % par],
                                       B_["qblk%d" % par])
            ip = max(i - 1, 0)
            for ck in range(2):
                P.dma("sp", lambda h, ck=ck: h.dma_start(out=kc[:, 4 * ck:4 * ck + 4, :],
                                                         in_=kin_v[ck][:, :, i * 128:(i + 1) * 128]),
                      reads=[db["kin"]], writes=[kcb])
                for rr in range(4):
                    P.dma("sp", lambda h, ck=ck, rr=rr: h.dma_start(
                        out=kcand[:, rr, 4 * ck:4 * ck + 4, :], in_=kout_v[ck][:, rr, :, i * 128:(i + 1) * 128]),
                        reads=[db["kout"]], writes=[B_["kcand"]])
                P.dma("sp", lambda h, ck=ck: h.dma_start(
                    out=kcand[:, 4, 4 * ck:4 * ck + 4, :], in_=kout_v[ck][:, 3, :, ip * 128:(ip + 1) * 128]),
                    reads=[db["kout"]], writes=[B_["kcand"]])
            P.dma("sp", lambda h: h.dma_start(out=vc[:, :], in_=vin_v[i // 8][:, i % 8, :]), reads=[db["vin"]], writes=[vcb])
            P.dma("sp", lambda h: h.dma_start(out=vcand[:, 0:4, :], in_=vout_v[i // 8][:, :, i % 8, :]),
                  reads=[db["vout"]], writes=[B_["vcand"]])
            P.dma("sp", lambda h: h.dma_start(out=vcand[:, 4, :], in_=vout_v[ip // 8][:, 3, ip % 8, :]),
                  reads=[db["vout"]], writes=[B_["vcand"]])
            P.dma("sp", lambda h: h.dma_start(out=qblk[:, :, :], in_=q_v[:, :, i * 128:(i + 1) * 128]),
                  reads=[q_b], writes=[qbb])

        def emit_block_select(i):
            par = i % 2
            kp, vp = kp2_[par], vp2_[par]
            kpb, vpb = B_["kp%d" % par], B_["vp%d" % par]
            kpf = kp[:, :, :].rearrange("d h t -> d (h t)")
            P.op("dve", lambda h: h.tensor_scalar(out=kpf, in0=kcand[:, 0, :, :].rearrange("d h t -> d (h t)"),
                                                  scalar1=sel5[0:64, 0:1], scalar2=None, op0=ALU.mult),
                 reads=[B_["kcand"], B_["sel5"]], writes=[kpb])
            P.op("dve", lambda h: h.tensor_scalar(out=vp[:, :], in0=vcand[:, 0, :], scalar1=sel5[:, 0:1], scalar2=None,
                                                  op0=ALU.mult), reads=[B_["vcand"], B_["sel5"]], writes=[vpb])
            for cnd in range(1, 5):
                P.op("dve", lambda h, cnd=cnd: h.scalar_tensor_tensor(
                    out=kpf, in0=kcand[:, cnd, :, :].rearrange("d h t -> d (h t)"), scalar=sel5[0:64, cnd:cnd + 1], in1=kpf,
                    op0=ALU.mult, op1=ALU.add), reads=[B_["kcand"], B_["sel5"], kpb], writes=[kpb])
                P.op("dve", lambda h, cnd=cnd: h.scalar_tensor_tensor(
                    out=vp[:, :], in0=vcand[:, cnd, :], scalar=sel5[:, cnd:cnd + 1], in1=vp[:, :],
                    op0=ALU.mult, op1=ALU.add), reads=[B_["vcand"], B_["sel5"], vpb], writes=[vpb])

        sw_steps = [(i, hk) for i in range(NBLK) for hk in range(8)]
        sA = {}
        sB = {}

        def stageA(si):
            i, hk = sw_steps[si]
            par = i % 2
            qsl = qb2[par][:, hk * 4:(hk + 1) * 4, :]
            pc, pcb = PS_C.next()
            pp, ppb = PS_P.next()
            P.op("pe", mm(pc[:, :], kc2[par][:, hk, :], qsl, True, True), reads=[B_["kc%d" % par], B_["qblk%d" % par]], writes=[pcb])
            P.op("pe", mm(pp[:, :], kp2_[par][:, hk, :], qsl, True, True), reads=[B_["kp%d" % par], B_["qblk%d" % par]], writes=[ppb])
            sA[si] = (pc, pcb, pp, ppb)

        def stageB(si):
            i, hk = sw_steps[si]
            par = i % 2
            vc, vp, ablk = vc2[par], vp2_[par], ab2[par]
            vcb, vpb, abb = B_["vc%d" % par], B_["vp%d" % par], B_["ablk%d" % par]
            pc, pcb, pp, ppb = sA.pop(si)
            mpv, mpvb = (mprev0, B_["mprev0"]) if i == 0 else (mprev, B_["mprev"])
            tc_, tcb = ptc.next()
            tp_, tpb = ptp.next()
            P.op("act", lambda h: h.activation(out=tc_[:, :], in_=pc[:, :], func=AF.Exp, scale=scale), reads=[pcb], writes=[tcb])
            P.op("act", lambda h: h.activation(out=tp_[:, :], in_=pp[:, :], func=AF.Exp, scale=scale), reads=[ppb], writes=[tpb])
            P.op("pool", lambda h: h.tensor_tensor(
                out=tc_[:, :].rearrange("k (a q) -> k a q", a=4), in0=tc_[:, :].rearrange("k (a q) -> k a q", a=4),
                in1=mtri[:, :].rearrange("k (o q) -> k o q", o=1).broadcast_to([128, 4, 128]), op=ALU.mult),
                reads=[tcb, B_["mtri"]], writes=[tcb])
            P.op("dve", lambda h: h.tensor_tensor(
                out=tp_[:, :].rearrange("k (a q) -> k a q", a=4), in0=tp_[:, :].rearrange("k (a q) -> k a q", a=4),
                in1=mpv[:, :].rearrange("k (o q) -> k o q", o=1).broadcast_to([128, 4, 128]), op=ALU.mult),
                reads=[tpb, mpvb], writes=[tpb])
            po, pob = PS_O.next()
            pd, pdb = PS_D.next()
            P.op("pe", mm(po[0:64, :], vc[:, hk * 64:(hk + 1) * 64], tc_[:, :], True, False), reads=[vcb, tcb], writes=[pob])
            P.op("pe", mm(po[0:64, :], vp[:, hk * 64:(hk + 1) * 64], tp_[:, :], False, True), reads=[vpb, tpb], writes=[pob])
            P.op("pe", mm(pd[0:64, :], onesb_t[:, 0:64], tc_[:, :], True, False), reads=[ones_b, tcb], writes=[pdb])
            P.op("pe", mm(pd[0:64, :], onesb_t[:, 0:64], tp_[:, :], False, True), reads=[ones_b, tpb], writes=[pdb])
            sB[si] = (po, pob, pd, pdb)

        def stageB2(si):
            i, hk = sw_steps[si]
            par = i % 2
            ablk, abb = ab2[par], B_["ablk%d" % par]
            po, pob, pd, pdb = sB.pop(si)
            P.op("dve", lambda h: h.tensor_tensor(
                out=den_t[:, :].rearrange("d (a q) -> d a q", a=4), in0=pd[0:64, :].rearrange("d (a q) -> d a q", a=4),
                in1=sinke[:, hk * 4:(hk + 1) * 4].rearrange("d (a o) -> d a o", o=1).broadcast_to([64, 4, 128]), op=ALU.add),
                reads=[pdb, B_["sinke"]], writes=[B_["den"]])
            P.op("act", lambda h: h.activation(out=den_t[:, :], in_=den_t[:, :], func=AF.Ln), reads=[B_["den"]], writes=[B_["den"]])
            P.op("act", lambda h: h.activation(out=den_t[:, :], in_=den_t[:, :], func=AF.Exp, scale=-1.0),
                 reads=[B_["den"]], writes=[B_["den"]])
            P.op("dve", lambda h: h.tensor_tensor(
                out=ablk[:, hk * 4:(hk + 1) * 4, :], in0=po[0:64, :].rearrange("d (a q) -> d a q", a=4),
                in1=den_t[:, :].rearrange("d (a q) -> d a q", a=4), op=ALU.mult),
                reads=[pob, B_["den"]], writes=[abb])
            if hk == 7:
                P.dma("sp", lambda h: h.dma_start(out=att_v[:, :, i * 128:(i + 1) * 128], in_=ablk[:, :, :]),
                      reads=[abb], writes=[att_b])

        emit_block_loads(0)
        emit_block_select(0)
        stageA(0)
        for si in range(len(sw_steps)):
            bi, bh = sw_steps[si]
            if bh == 0 and bi + 1 < NBLK:
                emit_block_loads(bi + 1)
            if si + 1 < len(sw_steps):
                if sw_steps[si + 1][1] == 0:
                    emit_block_select(sw_steps[si + 1][0])
                stageA(si + 1)
            stageB(si)
            if si >= 1:
                stageB2(si - 1)
        stageB2(len(sw_steps) - 1)
        P.barrier()
        attn_out_phase(swa_w_out[0])
        P.barrier()

    for l in layers:
        compute_mod(l)
        kind = l % 3
        if do_mixer:
            if kind == 0:
                arena["off"] = phase_base
                fox_layer(l, l // 3)
            if kind == 2:
                arena["off"] = phase_base
                swa_layer(l)
            if kind == 1:
                arena["off"] = phase_base
                st = sgu_setup()
                for g in range(NG):
                    sgu_group(st, g)
                P.barrier()
        if do_ffn:
            P.barrier()
            ffn_big(l)
            P.barrier()

    for g in range(NG):
        load_xg(g)
        stage = yT_t[:, :, :].rearrange("p c t -> p (c t)").rearrange("p (b d) -> p b d", b=4)
        for b in range(4):
            for q in range(4):
                pst, psb = PY.next()
                for j in range(4):
                    c = q * 4 + j
                    P.op("pe", lambda h, pst=pst, b=b, c=c, j=j: h.transpose(
                        pst[:, j * 128:(j + 1) * 128], xg_t[:, c, b * 128:(b + 1) * 128], ident_t[:, :]),
                        reads=[xg_b, ident_b], writes=[psb])
                if q % 2 == 0:
                    P.op("act", lambda h, pst=pst, b=b, q=q: h.activation(out=stage[:, b, q * 512:(q + 1) * 512],
                                                                          in_=pst[:, :], func=AF.Copy),
                         reads=[psb], writes=[yT_b])
                else:
                    P.op("dve", lambda h, pst=pst, b=b, q=q: h.tensor_copy(out=stage[:, b, q * 512:(q + 1) * 512],
                                                                           in_=pst[:, :]), reads=[psb], writes=[yT_b])
        P.dma("sp", lambda h, g=g: h.dma_start(
            out=yout[g * TG:(g + 1) * TG, :].rearrange("(b p) d -> p b d", p=128), in_=stage),
            reads=[yT_b], writes=[yout_b])
    P.barrier()
    P.emit(nc)
    es.close()
    return nc


_TRI = np.tril(np.ones((128, 128), np.float32))


def _prep_inputs(inp, layers):
    x = np.asarray(inp["x"], np.float32)
    maps = []
    shared = {
        "ident": np.eye(128, dtype=np.float32),
        "trimask": _TRI,
        "ffn_w_gu": np.ascontiguousarray(np.asarray(inp["ffn_w_gu"], np.float32)[list(layers)]),
        "ffn_w_down": np.ascontiguousarray(np.asarray(inp["ffn_w_down"], np.float32)[list(layers)]),
        "sgu_w_in": np.ascontiguousarray(inp["sgu_w_in"], np.float32),
        "sgu_ln_g": np.ascontiguousarray(inp["sgu_ln_g"], np.float32),
        "sgu_ln_b": np.ascontiguousarray(inp["sgu_ln_b"], np.float32),
        "sgu_w_s": np.ascontiguousarray(inp["sgu_w_s"], np.float32),
        "sgu_b_s": np.ascontiguousarray(inp["sgu_b_s"], np.float32).reshape(1, 2048),
        "sgu_w_out": np.ascontiguousarray(inp["sgu_w_out"], np.float32),
    }
    for n in ("fox_w_in", "fox_b_f", "fox_w_out", "swa_w_in", "swa_sinks", "swa_w_out"):
        shared[n] = np.ascontiguousarray(inp[n], np.float32)
    shared["triu"] = np.ascontiguousarray(_TRI.T)
    shared["m_prev"] = np.ascontiguousarray(1.0 - _TRI.T)
    pm = np.zeros((64, 16), np.float32)
    for m_ in range(8):
        pm[m_ + 8, m_] = -1.0
        pm[m_, m_ + 8] = 1.0
    shared["pmat"] = pm
    inv = (500000.0 ** (-np.arange(0, 16, 2, dtype=np.float32) / np.float32(16))).astype(np.float32)
    shared["invf"] = np.concatenate([inv, inv]).reshape(16, 1).astype(np.float32)
    for n in ("mix_pre_g", "mix_post_g", "ffn_pre_g", "ffn_post_g"):
        shared[n] = np.ascontiguousarray(inp[n], np.float32).reshape(64, 128)
    for core in range(8):
        b, r = core // 4, core % 4
        xb = x[b].reshape(16, 4, 128, D)[:, r].reshape(TOK, D)
        m = dict(shared)
        m["xs"] = np.ascontiguousarray(xb)
        m["ada_w"] = np.ascontiguousarray(np.asarray(inp["ada_w"], np.float32)[list(layers)][:, :, r * 3072:(r + 1) * 3072])
        m["ada_b"] = np.ascontiguousarray(np.asarray(inp["ada_b"], np.float32)[list(layers)][:, r * 3072:(r + 1) * 3072])
        m["cvec"] = np.ascontiguousarray(inp["c"][b], np.float32).reshape(16, 128)
        pos = np.asarray(inp["positions"])[b].astype(np.int32)
        m["posin"] = np.ascontiguousarray(pos.reshape(16, 4, 128)[:, r].reshape(1, TOK))
        mf = np.zeros((128, 4, 128), np.float32)
        for j_ in range(4):
            if j_ < r:
                mf[:, j_, :] = 1.0
            elif j_ == r:
                mf[:, j_, :] = _TRI.T
        m["m_fox"] = mf.reshape(128, 512)
        m["m_prev0"] = np.zeros((128, 128), np.float32) if r == 0 else np.ascontiguousarray(1.0 - _TRI.T)
        oh = np.zeros((128, 4), np.float32)
        oh[:, r] = 1.0
        m["oh4"] = oh
        zo = np.zeros((128, 16, 64), np.float32)
        for li_ in range(16):
            zo[:, li_, 4 * li_ + r + 1:] = -30000.0
        m["zo"] = zo.reshape(128, 1024)
        s5 = np.zeros((128, 5), np.float32)
        s5[:, (r - 1) if r > 0 else 4] = 1.0
        m["sel5"] = s5
        maps.append(m)
    return maps


def run(inp, layers=(0, 1, 2, 3), **kw):
    nc = build_program(layers=layers, **kw)
    maps = _prep_inputs(inp, layers)
    res = run_bass_kernel_spmd(nc, maps, core_ids=list(range(8)))
    out = np.empty((2, SEQ, D), np.float32)
    for core in range(8):
        b, r = core // 4, core % 4
        out[b].reshape(16, 4, 128, D)[:, r] = res.results[core]["yout"].reshape(16, 128, D)
    return out


def kernel(**inputs):
    return run(inputs)
```

```python
import numpy as np
import ml_dtypes
from contextlib import ExitStack
import concourse.bass as bass
import concourse.mybir as mybir
from concourse.bass_utils import run_bass_kernel_spmd

F32 = mybir.dt.float32
BF16 = mybir.dt.bfloat16
I32 = mybir.dt.int32
AF = mybir.ActivationFunctionType
ALU = mybir.AluOpType

D = 2048
NCH = 16
SEQ = 8192
TOK = 2048
NBLK = 16
TG = 512
NG = TOK // TG
DFF = 5632
NFC = DFF // 128
EPS = 1e-6
FOX_IN = 6160
SWA_IN = 3072
ENGS = ("pe", "act", "dve", "pool", "sp")
BLOCKNAME = {"pe": "tensor", "act": "scalar", "dve": "vector", "pool": "gpsimd", "sp": "sync"}


class Buf:
    __slots__ = ("name", "w", "rs", "dtotal")

    def __init__(self, name):
        self.name = name
        self.w = None
        self.rs = {}
        self.dtotal = 0


class Plan:
    def __init__(self):
        self.recs = {e: [] for e in ENGS}
        self.seen = {e: {} for e in ENGS}
        self.dbufs = {}

    def _deps(self, eng, reads, writes, skipkey=None):
        need = {}
        seen = self.seen[eng]

        def add(tok):
            key, val = tok
            if key == ("E", "pe") and eng == "pe":
                return
            if key == skipkey:
                return
            if seen.get(key, -1) >= val:
                return
            if need.get(key, -1) < val:
                need[key] = val

        for b in reads:
            if b.w is not None:
                add(b.w)
        for b in writes:
            if b.w is not None:
                add(b.w)
            for k, v in b.rs.items():
                add((k, v))
        for k, v in need.items():
            seen[k] = v
            if k[0] == "E":
                self.recs[k[1]][v][3] = True
        return list(need.items())

    def op(self, eng, fn, reads=(), writes=()):
        waits = self._deps(eng, reads, writes)
        idx = len(self.recs[eng])
        self.recs[eng].append([waits, fn, None, False, 0])
        key = ("E", eng)
        for b in reads:
            if b.rs.get(key, -1) < idx:
                b.rs[key] = idx
        for b in writes:
            b.w = (key, idx)
            b.rs = {}

    def dma(self, eng, fn, reads=(), writes=(), dbuf=None, inc=16):
        if dbuf is None:
            dbuf = writes[0]
        waits = self._deps(eng, reads, writes, skipkey=("D", id(dbuf)))
        self.dbufs[id(dbuf)] = dbuf
        dbuf.dtotal += inc
        key = ("D", id(dbuf))
        val = dbuf.dtotal
        self.recs[eng].append([waits, fn, id(dbuf), False, inc])
        for b in reads:
            if b.rs.get(key, -1) < val:
                b.rs[key] = val
        for b in writes:
            b.w = (key, val)
            b.rs = {}

    def barrier(self, exclude=()):
        excl = set(id(b) for b in exclude)
        for e in ENGS:
            need = []
            seen = self.seen[e]
            for e2 in ENGS:
                if e2 == e or not self.recs[e2]:
                    continue
                idx = None
                for j in range(len(self.recs[e2]) - 1, -1, -1):
                    r = self.recs[e2][j]
                    if r[1] is not None and r[2] is None:
                        idx = j
                        break
                if idx is None:
                    continue
                key = ("E", e2)
                if seen.get(key, -1) < idx:
                    seen[key] = idx
                    self.recs[e2][idx][3] = True
                    need.append((key, idx))
            for bid, b in self.dbufs.items():
                key = ("D", bid)
                if bid in excl:
                    continue
                if b.dtotal > 0 and seen.get(key, -1) < b.dtotal:
                    seen[key] = b.dtotal
                    need.append((key, b.dtotal))
            if need:
                self.recs[e].append([need, None, None, False, 0])

    def emit(self, nc):
        vals = {}
        for e in ENGS:
            cnt = 0
            v = []
            for rec in self.recs[e]:
                if rec[3]:
                    cnt += 1
                v.append(cnt)
            vals[e] = v
            assert cnt < 60000, (e, cnt)
        with ExitStack() as es:
            esem = {e: es.enter_context(nc.semaphore("sem_" + e)) for e in ENGS}
            dsem = {}
            for n, bid in enumerate(self.dbufs):
                dsem[bid] = es.enter_context(nc.semaphore("dsem%d" % n))
            block = es.enter_context(nc.Block())
            for e in ENGS:
                def body(h, e=e):
                    for waits, fn, dma, flagged, inc in self.recs[e]:
                        for key, val in waits:
                            if key[0] == "E":
                                h.wait_ge(esem[key[1]], vals[key[1]][val])
                            else:
                                h.wait_ge(dsem[key[1]], val)
                        if fn is None:
                            continue
                        ins = fn(h)
                        if dma is not None:
                            ins.then_inc(dsem[dma], inc)
                        elif flagged:
                            ins.then_inc(esem[e], 1)
                getattr(block, BLOCKNAME[e])(body)


class Rot:
    def __init__(self, items):
        self.items = items
        self.i = 0

    def next(self):
        it = self.items[self.i % len(self.items)]
        self.i += 1
        return it


def build_program(layers=(0, 1, 2, 3), do_mixer=True, do_ffn=True):
    NL = len(layers)
    LI = {l: i for i, l in enumerate(layers)}
    nc = bass.Bass("TRN2", target_bir_lowering=False)
    P = Plan()

    def din(name, shape, dt=F32):
        return nc.dram_tensor(name, list(shape), dt, kind="ExternalInput").ap()

    xs = din("xs", [TOK, D])
    cvec = din("cvec", [16, 128])
    ident = din("ident", [128, 128])
    ada_w = din("ada_w", [NL, D, 3072])
    ada_b = din("ada_b", [NL, 3072])
    gains = [din(n, [64, 128]) for n in ("mix_pre_g", "mix_post_g", "ffn_pre_g", "ffn_post_g")]
    w_gu = din("ffn_w_gu", [NL, D, 2 * DFF])
    w_dn = din("ffn_w_down", [NL, DFF, D])
    sgu_w_in = din("sgu_w_in", [1, D, 2 * D])
    sgu_ln_g = din("sgu_ln_g", [1, D])
    sgu_ln_b = din("sgu_ln_b", [1, D])
    sgu_w_s = din("sgu_w_s", [1, 16, 128, 128])
    sgu_b_s = din("sgu_b_s", [1, 16 * 128])
    sgu_w_out = din("sgu_w_out", [1, D, D])
    trimask = din("trimask", [128, 128])
    fox_w_in = din("fox_w_in", [2, D, FOX_IN])
    fox_b_f = din("fox_b_f", [2, 16])
    fox_w_out = din("fox_w_out", [2, D, D])
    swa_w_in = din("swa_w_in", [1, D, SWA_IN])
    swa_sinks = din("swa_sinks", [1, 32])
    swa_w_out = din("swa_w_out", [1, D, D])
    posin = din("posin", [1, TOK], I32)
    triu = din("triu", [128, 128])
    m_prev_in = din("m_prev", [128, 128])
    m_fox_in = din("m_fox", [128, 4 * 128])
    m_prev0_in = din("m_prev0", [128, 128])
    zo_in = din("zo", [128, 1024])
    oh4_in = din("oh4", [128, 4])
    sel5_in = din("sel5", [128, 5])
    pmat_in = din("pmat", [64, 16])
    invf_in = din("invf", [16, 1])
    yout = nc.dram_tensor("yout", [TOK, D], F32, kind="ExternalOutput").ap()

    xT_s = nc.dram_tensor("xT_s", [D, TOK], F32).ap()
    xT_v = xT_s.rearrange("(c p) t -> p c t", p=128)
    xT_b = [Buf("xT_s%d" % g) for g in range(NG)]
    xTw_b = [Buf("xTw_s%d" % g) for g in range(NG)]
    yout_b = Buf("yout")
    modin = [nc.dram_tensor("modin%d" % i, [128, 24], F32) for i in range(4)]
    modout = [nc.dram_tensor("modout%d" % i, [4 * 128, 24], F32) for i in range(4)]
    modin_b, modout_b = Buf("modin"), Buf("modout")
    q_s = nc.dram_tensor("q_s", [D, TOK], BF16).ap()
    q_b = Buf("q_s")
    att_s = nc.dram_tensor("att_s", [D, TOK], BF16).ap()
    att_b = Buf("att_s")
    RG = [[0, 1, 2, 3], [4, 5, 6, 7]]
    fdr = {}
    fdr["kin"] = [nc.dram_tensor("fkin%d" % i, [256, TOK], BF16) for i in range(8)]
    fdr["kout"] = [nc.dram_tensor("fkout%d" % i, [4 * 256, TOK], BF16) for i in range(8)]
    fdr["vin"] = [nc.dram_tensor("fvin%d" % i, [128, 2 * 16 * 128], BF16) for i in range(8)]
    fdr["vout"] = [nc.dram_tensor("fvout%d" % i, [4 * 128, 2 * 16 * 128], BF16) for i in range(8)]
    fdr["lin"] = nc.dram_tensor("flin", [TOK, 16], F32)
    fdr["lout"] = nc.dram_tensor("flout", [4 * TOK, 16], F32)
    fdr["b"] = {k: Buf("f" + k) for k in ("kin", "vin", "lin", "lout")}
    fdr["b"]["kout"] = [Buf("fkout%d" % i) for i in range(8)]
    fdr["b"]["vout"] = [Buf("fvout%d" % i) for i in range(8)]
    swa_dr = {}
    swa_dr["kin"] = [nc.dram_tensor("skin%d" % i, [256, TOK], BF16) for i in range(2)]
    swa_dr["kout"] = [nc.dram_tensor("skout%d" % i, [4 * 256, TOK], BF16) for i in range(2)]
    swa_dr["vin"] = [nc.dram_tensor("svin%d" % i, [128, 8 * 512], BF16) for i in range(2)]
    swa_dr["vout"] = [nc.dram_tensor("svout%d" % i, [4 * 128, 8 * 512], BF16) for i in range(2)]
    swa_dr["b"] = {k: Buf("s" + k) for k in ("kin", "kout", "vin", "vout")}

    arena = {"off": 16640}

    def sb(name, shape, dt):
        nbytes = int(np.prod(shape[1:])) * (4 if dt in (F32, I32) else 2)
        off = (arena["off"] + 31) // 32 * 32
        arena["off"] = off + nbytes
        assert arena["off"] <= 229344, (name, arena["off"])
        t = nc.alloc_sbuf_tensor_at(name, list(shape), dt, offset=off)
        return t

    ident_t = sb("ident_t", [128, 128], F32)
    ident_b = Buf("ident")
    ones_t = sb("ones_t", [128, 128], F32)
    ones_b = Buf("ones")
    eps_t = sb("eps_t", [128, 1], F32)
    one11 = ones_t
    cact_t = sb("cact_t", [128, 16], BF16)
    cact_b = Buf("cact")
    gains_t = sb("gains_t", [128, 4, 64], F32)
    gains_b = Buf("gains")
    modT = sb("modT", [128, 96], F32)
    modp_t = sb("modp_t", [128, 24], F32)
    modp_b = Buf("modp")
    modT_b = Buf("modT")
    der_t = sb("der_t", [128, 4, 16], F32)
    der_b = Buf("der")
    onesb_t = sb("onesb_t", [128, 128], BF16)
    cst_t = sb("cst_t", [128, 4], F32)
    xg_off = (arena["off"] + 31) // 32 * 32
    xg_t = sb("xg_t", [128, NCH, TG], F32)
    xg_b = Buf("xg")
    hT_t = sb("hT_t", [128, NCH, TG], BF16)
    hT_b = Buf("hT")
    yT_t = sb("yT_t", [128, NCH, TG], F32)
    yT_b = Buf("yT")
    big_off = (arena["off"] + 31) // 32 * 32
    big_t = sb("big_t", [128, NFC, TG], BF16)
    big_b = Buf("big")
    wsl = []
    for i in range(2):
        t = sb("wslot%d" % i, [128, 8192], BF16)
        wsl.append((t, Buf("wslot%d" % i)))
    wrot = Rot(wsl)
    attn_lim = arena["off"]
    sqs = Rot([(sb("sq%d" % i, [128, TG], BF16), Buf("sq%d" % i)) for i in range(4)])
    tmps = Rot([(sb("tmp%d" % i, [128, TG], F32), Buf("tmp%d" % i)) for i in range(2)])
    rstd_t = sb("rstd_t", [128, TG], F32)
    rstd_b = Buf("rstd")
    rt_t = sb("rt_t", [128, TG], F32)
    rt_b = Buf("rt")
    row_t = sb("row_t", [1, 512], F32)
    row_b = Buf("row")
    brow_t = sb("brow_t", [1, 512], F32)
    brow_b = Buf("brow")
    small_t = sb("small_t", [128, 64], F32)
    small_b = Buf("small")
    phase_base = arena["off"]

    es = ExitStack()
    psum = []
    for i in range(8):
        t = es.enter_context(nc.psum_tensor("ps%d" % i, [128, 512], F32))
        psum.append((t, Buf("ps%d" % i)))
    PG = Rot([psum[0], psum[2]])
    PU = Rot([psum[1], psum[3]])
    PY = Rot([psum[4], psum[5]])
    PSSQ = psum[6]
    PM = psum[7]

    mm = lambda out, lhsT, rhs, st, sp: (lambda h: h.matmul(out, lhsT, rhs, start=st, stop=sp))

    wcache = {}
    wcache_b = Buf("wcache")

    def load_w_cols(W2d, col0, ncols, slot, slot_b, dst_col0=0, width=None, kch=NCH, ckey=None, g=0):
        width = width or ncols
        view = slot[:, 0:kch * width].rearrange("p (c n) -> p c n", n=width)
        dst = view[:, :, dst_col0:dst_col0 + ncols]
        if ckey is not None and g > 0:
            cap = wcache[ckey]
            P.dma("pool", lambda h: h.dma_start(out=dst, in_=cap.rearrange("p (c n) -> p c n", n=ncols)),
                  reads=[wcache_b], writes=[slot_b])
            return view
        src = W2d.rearrange("(c p) n -> p c n", p=128)[:, :, col0:col0 + ncols]
        P.dma("pool", lambda h: h.dma_start(out=dst, in_=src), writes=[slot_b])
        if ckey is not None:
            cap = nc.dram_tensor("wc_%d" % len(wcache), [128, kch * ncols], BF16).ap()
            wcache[ckey] = cap
            P.dma("sp", lambda h: h.dma_start(out=cap.rearrange("p (c n) -> p c n", n=ncols), in_=dst),
                  reads=[slot_b], writes=[wcache_b])
        return view

    def ssq_rstd(src_t, src_b, src_fn=None):
        pst, psb = PSSQ
        if src_fn is None:
            src_fn = lambda c: src_t[:, c, :]
        for c in range(NCH):
            sq, sqb = sqs.next()
            P.op("act", lambda h, sq=sq, c=c: h.activation(out=sq[:, :], in_=src_fn(c), func=AF.Square),
                 reads=[src_b], writes=[sqb])
            P.op("pe", mm(pst[:, :], onesb_t[:, :], sq[:, :], c == 0, c == NCH - 1),
                 reads=[ones_b, sqb], writes=[psb])
        P.op("act", lambda h: h.activation(out=rt_t[:, :], in_=pst[:, :], func=AF.Ln,
                                           bias=eps_t[:, 0:1], scale=1.0 / D),
             reads=[psb, ones_b], writes=[rt_b])
        P.op("act", lambda h: h.activation(out=rstd_t[:, :], in_=rt_t[:, :], func=AF.Exp, scale=-0.5),
             reads=[rt_b], writes=[rstd_b])

    def prenorm(acol, bcol):
        ssq_rstd(xg_t, xg_b)
        for c in range(NCH):
            tmp, tb = tmps.next()
            P.op("dve", lambda h, tmp=tmp, c=c: h.scalar_tensor_tensor(
                out=tmp[:, :], in0=xg_t[:, c, :], scalar=acol[:, c:c + 1], in1=rstd_t[:, :],
                op0=ALU.mult, op1=ALU.mult), reads=[xg_b, der_b, rstd_b], writes=[tb])
            P.op("act", lambda h, tmp=tmp, c=c: h.activation(
                out=hT_t[:, c, :], in_=tmp[:, :], func=AF.Identity, bias=bcol[:, c:c + 1], scale=1.0),
                reads=[tb, modT_b], writes=[hT_b])

    def postnorm_res(coef):
        ssq_rstd(yT_t, yT_b)
        for c in range(NCH):
            tmp, tb = tmps.next()
            P.op("dve", lambda h, tmp=tmp, c=c: h.scalar_tensor_tensor(
                out=tmp[:, :], in0=yT_t[:, c, :], scalar=coef[:, c:c + 1], in1=rstd_t[:, :],
                op0=ALU.mult, op1=ALU.mult), reads=[yT_b, der_b, rstd_b], writes=[tb])
            P.op("pool", lambda h, tmp=tmp, c=c: h.tensor_tensor(
                out=xg_t[:, c, :], in0=xg_t[:, c, :], in1=tmp[:, :], op=ALU.add),
                reads=[tb, xg_b], writes=[xg_b])

    def load_xg(g):
        P.dma("sp", lambda h: h.dma_start(out=xg_t[:, :, :], in_=xT_v[:, :, g * TG:(g + 1) * TG]),
              reads=[xT_b[g], xTw_b[g]], writes=[xg_b])

    def store_xg(g):
        P.dma("sp", lambda h: h.dma_start(out=xT_v[:, :, g * TG:(g + 1) * TG], in_=xg_t[:, :, :]),
              reads=[xg_b], writes=[xT_b[g]])

    def linear_fm(W2d, col0, n_oc, rhs_fn, rhs_bufs, kch, evac, ocw=128, per_load=4, ckey=None, g=0):
        oc = 0
        while oc < n_oc:
            nl = min(per_load, n_oc - oc)
            slot, slot_b = wrot.next()
            view = load_w_cols(W2d, col0 + oc * ocw, nl * ocw, slot, slot_b, kch=kch,
                               ckey=None if ckey is None else (ckey, oc), g=g)
            for j in range(nl):
                pst, psb = PY.next()
                for c in range(kch):
                    P.op("pe", mm(pst[0:ocw, :], view[:, c, j * ocw:(j + 1) * ocw], rhs_fn(c), c == 0, c == kch - 1),
                         reads=[slot_b] + rhs_bufs, writes=[psb])
                evac(oc + j, pst, psb)
            oc += nl

    P.dma("sp", lambda h: h.dma_start(out=ident_t[:, :], in_=ident[:, :]), writes=[ident_b])
    P.op("dve", lambda h: h.memset(ones_t[:, :], 1.0), writes=[ones_b])
    P.op("dve", lambda h: h.memset(eps_t[:, :], EPS), writes=[ones_b])
    P.op("dve", lambda h: h.memset(onesb_t[:, :], 1.0), writes=[ones_b])
    P.op("dve", lambda h: h.memset(cst_t[:, 0:1], -float(np.pi)), writes=[ones_b])
    for k in range(4):
        tmp, tb = tmps.next()
        P.dma("sp", lambda h, tmp=tmp, k=k: h.dma_start(out=tmp[0:64, 0:128], in_=gains[k][:, :]), writes=[tb])
        pst, psb = PM
        P.op("pe", lambda h, tmp=tmp: h.transpose(pst[:, 0:64], tmp[0:64, 0:128], ident_t[0:64, 0:64]),
             reads=[tb, ident_b], writes=[psb])
        P.op("dve", lambda h, k=k: h.tensor_copy(out=gains_t[:, k, :], in_=pst[:, 0:64]), reads=[psb], writes=[gains_b])
    tmp, tb = tmps.next()
    P.dma("sp", lambda h, tmp=tmp: h.dma_start(out=tmp[0:16, 0:128], in_=cvec[:, :]), writes=[tb])
    pst, psb = PM
    P.op("pe", lambda h, tmp=tmp: h.transpose(pst[:, 0:16], tmp[0:16, 0:128], ident_t[0:16, 0:16]),
         reads=[tb, ident_b], writes=[psb])
    P.op("act", lambda h: h.activation(out=cact_t[:, :], in_=pst[:, 0:16], func=AF.Silu), reads=[psb], writes=[cact_b])

    xblk = sb("xblk", [128, 4, D], F32) if False else None
    for g in range(NG):
        stage = yT_t[:, :, :].rearrange("p c t -> p (c t)").rearrange("p (b d) -> p b d", b=4)
        P.dma("sp", lambda h, g=g: h.dma_start(
            out=stage, in_=xs[g * TG:(g + 1) * TG, :].rearrange("(b p) d -> p b d", p=128)), writes=[yT_b])
        for c in range(NCH):
            pst, psb = PY.next()
            for b in range(4):
                P.op("pe", lambda h, pst=pst, b=b, c=c: h.transpose(
                    pst[:, b * 128:(b + 1) * 128], stage[:, b, c * 128:(c + 1) * 128], ident_t[:, :]),
                    reads=[yT_b, ident_b], writes=[psb])
            eng = "act" if c % 2 == 0 else "dve"
            if eng == "act":
                P.op("act", lambda h, pst=pst, c=c: h.activation(out=xg_t[:, c, :], in_=pst[:, :], func=AF.Copy),
                     reads=[psb], writes=[xg_b])
            else:
                P.op("dve", lambda h, pst=pst, c=c: h.tensor_copy(out=xg_t[:, c, :], in_=pst[:, :]),
                     reads=[psb], writes=[xg_b])
        store_xg(g)

    def compute_mod(l):
        for cg in range(6):
            slot, slot_b = wrot.next()
            view = load_w_cols(ada_w[LI[l]], cg * 512, 512, slot, slot_b)
            P.dma("sp", lambda h, cg=cg: h.dma_start(out=brow_t[0:1, :], in_=ada_b[LI[l]:LI[l] + 1, cg * 512:(cg + 1) * 512]),
                  writes=[brow_b])
            pst, psb = PY.next()
            for c in range(NCH):
                P.op("pe", mm(pst[0:1, :], cact_t[:, c:c + 1], view[:, c, :], c == 0, c == NCH - 1),
                     reads=[cact_b, slot_b], writes=[psb])
            P.op("dve", lambda h, pst=pst: h.tensor_tensor(out=row_t[0:1, :], in0=pst[0:1, :], in1=brow_t[0:1, :],
                                                           op=ALU.add), reads=[psb, brow_b], writes=[row_b])
            pm, pmb = PM
            for j in range(4):
                P.op("pe", mm(pm[:, j:j + 1], row_t[0:1, j * 128:(j + 1) * 128], one11[0:1, 0:1], True, True),
                     reads=[row_b, ones_b], writes=[pmb])
            P.op("dve", lambda h, cg=cg: h.tensor_copy(out=modp_t[:, cg * 4:(cg + 1) * 4], in_=pm[:, 0:4]),
                 reads=[pmb], writes=[modp_b])
        P.dma("sp", lambda h: h.dma_start(out=modin[l].ap(), in_=modp_t[:, :]), reads=[modp_b], writes=[modin_b])
        P.dma("pool", lambda h: h.collective_compute("AllGather", ALU.bypass, replica_groups=RG,
                                                     ins=[modin[l].ap().opt()], outs=[modout[l].ap().opt()]),
              reads=[modin_b], writes=[modout_b], inc=1)
        P.dma("sp", lambda h: h.dma_start(out=modT[:, :].rearrange("p (r c) -> p r c", r=4),
                                          in_=modout[l].ap().rearrange("(r p) c -> p r c", r=4)),
              reads=[modout_b], writes=[modT_b])
        for which, (sc_i, gate_i, pre_k, post_k) in enumerate(((1, 2, 0, 1), (4, 5, 2, 3))):
            P.op("dve", lambda h, sc_i=sc_i: h.tensor_scalar_add(out=small_t[:, 0:16], in0=modT[:, sc_i * 16:(sc_i + 1) * 16],
                                                                 scalar1=1.0), reads=[modT_b], writes=[small_b])
            P.op("dve", lambda h, which=which, pre_k=pre_k: h.tensor_tensor(
                out=der_t[:, 2 * which, :], in0=small_t[:, 0:16], in1=gains_t[:, pre_k, l * 16:(l + 1) * 16], op=ALU.mult),
                reads=[small_b, gains_b], writes=[der_b])
            P.op("dve", lambda h, which=which, gate_i=gate_i, post_k=post_k: h.tensor_tensor(
                out=der_t[:, 2 * which + 1, :], in0=modT[:, gate_i * 16:(gate_i + 1) * 16],
                in1=gains_t[:, post_k, l * 16:(l + 1) * 16], op=ALU.mult),
                reads=[modT_b, gains_b], writes=[der_b])

    def ffn_group(l, g):
        load_xg(g)
        prenorm(der_t[:, 2, :], modT[:, 48:64])
        for fc in range(NFC):
            slot, slot_b = wrot.next()
            view = load_w_cols(w_gu[LI[l]], fc * 128, 128, slot, slot_b, dst_col0=0, width=256)
            load_w_cols(w_gu[LI[l]], DFF + fc * 128, 128, slot, slot_b, dst_col0=128, width=256)
            pg, pgb = PG.next()
            pu, pub = PU.next()
            for c in range(NCH):
                P.op("pe", mm(pg[:, :], view[:, c, 0:128], hT_t[:, c, :], c == 0, c == NCH - 1),
                     reads=[slot_b, hT_b], writes=[pgb])
            for c in range(NCH):
                P.op("pe", mm(pu[:, :], view[:, c, 128:256], hT_t[:, c, :], c == 0, c == NCH - 1),
                     reads=[slot_b, hT_b], writes=[pub])
            tmp, tb = tmps.next()
            P.op("act", lambda h, pg=pg, tmp=tmp: h.activation(out=tmp[:, :], in_=pg[:, :], func=AF.Silu),
                 reads=[pgb], writes=[tb])
            P.op("dve", lambda h, pu=pu, tmp=tmp, fc=fc: h.tensor_tensor(
                out=big_t[:, fc, :], in0=tmp[:, :], in1=pu[:, :], op=ALU.mult), reads=[tb, pub], writes=[big_b])
        for dc in range(NCH):
            slot, slot_b = wrot.next()
            view = slot[:, 0:NFC * 128].rearrange("p (j o) -> p j o", o=128)
            src = w_dn[LI[l]].rearrange("(j p) o -> p j o", p=128)[:, :, dc * 128:(dc + 1) * 128]
            P.dma("pool", lambda h, view=view, src=src: h.dma_start(out=view, in_=src), writes=[slot_b])
            py, pyb = PY.next()
            for j in range(NFC):
                P.op("pe", mm(py[:, :], view[:, j, :], big_t[:, j, :], j == 0, j == NFC - 1),
                     reads=[slot_b, big_b], writes=[pyb])
            P.op("act", lambda h, py=py, dc=dc: h.activation(out=yT_t[:, dc, :], in_=py[:, :], func=AF.Copy),
                 reads=[pyb], writes=[yT_b])
        postnorm_res(der_t[:, 3, :])
        store_xg(g)

    TG2 = 1024
    assert attn_lim - xg_off >= 159744, (attn_lim, xg_off)
    XY = nc.alloc_sbuf_tensor_at("f_xy", [128, NCH, TG2], F32, offset=xg_off)
    H2 = nc.alloc_sbuf_tensor_at("f_h2", [128, NCH, TG2], BF16, offset=xg_off + 65536)
    A2 = nc.alloc_sbuf_tensor_at("f_a2", [128, 22, TG2], BF16, offset=xg_off + 98304)
    fws = [(nc.alloc_sbuf_tensor_at("f_w%d" % i, [128, 4096], BF16, offset=xg_off + 143360 + i * 8192), Buf("f_w%d" % i))
           for i in range(2)]
    fws += [(nc.alloc_sbuf_tensor_at("f_w%d" % (2 + i), [128, 4096], BF16, offset=phase_base + i * 8192), Buf("f_w%d" % (2 + i)))
            for i in range(2)]
    fwrot = Rot(fws)
    xins = Rot([(nc.alloc_sbuf_tensor_at("f_xin%d" % i, [128, 512], F32, offset=phase_base + 16384 + i * 2048), Buf("f_xin%d" % i))
                for i in range(4)])
    xouts = Rot([(nc.alloc_sbuf_tensor_at("f_xo%d" % i, [128, 512], F32, offset=phase_base + 24576 + i * 2048), Buf("f_xo%d" % i))
                 for i in range(4)])
    XY_b, H2_b, A2_b = Buf("f_xy"), Buf("f_h2"), Buf("f_a2")

    def ffn_big(l):
        acol, bcol, coef = der_t[:, 2, :], modT[:, 48:64], der_t[:, 3, :]
        Wgu = w_gu[LI[l]]
        Wdn = w_dn[LI[l]].rearrange("(j p) o -> p j o", p=128)
        for g2 in range(2):
            t0 = g2 * TG2
            gb = [2 * g2, 2 * g2 + 1]
            P.dma("sp", lambda h, t0=t0: h.dma_start(out=XY[:, :, :], in_=xT_v[:, :, t0:t0 + TG2]),
                  reads=[xT_b[gb[0]], xT_b[gb[1]], xTw_b[gb[0]], xTw_b[gb[1]]], writes=[XY_b])
            for th in range(2):
                hs = slice(th * 512, (th + 1) * 512)
                ssq_rstd(None, XY_b, src_fn=lambda c, hs=hs: XY[:, c, hs])
                for c in range(NCH):
                    tmp, tb = tmps.next()
                    P.op("dve", lambda h, tmp=tmp, c=c, hs=hs: h.scalar_tensor_tensor(
                        out=tmp[:, :], in0=XY[:, c, hs], scalar=acol[:, c:c + 1], in1=rstd_t[:, :],
                        op0=ALU.mult, op1=ALU.mult), reads=[XY_b, der_b, rstd_b], writes=[tb])
                    P.op("act", lambda h, tmp=tmp, c=c, hs=hs: h.activation(
                        out=H2[:, c, hs], in_=tmp[:, :], func=AF.Identity, bias=bcol[:, c:c + 1], scale=1.0),
                        reads=[tb, modT_b], writes=[H2_b])
            for fh in range(2):
                for fcl in range(22):
                    fc = fh * 22 + fcl
                    slot, slot_b = fwrot.next()
                    view = load_w_cols(Wgu, fc * 128, 128, slot, slot_b, dst_col0=0, width=256)
                    load_w_cols(Wgu, DFF + fc * 128, 128, slot, slot_b, dst_col0=128, width=256)
                    for th in range(2):
                        hs = slice(th * 512, (th + 1) * 512)
                        pg, pgb = PG.next()
                        pu, pub = PU.next()
                        for c in range(NCH):
                            P.op("pe", mm(pg[:, :], view[:, c, 0:128], H2[:, c, hs], c == 0, c == NCH - 1),
                                 reads=[slot_b, H2_b], writes=[pgb])
                        for c in range(NCH):
                            P.op("pe", mm(pu[:, :], view[:, c, 128:256], H2[:, c, hs], c == 0, c == NCH - 1),
                                 reads=[slot_b, H2_b], writes=[pub])
                        tmp, tb = tmps.next()
                        P.op("act", lambda h, pg=pg, tmp=tmp: h.activation(out=tmp[:, :], in_=pg[:, :], func=AF.Silu),
                             reads=[pgb], writes=[tb])
                        P.op("dve", lambda h, pu=pu, tmp=tmp, fcl=fcl, hs=hs: h.tensor_tensor(
                            out=A2[:, fcl, hs], in0=tmp[:, :], in1=pu[:, :], op=ALU.mult), reads=[tb, pub], writes=[A2_b])
                for dc in range(NCH):
                    slot, slot_b = fwrot.next()
                    view = slot[:, 0:22 * 128].rearrange("p (j o) -> p j o", o=128)
                    src = Wdn[:, fh * 22:(fh + 1) * 22, dc * 128:(dc + 1) * 128]
                    P.dma("pool", lambda h, view=view, src=src: h.dma_start(out=view, in_=src), writes=[slot_b])
                    for th in range(2):
                        hs = slice(th * 512, (th + 1) * 512)
                        py, pyb = PY.next()
                        for j in range(22):
                            P.op("pe", mm(py[:, :], view[:, j, :], A2[:, j, hs], j == 0, j == 21),
                                 reads=[slot_b, A2_b], writes=[pyb])
                        if fh == 0:
                            P.op("act", lambda h, py=py, dc=dc, hs=hs: h.activation(out=XY[:, dc, hs], in_=py[:, :], func=AF.Copy),
                                 reads=[pyb], writes=[XY_b])
                        else:
                            P.op("dve", lambda h, py=py, dc=dc, hs=hs: h.tensor_tensor(out=XY[:, dc, hs], in0=py[:, :],
                                                                                       in1=XY[:, dc, hs], op=ALU.add),
                                 reads=[pyb, XY_b], writes=[XY_b])
            for th in range(2):
                hs = slice(th * 512, (th + 1) * 512)
                gg = gb[th]
                ssq_rstd(None, XY_b, src_fn=lambda c, hs=hs: XY[:, c, hs])
                for c in range(NCH):
                    xin, xinb = xins.next()
                    xo, xob = xouts.next()
                    P.dma("sp", lambda h, xin=xin, c=c, gg=gg: h.dma_start(out=xin[:, :], in_=xT_v[:, c, gg * 512:(gg + 1) * 512]),
                          reads=[xT_b[gg]], writes=[xinb])
                    tmp, tb = tmps.next()
                    P.op("dve", lambda h, tmp=tmp, c=c, hs=hs: h.scalar_tensor_tensor(
                        out=tmp[:, :], in0=XY[:, c, hs], scalar=coef[:, c:c + 1], in1=rstd_t[:, :],
                        op0=ALU.mult, op1=ALU.mult), reads=[XY_b, der_b, rstd_b], writes=[tb])
                    P.op("pool", lambda h, tmp=tmp, xin=xin, xo=xo: h.tensor_tensor(
                        out=xo[:, :], in0=xin[:, :], in1=tmp[:, :], op=ALU.add), reads=[tb, xinb], writes=[xob])
                    P.dma("sp", lambda h, xo=xo, c=c, gg=gg: h.dma_start(out=xT_v[:, c, gg * 512:(gg + 1) * 512], in_=xo[:, :]),
                          reads=[xob], writes=[xTw_b[gg]])

    def sgu_setup():
        st = {}
        st["wsT"] = sb("sgu_wsT", [128, 16, 128], BF16)
        st["bs"] = sb("sgu_bs", [128, 16, 128], F32)
        st["lng"] = sb("sgu_lng", [128, D], F32)
        st["lnb"] = sb("sgu_lnb", [128, D], F32)
        st["tri"] = sb("sgu_tri", [128, 128], F32)
        st["stat"] = sb("sgu_stat", [128, 8], F32)
        st["vtm"] = nc.alloc_sbuf_tensor_at("sgu_vtm", [128, 4, D], BF16, offset=big_off + 16 * TG * 2)
        st["b"] = {k: Buf("sgu_" + k) for k in ("wsT", "bs", "lng", "lnb", "vtm", "tri", "stat")}
        b = st["b"]
        P.dma("sp", lambda h: h.dma_start(out=st["tri"][:, :], in_=trimask[:, :]), writes=[b["tri"]])
        P.dma("sp", lambda h: h.dma_start(out=st["bs"][:, :, :].rearrange("p g t -> p (g t)"),
                                          in_=sgu_b_s[0, :].partition_broadcast(128)), writes=[b["bs"]])
        P.dma("sp", lambda h: h.dma_start(out=st["lng"][:, :], in_=sgu_ln_g[0, :].partition_broadcast(128)),
              writes=[b["lng"]])
        P.dma("sp", lambda h: h.dma_start(out=st["lnb"][:, :], in_=sgu_ln_b[0, :].partition_broadcast(128)),
              writes=[b["lnb"]])
        for gi in range(16):
            tmp, tb = tmps.next()
            P.dma("sp", lambda h, tmp=tmp, gi=gi: h.dma_start(out=tmp[:, 0:128], in_=sgu_w_s[0, gi, :, :]), writes=[tb])
            P.op("dve", lambda h, tmp=tmp: h.tensor_tensor(out=tmp[:, 128:256], in0=tmp[:, 0:128], in1=st["tri"][:, :],
                                                           op=ALU.mult), reads=[tb, b["tri"]], writes=[tb])
            pst, psb = PM
            P.op("pe", lambda h, tmp=tmp, pst=pst: h.transpose(pst[:, 0:128], tmp[:, 128:256], ident_t[:, :]),
                 reads=[tb, ident_b], writes=[psb])
            P.op("dve", lambda h, gi=gi, pst=pst: h.tensor_copy(out=st["wsT"][:, gi, :], in_=pst[:, 0:128]),
                 reads=[psb], writes=[b["wsT"]])
        return st

    def sgu_group(st, g):
        b = st["b"]
        load_xg(g)
        prenorm(der_t[:, 0, :], modT[:, 0:16])
        W = sgu_w_in[0]
        def evac_u(oc, pst, psb):
            P.op("act", lambda h: h.activation(out=big_t[:, oc, :], in_=pst[:, :], func=AF.Gelu),
                 reads=[psb], writes=[big_b])
        linear_fm(W, 0, 16, lambda c: hT_t[:, c, :], [hT_b], NCH, evac_u, ckey="sgu", g=g)
        zv4 = yT_t[:, :, :].rearrange("p c t -> p (c t)").rearrange("p (b d) -> p b d", b=4)
        for cg in range(4):
            slot, slot_b = wrot.next()
            view = load_w_cols(W, D + cg * 512, 512, slot, slot_b, ckey=("sgv", cg), g=g)
            for blk in range(4):
                pst, psb = PY.next()
                for c in range(NCH):
                    P.op("pe", mm(pst[:, :], hT_t[:, c, blk * 128:(blk + 1) * 128], view[:, c, :], c == 0, c == NCH - 1),
                         reads=[hT_b, slot_b], writes=[psb])
                P.op("act", lambda h, pst=pst, cg=cg, blk=blk: h.activation(
                    out=zv4[:, blk, cg * 512:(cg + 1) * 512], in_=pst[:, :], func=AF.Gelu), reads=[psb], writes=[yT_b])
        stat = st["stat"]
        for blk in range(4):
            zv = zv4[:, blk, :]
            P.op("dve", lambda h, zv=zv: h.tensor_reduce(out=stat[:, 0:1], in_=zv, axis=mybir.AxisListType.X, op=ALU.add),
                 reads=[yT_b], writes=[b["stat"]])
            P.op("act", lambda h, zv=zv, blk=blk: h.activation(out=st["vtm"][:, blk, :], in_=zv, func=AF.Square,
                                                               accum_out=stat[:, 1:2]),
                 reads=[yT_b], writes=[b["stat"], b["vtm"]])
            P.op("dve", lambda h: h.tensor_scalar_mul(out=stat[:, 2:3], in0=stat[:, 0:1], scalar1=1.0 / D),
                 reads=[b["stat"]], writes=[b["stat"]])
            P.op("dve", lambda h: h.tensor_tensor(out=stat[:, 3:4], in0=stat[:, 2:3], in1=stat[:, 2:3], op=ALU.mult),
                 reads=[b["stat"]], writes=[b["stat"]])
            P.op("dve", lambda h: h.scalar_tensor_tensor(out=stat[:, 4:5], in0=stat[:, 1:2], scalar=1.0 / D,
                                                         in1=stat[:, 3:4], op0=ALU.mult, op1=ALU.subtract),
                 reads=[b["stat"]], writes=[b["stat"]])
            P.op("act", lambda h: h.activation(out=stat[:, 5:6], in_=stat[:, 4:5], func=AF.Sqrt, bias=eps_t[:, 0:1],
                                               scale=1.0), reads=[b["stat"], ones_b], writes=[b["stat"]])
            P.op("dve", lambda h: h.reciprocal(out=stat[:, 6:7], in_=stat[:, 5:6]), reads=[b["stat"]], writes=[b["stat"]])
            P.op("dve", lambda h, zv=zv: h.tensor_scalar(out=zv, in0=zv, scalar1=stat[:, 2:3], scalar2=stat[:, 6:7],
                                                         op0=ALU.subtract, op1=ALU.mult), reads=[b["stat"], yT_b], writes=[yT_b])
            P.op("pool", lambda h, zv=zv: h.tensor_tensor(out=zv, in0=zv, in1=st["lng"][:, :], op=ALU.mult),
                 reads=[yT_b, b["lng"]], writes=[yT_b])
            P.op("dve", lambda h, zv=zv, blk=blk: h.tensor_tensor(out=st["vtm"][:, blk, :], in0=zv, in1=st["lnb"][:, :],
                                                                  op=ALU.add), reads=[yT_b, b["lnb"]], writes=[b["vtm"]])
        for gi in range(16):
            pst, psb = PY.next()
            for blk in range(4):
                P.op("pe", mm(pst[:, blk * 128:(blk + 1) * 128], st["vtm"][:, blk, gi * 128:(gi + 1) * 128],
                              st["wsT"][:, gi, :], True, True), reads=[b["vtm"], b["wsT"]], writes=[psb])
            tmp, tb = tmps.next()
            P.op("dve", lambda h, pst=pst, tmp=tmp, gi=gi: h.tensor_tensor(
                out=tmp[:, :].rearrange("p (b t) -> p b t", b=4), in0=pst[:, :].rearrange("p (b t) -> p b t", b=4),
                in1=st["bs"][:, gi:gi + 1, :].broadcast_to([128, 4, 128]), op=ALU.add),
                reads=[psb, b["bs"]], writes=[tb])
            P.op("pool", lambda h, tmp=tmp, gi=gi: h.tensor_tensor(out=big_t[:, gi, :], in0=big_t[:, gi, :], in1=tmp[:, :],
                                                                   op=ALU.mult), reads=[tb, big_b], writes=[big_b])
        def evac_y(oc, pst, psb):
            P.op("act", lambda h: h.activation(out=yT_t[:, oc, :], in_=pst[:, :], func=AF.Copy),
                 reads=[psb], writes=[yT_b])
        linear_fm(sgu_w_out[0], 0, 16, lambda c: big_t[:, c, :], [big_b], NCH, evac_y, ckey="sgo", g=g)
        postnorm_res(der_t[:, 1, :])
        store_xg(g)

    def at_alloc(state, name, shape, dt):
        nbytes = int(np.prod(shape[1:])) * (4 if dt in (F32, I32) else 2)
        off = (state["off"] + 31) // 32 * 32
        state["off"] = off + nbytes
        assert state["off"] <= attn_lim, (name, state["off"], attn_lim)
        state["n"] += 1
        return nc.alloc_sbuf_tensor_at("%s_%d" % (name, state["n"]), list(shape), dt, offset=off)

    def evac_to(dst_fn, dst_buf, func=None, eng="act"):
        def ev(oc, pst, psb):
            P.op("act", lambda h: h.activation(out=dst_fn(oc), in_=pst[0:dst_fn(oc).shape[0], :], func=AF.Copy),
                 reads=[psb], writes=[dst_buf])
        return ev

    def attn_out_phase(W2d, ckey=None):
        for g in range(NG):
            P.dma("sp", lambda h, g=g: h.dma_start(
                out=big_t[:, 0:16, :], in_=att_s.rearrange("(c p) t -> p c t", p=128)[:, :, g * TG:(g + 1) * TG]),
                reads=[att_b], writes=[big_b])
            load_xg(g)
            linear_fm(W2d, 0, 16, lambda c: big_t[:, c, :], [big_b], NCH,
                      evac_to(lambda oc: yT_t[:, oc, :], yT_b), ckey=ckey, g=g)
            postnorm_res(der_t[:, 1, :])
            store_xg(g)

    def all_gather(src, src_b, dst, dst_b):
        P.dma("pool", lambda h: h.collective_compute("AllGather", ALU.bypass, replica_groups=RG,
                                                     ins=[src.ap().opt()], outs=[dst.ap().opt()]),
              reads=[src_b], writes=[dst_b], inc=1)

    fox_cache = {}

    def fox_layer(l, j):
        dr = fdr
        db = dr["b"]
        W = fox_w_in[j]
        kin_v = [t.ap().rearrange("(c p) t -> p c t", p=128) for t in dr["kin"]]
        vin_v = [t.ap().rearrange("p (h i d) -> p h i d", h=2, i=16) for t in dr["vin"]]
        lin_v = dr["lin"].ap().rearrange("(i p) h -> p i h", p=128)
        ph = {"off": phase_base, "n": 100 * l}
        bf_t = sb("fox_bf%d" % l, [128, 16], F32)
        vst_t = sb("fox_vst%d" % l, [128, 4, 512], BF16)
        lf_t = sb("fox_lf%d" % l, [128, 4, 16], F32)
        bf_b, vst_b, lf_b = fox_cache.setdefault("p1", (Buf("bf"), Buf("vst"), Buf("lf")))
        P.dma("sp", lambda h: h.dma_start(out=bf_t[:, :], in_=fox_b_f[j, :].partition_broadcast(128)), writes=[bf_b])
        for g in range(NG):
            load_xg(g)
            prenorm(der_t[:, 0, :], modT[:, 0:16])
            linear_fm(W, 0, 16, lambda c: hT_t[:, c, :], [hT_b], NCH, evac_to(lambda oc: big_t[:, oc, :], big_b),
                      ckey=("fq", j), g=g)
            P.dma("sp", lambda h, g=g: h.dma_start(
                out=q_s.rearrange("(c p) t -> p c t", p=128)[:, :, g * TG:(g + 1) * TG], in_=big_t[:, 0:16, :]),
                reads=[big_b], writes=[q_b])
            linear_fm(W, D, 16, lambda c: hT_t[:, c, :], [hT_b], NCH, evac_to(lambda oc: big_t[:, 16 + oc, :], big_b),
                      ckey=("fk", j), g=g)
            for ck in range(8):
                P.dma("sp", lambda h, g=g, ck=ck: h.dma_start(out=kin_v[ck][:, :, g * TG:(g + 1) * TG],
                                                               in_=big_t[:, 16 + 2 * ck:18 + 2 * ck, :]),
                      reads=[big_b], writes=[db["kin"]])
            for cg in range(4):
                slot, slot_b = wrot.next()
                view = load_w_cols(W, 2 * D + cg * 512, 512, slot, slot_b, ckey=("fv", j, cg), g=g)
                for blk in range(4):
                    pst, psb = PY.next()
                    for c in range(NCH):
                        P.op("pe", mm(pst[:, :], hT_t[:, c, blk * 128:(blk + 1) * 128], view[:, c, :], c == 0, c == NCH - 1),
                             reads=[hT_b, slot_b], writes=[psb])
                    P.op("act", lambda h, pst=pst, blk=blk: h.activation(out=vst_t[:, blk, :], in_=pst[:, :], func=AF.Copy),
                         reads=[psb], writes=[vst_b])
                for hh in range(4):
                    P.dma("sp", lambda h, g=g, cg=cg, hh=hh: h.dma_start(
                        out=vin_v[(cg * 4 + hh) // 2][:, (cg * 4 + hh) % 2, g * 4:(g + 1) * 4, :],
                        in_=vst_t[:, :, hh * 128:(hh + 1) * 128]), reads=[vst_b], writes=[db["vin"]])
            slot, slot_b = wrot.next()
            view = load_w_cols(W, 3 * D, 16, slot, slot_b, ckey=("ffg", j), g=g)
            for blk in range(4):
                pst, psb = PY.next()
                for c in range(NCH):
                    P.op("pe", mm(pst[:, 0:16], hT_t[:, c, blk * 128:(blk + 1) * 128], view[:, c, :], c == 0, c == NCH - 1),
                         reads=[hT_b, slot_b], writes=[psb])
                P.op("dve", lambda h, pst=pst, blk=blk: h.tensor_tensor(out=lf_t[:, blk, :], in0=pst[:, 0:16], in1=bf_t[:, :],
                                                                        op=ALU.add), reads=[psb, bf_b], writes=[lf_b])
            lf2 = lf_t[:, :, :].rearrange("p b h -> p (b h)")
            P.op("act", lambda h: h.activation(out=lf2, in_=lf2, func=AF.Exp, scale=-1.0), reads=[lf_b], writes=[lf_b])
            P.op("act", lambda h: h.activation(out=lf2, in_=lf2, func=AF.Ln, bias=1.0, scale=1.0), reads=[lf_b], writes=[lf_b])
            P.op("dve", lambda h: h.tensor_scalar_mul(out=lf2, in0=lf2, scalar1=-1.0), reads=[lf_b], writes=[lf_b])
            P.dma("sp", lambda h, g=g: h.dma_start(out=lin_v[:, g * 4:(g + 1) * 4, :], in_=lf_t[:, :, :]),
                  reads=[lf_b], writes=[db["lin"]])
        all_gather(dr["lin"], db["lin"], dr["lout"], db["lout"])
        for ck in range(8):
            all_gather(dr["kin"][ck], db["kin"], dr["kout"][ck], db["kout"][ck])
            all_gather(dr["vin"][ck], db["vin"], dr["vout"][ck], db["vout"][ck])
        P.barrier(exclude=db["kout"] + db["vout"])
        A = {"off": xg_off, "n": 1000 * (l + 1)}
        lfa = at_alloc(A, "lfa", [128, 64, 16], F32)
        ftm = at_alloc(A, "ftm", [128, 64, 16], F32)
        tbc = at_alloc(A, "tbc", [128, 64, 16], F32)
        pfa = at_alloc(A, "pfa", [128, 64, 16], F32)
        pfb = at_alloc(A, "pfb", [128, 64, 16], F32)
        fref = at_alloc(A, "fref", [128, 16, 16], F32)
        triu_t = at_alloc(A, "triu", [128, 128], F32)
        oh4_t = at_alloc(A, "oh4", [128, 4], F32)
        mfox_t = at_alloc(A, "mfox", [128, 4, 128], BF16)
        zo_t = at_alloc(A, "zo", [128, 16, 64], F32)
        qh = [at_alloc(A, "qh%d" % i, [128, TOK], BF16) for i in range(2)]
        kh = [at_alloc(A, "kh%d" % i, [128, 4, TOK], BF16) for i in range(2)]
        vh = [at_alloc(A, "vh%d" % i, [128, 4, 16, 128], BF16) for i in range(2)]
        bh = [at_alloc(A, "bh%d" % i, [128, 16, 64], F32) for i in range(2)]
        ah = [at_alloc(A, "ah%d" % i, [128, TOK], BF16) for i in range(2)]
        pts = Rot([(at_alloc(A, "pt%d" % i, [128, 512], BF16), [Buf("pt%d_%d" % (i, m)) for m in range(4)]) for i in range(4)])
        rec_t = at_alloc(A, "rec", [128, 512], F32)
        B_ = fox_cache.setdefault("B_", {k: Buf("fx_" + k) for k in (
            "lfa", "ftm", "tbc", "pfa", "pfb", "fref", "triu", "oh4", "mfox", "rec",
            "qh0", "qh1", "kh0", "kh1", "vh0", "vh1", "bh0", "bh1", "ah0", "ah1")})
        P.dma("sp", lambda h: h.dma_start(out=triu_t[:, :], in_=triu[:, :]), writes=[B_["triu"]])
        P.dma("sp", lambda h: h.dma_start(out=oh4_t[:, :], in_=oh4_in[:, :]), writes=[B_["oh4"]])
        P.dma("sp", lambda h: h.dma_start(out=zo_t[:, :, :].rearrange("p a b -> p (a b)"), in_=zo_in[:, :]), writes=[B_["oh4"]])
        P.dma("pool", lambda h: h.dma_start(out=mfox_t[:, :, :].rearrange("p j q -> p (j q)"), in_=m_fox_in[:, :]),
              writes=[B_["mfox"]])
        lout_ap = dr["lout"].ap()
        for rr in range(4):
            P.dma("sp", lambda h, rr=rr: h.dma_start(
                out=lfa[:, :, :].rearrange("p (i r) h -> p r i h", r=4)[:, rr],
                in_=lout_ap[rr * TOK:(rr + 1) * TOK, :].rearrange("(i p) h -> p i h", p=128)),
                reads=[db["lout"]], writes=[B_["lfa"]])
        lfa2 = lfa[:, :, :].rearrange("p g h -> p (g h)")
        ftm2 = ftm[:, :, :].rearrange("p g h -> p (g h)")
        tbc2 = tbc[:, :, :].rearrange("p g h -> p (g h)")
        pfa2 = pfa[:, :, :].rearrange("p g h -> p (g h)")
        pfb2 = pfb[:, :, :].rearrange("p g h -> p (g h)")
        for half in range(2):
            cs = slice(half * 512, (half + 1) * 512)
            pst, psb = PY.next()
            P.op("pe", mm(pst[:, :], triu_t[:, :], lfa2[:, cs], True, True), reads=[B_["triu"], B_["lfa"]], writes=[psb])
            P.op("dve", lambda h, pst=pst, cs=cs: h.tensor_copy(out=ftm2[:, cs], in_=pst[:, :]), reads=[psb], writes=[B_["ftm"]])
            pst, psb = PY.next()
            P.op("pe", mm(pst[:, :], ones_t[:, :], lfa2[:, cs], True, True), reads=[ones_b, B_["lfa"]], writes=[psb])
            P.op("dve", lambda h, pst=pst, cs=cs: h.tensor_copy(out=tbc2[:, cs], in_=pst[:, :]), reads=[psb], writes=[B_["tbc"]])
        P.op("dve", lambda h: h.tensor_copy(out=pfa2, in_=tbc2), reads=[B_["tbc"]], writes=[B_["pfa"]])
        cur, curb, oth, othb = pfa2, B_["pfa"], pfb2, B_["pfb"]
        sh = 1
        while sh < 64:
            w = sh * 16
            P.op("dve", lambda h, cur=cur, oth=oth, w=w: h.tensor_copy(out=oth[:, 0:w], in_=cur[:, 0:w]),
                 reads=[curb], writes=[othb])
            P.op("dve", lambda h, cur=cur, oth=oth, w=w: h.tensor_tensor(out=oth[:, w:1024], in0=cur[:, w:1024],
                                                                         in1=cur[:, 0:1024 - w], op=ALU.add),
                 reads=[curb], writes=[othb])
            cur, curb, oth, othb = oth, othb, cur, curb
            sh *= 2
        P.op("dve", lambda h, cur=cur, oth=oth: h.tensor_tensor(out=oth, in0=cur, in1=tbc2, op=ALU.subtract),
             reads=[curb, B_["tbc"]], writes=[othb])
        P.op("dve", lambda h, oth=oth: h.tensor_tensor(out=ftm2, in0=ftm2, in1=oth, op=ALU.add),
             reads=[othb, B_["ftm"]], writes=[B_["ftm"]])
        P.op("dve", lambda h, cur=cur, oth=oth: h.scalar_tensor_tensor(out=oth, in0=tbc2, scalar=-0.5, in1=cur,
                                                                        op0=ALU.mult, op1=ALU.add),
             reads=[curb, B_["tbc"]], writes=[othb])
        fmid4 = (pfa if oth is pfa2 else pfb)[:, :, :].rearrange("p (i r) h -> p r i h", r=4)
        P.op("dve", lambda h: h.tensor_scalar(out=fref[:, :, :], in0=fmid4[:, 0], scalar1=oh4_t[:, 0:1], scalar2=None,
                                              op0=ALU.mult), reads=[othb, B_["oh4"]], writes=[B_["fref"]])
        for rr in range(1, 4):
            P.op("dve", lambda h, rr=rr: h.scalar_tensor_tensor(out=fref[:, :, :], in0=fmid4[:, rr], scalar=oh4_t[:, rr:rr + 1],
                                                                in1=fref[:, :, :], op0=ALU.mult, op1=ALU.add),
                 reads=[othb, B_["oh4"], B_["fref"]], writes=[B_["fref"]])
        kout_v = [t.ap().rearrange("(r c d) t -> d r c t", r=4, d=128) for t in dr["kout"]]
        vout_v = [t.ap().rearrange("(r p) (h i d) -> p r h i d", r=4, h=2, i=16) for t in dr["vout"]]
        PS_ST = Rot([psum[0], psum[1], psum[2], psum[7]])
        PS_O = Rot([psum[3], psum[4]])
        PS_D = Rot([psum[5], psum[6]])
        scale = 128.0 ** -0.5
        def head_res(hd):
            s2 = hd % 2
            return (qh[s2], kh[s2], vh[s2], bh[s2], ah[s2],
                    B_["qh%d" % s2], B_["kh%d" % s2], B_["vh%d" % s2], B_["bh%d" % s2], B_["ah%d" % s2])

        def emit_head_loads(hd):
            qt, kt, vt, bt, at, qb, kb, vb, bb, ab = head_res(hd)
            P.dma("sp", lambda h: h.dma_start(out=qt[:, :], in_=q_s[hd * 128:(hd + 1) * 128, :]), reads=[q_b], writes=[qb])
            P.dma("sp", lambda h: h.dma_start(out=kt[:, :, :], in_=kout_v[hd // 2][:, :, hd % 2, :]),
                  reads=[db["kout"][hd // 2]], writes=[kb])
            P.dma("sp", lambda h: h.dma_start(out=vt[:, :, :, :], in_=vout_v[hd // 2][:, :, hd % 2]),
                  reads=[db["vout"][hd // 2]], writes=[vb])
            P.op("dve", lambda h: h.tensor_tensor(
                out=bt[:, :, :], in0=fref[:, :, hd:hd + 1].broadcast_to([128, 16, 64]),
                in1=ftm[:, :, hd:hd + 1].rearrange("p g o -> p o g").broadcast_to([128, 16, 64]), op=ALU.subtract),
                reads=[B_["fref"], B_["ftm"]], writes=[bb])
            P.op("dve", lambda h: h.tensor_tensor(out=bt[:, :, :], in0=bt[:, :, :], in1=zo_t[:, :, :], op=ALU.add),
                 reads=[bb, B_["oh4"]], writes=[bb])

        steps = [(hd, jq, gk) for hd in range(16) for jq in range(4) for gk in range(16 * (jq + 1))]
        nst = len(steps)
        qk = {}
        acc = {}

        def emit_qk(idx):
            hd, jq, gk = steps[idx]
            qt, kt, vt, bt, at, qb, kb, vb, bb, ab = head_res(hd)
            rr, ii = gk % 4, gk // 4
            mmin = max(0, -(-(gk - 3 - 16 * jq) // 4))
            c0 = mmin * 128
            pst, psb = PS_ST.next()
            P.op("pe", mm(pst[:, c0:512], kt[:, rr, ii * 128:(ii + 1) * 128], qt[:, jq * 512 + c0:(jq + 1) * 512],
                          True, True), reads=[kb, qb], writes=[psb])
            qk[idx] = (pst, psb, mmin, c0)

        def emit_exp(idx):
            hd, jq, gk = steps[idx]
            qt, kt, vt, bt, at, qb, kb, vb, bb, ab = head_res(hd)
            pst, psb, mmin, c0 = qk[idx]
            pt, ptb = pts.next()
            for m in range(mmin, 4):
                li = 4 * jq + m
                P.op("act", lambda h, m=m, li=li: h.activation(
                    out=pt[:, m * 128:(m + 1) * 128], in_=pst[:, m * 128:(m + 1) * 128], func=AF.Exp,
                    bias=bt[:, li, gk:gk + 1], scale=scale), reads=[psb, bb], writes=[ptb[m]])
                jm = gk - 4 * li
                if 0 <= jm <= 3:
                    P.op("pool", lambda h, m=m, jm=jm: h.tensor_tensor(
                        out=pt[:, m * 128:(m + 1) * 128], in0=pt[:, m * 128:(m + 1) * 128], in1=mfox_t[:, jm, :],
                        op=ALU.mult), reads=[ptb[m], B_["mfox"]], writes=[ptb[m]])
            qk[idx] = (pst, psb, mmin, c0, pt, ptb)

        def emit_pv(idx):
            hd, jq, gk = steps[idx]
            qt, kt, vt, bt, at, qb, kb, vb, bb, ab = head_res(hd)
            pst, psb, mmin, c0, pt, ptb = qk.pop(idx)
            rr, ii = gk % 4, gk // 4
            ng = 16 * (jq + 1)
            if gk == 0:
                acc[(hd, jq)] = (PS_O.next(), PS_D.next())
            (po, pob), (pd, pdb) = acc[(hd, jq)]
            P.op("pe", mm(po[:, c0:512], vt[:, rr, ii, :], pt[:, c0:512], gk == 0, gk == ng - 1),
                 reads=[vb] + ptb[mmin:], writes=[pob])
            P.op("pe", mm(pd[:, c0:512], onesb_t[:, :], pt[:, c0:512], gk == 0, gk == ng - 1),
                 reads=[ones_b] + ptb[mmin:], writes=[pdb])
            if gk == ng - 1:
                del acc[(hd, jq)]
                P.op("dve", lambda h: h.reciprocal(out=rec_t[:, :], in_=pd[:, :]), reads=[pdb], writes=[B_["rec"]])
                P.op("dve", lambda h: h.tensor_tensor(out=at[:, jq * 512:(jq + 1) * 512], in0=po[:, :],
                                                      in1=rec_t[:, :], op=ALU.mult),
                     reads=[pob, B_["rec"]], writes=[ab])
                if jq == 3:
                    P.dma("sp", lambda h: h.dma_start(out=att_s[hd * 128:(hd + 1) * 128, :], in_=at[:, :]),
                          reads=[ab], writes=[att_b])
                    if hd + 2 < 16:
                        emit_head_loads(hd + 2)

        emit_head_loads(0)
        emit_head_loads(1)
        emit_qk(0)
        emit_qk(1)
        emit_qk(2)
        for idx in range(nst):
            emit_exp(idx)
            if idx + 3 < nst:
                emit_qk(idx + 3)
            emit_pv(idx)
        P.barrier()
        attn_out_phase(fox_w_out[j], ckey=("fo", j))
        P.barrier()

    def swa_layer(l):
        dr = swa_dr
        db = dr["b"]
        W = swa_w_in[0]
        kin_v = [t.ap().rearrange("(h d) t -> d h t", d=64) for t in dr["kin"]]
        vin_v = [t.ap().rearrange("p (i f) -> p i f", i=8) for t in dr["vin"]]
        q_v = q_s.rearrange("(h d) t -> d h t", d=64)
        att_v = att_s.rearrange("(h d) t -> d h t", d=64)
        c16 = sb("swa_c16", [16, TOK], F32)
        s16 = sb("swa_s16", [16, TOK], F32)
        pmat_t = sb("swa_pmat", [64, 16], BF16)
        invf_t = sb("swa_invf", [16, 1], F32)
        vst_t = sb("swa_vst", [128, 4, 512], BF16)
        SB_ = {k: Buf("sw_" + k) for k in ("c16", "s16", "pmat", "invf", "vst", "posi")}
        posi_t = sb("swa_posi", [16, TOK], I32)
        P.dma("sp", lambda h: h.dma_start(out=posi_t[:, :], in_=posin[0, :].partition_broadcast(16)), writes=[SB_["posi"]])
        P.dma("pool", lambda h: h.dma_start(out=pmat_t[:, :], in_=pmat_in[:, :]), writes=[SB_["pmat"]])
        P.dma("sp", lambda h: h.dma_start(out=invf_t[:, :], in_=invf_in[:, :]), writes=[SB_["invf"]])
        PI = float(np.pi)
        C1 = 6.28125
        C2 = float(2 * np.pi - 6.28125)
        ang = yT_t[0:16, 8:12, :].rearrange("p c t -> p (c t)")
        nf = yT_t[0:16, 0:4, :].rearrange("p c t -> p (c t)")
        mk = yT_t[0:16, 4:8, :].rearrange("p c t -> p (c t)")
        ys, yc = s16[:, :], c16[:, :]
        RW = dict(reads=[SB_["posi"], SB_["invf"], yT_b, SB_["s16"], SB_["c16"]],
                  writes=[yT_b, SB_["s16"], SB_["c16"], SB_["posi"]])
        P.op("dve", lambda h: h.tensor_copy(out=ang, in_=posi_t[:, :]), **RW)
        P.op("dve", lambda h: h.tensor_scalar_mul(out=ang, in0=ang, scalar1=invf_t[:, 0:1]), **RW)
        P.op("dve", lambda h: h.tensor_scalar_mul(out=nf, in0=ang, scalar1=float(1.0 / (2 * np.pi))), **RW)
        P.op("dve", lambda h: h.tensor_copy(out=posi_t[:, :], in_=nf), **RW)
        P.op("dve", lambda h: h.tensor_copy(out=nf, in_=posi_t[:, :]), **RW)
        P.op("dve", lambda h: h.scalar_tensor_tensor(out=ys, in0=nf, scalar=-C1, in1=ang, op0=ALU.mult, op1=ALU.add), **RW)
        P.op("dve", lambda h: h.scalar_tensor_tensor(out=ys, in0=nf, scalar=-C2, in1=ys, op0=ALU.mult, op1=ALU.add), **RW)
        P.op("dve", lambda h: h.tensor_single_scalar(out=mk, in_=ys, scalar=PI, op=ALU.is_gt), **RW)
        P.op("dve", lambda h: h.scalar_tensor_tensor(out=ys, in0=mk, scalar=-2 * PI, in1=ys, op0=ALU.mult, op1=ALU.add), **RW)
        P.op("dve", lambda h: h.tensor_single_scalar(out=mk, in_=ys, scalar=-PI, op=ALU.is_lt), **RW)
        P.op("dve", lambda h: h.scalar_tensor_tensor(out=ys, in0=mk, scalar=2 * PI, in1=ys, op0=ALU.mult, op1=ALU.add), **RW)
        P.op("dve", lambda h: h.tensor_scalar_add(out=yc, in0=ys, scalar1=PI / 2), **RW)
        P.op("dve", lambda h: h.tensor_single_scalar(out=mk, in_=yc, scalar=PI, op=ALU.is_gt), **RW)
        P.op("dve", lambda h: h.scalar_tensor_tensor(out=yc, in0=mk, scalar=-2 * PI, in1=yc, op0=ALU.mult, op1=ALU.add), **RW)
        P.op("act", lambda h: h.activation(out=ys, in_=ys, func=AF.Sin), reads=[SB_["s16"]], writes=[SB_["s16"]])
        P.op("act", lambda h: h.activation(out=yc, in_=yc, func=AF.Sin), reads=[SB_["c16"]], writes=[SB_["c16"]])

        def rope(tile_fn, nheads, g):
            for hh in range(nheads):
                pst, psb = PM
                P.op("pe", mm(pst[0:16, :], pmat_t[:, :], tile_fn(hh), True, True), reads=[SB_["pmat"], big_b], writes=[psb])
                t1, t1b = tmps.next()
                t2, t2b = tmps.next()
                P.op("dve", lambda h, t1=t1, hh=hh: h.tensor_tensor(out=t1[0:16, :], in0=tile_fn(hh)[0:16, :],
                                                                    in1=c16[:, g * TG:(g + 1) * TG], op=ALU.mult),
                     reads=[big_b, SB_["c16"]], writes=[t1b])
                P.op("dve", lambda h, t2=t2, pst=pst: h.tensor_tensor(out=t2[0:16, :], in0=pst[0:16, :],
                                                                      in1=s16[:, g * TG:(g + 1) * TG], op=ALU.mult),
                     reads=[psb, SB_["s16"]], writes=[t2b])
                P.op("pool", lambda h, t1=t1, t2=t2, hh=hh: h.tensor_tensor(out=tile_fn(hh)[0:16, :], in0=t1[0:16, :],
                                                                            in1=t2[0:16, :], op=ALU.add),
                     reads=[t1b, t2b], writes=[big_b])

        for g in range(NG):
            load_xg(g)
            prenorm(der_t[:, 0, :], modT[:, 0:16])
            linear_fm(W, 0, 32, lambda c: hT_t[:, c, :], [hT_b], NCH, evac_to(lambda oc: big_t[0:64, oc, :], big_b),
                      ocw=64, per_load=8, ckey="swq", g=g)
            rope(lambda hh: big_t[0:64, hh, :], 32, g)
            P.dma("sp", lambda h, g=g: h.dma_start(out=q_v[:, :, g * TG:(g + 1) * TG], in_=big_t[0:64, 0:32, :]),
                  reads=[big_b], writes=[q_b])
            linear_fm(W, D, 8, lambda c: hT_t[:, c, :], [hT_b], NCH, evac_to(lambda oc: big_t[0:64, 32 + oc, :], big_b),
                      ocw=64, per_load=8, ckey="swk", g=g)
            rope(lambda hh: big_t[0:64, 32 + hh, :], 8, g)
            for ck in range(2):
                P.dma("sp", lambda h, g=g, ck=ck: h.dma_start(out=kin_v[ck][:, :, g * TG:(g + 1) * TG],
                                                               in_=big_t[0:64, 32 + 4 * ck:36 + 4 * ck, :]),
                      reads=[big_b], writes=[db["kin"]])
            slot, slot_b = wrot.next()
            view = load_w_cols(W, D + 512, 512, slot, slot_b, ckey="swv", g=g)
            for blk in range(4):
                pst, psb = PY.next()
                for c in range(NCH):
                    P.op("pe", mm(pst[:, :], hT_t[:, c, blk * 128:(blk + 1) * 128], view[:, c, :], c == 0, c == NCH - 1),
                         reads=[hT_b, slot_b], writes=[psb])
                P.op("act", lambda h, pst=pst, blk=blk: h.activation(out=vst_t[:, blk, :], in_=pst[:, :], func=AF.Copy),
                     reads=[psb], writes=[SB_["vst"]])
            P.dma("sp", lambda h, g=g: h.dma_start(out=vin_v[g // 2][:, (g % 2) * 4:(g % 2) * 4 + 4, :], in_=vst_t[:, :, :]),
                  reads=[SB_["vst"]], writes=[db["vin"]])
        for ck in range(2):
            all_gather(dr["kin"][ck], db["kin"], dr["kout"][ck], db["kout"])
            all_gather(dr["vin"][ck], db["vin"], dr["vout"][ck], db["vout"])
        P.barrier()
        A = {"off": xg_off, "n": 5000}
        kc2 = [at_alloc(A, "kc%d" % i, [64, 8, 128], BF16) for i in range(2)]
        vc2 = [at_alloc(A, "vc%d" % i, [128, 512], BF16) for i in range(2)]
        kp2_ = [at_alloc(A, "kp%d" % i, [64, 8, 128], BF16) for i in range(2)]
        vp2_ = [at_alloc(A, "vp%d" % i, [128, 512], BF16) for i in range(2)]
        qb2 = [at_alloc(A, "qblk%d" % i, [64, 32, 128], BF16) for i in range(2)]
        ab2 = [at_alloc(A, "ablk%d" % i, [64, 32, 128], BF16) for i in range(2)]
        kcand = at_alloc(A, "kcand", [64, 5, 8, 128], BF16)
        vcand = at_alloc(A, "vcand", [128, 5, 512], BF16)
        mtri = at_alloc(A, "mtri", [128, 128], BF16)
        mprev = at_alloc(A, "mprev", [128, 128], BF16)
        mprev0 = at_alloc(A, "mprev0", [128, 128], BF16)
        sel5 = at_alloc(A, "sel5", [128, 5], F32)
        sinke = at_alloc(A, "sinke", [64, 32], F32)
        den_t = at_alloc(A, "den", [64, 512], F32)
        ptc = Rot([(at_alloc(A, "ptc%d" % i, [128, 512], BF16), Buf("ptc%d" % i)) for i in range(2)])
        ptp = Rot([(at_alloc(A, "ptp%d" % i, [128, 512], BF16), Buf("ptp%d" % i)) for i in range(2)])
        B_ = {k: Buf("sw3_" + k) for k in ("kc0", "kc1", "vc0", "vc1", "kcand", "vcand", "kp0", "kp1", "vp0", "vp1",
                                           "qblk0", "qblk1", "ablk0", "ablk1", "mtri", "mprev",
                                           "mprev0", "sel5", "sinke", "den")}
        P.dma("pool", lambda h: h.dma_start(out=mtri[:, :], in_=triu[:, :]), writes=[B_["mtri"]])
        P.dma("pool", lambda h: h.dma_start(out=mprev[:, :], in_=m_prev_in[:, :]), writes=[B_["mprev"]])
        P.dma("pool", lambda h: h.dma_start(out=mprev0[:, :], in_=m_prev0_in[:, :]), writes=[B_["mprev0"]])
        P.dma("sp", lambda h: h.dma_start(out=sel5[:, :], in_=sel5_in[:, :]), writes=[B_["sel5"]])
        P.dma("sp", lambda h: h.dma_start(out=sinke[:, :], in_=swa_sinks[0, :].partition_broadcast(64)), writes=[B_["sinke"]])
        P.op("act", lambda h: h.activation(out=sinke[:, :], in_=sinke[:, :], func=AF.Exp), reads=[B_["sinke"]], writes=[B_["sinke"]])
        kout_v = [t.ap().rearrange("(r h d) t -> d r h t", r=4, d=64) for t in dr["kout"]]
        vout_v = [t.ap().rearrange("(r p) (i f) -> p r i f", r=4, i=8) for t in dr["vout"]]
        PS_C = Rot([psum[0], psum[1]])
        PS_P = Rot([psum[2], psum[3]])
        PS_O = Rot([psum[4], psum[5]])
        PS_D = Rot([psum[6], psum[7]])
        scale = 64.0 ** -0.5

        def emit_block_loads(i):
            par = i % 2
            kc, vc, kp, vp, qblk = kc2[par], vc2[par], kp2_[par], vp2_[par], qb2[par]
            kcb, vcb, kpb, vpb, qbb = (B_["kc%d" % par], B_["vc%d" % par], B_["kp%d" % par], B_["vp%d" % par],
                                       B_["qblk%d" % par])
            ip = max(i - 1, 0)
            for ck in range(2):
                P.dma("sp", lambda h, ck=ck: h.dma_start(out=kc[:, 4 * ck:4 * ck + 4, :],
                                                         in_=kin_v[ck][:, :, i * 128:(i + 1) * 128]),
                      reads=[db["kin"]], writes=[kcb])
                for rr in range(4):
                    P.dma("sp", lambda h, ck=ck, rr=rr: h.dma_start(
                        out=kcand[:, rr, 4 * ck:4 * ck + 4, :], in_=kout_v[ck][:, rr, :, i * 128:(i + 1) * 128]),
                        reads=[db["kout"]], writes=[B_["kcand"]])
                P.dma("sp", lambda h, ck=ck: h.dma_start(
                    out=kcand[:, 4, 4 * ck:4 * ck + 4, :], in_=kout_v[ck][:, 3, :, ip * 128:(ip + 1) * 128]),
                    reads=[db["kout"]], writes=[B_["kcand"]])
            P.dma("sp", lambda h: h.dma_start(out=vc[:, :], in_=vin_v[i // 8][:, i % 8, :]), reads=[db["vin"]], writes=[vcb])
            P.dma("sp", lambda h: h.dma_start(out=vcand[:, 0:4, :], in_=vout_v[i // 8][:, :, i % 8, :]),
                  reads=[db["vout"]], writes=[B_["vcand"]])
            P.dma("sp", lambda h: h.dma_start(out=vcand[:, 4, :], in_=vout_v[ip // 8][:, 3, ip % 8, :]),
                  reads=[db["vout"]], writes=[B_["vcand"]])
            P.dma("sp", lambda h: h.dma_start(out=qblk[:, :, :], in_=q_v[:, :, i * 128:(i + 1) * 128]),
                  reads=[q_b], writes=[qbb])

        def emit_block_select(i):
            par = i % 2
            kp, vp = kp2_[par], vp2_[par]
            kpb, vpb = B_["kp%d" % par], B_["vp%d" % par]
            kpf = kp[:, :, :].rearrange("d h t -> d (h t)")
            P.op("dve", lambda h: h.tensor_scalar(out=kpf, in0=kcand[:, 0, :, :].rearrange("d h t -> d (h t)"),
                                                  scalar1=sel5[0:64, 0:1], scalar2=None, op0=ALU.mult),
                 reads=[B_["kcand"], B_["sel5"]], writes=[kpb])
            P.op("dve", lambda h: h.tensor_scalar(out=vp[:, :], in0=vcand[:, 0, :], scalar1=sel5[:, 0:1], scalar2=None,
                                                  op0=ALU.mult), reads=[B_["vcand"], B_["sel5"]], writes=[vpb])
            for cnd in range(1, 5):
                P.op("dve", lambda h, cnd=cnd: h.scalar_tensor_tensor(
                    out=kpf, in0=kcand[:, cnd, :, :].rearrange("d h t -> d (h t)"), scalar=sel5[0:64, cnd:cnd + 1], in1=kpf,
                    op0=ALU.mult, op1=ALU.add), reads=[B_["kcand"], B_["sel5"], kpb], writes=[kpb])
                P.op("dve", lambda h, cnd=cnd: h.scalar_tensor_tensor(
                    out=vp[:, :], in0=vcand[:, cnd, :], scalar=sel5[:, cnd:cnd + 1], in1=vp[:, :],
                    op0=ALU.mult, op1=ALU.add), reads=[B_["vcand"], B_["sel5"], vpb], writes=[vpb])

        sw_steps = [(i, hk) for i in range(NBLK) for hk in range(8)]
        sA = {}
        sB = {}

        def stageA(si):
            i, hk = sw_steps[si]
            par = i % 2
            qsl = qb2[par][:, hk * 4:(hk + 1) * 4, :]
            pc, pcb = PS_C.next()
            pp, ppb = PS_P.next()
            P.op("pe", mm(pc[:, :], kc2[par][:, hk, :], qsl, True, True), reads=[B_["kc%d" % par], B_["qblk%d" % par]], writes=[pcb])
            P.op("pe", mm(pp[:, :], kp2_[par][:, hk, :], qsl, True, True), reads=[B_["kp%d" % par], B_["qblk%d" % par]], writes=[ppb])
            sA[si] = (pc, pcb, pp, ppb)

        def stageB(si):
            i, hk = sw_steps[si]
            par = i % 2
            vc, vp, ablk = vc2[par], vp2_[par], ab2[par]
            vcb, vpb, abb = B_["vc%d" % par], B_["vp%d" % par], B_["ablk%d" % par]
            pc, pcb, pp, ppb = sA.pop(si)
            mpv, mpvb = (mprev0, B_["mprev0"]) if i == 0 else (mprev, B_["mprev"])
            tc_, tcb = ptc.next()
            tp_, tpb = ptp.next()
            P.op("act", lambda h: h.activation(out=tc_[:, :], in_=pc[:, :], func=AF.Exp, scale=scale), reads=[pcb], writes=[tcb])
            P.op("act", lambda h: h.activation(out=tp_[:, :], in_=pp[:, :], func=AF.Exp, scale=scale), reads=[ppb], writes=[tpb])
            P.op("pool", lambda h: h.tensor_tensor(
                out=tc_[:, :].rearrange("k (a q) -> k a q", a=4), in0=tc_[:, :].rearrange("k (a q) -> k a q", a=4),
                in1=mtri[:, :].rearrange("k (o q) -> k o q", o=1).broadcast_to([128, 4, 128]), op=ALU.mult),
                reads=[tcb, B_["mtri"]], writes=[tcb])
            P.op("dve", lambda h: h.tensor_tensor(
                out=tp_[:, :].rearrange("k (a q) -> k a q", a=4), in0=tp_[:, :].rearrange("k (a q) -> k a q", a=4),
                in1=mpv[:, :].rearrange("k (o q) -> k o q", o=1).broadcast_to([128, 4, 128]), op=ALU.mult),
                reads=[tpb, mpvb], writes=[tpb])
            po, pob = PS_O.next()
            pd, pdb = PS_D.next()
            P.op("pe", mm(po[0:64, :], vc[:, hk * 64:(hk + 1) * 64], tc_[:, :], True, False), reads=[vcb, tcb], writes=[pob])
            P.op("pe", mm(po[0:64, :], vp[:, hk * 64:(hk + 1) * 64], tp_[:, :], False, True), reads=[vpb, tpb], writes=[pob])
            P.op("pe", mm(pd[0:64, :], onesb_t[:, 0:64], tc_[:, :], True, False), reads=[ones_b, tcb], writes=[pdb])
            P.op("pe", mm(pd[0:64, :], onesb_t[:, 0:64], tp_[:, :], False, True), reads=[ones_b, tpb], writes=[pdb])
            sB[si] = (po, pob, pd, pdb)

        def stageB2(si):
            i, hk = sw_steps[si]
            par = i % 2
            ablk, abb = ab2[par], B_["ablk%d" % par]
            po, pob, pd, pdb = sB.pop(si)
            P.op("dve", lambda h: h.tensor_tensor(
                out=den_t[:, :].rearrange("d (a q) -> d a q", a=4), in0=pd[0:64, :].rearrange("d (a q) -> d a q", a=4),
                in1=sinke[:, hk * 4:(hk + 1) * 4].rearrange("d (a o) -> d a o", o=1).broadcast_to([64, 4, 128]), op=ALU.add),
                reads=[pdb, B_["sinke"]], writes=[B_["den"]])
            P.op("act", lambda h: h.activation(out=den_t[:, :], in_=den_t[:, :], func=AF.Ln), reads=[B_["den"]], writes=[B_["den"]])
            P.op("act", lambda h: h.activation(out=den_t[:, :], in_=den_t[:, :], func=AF.Exp, scale=-1.0),
                 reads=[B_["den"]], writes=[B_["den"]])
            P.op("dve", lambda h: h.tensor_tensor(
                out=ablk[:, hk * 4:(hk + 1) * 4, :], in0=po[0:64, :].rearrange("d (a q) -> d a q", a=4),
                in1=den_t[:, :].rearrange("d (a q) -> d a q", a=4), op=ALU.mult),
                reads=[pob, B_["den"]], writes=[abb])
            if hk == 7:
                P.dma("sp", lambda h: h.dma_start(out=att_v[:, :, i * 128:(i + 1) * 128], in_=ablk[:, :, :]),
                      reads=[abb], writes=[att_b])

        emit_block_loads(0)
        emit_block_select(0)
        stageA(0)
        for si in range(len(sw_steps)):
            bi, bh = sw_steps[si]
            if bh == 0 and bi + 1 < NBLK:
                emit_block_loads(bi + 1)
            if si + 1 < len(sw_steps):
                if sw_steps[si + 1][1] == 0:
                    emit_block_select(sw_steps[si + 1][0])
                stageA(si + 1)
            stageB(si)
            if si >= 1:
                stageB2(si - 1)
        stageB2(len(sw_steps) - 1)
        P.barrier()
        attn_out_phase(swa_w_out[0], ckey="swo")
        P.barrier()

    for l in layers:
        compute_mod(l)
        kind = l % 3
        if do_mixer:
            if kind == 0:
                arena["off"] = phase_base
                fox_layer(l, l // 3)
            if kind == 2:
                arena["off"] = phase_base
                swa_layer(l)
            if kind == 1:
                arena["off"] = phase_base
                st = sgu_setup()
                for g in range(NG):
                    sgu_group(st, g)
                P.barrier()
        if do_ffn:
            P.barrier()
            ffn_big(l)
            P.barrier()

    for g in range(NG):
        load_xg(g)
        stage = yT_t[:, :, :].rearrange("p c t -> p (c t)").rearrange("p (b d) -> p b d", b=4)
        for b in range(4):
            for q in range(4):
                pst, psb = PY.next()
                for j in range(4):
                    c = q * 4 + j
                    P.op("pe", lambda h, pst=pst, b=b, c=c, j=j: h.transpose(
                        pst[:, j * 128:(j + 1) * 128], xg_t[:, c, b * 128:(b + 1) * 128], ident_t[:, :]),
                        reads=[xg_b, ident_b], writes=[psb])
                if q % 2 == 0:
                    P.op("act", lambda h, pst=pst, b=b, q=q: h.activation(out=stage[:, b, q * 512:(q + 1) * 512],
                                                                          in_=pst[:, :], func=AF.Copy),
                         reads=[psb], writes=[yT_b])
                else:
                    P.op("dve", lambda h, pst=pst, b=b, q=q: h.tensor_copy(out=stage[:, b, q * 512:(q + 1) * 512],
                                                                           in_=pst[:, :]), reads=[psb], writes=[yT_b])
        P.dma("sp", lambda h, g=g: h.dma_start(
            out=yout[g * TG:(g + 1) * TG, :].rearrange("(b p) d -> p b d", p=128), in_=stage),
            reads=[yT_b], writes=[yout_b])
    P.barrier()
    P.emit(nc)
    es.close()
    return nc


_TRI = np.tril(np.ones((128, 128), np.float32))


def _prep_inputs(inp, layers):
    x = np.asarray(inp["x"], np.float32)
    maps = []
    shared = {
        "ident": np.eye(128, dtype=np.float32),
        "trimask": _TRI,
        "ffn_w_gu": np.ascontiguousarray(np.asarray(inp["ffn_w_gu"], np.float32)[list(layers)]),
        "ffn_w_down": np.ascontiguousarray(np.asarray(inp["ffn_w_down"], np.float32)[list(layers)]),
        "sgu_w_in": np.ascontiguousarray(inp["sgu_w_in"], np.float32),
        "sgu_ln_g": np.ascontiguousarray(inp["sgu_ln_g"], np.float32),
        "sgu_ln_b": np.ascontiguousarray(inp["sgu_ln_b"], np.float32),
        "sgu_w_s": np.ascontiguousarray(inp["sgu_w_s"], np.float32),
        "sgu_b_s": np.ascontiguousarray(inp["sgu_b_s"], np.float32).reshape(1, 2048),
        "sgu_w_out": np.ascontiguousarray(inp["sgu_w_out"], np.float32),
    }
    for n in ("fox_w_in", "fox_b_f", "fox_w_out", "swa_w_in", "swa_sinks", "swa_w_out"):
        shared[n] = np.ascontiguousarray(inp[n], np.float32)
    shared["triu"] = np.ascontiguousarray(_TRI.T)
    shared["m_prev"] = np.ascontiguousarray(1.0 - _TRI.T)
    pm = np.zeros((64, 16), np.float32)
    for m_ in range(8):
        pm[m_ + 8, m_] = -1.0
        pm[m_, m_ + 8] = 1.0
    shared["pmat"] = pm
    inv = (500000.0 ** (-np.arange(0, 16, 2, dtype=np.float32) / np.float32(16))).astype(np.float32)
    shared["invf"] = np.concatenate([inv, inv]).reshape(16, 1).astype(np.float32)
    for n in ("mix_pre_g", "mix_post_g", "ffn_pre_g", "ffn_post_g"):
        shared[n] = np.ascontiguousarray(inp[n], np.float32).reshape(64, 128)
    for core in range(8):
        b, r = core // 4, core % 4
        xb = x[b].reshape(16, 4, 128, D)[:, r].reshape(TOK, D)
        m = dict(shared)
        m["xs"] = np.ascontiguousarray(xb)
        m["ada_w"] = np.ascontiguousarray(np.asarray(inp["ada_w"], np.float32)[list(layers)][:, :, r * 3072:(r + 1) * 3072])
        m["ada_b"] = np.ascontiguousarray(np.asarray(inp["ada_b"], np.float32)[list(layers)][:, r * 3072:(r + 1) * 3072])
        m["cvec"] = np.ascontiguousarray(inp["c"][b], np.float32).reshape(16, 128)
        pos = np.asarray(inp["positions"])[b].astype(np.int32)
        m["posin"] = np.ascontiguousarray(pos.reshape(16, 4, 128)[:, r].reshape(1, TOK))
        mf = np.zeros((128, 4, 128), np.float32)
        for j_ in range(4):
            if j_ < r:
                mf[:, j_, :] = 1.0
            elif j_ == r:
                mf[:, j_, :] = _TRI.T
        m["m_fox"] = mf.reshape(128, 512)
        m["m_prev0"] = np.zeros((128, 128), np.float32) if r == 0 else np.ascontiguousarray(1.0 - _TRI.T)
        oh = np.zeros((128, 4), np.float32)
        oh[:, r] = 1.0
        m["oh4"] = oh
        zo = np.zeros((128, 16, 64), np.float32)
        for li_ in range(16):
            zo[:, li_, 4 * li_ + r + 1:] = -30000.0
        m["zo"] = zo.reshape(128, 1024)
        s5 = np.zeros((128, 5), np.float32)
        s5[:, (r - 1) if r > 0 else 4] = 1.0
        m["sel5"] = s5
        maps.append(m)
    return maps


def run(inp, layers=(0, 1, 2, 3), **kw):
    nc = build_program(layers=layers, **kw)
    maps = _prep_inputs(inp, layers)
    res = run_bass_kernel_spmd(nc, maps, core_ids=list(range(8)))
    out = np.empty((2, SEQ, D), np.float32)
    for core in range(8):
        b, r = core // 4, core % 4
        out[b].reshape(16, 4, 128, D)[:, r] = res.results[core]["yout"].reshape(16, 128, D)
    return out


def kernel(**inputs):
    return run(inputs)
```

```python
import numpy as np
import ml_dtypes
from contextlib import ExitStack
import concourse.bass as bass
import concourse.mybir as mybir
from concourse.bass_utils import run_bass_kernel_spmd

F32 = mybir.dt.float32
BF16 = mybir.dt.bfloat16
I32 = mybir.dt.int32
AF = mybir.ActivationFunctionType
ALU = mybir.AluOpType

D = 2048
NCH = 16
SEQ = 8192
TOK = 2048
NBLK = 16
TG = 512
NG = TOK // TG
DFF = 5632
NFC = DFF // 128
EPS = 1e-6
FOX_IN = 6160
SWA_IN = 3072
ENGS = ("pe", "act", "dve", "pool", "sp")
BLOCKNAME = {"pe": "tensor", "act": "scalar", "dve": "vector", "pool": "gpsimd", "sp": "sync"}


class Buf:
    __slots__ = ("name", "w", "rs", "dtotal")

    def __init__(self, name):
        self.name = name
        self.w = None
        self.rs = {}
        self.dtotal = 0


class Plan:
    def __init__(self):
        self.recs = {e: [] for e in ENGS}
        self.seen = {e: {} for e in ENGS}
        self.dbufs = {}

    def _deps(self, eng, reads, writes, skipkey=None):
        need = {}
        seen = self.seen[eng]

        def add(tok):
            key, val = tok
            if key == ("E", "pe") and eng == "pe":
                return
            if key == skipkey:
                return
            if seen.get(key, -1) >= val:
                return
            if need.get(key, -1) < val:
                need[key] = val

        for b in reads:
            if b.w is not None:
                add(b.w)
        for b in writes:
            if b.w is not None:
                add(b.w)
            for k, v in b.rs.items():
                add((k, v))
        for k, v in need.items():
            seen[k] = v
            if k[0] == "E":
                self.recs[k[1]][v][3] = True
        return list(need.items())

    def op(self, eng, fn, reads=(), writes=()):
        waits = self._deps(eng, reads, writes)
        idx = len(self.recs[eng])
        self.recs[eng].append([waits, fn, None, False, 0])
        key = ("E", eng)
        for b in reads:
            if b.rs.get(key, -1) < idx:
                b.rs[key] = idx
        for b in writes:
            b.w = (key, idx)
            b.rs = {}

    def dma(self, eng, fn, reads=(), writes=(), dbuf=None, inc=16):
        if dbuf is None:
            dbuf = writes[0]
        waits = self._deps(eng, reads, writes, skipkey=("D", id(dbuf)))
        self.dbufs[id(dbuf)] = dbuf
        dbuf.dtotal += inc
        key = ("D", id(dbuf))
        val = dbuf.dtotal
        self.recs[eng].append([waits, fn, id(dbuf), False, inc])
        for b in reads:
            if b.rs.get(key, -1) < val:
                b.rs[key] = val
        for b in writes:
            b.w = (key, val)
            b.rs = {}

    def barrier(self, exclude=()):
        excl = set(id(b) for b in exclude)
        for e in ENGS:
            need = []
            seen = self.seen[e]
            for e2 in ENGS:
                if e2 == e or not self.recs[e2]:
                    continue
                idx = None
                for j in range(len(self.recs[e2]) - 1, -1, -1):
                    r = self.recs[e2][j]
                    if r[1] is not None and r[2] is None:
                        idx = j
                        break
                if idx is None:
                    continue
                key = ("E", e2)
                if seen.get(key, -1) < idx:
                    seen[key] = idx
                    self.recs[e2][idx][3] = True
                    need.append((key, idx))
            for bid, b in self.dbufs.items():
                key = ("D", bid)
                if bid in excl:
                    continue
                if b.dtotal > 0 and seen.get(key, -1) < b.dtotal:
                    seen[key] = b.dtotal
                    need.append((key, b.dtotal))
            if need:
                self.recs[e].append([need, None, None, False, 0])

    def emit(self, nc):
        vals = {}
        for e in ENGS:
            cnt = 0
            v = []
            for rec in self.recs[e]:
                if rec[3]:
                    cnt += 1
                v.append(cnt)
            vals[e] = v
            assert cnt < 60000, (e, cnt)
        with ExitStack() as es:
            esem = {e: es.enter_context(nc.semaphore("sem_" + e)) for e in ENGS}
            dsem = {}
            for n, bid in enumerate(self.dbufs):
                dsem[bid] = es.enter_context(nc.semaphore("dsem%d" % n))
            block = es.enter_context(nc.Block())
            for e in ENGS:
                def body(h, e=e):
                    for waits, fn, dma, flagged, inc in self.recs[e]:
                        for key, val in waits:
                            if key[0] == "E":
                                h.wait_ge(esem[key[1]], vals[key[1]][val])
                            else:
                                h.wait_ge(dsem[key[1]], val)
                        if fn is None:
                            continue
                        ins = fn(h)
                        if dma is not None:
                            ins.then_inc(dsem[dma], inc)
                        elif flagged:
                            ins.then_inc(esem[e], 1)
                getattr(block, BLOCKNAME[e])(body)


class Rot:
    def __init__(self, items):
        self.items = items
        self.i = 0

    def next(self):
        it = self.items[self.i % len(self.items)]
        self.i += 1
        return it


def build_program(layers=(0, 1, 2, 3), do_mixer=True, do_ffn=True):
    NL = len(layers)
    LI = {l: i for i, l in enumerate(layers)}
    nc = bass.Bass("TRN2", target_bir_lowering=False)
    P = Plan()

    def din(name, shape, dt=F32):
        return nc.dram_tensor(name, list(shape), dt, kind="ExternalInput").ap()

    xs = din("xs", [TOK, D])
    cvec = din("cvec", [16, 128])
    ident = din("ident", [128, 128])
    ada_w = din("ada_w", [NL, D, 3072])
    ada_b = din("ada_b", [NL, 3072])
    gains = [din(n, [64, 128]) for n in ("mix_pre_g", "mix_post_g", "ffn_pre_g", "ffn_post_g")]
    w_gu = din("ffn_w_gu", [NL, D, 2 * DFF])
    w_dn = din("ffn_w_down", [NL, DFF, D])
    sgu_w_in = din("sgu_w_in", [1, D, 2 * D])
    sgu_ln_g = din("sgu_ln_g", [1, D])
    sgu_ln_b = din("sgu_ln_b", [1, D])
    sgu_w_s = din("sgu_w_s", [1, 16, 128, 128])
    sgu_b_s = din("sgu_b_s", [1, 16 * 128])
    sgu_w_out = din("sgu_w_out", [1, D, D])
    trimask = din("trimask", [128, 128])
    fox_w_in = din("fox_w_in", [2, D, FOX_IN])
    fox_b_f = din("fox_b_f", [2, 16])
    fox_w_out = din("fox_w_out", [2, D, D])
    swa_w_in = din("swa_w_in", [1, D, SWA_IN])
    swa_sinks = din("swa_sinks", [1, 32])
    swa_w_out = din("swa_w_out", [1, D, D])
    posin = din("posin", [1, TOK], I32)
    triu = din("triu", [128, 128])
    m_prev_in = din("m_prev", [128, 128])
    m_fox_in = din("m_fox", [128, 4 * 128])
    m_prev0_in = din("m_prev0", [128, 128])
    zo_in = din("zo", [128, 1024])
    oh4_in = din("oh4", [128, 4])
    sel5_in = din("sel5", [128, 5])
    pmat_in = din("pmat", [64, 16])
    invf_in = din("invf", [16, 1])
    yout = nc.dram_tensor("yout", [TOK, D], F32, kind="ExternalOutput").ap()

    xT_s = nc.dram_tensor("xT_s", [D, TOK], F32).ap()
    xT_v = xT_s.rearrange("(c p) t -> p c t", p=128)
    xT_b = [Buf("xT_s%d" % g) for g in range(NG)]
    xTw_b = [Buf("xTw_s%d" % g) for g in range(NG)]
    yout_b = Buf("yout")
    modin = [nc.dram_tensor("modin%d" % i, [128, 24], F32) for i in range(4)]
    modout = [nc.dram_tensor("modout%d" % i, [4 * 128, 24], F32) for i in range(4)]
    modin_b, modout_b = Buf("modin"), Buf("modout")
    q_s = nc.dram_tensor("q_s", [D, TOK], BF16).ap()
    q_b = Buf("q_s")
    att_s = nc.dram_tensor("att_s", [D, TOK], BF16).ap()
    att_b = Buf("att_s")
    RG = [[0, 1, 2, 3], [4, 5, 6, 7]]
    fdr = {}
    fdr["kin"] = [nc.dram_tensor("fkin%d" % i, [256, TOK], BF16) for i in range(8)]
    fdr["kout"] = [nc.dram_tensor("fkout%d" % i, [4 * 256, TOK], BF16) for i in range(8)]
    fdr["vin"] = [nc.dram_tensor("fvin%d" % i, [128, 2 * 16 * 128], BF16) for i in range(8)]
    fdr["vout"] = [nc.dram_tensor("fvout%d" % i, [4 * 128, 2 * 16 * 128], BF16) for i in range(8)]
    fdr["lin"] = nc.dram_tensor("flin", [TOK, 16], F32)
    fdr["lout"] = nc.dram_tensor("flout", [4 * TOK, 16], F32)
    fdr["b"] = {k: Buf("f" + k) for k in ("kin", "vin", "lin", "lout")}
    fdr["b"]["kout"] = [Buf("fkout%d" % i) for i in range(8)]
    fdr["b"]["vout"] = [Buf("fvout%d" % i) for i in range(8)]
    swa_dr = {}
    swa_dr["kin"] = [nc.dram_tensor("skin%d" % i, [256, TOK], BF16) for i in range(2)]
    swa_dr["kout"] = [nc.dram_tensor("skout%d" % i, [4 * 256, TOK], BF16) for i in range(2)]
    swa_dr["vin"] = [nc.dram_tensor("svin%d" % i, [128, 8 * 512], BF16) for i in range(2)]
    swa_dr["vout"] = [nc.dram_tensor("svout%d" % i, [4 * 128, 8 * 512], BF16) for i in range(2)]
    swa_dr["b"] = {k: Buf("s" + k) for k in ("kin", "kout", "vin", "vout")}

    arena = {"off": 16640}

    def sb(name, shape, dt):
        nbytes = int(np.prod(shape[1:])) * (4 if dt in (F32, I32) else 2)
        off = (arena["off"] + 31) // 32 * 32
        arena["off"] = off + nbytes
        assert arena["off"] <= 229344, (name, arena["off"])
        t = nc.alloc_sbuf_tensor_at(name, list(shape), dt, offset=off)
        return t

    ident_t = sb("ident_t", [128, 128], F32)
    ident_b = Buf("ident")
    ones_t = sb("ones_t", [128, 128], F32)
    ones_b = Buf("ones")
    eps_t = sb("eps_t", [128, 1], F32)
    one11 = ones_t
    cact_t = sb("cact_t", [128, 16], BF16)
    cact_b = Buf("cact")
    gains_t = sb("gains_t", [128, 4, 64], F32)
    gains_b = Buf("gains")
    modT = sb("modT", [128, 96], F32)
    modp_t = sb("modp_t", [128, 24], F32)
    modp_b = Buf("modp")
    modT_b = Buf("modT")
    der_t = sb("der_t", [128, 4, 16], F32)
    der_b = Buf("der")
    onesb_t = sb("onesb_t", [128, 128], BF16)
    cst_t = sb("cst_t", [128, 4], F32)
    xg_off = (arena["off"] + 31) // 32 * 32
    xg_t = sb("xg_t", [128, NCH, TG], F32)
    xg_b = Buf("xg")
    hT_t = sb("hT_t", [128, NCH, TG], BF16)
    hT_b = Buf("hT")
    yT_t = sb("yT_t", [128, NCH, TG], F32)
    yT_b = Buf("yT")
    big_off = (arena["off"] + 31) // 32 * 32
    big_t = sb("big_t", [128, NFC, TG], BF16)
    big_b = Buf("big")
    wsl = []
    for i in range(2):
        t = sb("wslot%d" % i, [128, 8192], BF16)
        wsl.append((t, Buf("wslot%d" % i)))
    wrot = Rot(wsl)
    attn_lim = arena["off"]
    sqs = Rot([(sb("sq%d" % i, [128, TG], BF16), Buf("sq%d" % i)) for i in range(4)])
    tmps = Rot([(sb("tmp%d" % i, [128, TG], F32), Buf("tmp%d" % i)) for i in range(2)])
    rstd_t = sb("rstd_t", [128, TG], F32)
    rstd_b = Buf("rstd")
    rt_t = sb("rt_t", [128, TG], F32)
    rt_b = Buf("rt")
    row_t = sb("row_t", [1, 512], F32)
    row_b = Buf("row")
    brow_t = sb("brow_t", [1, 512], F32)
    brow_b = Buf("brow")
    small_t = sb("small_t", [128, 64], F32)
    small_b = Buf("small")
    phase_base = arena["off"]

    es = ExitStack()
    psum = []
    for i in range(8):
        t = es.enter_context(nc.psum_tensor("ps%d" % i, [128, 512], F32))
        psum.append((t, Buf("ps%d" % i)))
    PG = Rot([psum[0], psum[2]])
    PU = Rot([psum[1], psum[3]])
    PY = Rot([psum[4], psum[5]])
    PY6 = Rot([psum[0], psum[1], psum[2], psum[3], psum[4], psum[5]])
    PSSQ = psum[6]
    PM = psum[7]

    mm = lambda out, lhsT, rhs, st, sp: (lambda h: h.matmul(out, lhsT, rhs, start=st, stop=sp))

    wcache = {}
    wcache_b = Buf("wcache")

    def load_w_cols(W2d, col0, ncols, slot, slot_b, dst_col0=0, width=None, kch=NCH, ckey=None, g=0):
        width = width or ncols
        view = slot[:, 0:kch * width].rearrange("p (c n) -> p c n", n=width)
        dst = view[:, :, dst_col0:dst_col0 + ncols]
        if ckey is not None and g > 0:
            cap = wcache[ckey]
            P.dma("pool", lambda h: h.dma_start(out=dst, in_=cap.rearrange("p (c n) -> p c n", n=ncols)),
                  reads=[wcache_b], writes=[slot_b])
            return view
        src = W2d.rearrange("(c p) n -> p c n", p=128)[:, :, col0:col0 + ncols]
        P.dma("pool", lambda h: h.dma_start(out=dst, in_=src), writes=[slot_b])
        if ckey is not None:
            cap = nc.dram_tensor("wc_%d" % len(wcache), [128, kch * ncols], BF16).ap()
            wcache[ckey] = cap
            P.dma("sp", lambda h: h.dma_start(out=cap.rearrange("p (c n) -> p c n", n=ncols), in_=dst),
                  reads=[slot_b], writes=[wcache_b])
        return view

    def ssq_rstd(src_t, src_b, src_fn=None):
        pst, psb = PSSQ
        if src_fn is None:
            src_fn = lambda c: src_t[:, c, :]
        for c in range(NCH):
            sq, sqb = sqs.next()
            P.op("act", lambda h, sq=sq, c=c: h.activation(out=sq[:, :], in_=src_fn(c), func=AF.Square),
                 reads=[src_b], writes=[sqb])
            P.op("pe", mm(pst[:, :], onesb_t[:, :], sq[:, :], c == 0, c == NCH - 1),
                 reads=[ones_b, sqb], writes=[psb])
        P.op("act", lambda h: h.activation(out=rt_t[:, :], in_=pst[:, :], func=AF.Ln,
                                           bias=eps_t[:, 0:1], scale=1.0 / D),
             reads=[psb, ones_b], writes=[rt_b])
        P.op("act", lambda h: h.activation(out=rstd_t[:, :], in_=rt_t[:, :], func=AF.Exp, scale=-0.5),
             reads=[rt_b], writes=[rstd_b])

    def prenorm(acol, bcol):
        ssq_rstd(xg_t, xg_b)
        for c in range(NCH):
            tmp, tb = tmps.next()
            P.op("dve", lambda h, tmp=tmp, c=c: h.scalar_tensor_tensor(
                out=tmp[:, :], in0=xg_t[:, c, :], scalar=acol[:, c:c + 1], in1=rstd_t[:, :],
                op0=ALU.mult, op1=ALU.mult), reads=[xg_b, der_b, rstd_b], writes=[tb])
            P.op("act", lambda h, tmp=tmp, c=c: h.activation(
                out=hT_t[:, c, :], in_=tmp[:, :], func=AF.Identity, bias=bcol[:, c:c + 1], scale=1.0),
                reads=[tb, modT_b], writes=[hT_b])

    def postnorm_res(coef):
        ssq_rstd(yT_t, yT_b)
        for c in range(NCH):
            tmp, tb = tmps.next()
            P.op("dve", lambda h, tmp=tmp, c=c: h.scalar_tensor_tensor(
                out=tmp[:, :], in0=yT_t[:, c, :], scalar=coef[:, c:c + 1], in1=rstd_t[:, :],
                op0=ALU.mult, op1=ALU.mult), reads=[yT_b, der_b, rstd_b], writes=[tb])
            P.op("pool", lambda h, tmp=tmp, c=c: h.tensor_tensor(
                out=xg_t[:, c, :], in0=xg_t[:, c, :], in1=tmp[:, :], op=ALU.add),
                reads=[tb, xg_b], writes=[xg_b])

    def load_xg(g):
        P.dma("sp", lambda h: h.dma_start(out=xg_t[:, :, :], in_=xT_v[:, :, g * TG:(g + 1) * TG]),
              reads=[xT_b[g], xTw_b[g]], writes=[xg_b])

    def store_xg(g):
        P.dma("sp", lambda h: h.dma_start(out=xT_v[:, :, g * TG:(g + 1) * TG], in_=xg_t[:, :, :]),
              reads=[xg_b], writes=[xT_b[g]])

    def linear_fm(W2d, col0, n_oc, rhs_fn, rhs_bufs, kch, evac, ocw=128, per_load=4, ckey=None, g=0):
        oc = 0
        while oc < n_oc:
            nl = min(per_load, n_oc - oc)
            slot, slot_b = wrot.next()
            view = load_w_cols(W2d, col0 + oc * ocw, nl * ocw, slot, slot_b, kch=kch,
                               ckey=None if ckey is None else (ckey, oc), g=g)
            for j in range(nl):
                pst, psb = PY6.next()
                for c in range(kch):
                    P.op("pe", mm(pst[0:ocw, :], view[:, c, j * ocw:(j + 1) * ocw], rhs_fn(c), c == 0, c == kch - 1),
                         reads=[slot_b] + rhs_bufs, writes=[psb])
                evac(oc + j, pst, psb)
            oc += nl

    P.dma("sp", lambda h: h.dma_start(out=ident_t[:, :], in_=ident[:, :]), writes=[ident_b])
    P.op("dve", lambda h: h.memset(ones_t[:, :], 1.0), writes=[ones_b])
    P.op("dve", lambda h: h.memset(eps_t[:, :], EPS), writes=[ones_b])
    P.op("dve", lambda h: h.memset(onesb_t[:, :], 1.0), writes=[ones_b])
    P.op("dve", lambda h: h.memset(cst_t[:, 0:1], -float(np.pi)), writes=[ones_b])
    for k in range(4):
        tmp, tb = tmps.next()
        P.dma("sp", lambda h, tmp=tmp, k=k: h.dma_start(out=tmp[0:64, 0:128], in_=gains[k][:, :]), writes=[tb])
        pst, psb = PM
        P.op("pe", lambda h, tmp=tmp: h.transpose(pst[:, 0:64], tmp[0:64, 0:128], ident_t[0:64, 0:64]),
             reads=[tb, ident_b], writes=[psb])
        P.op("dve", lambda h, k=k: h.tensor_copy(out=gains_t[:, k, :], in_=pst[:, 0:64]), reads=[psb], writes=[gains_b])
    tmp, tb = tmps.next()
    P.dma("sp", lambda h, tmp=tmp: h.dma_start(out=tmp[0:16, 0:128], in_=cvec[:, :]), writes=[tb])
    pst, psb = PM
    P.op("pe", lambda h, tmp=tmp: h.transpose(pst[:, 0:16], tmp[0:16, 0:128], ident_t[0:16, 0:16]),
         reads=[tb, ident_b], writes=[psb])
    P.op("act", lambda h: h.activation(out=cact_t[:, :], in_=pst[:, 0:16], func=AF.Silu), reads=[psb], writes=[cact_b])

    xblk = sb("xblk", [128, 4, D], F32) if False else None
    for g in range(NG):
        stage = yT_t[:, :, :].rearrange("p c t -> p (c t)").rearrange("p (b d) -> p b d", b=4)
        P.dma("sp", lambda h, g=g: h.dma_start(
            out=stage, in_=xs[g * TG:(g + 1) * TG, :].rearrange("(b p) d -> p b d", p=128)), writes=[yT_b])
        for c in range(NCH):
            pst, psb = PY.next()
            for b in range(4):
                P.op("pe", lambda h, pst=pst, b=b, c=c: h.transpose(
                    pst[:, b * 128:(b + 1) * 128], stage[:, b, c * 128:(c + 1) * 128], ident_t[:, :]),
                    reads=[yT_b, ident_b], writes=[psb])
            eng = "act" if c % 2 == 0 else "dve"
            if eng == "act":
                P.op("act", lambda h, pst=pst, c=c: h.activation(out=xg_t[:, c, :], in_=pst[:, :], func=AF.Copy),
                     reads=[psb], writes=[xg_b])
            else:
                P.op("dve", lambda h, pst=pst, c=c: h.tensor_copy(out=xg_t[:, c, :], in_=pst[:, :]),
                     reads=[psb], writes=[xg_b])
        store_xg(g)

    def compute_mod(l):
        for cg in range(6):
            slot, slot_b = wrot.next()
            view = load_w_cols(ada_w[LI[l]], cg * 512, 512, slot, slot_b)
            P.dma("sp", lambda h, cg=cg: h.dma_start(out=brow_t[0:1, :], in_=ada_b[LI[l]:LI[l] + 1, cg * 512:(cg + 1) * 512]),
                  writes=[brow_b])
            pst, psb = PY.next()
            for c in range(NCH):
                P.op("pe", mm(pst[0:1, :], cact_t[:, c:c + 1], view[:, c, :], c == 0, c == NCH - 1),
                     reads=[cact_b, slot_b], writes=[psb])
            P.op("dve", lambda h, pst=pst: h.tensor_tensor(out=row_t[0:1, :], in0=pst[0:1, :], in1=brow_t[0:1, :],
                                                           op=ALU.add), reads=[psb, brow_b], writes=[row_b])
            pm, pmb = PM
            for j in range(4):
                P.op("pe", mm(pm[:, j:j + 1], row_t[0:1, j * 128:(j + 1) * 128], one11[0:1, 0:1], True, True),
                     reads=[row_b, ones_b], writes=[pmb])
            P.op("dve", lambda h, cg=cg: h.tensor_copy(out=modp_t[:, cg * 4:(cg + 1) * 4], in_=pm[:, 0:4]),
                 reads=[pmb], writes=[modp_b])
        P.dma("sp", lambda h: h.dma_start(out=modin[l].ap(), in_=modp_t[:, :]), reads=[modp_b], writes=[modin_b])
        P.dma("pool", lambda h: h.collective_compute("AllGather", ALU.bypass, replica_groups=RG,
                                                     ins=[modin[l].ap().opt()], outs=[modout[l].ap().opt()]),
              reads=[modin_b], writes=[modout_b], inc=1)
        P.dma("sp", lambda h: h.dma_start(out=modT[:, :].rearrange("p (r c) -> p r c", r=4),
                                          in_=modout[l].ap().rearrange("(r p) c -> p r c", r=4)),
              reads=[modout_b], writes=[modT_b])
        for which, (sc_i, gate_i, pre_k, post_k) in enumerate(((1, 2, 0, 1), (4, 5, 2, 3))):
            P.op("dve", lambda h, sc_i=sc_i: h.tensor_scalar_add(out=small_t[:, 0:16], in0=modT[:, sc_i * 16:(sc_i + 1) * 16],
                                                                 scalar1=1.0), reads=[modT_b], writes=[small_b])
            P.op("dve", lambda h, which=which, pre_k=pre_k: h.tensor_tensor(
                out=der_t[:, 2 * which, :], in0=small_t[:, 0:16], in1=gains_t[:, pre_k, l * 16:(l + 1) * 16], op=ALU.mult),
                reads=[small_b, gains_b], writes=[der_b])
            P.op("dve", lambda h, which=which, gate_i=gate_i, post_k=post_k: h.tensor_tensor(
                out=der_t[:, 2 * which + 1, :], in0=modT[:, gate_i * 16:(gate_i + 1) * 16],
                in1=gains_t[:, post_k, l * 16:(l + 1) * 16], op=ALU.mult),
                reads=[modT_b, gains_b], writes=[der_b])

    def ffn_group(l, g):
        load_xg(g)
        prenorm(der_t[:, 2, :], modT[:, 48:64])
        for fc in range(NFC):
            slot, slot_b = wrot.next()
            view = load_w_cols(w_gu[LI[l]], fc * 128, 128, slot, slot_b, dst_col0=0, width=256)
            load_w_cols(w_gu[LI[l]], DFF + fc * 128, 128, slot, slot_b, dst_col0=128, width=256)
            pg, pgb = PG.next()
            pu, pub = PU.next()
            for c in range(NCH):
                P.op("pe", mm(pg[:, :], view[:, c, 0:128], hT_t[:, c, :], c == 0, c == NCH - 1),
                     reads=[slot_b, hT_b], writes=[pgb])
            for c in range(NCH):
                P.op("pe", mm(pu[:, :], view[:, c, 128:256], hT_t[:, c, :], c == 0, c == NCH - 1),
                     reads=[slot_b, hT_b], writes=[pub])
            tmp, tb = tmps.next()
            P.op("act", lambda h, pg=pg, tmp=tmp: h.activation(out=tmp[:, :], in_=pg[:, :], func=AF.Silu),
                 reads=[pgb], writes=[tb])
            P.op("dve", lambda h, pu=pu, tmp=tmp, fc=fc: h.tensor_tensor(
                out=big_t[:, fc, :], in0=tmp[:, :], in1=pu[:, :], op=ALU.mult), reads=[tb, pub], writes=[big_b])
        for dc in range(NCH):
            slot, slot_b = wrot.next()
            view = slot[:, 0:NFC * 128].rearrange("p (j o) -> p j o", o=128)
            src = w_dn[LI[l]].rearrange("(j p) o -> p j o", p=128)[:, :, dc * 128:(dc + 1) * 128]
            P.dma("pool", lambda h, view=view, src=src: h.dma_start(out=view, in_=src), writes=[slot_b])
            py, pyb = PY.next()
            for j in range(NFC):
                P.op("pe", mm(py[:, :], view[:, j, :], big_t[:, j, :], j == 0, j == NFC - 1),
                     reads=[slot_b, big_b], writes=[pyb])
            P.op("act", lambda h, py=py, dc=dc: h.activation(out=yT_t[:, dc, :], in_=py[:, :], func=AF.Copy),
                 reads=[pyb], writes=[yT_b])
        postnorm_res(der_t[:, 3, :])
        store_xg(g)

    TG2 = 1024
    assert attn_lim - xg_off >= 159744, (attn_lim, xg_off)
    XY = nc.alloc_sbuf_tensor_at("f_xy", [128, NCH, TG2], F32, offset=xg_off)
    H2 = nc.alloc_sbuf_tensor_at("f_h2", [128, NCH, TG2], BF16, offset=xg_off + 65536)
    A2 = nc.alloc_sbuf_tensor_at("f_a2", [128, 22, TG2], BF16, offset=xg_off + 98304)
    fws = [(nc.alloc_sbuf_tensor_at("f_w%d" % i, [128, 4096], BF16, offset=xg_off + 143360 + i * 8192), Buf("f_w%d" % i))
           for i in range(2)]
    fws += [(nc.alloc_sbuf_tensor_at("f_w%d" % (2 + i), [128, 4096], BF16, offset=phase_base + i * 8192), Buf("f_w%d" % (2 + i)))
            for i in range(2)]
    fwrot = Rot(fws)
    xins = Rot([(nc.alloc_sbuf_tensor_at("f_xin%d" % i, [128, 512], F32, offset=phase_base + 16384 + i * 2048), Buf("f_xin%d" % i))
                for i in range(4)])
    xouts = Rot([(nc.alloc_sbuf_tensor_at("f_xo%d" % i, [128, 512], F32, offset=phase_base + 24576 + i * 2048), Buf("f_xo%d" % i))
                 for i in range(4)])
    XY_b, H2_b, A2_b = Buf("f_xy"), Buf("f_h2"), Buf("f_a2")

    def ffn_big(l):
        acol, bcol, coef = der_t[:, 2, :], modT[:, 48:64], der_t[:, 3, :]
        Wgu = w_gu[LI[l]]
        Wdn = w_dn[LI[l]].rearrange("(j p) o -> p j o", p=128)
        for g2 in range(2):
            t0 = g2 * TG2
            gb = [2 * g2, 2 * g2 + 1]
            P.dma("sp", lambda h, t0=t0: h.dma_start(out=XY[:, :, :], in_=xT_v[:, :, t0:t0 + TG2]),
                  reads=[xT_b[gb[0]], xT_b[gb[1]], xTw_b[gb[0]], xTw_b[gb[1]]], writes=[XY_b])
            for th in range(2):
                hs = slice(th * 512, (th + 1) * 512)
                ssq_rstd(None, XY_b, src_fn=lambda c, hs=hs: XY[:, c, hs])
                for c in range(NCH):
                    tmp, tb = tmps.next()
                    P.op("dve", lambda h, tmp=tmp, c=c, hs=hs: h.scalar_tensor_tensor(
                        out=tmp[:, :], in0=XY[:, c, hs], scalar=acol[:, c:c + 1], in1=rstd_t[:, :],
                        op0=ALU.mult, op1=ALU.mult), reads=[XY_b, der_b, rstd_b], writes=[tb])
                    P.op("act", lambda h, tmp=tmp, c=c, hs=hs: h.activation(
                        out=H2[:, c, hs], in_=tmp[:, :], func=AF.Identity, bias=bcol[:, c:c + 1], scale=1.0),
                        reads=[tb, modT_b], writes=[H2_b])
            for fh in range(2):
                for fcl in range(22):
                    fc = fh * 22 + fcl
                    slot, slot_b = fwrot.next()
                    view = load_w_cols(Wgu, fc * 128, 128, slot, slot_b, dst_col0=0, width=256)
                    load_w_cols(Wgu, DFF + fc * 128, 128, slot, slot_b, dst_col0=128, width=256)
                    for th in range(2):
                        hs = slice(th * 512, (th + 1) * 512)
                        pg, pgb = PG.next()
                        pu, pub = PU.next()
                        for c in range(NCH):
                            P.op("pe", mm(pg[:, :], view[:, c, 0:128], H2[:, c, hs], c == 0, c == NCH - 1),
                                 reads=[slot_b, H2_b], writes=[pgb])
                        for c in range(NCH):
                            P.op("pe", mm(pu[:, :], view[:, c, 128:256], H2[:, c, hs], c == 0, c == NCH - 1),
                                 reads=[slot_b, H2_b], writes=[pub])
                        tmp, tb = tmps.next()
                        P.op("act", lambda h, pg=pg, tmp=tmp: h.activation(out=tmp[:, :], in_=pg[:, :], func=AF.Silu),
                             reads=[pgb], writes=[tb])
                        P.op("dve", lambda h, pu=pu, tmp=tmp, fcl=fcl, hs=hs: h.tensor_tensor(
                            out=A2[:, fcl, hs], in0=tmp[:, :], in1=pu[:, :], op=ALU.mult), reads=[tb, pub], writes=[A2_b])
                for dc in range(NCH):
                    slot, slot_b = fwrot.next()
                    view = slot[:, 0:22 * 128].rearrange("p (j o) -> p j o", o=128)
                    src = Wdn[:, fh * 22:(fh + 1) * 22, dc * 128:(dc + 1) * 128]
                    P.dma("pool", lambda h, view=view, src=src: h.dma_start(out=view, in_=src), writes=[slot_b])
                    for th in range(2):
                        hs = slice(th * 512, (th + 1) * 512)
                        py, pyb = PY.next()
                        for j in range(22):
                            P.op("pe", mm(py[:, :], view[:, j, :], A2[:, j, hs], j == 0, j == 21),
                                 reads=[slot_b, A2_b], writes=[pyb])
                        if fh == 0:
                            P.op("act", lambda h, py=py, dc=dc, hs=hs: h.activation(out=XY[:, dc, hs], in_=py[:, :], func=AF.Copy),
                                 reads=[pyb], writes=[XY_b])
                        else:
                            P.op("dve", lambda h, py=py, dc=dc, hs=hs: h.tensor_tensor(out=XY[:, dc, hs], in0=py[:, :],
                                                                                       in1=XY[:, dc, hs], op=ALU.add),
                                 reads=[pyb, XY_b], writes=[XY_b])
            for th in range(2):
                hs = slice(th * 512, (th + 1) * 512)
                gg = gb[th]
                ssq_rstd(None, XY_b, src_fn=lambda c, hs=hs: XY[:, c, hs])
                for c in range(NCH):
                    xin, xinb = xins.next()
                    xo, xob = xouts.next()
                    P.dma("sp", lambda h, xin=xin, c=c, gg=gg: h.dma_start(out=xin[:, :], in_=xT_v[:, c, gg * 512:(gg + 1) * 512]),
                          reads=[xT_b[gg]], writes=[xinb])
                    tmp, tb = tmps.next()
                    P.op("dve", lambda h, tmp=tmp, c=c, hs=hs: h.scalar_tensor_tensor(
                        out=tmp[:, :], in0=XY[:, c, hs], scalar=coef[:, c:c + 1], in1=rstd_t[:, :],
                        op0=ALU.mult, op1=ALU.mult), reads=[XY_b, der_b, rstd_b], writes=[tb])
                    P.op("pool", lambda h, tmp=tmp, xin=xin, xo=xo: h.tensor_tensor(
                        out=xo[:, :], in0=xin[:, :], in1=tmp[:, :], op=ALU.add), reads=[tb, xinb], writes=[xob])
                    P.dma("sp", lambda h, xo=xo, c=c, gg=gg: h.dma_start(out=xT_v[:, c, gg * 512:(gg + 1) * 512], in_=xo[:, :]),
                          reads=[xob], writes=[xTw_b[gg]])

    def sgu_setup():
        st = {}
        st["wsT"] = sb("sgu_wsT", [128, 16, 128], BF16)
        st["bs"] = sb("sgu_bs", [128, 16, 128], F32)
        st["lng"] = sb("sgu_lng", [128, D], F32)
        st["lnb"] = sb("sgu_lnb", [128, D], F32)
        st["tri"] = sb("sgu_tri", [128, 128], F32)
        st["stat"] = sb("sgu_stat", [128, 8], F32)
        st["vtm"] = nc.alloc_sbuf_tensor_at("sgu_vtm", [128, 4, D], BF16, offset=big_off + 16 * TG * 2)
        st["b"] = {k: Buf("sgu_" + k) for k in ("wsT", "bs", "lng", "lnb", "vtm", "tri", "stat")}
        b = st["b"]
        P.dma("sp", lambda h: h.dma_start(out=st["tri"][:, :], in_=trimask[:, :]), writes=[b["tri"]])
        P.dma("sp", lambda h: h.dma_start(out=st["bs"][:, :, :].rearrange("p g t -> p (g t)"),
                                          in_=sgu_b_s[0, :].partition_broadcast(128)), writes=[b["bs"]])
        P.dma("sp", lambda h: h.dma_start(out=st["lng"][:, :], in_=sgu_ln_g[0, :].partition_broadcast(128)),
              writes=[b["lng"]])
        P.dma("sp", lambda h: h.dma_start(out=st["lnb"][:, :], in_=sgu_ln_b[0, :].partition_broadcast(128)),
              writes=[b["lnb"]])
        for gi in range(16):
            tmp, tb = tmps.next()
            P.dma("sp", lambda h, tmp=tmp, gi=gi: h.dma_start(out=tmp[:, 0:128], in_=sgu_w_s[0, gi, :, :]), writes=[tb])
            P.op("dve", lambda h, tmp=tmp: h.tensor_tensor(out=tmp[:, 128:256], in0=tmp[:, 0:128], in1=st["tri"][:, :],
                                                           op=ALU.mult), reads=[tb, b["tri"]], writes=[tb])
            pst, psb = PM
            P.op("pe", lambda h, tmp=tmp, pst=pst: h.transpose(pst[:, 0:128], tmp[:, 128:256], ident_t[:, :]),
                 reads=[tb, ident_b], writes=[psb])
            P.op("dve", lambda h, gi=gi, pst=pst: h.tensor_copy(out=st["wsT"][:, gi, :], in_=pst[:, 0:128]),
                 reads=[psb], writes=[b["wsT"]])
        return st

    def sgu_group(st, g):
        b = st["b"]
        load_xg(g)
        prenorm(der_t[:, 0, :], modT[:, 0:16])
        W = sgu_w_in[0]
        def evac_u(oc, pst, psb):
            P.op("act", lambda h: h.activation(out=big_t[:, oc, :], in_=pst[:, :], func=AF.Gelu),
                 reads=[psb], writes=[big_b])
        linear_fm(W, 0, 16, lambda c: hT_t[:, c, :], [hT_b], NCH, evac_u, ckey="sgu", g=g)
        zv4 = yT_t[:, :, :].rearrange("p c t -> p (c t)").rearrange("p (b d) -> p b d", b=4)
        for cg in range(4):
            slot, slot_b = wrot.next()
            view = load_w_cols(W, D + cg * 512, 512, slot, slot_b, ckey=("sgv", cg), g=g)
            for blk in range(4):
                pst, psb = PY.next()
                for c in range(NCH):
                    P.op("pe", mm(pst[:, :], hT_t[:, c, blk * 128:(blk + 1) * 128], view[:, c, :], c == 0, c == NCH - 1),
                         reads=[hT_b, slot_b], writes=[psb])
                P.op("act", lambda h, pst=pst, cg=cg, blk=blk: h.activation(
                    out=zv4[:, blk, cg * 512:(cg + 1) * 512], in_=pst[:, :], func=AF.Gelu), reads=[psb], writes=[yT_b])
        stat = st["stat"]
        for blk in range(4):
            zv = zv4[:, blk, :]
            P.op("dve", lambda h, zv=zv: h.tensor_reduce(out=stat[:, 0:1], in_=zv, axis=mybir.AxisListType.X, op=ALU.add),
                 reads=[yT_b], writes=[b["stat"]])
            P.op("act", lambda h, zv=zv, blk=blk: h.activation(out=st["vtm"][:, blk, :], in_=zv, func=AF.Square,
                                                               accum_out=stat[:, 1:2]),
                 reads=[yT_b], writes=[b["stat"], b["vtm"]])
            P.op("dve", lambda h: h.tensor_scalar_mul(out=stat[:, 2:3], in0=stat[:, 0:1], scalar1=1.0 / D),
                 reads=[b["stat"]], writes=[b["stat"]])
            P.op("dve", lambda h: h.tensor_tensor(out=stat[:, 3:4], in0=stat[:, 2:3], in1=stat[:, 2:3], op=ALU.mult),
                 reads=[b["stat"]], writes=[b["stat"]])
            P.op("dve", lambda h: h.scalar_tensor_tensor(out=stat[:, 4:5], in0=stat[:, 1:2], scalar=1.0 / D,
                                                         in1=stat[:, 3:4], op0=ALU.mult, op1=ALU.subtract),
                 reads=[b["stat"]], writes=[b["stat"]])
            P.op("act", lambda h: h.activation(out=stat[:, 5:6], in_=stat[:, 4:5], func=AF.Sqrt, bias=eps_t[:, 0:1],
                                               scale=1.0), reads=[b["stat"], ones_b], writes=[b["stat"]])
            P.op("dve", lambda h: h.reciprocal(out=stat[:, 6:7], in_=stat[:, 5:6]), reads=[b["stat"]], writes=[b["stat"]])
            P.op("dve", lambda h, zv=zv: h.tensor_scalar(out=zv, in0=zv, scalar1=stat[:, 2:3], scalar2=stat[:, 6:7],
                                                         op0=ALU.subtract, op1=ALU.mult), reads=[b["stat"], yT_b], writes=[yT_b])
            P.op("pool", lambda h, zv=zv: h.tensor_tensor(out=zv, in0=zv, in1=st["lng"][:, :], op=ALU.mult),
                 reads=[yT_b, b["lng"]], writes=[yT_b])
            P.op("dve", lambda h, zv=zv, blk=blk: h.tensor_tensor(out=st["vtm"][:, blk, :], in0=zv, in1=st["lnb"][:, :],
                                                                  op=ALU.add), reads=[yT_b, b["lnb"]], writes=[b["vtm"]])
        for gi in range(16):
            pst, psb = PY.next()
            for blk in range(4):
                P.op("pe", mm(pst[:, blk * 128:(blk + 1) * 128], st["vtm"][:, blk, gi * 128:(gi + 1) * 128],
                              st["wsT"][:, gi, :], True, True), reads=[b["vtm"], b["wsT"]], writes=[psb])
            tmp, tb = tmps.next()
            P.op("dve", lambda h, pst=pst, tmp=tmp, gi=gi: h.tensor_tensor(
                out=tmp[:, :].rearrange("p (b t) -> p b t", b=4), in0=pst[:, :].rearrange("p (b t) -> p b t", b=4),
                in1=st["bs"][:, gi:gi + 1, :].broadcast_to([128, 4, 128]), op=ALU.add),
                reads=[psb, b["bs"]], writes=[tb])
            P.op("pool", lambda h, tmp=tmp, gi=gi: h.tensor_tensor(out=big_t[:, gi, :], in0=big_t[:, gi, :], in1=tmp[:, :],
                                                                   op=ALU.mult), reads=[tb, big_b], writes=[big_b])
        def evac_y(oc, pst, psb):
            P.op("act", lambda h: h.activation(out=yT_t[:, oc, :], in_=pst[:, :], func=AF.Copy),
                 reads=[psb], writes=[yT_b])
        linear_fm(sgu_w_out[0], 0, 16, lambda c: big_t[:, c, :], [big_b], NCH, evac_y, ckey="sgo", g=g)
        postnorm_res(der_t[:, 1, :])
        store_xg(g)

    def at_alloc(state, name, shape, dt):
        nbytes = int(np.prod(shape[1:])) * (4 if dt in (F32, I32) else 2)
        off = (state["off"] + 31) // 32 * 32
        state["off"] = off + nbytes
        assert state["off"] <= attn_lim, (name, state["off"], attn_lim)
        state["n"] += 1
        return nc.alloc_sbuf_tensor_at("%s_%d" % (name, state["n"]), list(shape), dt, offset=off)

    def evac_to(dst_fn, dst_buf, func=None, eng="act"):
        def ev(oc, pst, psb):
            P.op("act", lambda h: h.activation(out=dst_fn(oc), in_=pst[0:dst_fn(oc).shape[0], :], func=AF.Copy),
                 reads=[psb], writes=[dst_buf])
        return ev

    def attn_out_phase(W2d, ckey=None):
        for g in range(NG):
            P.dma("sp", lambda h, g=g: h.dma_start(
                out=big_t[:, 0:16, :], in_=att_s.rearrange("(c p) t -> p c t", p=128)[:, :, g * TG:(g + 1) * TG]),
                reads=[att_b], writes=[big_b])
            load_xg(g)
            linear_fm(W2d, 0, 16, lambda c: big_t[:, c, :], [big_b], NCH,
                      evac_to(lambda oc: yT_t[:, oc, :], yT_b), ckey=ckey, g=g)
            postnorm_res(der_t[:, 1, :])
            store_xg(g)

    def all_gather(src, src_b, dst, dst_b):
        P.dma("pool", lambda h: h.collective_compute("AllGather", ALU.bypass, replica_groups=RG,
                                                     ins=[src.ap().opt()], outs=[dst.ap().opt()]),
              reads=[src_b], writes=[dst_b], inc=1)

    fox_cache = {}

    def fox_layer(l, j):
        dr = fdr
        db = dr["b"]
        W = fox_w_in[j]
        kin_v = [t.ap().rearrange("(c p) t -> p c t", p=128) for t in dr["kin"]]
        vin_v = [t.ap().rearrange("p (h i d) -> p h i d", h=2, i=16) for t in dr["vin"]]
        lin_v = dr["lin"].ap().rearrange("(i p) h -> p i h", p=128)
        ph = {"off": phase_base, "n": 100 * l}
        bf_t = sb("fox_bf%d" % l, [128, 16], F32)
        vst_t = sb("fox_vst%d" % l, [128, 4, 512], BF16)
        lf_t = sb("fox_lf%d" % l, [128, 4, 16], F32)
        bf_b, vst_b, lf_b = fox_cache.setdefault("p1", (Buf("bf"), Buf("vst"), Buf("lf")))
        P.dma("sp", lambda h: h.dma_start(out=bf_t[:, :], in_=fox_b_f[j, :].partition_broadcast(128)), writes=[bf_b])
        for g in range(NG):
            load_xg(g)
            prenorm(der_t[:, 0, :], modT[:, 0:16])
            linear_fm(W, 0, 16, lambda c: hT_t[:, c, :], [hT_b], NCH, evac_to(lambda oc: big_t[:, oc, :], big_b),
                      ckey=("fq", j), g=g)
            P.dma("sp", lambda h, g=g: h.dma_start(
                out=q_s.rearrange("(c p) t -> p c t", p=128)[:, :, g * TG:(g + 1) * TG], in_=big_t[:, 0:16, :]),
                reads=[big_b], writes=[q_b])
            linear_fm(W, D, 16, lambda c: hT_t[:, c, :], [hT_b], NCH, evac_to(lambda oc: big_t[:, 16 + oc, :], big_b),
                      ckey=("fk", j), g=g)
            for ck in range(8):
                P.dma("sp", lambda h, g=g, ck=ck: h.dma_start(out=kin_v[ck][:, :, g * TG:(g + 1) * TG],
                                                               in_=big_t[:, 16 + 2 * ck:18 + 2 * ck, :]),
                      reads=[big_b], writes=[db["kin"]])
            for cg in range(4):
                slot, slot_b = wrot.next()
                view = load_w_cols(W, 2 * D + cg * 512, 512, slot, slot_b, ckey=("fv", j, cg), g=g)
                for blk in range(4):
                    pst, psb = PY.next()
                    for c in range(NCH):
                        P.op("pe", mm(pst[:, :], hT_t[:, c, blk * 128:(blk + 1) * 128], view[:, c, :], c == 0, c == NCH - 1),
                             reads=[hT_b, slot_b], writes=[psb])
                    P.op("act", lambda h, pst=pst, blk=blk: h.activation(out=vst_t[:, blk, :], in_=pst[:, :], func=AF.Copy),
                         reads=[psb], writes=[vst_b])
                for hh in range(4):
                    P.dma("sp", lambda h, g=g, cg=cg, hh=hh: h.dma_start(
                        out=vin_v[(cg * 4 + hh) // 2][:, (cg * 4 + hh) % 2, g * 4:(g + 1) * 4, :],
                        in_=vst_t[:, :, hh * 128:(hh + 1) * 128]), reads=[vst_b], writes=[db["vin"]])
            slot, slot_b = wrot.next()
            view = load_w_cols(W, 3 * D, 16, slot, slot_b, ckey=("ffg", j), g=g)
            for blk in range(4):
                pst, psb = PY.next()
                for c in range(NCH):
                    P.op("pe", mm(pst[:, 0:16], hT_t[:, c, blk * 128:(blk + 1) * 128], view[:, c, :], c == 0, c == NCH - 1),
                         reads=[hT_b, slot_b], writes=[psb])
                P.op("dve", lambda h, pst=pst, blk=blk: h.tensor_tensor(out=lf_t[:, blk, :], in0=pst[:, 0:16], in1=bf_t[:, :],
                                                                        op=ALU.add), reads=[psb, bf_b], writes=[lf_b])
            lf2 = lf_t[:, :, :].rearrange("p b h -> p (b h)")
            P.op("act", lambda h: h.activation(out=lf2, in_=lf2, func=AF.Exp, scale=-1.0), reads=[lf_b], writes=[lf_b])
            P.op("act", lambda h: h.activation(out=lf2, in_=lf2, func=AF.Ln, bias=1.0, scale=1.0), reads=[lf_b], writes=[lf_b])
            P.op("dve", lambda h: h.tensor_scalar_mul(out=lf2, in0=lf2, scalar1=-1.0), reads=[lf_b], writes=[lf_b])
            P.dma("sp", lambda h, g=g: h.dma_start(out=lin_v[:, g * 4:(g + 1) * 4, :], in_=lf_t[:, :, :]),
                  reads=[lf_b], writes=[db["lin"]])
        all_gather(dr["lin"], db["lin"], dr["lout"], db["lout"])
        for ck in range(8):
            all_gather(dr["kin"][ck], db["kin"], dr["kout"][ck], db["kout"][ck])
            all_gather(dr["vin"][ck], db["vin"], dr["vout"][ck], db["vout"][ck])
        P.barrier(exclude=db["kout"] + db["vout"])
        A = {"off": xg_off, "n": 1000 * (l + 1)}
        lfa = at_alloc(A, "lfa", [128, 64, 16], F32)
        ftm = at_alloc(A, "ftm", [128, 64, 16], F32)
        tbc = at_alloc(A, "tbc", [128, 64, 16], F32)
        pfa = at_alloc(A, "pfa", [128, 64, 16], F32)
        pfb = at_alloc(A, "pfb", [128, 64, 16], F32)
        fref = at_alloc(A, "fref", [128, 16, 16], F32)
        triu_t = at_alloc(A, "triu", [128, 128], F32)
        oh4_t = at_alloc(A, "oh4", [128, 4], F32)
        mfox_t = at_alloc(A, "mfox", [128, 4, 128], BF16)
        zo_t = at_alloc(A, "zo", [128, 16, 64], F32)
        qh = [at_alloc(A, "qh%d" % i, [128, TOK], BF16) for i in range(2)]
        kh = [at_alloc(A, "kh%d" % i, [128, 4, TOK], BF16) for i in range(2)]
        vh = [at_alloc(A, "vh%d" % i, [128, 4, 16, 128], BF16) for i in range(2)]
        bh = [at_alloc(A, "bh%d" % i, [128, 16, 64], F32) for i in range(2)]
        ah = [at_alloc(A, "ah%d" % i, [128, TOK], BF16) for i in range(2)]
        pts = Rot([(at_alloc(A, "pt%d" % i, [128, 512], BF16), [Buf("pt%d_%d" % (i, m)) for m in range(4)]) for i in range(4)])
        rec_t = at_alloc(A, "rec", [128, 512], F32)
        B_ = fox_cache.setdefault("B_", {k: Buf("fx_" + k) for k in (
            "lfa", "ftm", "tbc", "pfa", "pfb", "fref", "triu", "oh4", "mfox", "rec",
            "qh0", "qh1", "kh0", "kh1", "vh0", "vh1", "bh0", "bh1", "ah0", "ah1")})
        P.dma("sp", lambda h: h.dma_start(out=triu_t[:, :], in_=triu[:, :]), writes=[B_["triu"]])
        P.dma("sp", lambda h: h.dma_start(out=oh4_t[:, :], in_=oh4_in[:, :]), writes=[B_["oh4"]])
        P.dma("sp", lambda h: h.dma_start(out=zo_t[:, :, :].rearrange("p a b -> p (a b)"), in_=zo_in[:, :]), writes=[B_["oh4"]])
        P.dma("pool", lambda h: h.dma_start(out=mfox_t[:, :, :].rearrange("p j q -> p (j q)"), in_=m_fox_in[:, :]),
              writes=[B_["mfox"]])
        lout_ap = dr["lout"].ap()
        for rr in range(4):
            P.dma("sp", lambda h, rr=rr: h.dma_start(
                out=lfa[:, :, :].rearrange("p (i r) h -> p r i h", r=4)[:, rr],
                in_=lout_ap[rr * TOK:(rr + 1) * TOK, :].rearrange("(i p) h -> p i h", p=128)),
                reads=[db["lout"]], writes=[B_["lfa"]])
        lfa2 = lfa[:, :, :].rearrange("p g h -> p (g h)")
        ftm2 = ftm[:, :, :].rearrange("p g h -> p (g h)")
        tbc2 = tbc[:, :, :].rearrange("p g h -> p (g h)")
        pfa2 = pfa[:, :, :].rearrange("p g h -> p (g h)")
        pfb2 = pfb[:, :, :].rearrange("p g h -> p (g h)")
        for half in range(2):
            cs = slice(half * 512, (half + 1) * 512)
            pst, psb = PY.next()
            P.op("pe", mm(pst[:, :], triu_t[:, :], lfa2[:, cs], True, True), reads=[B_["triu"], B_["lfa"]], writes=[psb])
            P.op("dve", lambda h, pst=pst, cs=cs: h.tensor_copy(out=ftm2[:, cs], in_=pst[:, :]), reads=[psb], writes=[B_["ftm"]])
            pst, psb = PY.next()
            P.op("pe", mm(pst[:, :], ones_t[:, :], lfa2[:, cs], True, True), reads=[ones_b, B_["lfa"]], writes=[psb])
            P.op("dve", lambda h, pst=pst, cs=cs: h.tensor_copy(out=tbc2[:, cs], in_=pst[:, :]), reads=[psb], writes=[B_["tbc"]])
        P.op("dve", lambda h: h.tensor_copy(out=pfa2, in_=tbc2), reads=[B_["tbc"]], writes=[B_["pfa"]])
        cur, curb, oth, othb = pfa2, B_["pfa"], pfb2, B_["pfb"]
        sh = 1
        while sh < 64:
            w = sh * 16
            P.op("dve", lambda h, cur=cur, oth=oth, w=w: h.tensor_copy(out=oth[:, 0:w], in_=cur[:, 0:w]),
                 reads=[curb], writes=[othb])
            P.op("dve", lambda h, cur=cur, oth=oth, w=w: h.tensor_tensor(out=oth[:, w:1024], in0=cur[:, w:1024],
                                                                         in1=cur[:, 0:1024 - w], op=ALU.add),
                 reads=[curb], writes=[othb])
            cur, curb, oth, othb = oth, othb, cur, curb
            sh *= 2
        P.op("dve", lambda h, cur=cur, oth=oth: h.tensor_tensor(out=oth, in0=cur, in1=tbc2, op=ALU.subtract),
             reads=[curb, B_["tbc"]], writes=[othb])
        P.op("dve", lambda h, oth=oth: h.tensor_tensor(out=ftm2, in0=ftm2, in1=oth, op=ALU.add),
             reads=[othb, B_["ftm"]], writes=[B_["ftm"]])
        P.op("dve", lambda h, cur=cur, oth=oth: h.scalar_tensor_tensor(out=oth, in0=tbc2, scalar=-0.5, in1=cur,
                                                                        op0=ALU.mult, op1=ALU.add),
             reads=[curb, B_["tbc"]], writes=[othb])
        fmid4 = (pfa if oth is pfa2 else pfb)[:, :, :].rearrange("p (i r) h -> p r i h", r=4)
        P.op("dve", lambda h: h.tensor_scalar(out=fref[:, :, :], in0=fmid4[:, 0], scalar1=oh4_t[:, 0:1], scalar2=None,
                                              op0=ALU.mult), reads=[othb, B_["oh4"]], writes=[B_["fref"]])
        for rr in range(1, 4):
            P.op("dve", lambda h, rr=rr: h.scalar_tensor_tensor(out=fref[:, :, :], in0=fmid4[:, rr], scalar=oh4_t[:, rr:rr + 1],
                                                                in1=fref[:, :, :], op0=ALU.mult, op1=ALU.add),
                 reads=[othb, B_["oh4"], B_["fref"]], writes=[B_["fref"]])
        kout_v = [t.ap().rearrange("(r c d) t -> d r c t", r=4, d=128) for t in dr["kout"]]
        vout_v = [t.ap().rearrange("(r p) (h i d) -> p r h i d", r=4, h=2, i=16) for t in dr["vout"]]
        PS_ST = Rot([psum[0], psum[1], psum[2], psum[7]])
        PS_O = Rot([psum[3], psum[4]])
        PS_D = Rot([psum[5], psum[6]])
        scale = 128.0 ** -0.5
        def head_res(hd):
            s2 = hd % 2
            return (qh[s2], kh[s2], vh[s2], bh[s2], ah[s2],
                    B_["qh%d" % s2], B_["kh%d" % s2], B_["vh%d" % s2], B_["bh%d" % s2], B_["ah%d" % s2])

        def emit_head_loads(hd):
            qt, kt, vt, bt, at, qb, kb, vb, bb, ab = head_res(hd)
            P.dma("sp", lambda h: h.dma_start(out=qt[:, :], in_=q_s[hd * 128:(hd + 1) * 128, :]), reads=[q_b], writes=[qb])
            P.dma("sp", lambda h: h.dma_start(out=kt[:, :, :], in_=kout_v[hd // 2][:, :, hd % 2, :]),
                  reads=[db["kout"][hd // 2]], writes=[kb])
            P.dma("sp", lambda h: h.dma_start(out=vt[:, :, :, :], in_=vout_v[hd // 2][:, :, hd % 2]),
                  reads=[db["vout"][hd // 2]], writes=[vb])
            P.op("dve", lambda h: h.tensor_tensor(
                out=bt[:, :, :], in0=fref[:, :, hd:hd + 1].broadcast_to([128, 16, 64]),
                in1=ftm[:, :, hd:hd + 1].rearrange("p g o -> p o g").broadcast_to([128, 16, 64]), op=ALU.subtract),
                reads=[B_["fref"], B_["ftm"]], writes=[bb])
            P.op("dve", lambda h: h.tensor_tensor(out=bt[:, :, :], in0=bt[:, :, :], in1=zo_t[:, :, :], op=ALU.add),
                 reads=[bb, B_["oh4"]], writes=[bb])

        steps = [(hd, jq, gk) for hd in range(16) for jq in range(4) for gk in range(16 * (jq + 1))]
        nst = len(steps)
        qk = {}
        acc = {}

        def emit_qk(idx):
            hd, jq, gk = steps[idx]
            qt, kt, vt, bt, at, qb, kb, vb, bb, ab = head_res(hd)
            rr, ii = gk % 4, gk // 4
            mmin = max(0, -(-(gk - 3 - 16 * jq) // 4))
            c0 = mmin * 128
            pst, psb = PS_ST.next()
            P.op("pe", mm(pst[:, c0:512], kt[:, rr, ii * 128:(ii + 1) * 128], qt[:, jq * 512 + c0:(jq + 1) * 512],
                          True, True), reads=[kb, qb], writes=[psb])
            qk[idx] = (pst, psb, mmin, c0)

        def emit_exp(idx):
            hd, jq, gk = steps[idx]
            qt, kt, vt, bt, at, qb, kb, vb, bb, ab = head_res(hd)
            pst, psb, mmin, c0 = qk[idx]
            pt, ptb = pts.next()
            for m in range(mmin, 4):
                li = 4 * jq + m
                P.op("act", lambda h, m=m, li=li: h.activation(
                    out=pt[:, m * 128:(m + 1) * 128], in_=pst[:, m * 128:(m + 1) * 128], func=AF.Exp,
                    bias=bt[:, li, gk:gk + 1], scale=scale), reads=[psb, bb], writes=[ptb[m]])
                jm = gk - 4 * li
                if 0 <= jm <= 3:
                    P.op("pool", lambda h, m=m, jm=jm: h.tensor_tensor(
                        out=pt[:, m * 128:(m + 1) * 128], in0=pt[:, m * 128:(m + 1) * 128], in1=mfox_t[:, jm, :],
                        op=ALU.mult), reads=[ptb[m], B_["mfox"]], writes=[ptb[m]])
            qk[idx] = (pst, psb, mmin, c0, pt, ptb)

        def emit_pv(idx):
            hd, jq, gk = steps[idx]
            qt, kt, vt, bt, at, qb, kb, vb, bb, ab = head_res(hd)
            pst, psb, mmin, c0, pt, ptb = qk.pop(idx)
            rr, ii = gk % 4, gk // 4
            ng = 16 * (jq + 1)
            if gk == 0:
                acc[(hd, jq)] = (PS_O.next(), PS_D.next())
            (po, pob), (pd, pdb) = acc[(hd, jq)]
            P.op("pe", mm(po[:, c0:512], vt[:, rr, ii, :], pt[:, c0:512], gk == 0, gk == ng - 1),
                 reads=[vb] + ptb[mmin:], writes=[pob])
            P.op("pe", mm(pd[:, c0:512], onesb_t[:, :], pt[:, c0:512], gk == 0, gk == ng - 1),
                 reads=[ones_b] + ptb[mmin:], writes=[pdb])
            if gk == ng - 1:
                del acc[(hd, jq)]
                P.op("dve", lambda h: h.reciprocal(out=rec_t[:, :], in_=pd[:, :]), reads=[pdb], writes=[B_["rec"]])
                P.op("dve", lambda h: h.tensor_tensor(out=at[:, jq * 512:(jq + 1) * 512], in0=po[:, :],
                                                      in1=rec_t[:, :], op=ALU.mult),
                     reads=[pob, B_["rec"]], writes=[ab])
                if jq == 3:
                    P.dma("sp", lambda h: h.dma_start(out=att_s[hd * 128:(hd + 1) * 128, :], in_=at[:, :]),
                          reads=[ab], writes=[att_b])
                    if hd + 2 < 16:
                        emit_head_loads(hd + 2)

        emit_head_loads(0)
        emit_head_loads(1)
        emit_qk(0)
        emit_qk(1)
        emit_qk(2)
        for idx in range(nst):
            emit_exp(idx)
            if idx + 3 < nst:
                emit_qk(idx + 3)
            emit_pv(idx)
        P.barrier()
        attn_out_phase(fox_w_out[j], ckey=("fo", j))
        P.barrier()

    def swa_layer(l):
        dr = swa_dr
        db = dr["b"]
        W = swa_w_in[0]
        kin_v = [t.ap().rearrange("(h d) t -> d h t", d=64) for t in dr["kin"]]
        vin_v = [t.ap().rearrange("p (i f) -> p i f", i=8) for t in dr["vin"]]
        q_v = q_s.rearrange("(h d) t -> d h t", d=64)
        att_v = att_s.rearrange("(h d) t -> d h t", d=64)
        c16 = sb("swa_c16", [16, TOK], F32)
        s16 = sb("swa_s16", [16, TOK], F32)
        pmat_t = sb("swa_pmat", [64, 16], BF16)
        invf_t = sb("swa_invf", [16, 1], F32)
        vst_t = sb("swa_vst", [128, 4, 512], BF16)
        SB_ = {k: Buf("sw_" + k) for k in ("c16", "s16", "pmat", "invf", "vst", "posi")}
        posi_t = sb("swa_posi", [16, TOK], I32)
        P.dma("sp", lambda h: h.dma_start(out=posi_t[:, :], in_=posin[0, :].partition_broadcast(16)), writes=[SB_["posi"]])
        P.dma("pool", lambda h: h.dma_start(out=pmat_t[:, :], in_=pmat_in[:, :]), writes=[SB_["pmat"]])
        P.dma("sp", lambda h: h.dma_start(out=invf_t[:, :], in_=invf_in[:, :]), writes=[SB_["invf"]])
        PI = float(np.pi)
        C1 = 6.28125
        C2 = float(2 * np.pi - 6.28125)
        ang = yT_t[0:16, 8:12, :].rearrange("p c t -> p (c t)")
        nf = yT_t[0:16, 0:4, :].rearrange("p c t -> p (c t)")
        mk = yT_t[0:16, 4:8, :].rearrange("p c t -> p (c t)")
        ys, yc = s16[:, :], c16[:, :]
        RW = dict(reads=[SB_["posi"], SB_["invf"], yT_b, SB_["s16"], SB_["c16"]],
                  writes=[yT_b, SB_["s16"], SB_["c16"], SB_["posi"]])
        P.op("dve", lambda h: h.tensor_copy(out=ang, in_=posi_t[:, :]), **RW)
        P.op("dve", lambda h: h.tensor_scalar_mul(out=ang, in0=ang, scalar1=invf_t[:, 0:1]), **RW)
        P.op("dve", lambda h: h.tensor_scalar_mul(out=nf, in0=ang, scalar1=float(1.0 / (2 * np.pi))), **RW)
        P.op("dve", lambda h: h.tensor_copy(out=posi_t[:, :], in_=nf), **RW)
        P.op("dve", lambda h: h.tensor_copy(out=nf, in_=posi_t[:, :]), **RW)
        P.op("dve", lambda h: h.scalar_tensor_tensor(out=ys, in0=nf, scalar=-C1, in1=ang, op0=ALU.mult, op1=ALU.add), **RW)
        P.op("dve", lambda h: h.scalar_tensor_tensor(out=ys, in0=nf, scalar=-C2, in1=ys, op0=ALU.mult, op1=ALU.add), **RW)
        P.op("dve", lambda h: h.tensor_single_scalar(out=mk, in_=ys, scalar=PI, op=ALU.is_gt), **RW)
        P.op("dve", lambda h: h.scalar_tensor_tensor(out=ys, in0=mk, scalar=-2 * PI, in1=ys, op0=ALU.mult, op1=ALU.add), **RW)
        P.op("dve", lambda h: h.tensor_single_scalar(out=mk, in_=ys, scalar=-PI, op=ALU.is_lt), **RW)
        P.op("dve", lambda h: h.scalar_tensor_tensor(out=ys, in0=mk, scalar=2 * PI, in1=ys, op0=ALU.mult, op1=ALU.add), **RW)
        P.op("dve", lambda h: h.tensor_scalar_add(out=yc, in0=ys, scalar1=PI / 2), **RW)
        P.op("dve", lambda h: h.tensor_single_scalar(out=mk, in_=yc, scalar=PI, op=ALU.is_gt), **RW)
        P.op("dve", lambda h: h.scalar_tensor_tensor(out=yc, in0=mk, scalar=-2 * PI, in1=yc, op0=ALU.mult, op1=ALU.add), **RW)
        P.op("act", lambda h: h.activation(out=ys, in_=ys, func=AF.Sin), reads=[SB_["s16"]], writes=[SB_["s16"]])
        P.op("act", lambda h: h.activation(out=yc, in_=yc, func=AF.Sin), reads=[SB_["c16"]], writes=[SB_["c16"]])

        def rope(tile_fn, nheads, g):
            for hh in range(nheads):
                pst, psb = PM
                P.op("pe", mm(pst[0:16, :], pmat_t[:, :], tile_fn(hh), True, True), reads=[SB_["pmat"], big_b], writes=[psb])
                t1, t1b = tmps.next()
                t2, t2b = tmps.next()
                P.op("dve", lambda h, t1=t1, hh=hh: h.tensor_tensor(out=t1[0:16, :], in0=tile_fn(hh)[0:16, :],
                                                                    in1=c16[:, g * TG:(g + 1) * TG], op=ALU.mult),
                     reads=[big_b, SB_["c16"]], writes=[t1b])
                P.op("dve", lambda h, t2=t2, pst=pst: h.tensor_tensor(out=t2[0:16, :], in0=pst[0:16, :],
                                                                      in1=s16[:, g * TG:(g + 1) * TG], op=ALU.mult),
                     reads=[psb, SB_["s16"]], writes=[t2b])
                P.op("pool", lambda h, t1=t1, t2=t2, hh=hh: h.tensor_tensor(out=tile_fn(hh)[0:16, :], in0=t1[0:16, :],
                                                                            in1=t2[0:16, :], op=ALU.add),
                     reads=[t1b, t2b], writes=[big_b])

        for g in range(NG):
            load_xg(g)
            prenorm(der_t[:, 0, :], modT[:, 0:16])
            linear_fm(W, 0, 32, lambda c: hT_t[:, c, :], [hT_b], NCH, evac_to(lambda oc: big_t[0:64, oc, :], big_b),
                      ocw=64, per_load=8, ckey="swq", g=g)
            rope(lambda hh: big_t[0:64, hh, :], 32, g)
            P.dma("sp", lambda h, g=g: h.dma_start(out=q_v[:, :, g * TG:(g + 1) * TG], in_=big_t[0:64, 0:32, :]),
                  reads=[big_b], writes=[q_b])
            linear_fm(W, D, 8, lambda c: hT_t[:, c, :], [hT_b], NCH, evac_to(lambda oc: big_t[0:64, 32 + oc, :], big_b),
                      ocw=64, per_load=8, ckey="swk", g=g)
            rope(lambda hh: big_t[0:64, 32 + hh, :], 8, g)
            for ck in range(2):
                P.dma("sp", lambda h, g=g, ck=ck: h.dma_start(out=kin_v[ck][:, :, g * TG:(g + 1) * TG],
                                                               in_=big_t[0:64, 32 + 4 * ck:36 + 4 * ck, :]),
                      reads=[big_b], writes=[db["kin"]])
            slot, slot_b = wrot.next()
            view = load_w_cols(W, D + 512, 512, slot, slot_b, ckey="swv", g=g)
            for blk in range(4):
                pst, psb = PY.next()
                for c in range(NCH):
                    P.op("pe", mm(pst[:, :], hT_t[:, c, blk * 128:(blk + 1) * 128], view[:, c, :], c == 0, c == NCH - 1),
                         reads=[hT_b, slot_b], writes=[psb])
                P.op("act", lambda h, pst=pst, blk=blk: h.activation(out=vst_t[:, blk, :], in_=pst[:, :], func=AF.Copy),
                     reads=[psb], writes=[SB_["vst"]])
            P.dma("sp", lambda h, g=g: h.dma_start(out=vin_v[g // 2][:, (g % 2) * 4:(g % 2) * 4 + 4, :], in_=vst_t[:, :, :]),
                  reads=[SB_["vst"]], writes=[db["vin"]])
        for ck in range(2):
            all_gather(dr["kin"][ck], db["kin"], dr["kout"][ck], db["kout"])
            all_gather(dr["vin"][ck], db["vin"], dr["vout"][ck], db["vout"])
        P.barrier()
        A = {"off": xg_off, "n": 5000}
        kc2 = [at_alloc(A, "kc%d" % i, [64, 8, 128], BF16) for i in range(2)]
        vc2 = [at_alloc(A, "vc%d" % i, [128, 512], BF16) for i in range(2)]
        kp2_ = [at_alloc(A, "kp%d" % i, [64, 8, 128], BF16) for i in range(2)]
        vp2_ = [at_alloc(A, "vp%d" % i, [128, 512], BF16) for i in range(2)]
        qb2 = [at_alloc(A, "qblk%d" % i, [64, 32, 128], BF16) for i in range(2)]
        ab2 = [at_alloc(A, "ablk%d" % i, [64, 32, 128], BF16) for i in range(2)]
        kcand = at_alloc(A, "kcand", [64, 5, 8, 128], BF16)
        vcand = at_alloc(A, "vcand", [128, 5, 512], BF16)
        mtri = at_alloc(A, "mtri", [128, 128], BF16)
        mprev = at_alloc(A, "mprev", [128, 128], BF16)
        mprev0 = at_alloc(A, "mprev0", [128, 128], BF16)
        sel5 = at_alloc(A, "sel5", [128, 5], F32)
        sinke = at_alloc(A, "sinke", [64, 32], F32)
        den_t = at_alloc(A, "den", [64, 512], F32)
        ptc = Rot([(at_alloc(A, "ptc%d" % i, [128, 512], BF16), Buf("ptc%d" % i)) for i in range(2)])
        ptp = Rot([(at_alloc(A, "ptp%d" % i, [128, 512], BF16), Buf("ptp%d" % i)) for i in range(2)])
        B_ = {k: Buf("sw3_" + k) for k in ("kc0", "kc1", "vc0", "vc1", "kcand", "vcand", "kp0", "kp1", "vp0", "vp1",
                                           "qblk0", "qblk1", "ablk0", "ablk1", "mtri", "mprev",
                                           "mprev0", "sel5", "sinke", "den")}
        P.dma("pool", lambda h: h.dma_start(out=mtri[:, :], in_=triu[:, :]), writes=[B_["mtri"]])
        P.dma("pool", lambda h: h.dma_start(out=mprev[:, :], in_=m_prev_in[:, :]), writes=[B_["mprev"]])
        P.dma("pool", lambda h: h.dma_start(out=mprev0[:, :], in_=m_prev0_in[:, :]), writes=[B_["mprev0"]])
        P.dma("sp", lambda h: h.dma_start(out=sel5[:, :], in_=sel5_in[:, :]), writes=[B_["sel5"]])
        P.dma("sp", lambda h: h.dma_start(out=sinke[:, :], in_=swa_sinks[0, :].partition_broadcast(64)), writes=[B_["sinke"]])
        P.op("act", lambda h: h.activation(out=sinke[:, :], in_=sinke[:, :], func=AF.Exp), reads=[B_["sinke"]], writes=[B_["sinke"]])
        kout_v = [t.ap().rearrange("(r h d) t -> d r h t", r=4, d=64) for t in dr["kout"]]
        vout_v = [t.ap().rearrange("(r p) (i f) -> p r i f", r=4, i=8) for t in dr["vout"]]
        PS_C = Rot([psum[0], psum[1]])
        PS_P = Rot([psum[2], psum[3]])
        PS_O = Rot([psum[4], psum[5]])
        PS_D = Rot([psum[6], psum[7]])
        scale = 64.0 ** -0.5

        def emit_block_loads(i):
            par = i % 2
            kc, vc, kp, vp, qblk = kc2[par], vc2[par], kp2_[par], vp2_[par], qb2[par]
            kcb, vcb, kpb, vpb, qbb = (B_["kc%d" % par], B_["vc%d" % par], B_["kp%d" % par], B_["vp%d" % par],
                                       B_["qblk%d" % par])
            ip = max(i - 1, 0)
            for ck in range(2):
                P.dma("sp", lambda h, ck=ck: h.dma_start(out=kc[:, 4 * ck:4 * ck + 4, :],
                                                         in_=kin_v[ck][:, :, i * 128:(i + 1) * 128]),
                      reads=[db["kin"]], writes=[kcb])
                for rr in range(4):
                    P.dma("sp", lambda h, ck=ck, rr=rr: h.dma_start(
                        out=kcand[:, rr, 4 * ck:4 * ck + 4, :], in_=kout_v[ck][:, rr, :, i * 128:(i + 1) * 128]),
                        reads=[db["kout"]], writes=[B_["kcand"]])
                P.dma("sp", lambda h, ck=ck: h.dma_start(
                    out=kcand[:, 4, 4 * ck:4 * ck + 4, :], in_=kout_v[ck][:, 3, :, ip * 128:(ip + 1) * 128]),
                    reads=[db["kout"]], writes=[B_["kcand"]])
            P.dma("sp", lambda h: h.dma_start(out=vc[:, :], in_=vin_v[i // 8][:, i % 8, :]), reads=[db["vin"]], writes=[vcb])
            P.dma("sp", lambda h: h.dma_start(out=vcand[:, 0:4, :], in_=vout_v[i // 8][:, :, i % 8, :]),
                  reads=[db["vout"]], writes=[B_["vcand"]])
            P.dma("sp", lambda h: h.dma_start(out=vcand[:, 4, :], in_=vout_v[ip // 8][:, 3, ip % 8, :]),
                  reads=[db["vout"]], writes=[B_["vcand"]])
            P.dma("sp", lambda h: h.dma_start(out=qblk[:, :, :], in_=q_v[:, :, i * 128:(i + 1) * 128]),
                  reads=[q_b], writes=[qbb])

        def emit_block_select(i):
            par = i % 2
            kp, vp = kp2_[par], vp2_[par]
            kpb, vpb = B_["kp%d" % par], B_["vp%d" % par]
            kpf = kp[:, :, :].rearrange("d h t -> d (h t)")
            P.op("dve", lambda h: h.tensor_scalar(out=kpf, in0=kcand[:, 0, :, :].rearrange("d h t -> d (h t)"),
                                                  scalar1=sel5[0:64, 0:1], scalar2=None, op0=ALU.mult),
                 reads=[B_["kcand"], B_["sel5"]], writes=[kpb])
            P.op("dve", lambda h: h.tensor_scalar(out=vp[:, :], in0=vcand[:, 0, :], scalar1=sel5[:, 0:1], scalar2=None,
                                                  op0=ALU.mult), reads=[B_["vcand"], B_["sel5"]], writes=[vpb])
            for cnd in range(1, 5):
                P.op("dve", lambda h, cnd=cnd: h.scalar_tensor_tensor(
                    out=kpf, in0=kcand[:, cnd, :, :].rearrange("d h t -> d (h t)"), scalar=sel5[0:64, cnd:cnd + 1], in1=kpf,
                    op0=ALU.mult, op1=ALU.add), reads=[B_["kcand"], B_["sel5"], kpb], writes=[kpb])
                P.op("dve", lambda h, cnd=cnd: h.scalar_tensor_tensor(
                    out=vp[:, :], in0=vcand[:, cnd, :], scalar=sel5[:, cnd:cnd + 1], in1=vp[:, :],
                    op0=ALU.mult, op1=ALU.add), reads=[B_["vcand"], B_["sel5"], vpb], writes=[vpb])

        sw_steps = [(i, hk) for i in range(NBLK) for hk in range(8)]
        sA = {}
        sB = {}

        def stageA(si):
            i, hk = sw_steps[si]
            par = i % 2
            qsl = qb2[par][:, hk * 4:(hk + 1) * 4, :]
            pc, pcb = PS_C.next()
            pp, ppb = PS_P.next()
            P.op("pe", mm(pc[:, :], kc2[par][:, hk, :], qsl, True, True), reads=[B_["kc%d" % par], B_["qblk%d" % par]], writes=[pcb])
            P.op("pe", mm(pp[:, :], kp2_[par][:, hk, :], qsl, True, True), reads=[B_["kp%d" % par], B_["qblk%d" % par]], writes=[ppb])
            sA[si] = (pc, pcb, pp, ppb)

        def stageB(si):
            i, hk = sw_steps[si]
            par = i % 2
            vc, vp, ablk = vc2[par], vp2_[par], ab2[par]
            vcb, vpb, abb = B_["vc%d" % par], B_["vp%d" % par], B_["ablk%d" % par]
            pc, pcb, pp, ppb = sA.pop(si)
            mpv, mpvb = (mprev0, B_["mprev0"]) if i == 0 else (mprev, B_["mprev"])
            tc_, tcb = ptc.next()
            tp_, tpb = ptp.next()
            P.op("act", lambda h: h.activation(out=tc_[:, :], in_=pc[:, :], func=AF.Exp, scale=scale), reads=[pcb], writes=[tcb])
            P.op("act", lambda h: h.activation(out=tp_[:, :], in_=pp[:, :], func=AF.Exp, scale=scale), reads=[ppb], writes=[tpb])
            P.op("pool", lambda h: h.tensor_tensor(
                out=tc_[:, :].rearrange("k (a q) -> k a q", a=4), in0=tc_[:, :].rearrange("k (a q) -> k a q", a=4),
                in1=mtri[:, :].rearrange("k (o q) -> k o q", o=1).broadcast_to([128, 4, 128]), op=ALU.mult),
                reads=[tcb, B_["mtri"]], writes=[tcb])
            P.op("dve", lambda h: h.tensor_tensor(
                out=tp_[:, :].rearrange("k (a q) -> k a q", a=4), in0=tp_[:, :].rearrange("k (a q) -> k a q", a=4),
                in1=mpv[:, :].rearrange("k (o q) -> k o q", o=1).broadcast_to([128, 4, 128]), op=ALU.mult),
                reads=[tpb, mpvb], writes=[tpb])
            po, pob = PS_O.next()
            pd, pdb = PS_D.next()
            P.op("pe", mm(po[0:64, :], vc[:, hk * 64:(hk + 1) * 64], tc_[:, :], True, False), reads=[vcb, tcb], writes=[pob])
            P.op("pe", mm(po[0:64, :], vp[:, hk * 64:(hk + 1) * 64], tp_[:, :], False, True), reads=[vpb, tpb], writes=[pob])
            P.op("pe", mm(pd[0:64, :], onesb_t[:, 0:64], tc_[:, :], True, False), reads=[ones_b, tcb], writes=[pdb])
            P.op("pe", mm(pd[0:64, :], onesb_t[:, 0:64], tp_[:, :], False, True), reads=[ones_b, tpb], writes=[pdb])
            sB[si] = (po, pob, pd, pdb)

        def stageB2(si):
            i, hk = sw_steps[si]
            par = i % 2
            ablk, abb = ab2[par], B_["ablk%d" % par]
            po, pob, pd, pdb = sB.pop(si)
            P.op("dve", lambda h: h.tensor_tensor(
                out=den_t[:, :].rearrange("d (a q) -> d a q", a=4), in0=pd[0:64, :].rearrange("d (a q) -> d a q", a=4),
                in1=sinke[:, hk * 4:(hk + 1) * 4].rearrange("d (a o) -> d a o", o=1).broadcast_to([64, 4, 128]), op=ALU.add),
                reads=[pdb, B_["sinke"]], writes=[B_["den"]])
            P.op("act", lambda h: h.activation(out=den_t[:, :], in_=den_t[:, :], func=AF.Ln), reads=[B_["den"]], writes=[B_["den"]])
            P.op("act", lambda h: h.activation(out=den_t[:, :], in_=den_t[:, :], func=AF.Exp, scale=-1.0),
                 reads=[B_["den"]], writes=[B_["den"]])
            P.op("dve", lambda h: h.tensor_tensor(
                out=ablk[:, hk * 4:(hk + 1) * 4, :], in0=po[0:64, :].rearrange("d (a q) -> d a q", a=4),
                in1=den_t[:, :].rearrange("d (a q) -> d a q", a=4), op=ALU.mult),
                reads=[pob, B_["den"]], writes=[abb])
            if hk == 7:
                P.dma("sp", lambda h: h.dma_start(out=att_v[:, :, i * 128:(i + 1) * 128], in_=ablk[:, :, :]),
                      reads=[abb], writes=[att_b])

        emit_block_loads(0)
        emit_block_select(0)
        stageA(0)
        for si in range(len(sw_steps)):
            bi, bh = sw_steps[si]
            if bh == 0 and bi + 1 < NBLK:
                emit_block_loads(bi + 1)
            if si + 1 < len(sw_steps):
                if sw_steps[si + 1][1] == 0:
                    emit_block_select(sw_steps[si + 1][0])
                stageA(si + 1)
            stageB(si)
            if si >= 1:
                stageB2(si - 1)
        stageB2(len(sw_steps) - 1)
        P.barrier()
        attn_out_phase(swa_w_out[0], ckey="swo")
        P.barrier()

    for l in layers:
        compute_mod(l)
        kind = l % 3
        if do_mixer:
            if kind == 0:
                arena["off"] = phase_base
                fox_layer(l, l // 3)
            if kind == 2:
                arena["off"] = phase_base
                swa_layer(l)
            if kind == 1:
                arena["off"] = phase_base
                st = sgu_setup()
                for g in range(NG):
                    sgu_group(st, g)
                P.barrier()
        if do_ffn:
            P.barrier()
            ffn_big(l)
            P.barrier()

    for g in range(NG):
        load_xg(g)
        stage = yT_t[:, :, :].rearrange("p c t -> p (c t)").rearrange("p (b d) -> p b d", b=4)
        for b in range(4):
            for q in range(4):
                pst, psb = PY.next()
                for j in range(4):
                    c = q * 4 + j
                    P.op("pe", lambda h, pst=pst, b=b, c=c, j=j: h.transpose(
                        pst[:, j * 128:(j + 1) * 128], xg_t[:, c, b * 128:(b + 1) * 128], ident_t[:, :]),
                        reads=[xg_b, ident_b], writes=[psb])
                if q % 2 == 0:
                    P.op("act", lambda h, pst=pst, b=b, q=q: h.activation(out=stage[:, b, q * 512:(q + 1) * 512],
                                                                          in_=pst[:, :], func=AF.Copy),
                         reads=[psb], writes=[yT_b])
                else:
                    P.op("dve", lambda h, pst=pst, b=b, q=q: h.tensor_copy(out=stage[:, b, q * 512:(q + 1) * 512],
                                                                           in_=pst[:, :]), reads=[psb], writes=[yT_b])
        P.dma("sp", lambda h, g=g: h.dma_start(
            out=yout[g * TG:(g + 1) * TG, :].rearrange("(b p) d -> p b d", p=128), in_=stage),
            reads=[yT_b], writes=[yout_b])
    P.barrier()
    P.emit(nc)
    es.close()
    return nc


_TRI = np.tril(np.ones((128, 128), np.float32))


def _prep_inputs(inp, layers):
    x = np.asarray(inp["x"], np.float32)
    maps = []
    shared = {
        "ident": np.eye(128, dtype=np.float32),
        "trimask": _TRI,
        "ffn_w_gu": np.ascontiguousarray(np.asarray(inp["ffn_w_gu"], np.float32)[list(layers)]),
        "ffn_w_down": np.ascontiguousarray(np.asarray(inp["ffn_w_down"], np.float32)[list(layers)]),
        "sgu_w_in": np.ascontiguousarray(inp["sgu_w_in"], np.float32),
        "sgu_ln_g": np.ascontiguousarray(inp["sgu_ln_g"], np.float32),
        "sgu_ln_b": np.ascontiguousarray(inp["sgu_ln_b"], np.float32),
        "sgu_w_s": np.ascontiguousarray(inp["sgu_w_s"], np.float32),
        "sgu_b_s": np.ascontiguousarray(inp["sgu_b_s"], np.float32).reshape(1, 2048),
        "sgu_w_out": np.ascontiguousarray(inp["sgu_w_out"], np.float32),
    }
    for n in ("fox_w_in", "fox_b_f", "fox_w_out", "swa_w_in", "swa_sinks", "swa_w_out"):
        shared[n] = np.ascontiguousarray(inp[n], np.float32)
    shared["triu"] = np.ascontiguousarray(_TRI.T)
    shared["m_prev"] = np.ascontiguousarray(1.0 - _TRI.T)
    pm = np.zeros((64, 16), np.float32)
    for m_ in range(8):
        pm[m_ + 8, m_] = -1.0
        pm[m_, m_ + 8] = 1.0
    shared["pmat"] = pm
    inv = (500000.0 ** (-np.arange(0, 16, 2, dtype=np.float32) / np.float32(16))).astype(np.float32)
    shared["invf"] = np.concatenate([inv, inv]).reshape(16, 1).astype(np.float32)
    for n in ("mix_pre_g", "mix_post_g", "ffn_pre_g", "ffn_post_g"):
        shared[n] = np.ascontiguousarray(inp[n], np.float32).reshape(64, 128)
    for core in range(8):
        b, r = core // 4, core % 4
        xb = x[b].reshape(16, 4, 128, D)[:, r].reshape(TOK, D)
        m = dict(shared)
        m["xs"] = np.ascontiguousarray(xb)
        m["ada_w"] = np.ascontiguousarray(np.asarray(inp["ada_w"], np.float32)[list(layers)][:, :, r * 3072:(r + 1) * 3072])
        m["ada_b"] = np.ascontiguousarray(np.asarray(inp["ada_b"], np.float32)[list(layers)][:, r * 3072:(r + 1) * 3072])
        m["cvec"] = np.ascontiguousarray(inp["c"][b], np.float32).reshape(16, 128)
        pos = np.asarray(inp["positions"])[b].astype(np.int32)
        m["posin"] = np.ascontiguousarray(pos.reshape(16, 4, 128)[:, r].reshape(1, TOK))
        mf = np.zeros((128, 4, 128), np.float32)
        for j_ in range(4):
            if j_ < r:
                mf[:, j_, :] = 1.0
            elif j_ == r:
                mf[:, j_, :] = _TRI.T
        m["m_fox"] = mf.reshape(128, 512)
        m["m_prev0"] = np.zeros((128, 128), np.float32) if r == 0 else np.ascontiguousarray(1.0 - _TRI.T)
        oh = np.zeros((128, 4), np.float32)
        oh[:, r] = 1.0
        m["oh4"] = oh
        zo = np.zeros((128, 16, 64), np.float32)
        for li_ in range(16):
            zo[:, li_, 4 * li_ + r + 1:] = -30000.0
        m["zo"] = zo.reshape(128, 1024)
        s5 = np.zeros((128, 5), np.float32)
        s5[:, (r - 1) if r > 0 else 4] = 1.0
        m["sel5"] = s5
        maps.append(m)
    return maps


def run(inp, layers=(0, 1, 2, 3), **kw):
    nc = build_program(layers=layers, **kw)
    maps = _prep_inputs(inp, layers)
    res = run_bass_kernel_spmd(nc, maps, core_ids=list(range(8)))
    out = np.empty((2, SEQ, D), np.float32)
    for core in range(8):
        b, r = core // 4, core % 4
        out[b].reshape(16, 4, 128, D)[:, r] = res.results[core]["yout"].reshape(16, 128, D)
    return out


def kernel(**inputs):
    return run(inputs)
```

```python
import numpy as np
import ml_dtypes
from contextlib import ExitStack
import concourse.bass as bass
import concourse.mybir as mybir
from concourse.bass_utils import run_bass_kernel_spmd

F32 = mybir.dt.float32
BF16 = mybir.dt.bfloat16
I32 = mybir.dt.int32
AF = mybir.ActivationFunctionType
ALU = mybir.AluOpType

D = 2048
NCH = 16
SEQ = 8192
TOK = 2048
NBLK = 16
TG = 512
NG = TOK // TG
DFF = 5632
NFC = DFF // 128
EPS = 1e-6
FOX_IN = 6160
SWA_IN = 3072
ENGS = ("pe", "act", "dve", "pool", "sp")
BLOCKNAME = {"pe": "tensor", "act": "scalar", "dve": "vector", "pool": "gpsimd", "sp": "sync"}


class Buf:
    __slots__ = ("name", "w", "rs", "dtotal")

    def __init__(self, name):
        self.name = name
        self.w = None
        self.rs = {}
        self.dtotal = 0


class Plan:
    def __init__(self):
        self.recs = {e: [] for e in ENGS}
        self.seen = {e: {} for e in ENGS}
        self.dbufs = {}

    def _deps(self, eng, reads, writes, skipkey=None):
        need = {}
        seen = self.seen[eng]

        def add(tok):
            key, val = tok
            if key == ("E", "pe") and eng == "pe":
                return
            if key == skipkey:
                return
            if seen.get(key, -1) >= val:
                return
            if need.get(key, -1) < val:
                need[key] = val

        for b in reads:
            if b.w is not None:
                add(b.w)
        for b in writes:
            if b.w is not None:
                add(b.w)
            for k, v in b.rs.items():
                add((k, v))
        for k, v in need.items():
            seen[k] = v
            if k[0] == "E":
                self.recs[k[1]][v][3] = True
        return list(need.items())

    def op(self, eng, fn, reads=(), writes=()):
        waits = self._deps(eng, reads, writes)
        idx = len(self.recs[eng])
        self.recs[eng].append([waits, fn, None, False, 0])
        key = ("E", eng)
        for b in reads:
            if b.rs.get(key, -1) < idx:
                b.rs[key] = idx
        for b in writes:
            b.w = (key, idx)
            b.rs = {}

    def dma(self, eng, fn, reads=(), writes=(), dbuf=None, inc=16):
        if dbuf is None:
            dbuf = writes[0]
        waits = self._deps(eng, reads, writes, skipkey=("D", id(dbuf)))
        self.dbufs[id(dbuf)] = dbuf
        dbuf.dtotal += inc
        key = ("D", id(dbuf))
        val = dbuf.dtotal
        self.recs[eng].append([waits, fn, id(dbuf), False, inc])
        for b in reads:
            if b.rs.get(key, -1) < val:
                b.rs[key] = val
        for b in writes:
            b.w = (key, val)
            b.rs = {}

    def barrier(self, exclude=()):
        excl = set(id(b) for b in exclude)
        for e in ENGS:
            need = []
            seen = self.seen[e]
            for e2 in ENGS:
                if e2 == e or not self.recs[e2]:
                    continue
                idx = None
                for j in range(len(self.recs[e2]) - 1, -1, -1):
                    r = self.recs[e2][j]
                    if r[1] is not None and r[2] is None:
                        idx = j
                        break
                if idx is None:
                    continue
                key = ("E", e2)
                if seen.get(key, -1) < idx:
                    seen[key] = idx
                    self.recs[e2][idx][3] = True
                    need.append((key, idx))
            for bid, b in self.dbufs.items():
                key = ("D", bid)
                if bid in excl:
                    continue
                if b.dtotal > 0 and seen.get(key, -1) < b.dtotal:
                    seen[key] = b.dtotal
                    need.append((key, b.dtotal))
            if need:
                self.recs[e].append([need, None, None, False, 0])

    def emit(self, nc):
        vals = {}
        for e in ENGS:
            cnt = 0
            v = []
            for rec in self.recs[e]:
                if rec[3]:
                    cnt += 1
                v.append(cnt)
            vals[e] = v
            assert cnt < 60000, (e, cnt)
        with ExitStack() as es:
            esem = {e: es.enter_context(nc.semaphore("sem_" + e)) for e in ENGS}
            dsem = {}
            for n, bid in enumerate(self.dbufs):
                dsem[bid] = es.enter_context(nc.semaphore("dsem%d" % n))
            block = es.enter_context(nc.Block())
            for e in ENGS:
                def body(h, e=e):
                    for waits, fn, dma, flagged, inc in self.recs[e]:
                        for key, val in waits:
                            if key[0] == "E":
                                h.wait_ge(esem[key[1]], vals[key[1]][val])
                            else:
                                h.wait_ge(dsem[key[1]], val)
                        if fn is None:
                            continue
                        ins = fn(h)
                        if dma is not None:
                            ins.then_inc(dsem[dma], inc)
                        elif flagged:
                            ins.then_inc(esem[e], 1)
                getattr(block, BLOCKNAME[e])(body)


class Rot:
    def __init__(self, items):
        self.items = items
        self.i = 0

    def next(self):
        it = self.items[self.i % len(self.items)]
        self.i += 1
        return it


def build_program(layers=(0, 1, 2, 3), do_mixer=True, do_ffn=True):
    NL = len(layers)
    LI = {l: i for i, l in enumerate(layers)}
    nc = bass.Bass("TRN2", target_bir_lowering=False)
    P = Plan()

    def din(name, shape, dt=F32):
        return nc.dram_tensor(name, list(shape), dt, kind="ExternalInput").ap()

    xs = din("xs", [TOK, D])
    cvec = din("cvec", [16, 128])
    ident = din("ident", [128, 128])
    ada_w = din("ada_w", [NL, D, 3072])
    ada_b = din("ada_b", [NL, 3072])
    gains = [din(n, [64, 128]) for n in ("mix_pre_g", "mix_post_g", "ffn_pre_g", "ffn_post_g")]
    w_gu = din("ffn_w_gu", [NL, D, 2 * DFF])
    w_dn = din("ffn_w_down", [NL, DFF, D])
    sgu_w_in = din("sgu_w_in", [1, D, 2 * D])
    sgu_ln_g = din("sgu_ln_g", [1, D])
    sgu_ln_b = din("sgu_ln_b", [1, D])
    sgu_w_s = din("sgu_w_s", [1, 16, 128, 128])
    sgu_b_s = din("sgu_b_s", [1, 16 * 128])
    sgu_w_out = din("sgu_w_out", [1, D, D])
    trimask = din("trimask", [128, 128])
    fox_w_in = din("fox_w_in", [2, D, FOX_IN])
    fox_b_f = din("fox_b_f", [2, 16])
    fox_w_out = din("fox_w_out", [2, D, D])
    swa_w_in = din("swa_w_in", [1, D, SWA_IN])
    swa_sinks = din("swa_sinks", [1, 32])
    swa_w_out = din("swa_w_out", [1, D, D])
    posin = din("posin", [1, TOK], I32)
    triu = din("triu", [128, 128])
    m_prev_in = din("m_prev", [128, 128])
    m_fox_in = din("m_fox", [128, 4 * 128])
    m_prev0_in = din("m_prev0", [128, 128])
    zo_in = din("zo", [128, 1024])
    oh4_in = din("oh4", [128, 4])
    sel5_in = din("sel5", [128, 5])
    pmat_in = din("pmat", [64, 16])
    invf_in = din("invf", [16, 1])
    yout = nc.dram_tensor("yout", [TOK, D], F32, kind="ExternalOutput").ap()

    xT_s = nc.dram_tensor("xT_s", [D, TOK], F32).ap()
    xT_v = xT_s.rearrange("(c p) t -> p c t", p=128)
    xT_b = [Buf("xT_s%d" % g) for g in range(NG)]
    xTw_b = [Buf("xTw_s%d" % g) for g in range(NG)]
    yout_b = Buf("yout")
    modin = [nc.dram_tensor("modin%d" % i, [128, 24], F32) for i in range(4)]
    modout = [nc.dram_tensor("modout%d" % i, [4 * 128, 24], F32) for i in range(4)]
    modin_b, modout_b = Buf("modin"), Buf("modout")
    q_s = nc.dram_tensor("q_s", [D, TOK], BF16).ap()
    q_b = Buf("q_s")
    att_s = nc.dram_tensor("att_s", [D, TOK], BF16).ap()
    att_b = Buf("att_s")
    RG = [[0, 1, 2, 3], [4, 5, 6, 7]]
    fdr = {}
    fdr["kin"] = [nc.dram_tensor("fkin%d" % i, [256, TOK], BF16) for i in range(8)]
    fdr["kout"] = [nc.dram_tensor("fkout%d" % i, [4 * 256, TOK], BF16) for i in range(8)]
    fdr["vin"] = [nc.dram_tensor("fvin%d" % i, [128, 2 * 16 * 128], BF16) for i in range(8)]
    fdr["vout"] = [nc.dram_tensor("fvout%d" % i, [4 * 128, 2 * 16 * 128], BF16) for i in range(8)]
    fdr["lin"] = nc.dram_tensor("flin", [TOK, 16], F32)
    fdr["lout"] = nc.dram_tensor("flout", [4 * TOK, 16], F32)
    fdr["b"] = {k: Buf("f" + k) for k in ("kin", "vin", "lin", "lout")}
    fdr["b"]["kout"] = [Buf("fkout%d" % i) for i in range(8)]
    fdr["b"]["vout"] = [Buf("fvout%d" % i) for i in range(8)]
    swa_dr = {}
    swa_dr["kin"] = [nc.dram_tensor("skin%d" % i, [256, TOK], BF16) for i in range(2)]
    swa_dr["kout"] = [nc.dram_tensor("skout%d" % i, [4 * 256, TOK], BF16) for i in range(2)]
    swa_dr["vin"] = [nc.dram_tensor("svin%d" % i, [128, 8 * 512], BF16) for i in range(2)]
    swa_dr["vout"] = [nc.dram_tensor("svout%d" % i, [4 * 128, 8 * 512], BF16) for i in range(2)]
    swa_dr["b"] = {k: Buf("s" + k) for k in ("kin", "kout", "vin", "vout")}

    arena = {"off": 16640}

    def sb(name, shape, dt):
        nbytes = int(np.prod(shape[1:])) * (4 if dt in (F32, I32) else 2)
        off = (arena["off"] + 31) // 32 * 32
        arena["off"] = off + nbytes
        assert arena["off"] <= 229344, (name, arena["off"])
        t = nc.alloc_sbuf_tensor_at(name, list(shape), dt, offset=off)
        return t

    ident_t = sb("ident_t", [128, 128], F32)
    ident_b = Buf("ident")
    ones_t = sb("ones_t", [128, 128], F32)
    ones_b = Buf("ones")
    eps_t = sb("eps_t", [128, 1], F32)
    one11 = ones_t
    cact_t = sb("cact_t", [128, 16], BF16)
    cact_b = Buf("cact")
    gains_t = sb("gains_t", [128, 4, 64], F32)
    gains_b = Buf("gains")
    modT = sb("modT", [128, 96], F32)
    modp_t = sb("modp_t", [128, 24], F32)
    modp_b = Buf("modp")
    modT_b = Buf("modT")
    der_t = sb("der_t", [128, 4, 16], F32)
    der_b = Buf("der")
    onesb_t = sb("onesb_t", [128, 128], BF16)
    cst_t = sb("cst_t", [128, 4], F32)
    xg_off = (arena["off"] + 31) // 32 * 32
    xg_t = sb("xg_t", [128, NCH, TG], F32)
    xg_b = Buf("xg")
    hT_t = sb("hT_t", [128, NCH, TG], BF16)
    hT_b = Buf("hT")
    yT_t = sb("yT_t", [128, NCH, TG], F32)
    yT_b = Buf("yT")
    big_off = (arena["off"] + 31) // 32 * 32
    big_t = sb("big_t", [128, NFC, TG], BF16)
    big_b = Buf("big")
    wsl = []
    for i in range(2):
        t = sb("wslot%d" % i, [128, 8192], BF16)
        wsl.append((t, Buf("wslot%d" % i)))
    wrot = Rot(wsl)
    attn_lim = arena["off"]
    sqs = Rot([(sb("sq%d" % i, [128, TG], BF16), Buf("sq%d" % i)) for i in range(4)])
    tmps = Rot([(sb("tmp%d" % i, [128, TG], F32), Buf("tmp%d" % i)) for i in range(2)])
    rstd_t = sb("rstd_t", [128, TG], F32)
    rstd_b = Buf("rstd")
    rt_t = sb("rt_t", [128, TG], F32)
    rt_b = Buf("rt")
    row_t = sb("row_t", [1, 512], F32)
    row_b = Buf("row")
    brow_t = sb("brow_t", [1, 512], F32)
    brow_b = Buf("brow")
    small_t = sb("small_t", [128, 64], F32)
    small_b = Buf("small")
    phase_base = arena["off"]

    es = ExitStack()
    psum = []
    for i in range(8):
        t = es.enter_context(nc.psum_tensor("ps%d" % i, [128, 512], F32))
        psum.append((t, Buf("ps%d" % i)))
    PG = Rot([psum[0], psum[2]])
    PU = Rot([psum[1], psum[3]])
    PY = Rot([psum[4], psum[5]])
    PY6 = Rot([psum[0], psum[1], psum[2], psum[3], psum[4], psum[5]])
    PSSQ = psum[6]
    PM = psum[7]

    mm = lambda out, lhsT, rhs, st, sp: (lambda h: h.matmul(out, lhsT, rhs, start=st, stop=sp))

    wcache = {}
    wcache_b = Buf("wcache")

    def load_w_cols(W2d, col0, ncols, slot, slot_b, dst_col0=0, width=None, kch=NCH, ckey=None, g=0):
        width = width or ncols
        view = slot[:, 0:kch * width].rearrange("p (c n) -> p c n", n=width)
        dst = view[:, :, dst_col0:dst_col0 + ncols]
        if ckey is not None and g > 0:
            cap = wcache[ckey]
            P.dma("pool", lambda h: h.dma_start(out=dst, in_=cap.rearrange("p (c n) -> p c n", n=ncols)),
                  reads=[wcache_b], writes=[slot_b])
            return view
        src = W2d.rearrange("(c p) n -> p c n", p=128)[:, :, col0:col0 + ncols]
        P.dma("pool", lambda h: h.dma_start(out=dst, in_=src), writes=[slot_b])
        if ckey is not None:
            cap = nc.dram_tensor("wc_%d" % len(wcache), [128, kch * ncols], BF16).ap()
            wcache[ckey] = cap
            P.dma("sp", lambda h: h.dma_start(out=cap.rearrange("p (c n) -> p c n", n=ncols), in_=dst),
                  reads=[slot_b], writes=[wcache_b])
        return view

    def ssq_rstd(src_t, src_b, src_fn=None):
        pst, psb = PSSQ
        if src_fn is None:
            src_fn = lambda c: src_t[:, c, :]
        for c in range(NCH):
            sq, sqb = sqs.next()
            if c % 2 == 0:
                P.op("act", lambda h, sq=sq, c=c: h.activation(out=sq[:, :], in_=src_fn(c), func=AF.Square),
                     reads=[src_b], writes=[sqb])
            else:
                P.op("dve", lambda h, sq=sq, c=c: h.tensor_tensor(out=sq[:, :], in0=src_fn(c), in1=src_fn(c), op=ALU.mult),
                     reads=[src_b], writes=[sqb])
            P.op("pe", mm(pst[:, :], onesb_t[:, :], sq[:, :], c == 0, c == NCH - 1),
                 reads=[ones_b, sqb], writes=[psb])
        P.op("act", lambda h: h.activation(out=rt_t[:, :], in_=pst[:, :], func=AF.Ln,
                                           bias=eps_t[:, 0:1], scale=1.0 / D),
             reads=[psb, ones_b], writes=[rt_b])
        P.op("act", lambda h: h.activation(out=rstd_t[:, :], in_=rt_t[:, :], func=AF.Exp, scale=-0.5),
             reads=[rt_b], writes=[rstd_b])

    def prenorm(acol, bcol):
        ssq_rstd(xg_t, xg_b)
        for c in range(NCH):
            tmp, tb = tmps.next()
            P.op("dve", lambda h, tmp=tmp, c=c: h.scalar_tensor_tensor(
                out=tmp[:, :], in0=xg_t[:, c, :], scalar=acol[:, c:c + 1], in1=rstd_t[:, :],
                op0=ALU.mult, op1=ALU.mult), reads=[xg_b, der_b, rstd_b], writes=[tb])
            P.op("act", lambda h, tmp=tmp, c=c: h.activation(
                out=hT_t[:, c, :], in_=tmp[:, :], func=AF.Identity, bias=bcol[:, c:c + 1], scale=1.0),
                reads=[tb, modT_b], writes=[hT_b])

    def postnorm_res(coef):
        ssq_rstd(yT_t, yT_b)
        for c in range(NCH):
            tmp, tb = tmps.next()
            P.op("dve", lambda h, tmp=tmp, c=c: h.scalar_tensor_tensor(
                out=tmp[:, :], in0=yT_t[:, c, :], scalar=coef[:, c:c + 1], in1=rstd_t[:, :],
                op0=ALU.mult, op1=ALU.mult), reads=[yT_b, der_b, rstd_b], writes=[tb])
            P.op("pool" if c % 2 == 0 else "dve", lambda h, tmp=tmp, c=c: h.tensor_tensor(
                out=xg_t[:, c, :], in0=xg_t[:, c, :], in1=tmp[:, :], op=ALU.add),
                reads=[tb, xg_b], writes=[xg_b])

    def load_xg(g):
        P.dma("sp", lambda h: h.dma_start(out=xg_t[:, :, :], in_=xT_v[:, :, g * TG:(g + 1) * TG]),
              reads=[xT_b[g], xTw_b[g]], writes=[xg_b])

    def store_xg(g):
        P.dma("sp", lambda h: h.dma_start(out=xT_v[:, :, g * TG:(g + 1) * TG], in_=xg_t[:, :, :]),
              reads=[xg_b], writes=[xT_b[g]])

    def linear_fm(W2d, col0, n_oc, rhs_fn, rhs_bufs, kch, evac, ocw=128, per_load=4, ckey=None, g=0):
        oc = 0
        while oc < n_oc:
            nl = min(per_load, n_oc - oc)
            slot, slot_b = wrot.next()
            view = load_w_cols(W2d, col0 + oc * ocw, nl * ocw, slot, slot_b, kch=kch,
                               ckey=None if ckey is None else (ckey, oc), g=g)
            for j in range(nl):
                pst, psb = PY6.next()
                for c in range(kch):
                    P.op("pe", mm(pst[0:ocw, :], view[:, c, j * ocw:(j + 1) * ocw], rhs_fn(c), c == 0, c == kch - 1),
                         reads=[slot_b] + rhs_bufs, writes=[psb])
                evac(oc + j, pst, psb)
            oc += nl

    P.dma("sp", lambda h: h.dma_start(out=ident_t[:, :], in_=ident[:, :]), writes=[ident_b])
    P.op("dve", lambda h: h.memset(ones_t[:, :], 1.0), writes=[ones_b])
    P.op("dve", lambda h: h.memset(eps_t[:, :], EPS), writes=[ones_b])
    P.op("dve", lambda h: h.memset(onesb_t[:, :], 1.0), writes=[ones_b])
    P.op("dve", lambda h: h.memset(cst_t[:, 0:1], -float(np.pi)), writes=[ones_b])
    for k in range(4):
        tmp, tb = tmps.next()
        P.dma("sp", lambda h, tmp=tmp, k=k: h.dma_start(out=tmp[0:64, 0:128], in_=gains[k][:, :]), writes=[tb])
        pst, psb = PM
        P.op("pe", lambda h, tmp=tmp: h.transpose(pst[:, 0:64], tmp[0:64, 0:128], ident_t[0:64, 0:64]),
             reads=[tb, ident_b], writes=[psb])
        P.op("dve", lambda h, k=k: h.tensor_copy(out=gains_t[:, k, :], in_=pst[:, 0:64]), reads=[psb], writes=[gains_b])
    tmp, tb = tmps.next()
    P.dma("sp", lambda h, tmp=tmp: h.dma_start(out=tmp[0:16, 0:128], in_=cvec[:, :]), writes=[tb])
    pst, psb = PM
    P.op("pe", lambda h, tmp=tmp: h.transpose(pst[:, 0:16], tmp[0:16, 0:128], ident_t[0:16, 0:16]),
         reads=[tb, ident_b], writes=[psb])
    P.op("act", lambda h: h.activation(out=cact_t[:, :], in_=pst[:, 0:16], func=AF.Silu), reads=[psb], writes=[cact_b])

    xblk = sb("xblk", [128, 4, D], F32) if False else None
    for g in range(NG):
        stage = yT_t[:, :, :].rearrange("p c t -> p (c t)").rearrange("p (b d) -> p b d", b=4)
        P.dma("sp", lambda h, g=g: h.dma_start(
            out=stage, in_=xs[g * TG:(g + 1) * TG, :].rearrange("(b p) d -> p b d", p=128)), writes=[yT_b])
        for c in range(NCH):
            pst, psb = PY.next()
            for b in range(4):
                P.op("pe", lambda h, pst=pst, b=b, c=c: h.transpose(
                    pst[:, b * 128:(b + 1) * 128], stage[:, b, c * 128:(c + 1) * 128], ident_t[:, :]),
                    reads=[yT_b, ident_b], writes=[psb])
            eng = "act" if c % 2 == 0 else "dve"
            if eng == "act":
                P.op("act", lambda h, pst=pst, c=c: h.activation(out=xg_t[:, c, :], in_=pst[:, :], func=AF.Copy),
                     reads=[psb], writes=[xg_b])
            else:
                P.op("dve", lambda h, pst=pst, c=c: h.tensor_copy(out=xg_t[:, c, :], in_=pst[:, :]),
                     reads=[psb], writes=[xg_b])
        store_xg(g)

    def compute_mod(l):
        for cg in range(6):
            slot, slot_b = wrot.next()
            view = load_w_cols(ada_w[LI[l]], cg * 512, 512, slot, slot_b)
            P.dma("sp", lambda h, cg=cg: h.dma_start(out=brow_t[0:1, :], in_=ada_b[LI[l]:LI[l] + 1, cg * 512:(cg + 1) * 512]),
                  writes=[brow_b])
            pst, psb = PY.next()
            for c in range(NCH):
                P.op("pe", mm(pst[0:1, :], cact_t[:, c:c + 1], view[:, c, :], c == 0, c == NCH - 1),
                     reads=[cact_b, slot_b], writes=[psb])
            P.op("dve", lambda h, pst=pst: h.tensor_tensor(out=row_t[0:1, :], in0=pst[0:1, :], in1=brow_t[0:1, :],
                                                           op=ALU.add), reads=[psb, brow_b], writes=[row_b])
            pm, pmb = PM
            for j in range(4):
                P.op("pe", mm(pm[:, j:j + 1], row_t[0:1, j * 128:(j + 1) * 128], one11[0:1, 0:1], True, True),
                     reads=[row_b, ones_b], writes=[pmb])
            P.op("dve", lambda h, cg=cg: h.tensor_copy(out=modp_t[:, cg * 4:(cg + 1) * 4], in_=pm[:, 0:4]),
                 reads=[pmb], writes=[modp_b])
        P.dma("sp", lambda h: h.dma_start(out=modin[l].ap(), in_=modp_t[:, :]), reads=[modp_b], writes=[modin_b])
        P.dma("pool", lambda h: h.collective_compute("AllGather", ALU.bypass, replica_groups=RG,
                                                     ins=[modin[l].ap().opt()], outs=[modout[l].ap().opt()]),
              reads=[modin_b], writes=[modout_b], inc=1)
        P.dma("sp", lambda h: h.dma_start(out=modT[:, :].rearrange("p (r c) -> p r c", r=4),
                                          in_=modout[l].ap().rearrange("(r p) c -> p r c", r=4)),
              reads=[modout_b], writes=[modT_b])
        for which, (sc_i, gate_i, pre_k, post_k) in enumerate(((1, 2, 0, 1), (4, 5, 2, 3))):
            P.op("dve", lambda h, sc_i=sc_i: h.tensor_scalar_add(out=small_t[:, 0:16], in0=modT[:, sc_i * 16:(sc_i + 1) * 16],
                                                                 scalar1=1.0), reads=[modT_b], writes=[small_b])
            P.op("dve", lambda h, which=which, pre_k=pre_k: h.tensor_tensor(
                out=der_t[:, 2 * which, :], in0=small_t[:, 0:16], in1=gains_t[:, pre_k, l * 16:(l + 1) * 16], op=ALU.mult),
                reads=[small_b, gains_b], writes=[der_b])
            P.op("dve", lambda h, which=which, gate_i=gate_i, post_k=post_k: h.tensor_tensor(
                out=der_t[:, 2 * which + 1, :], in0=modT[:, gate_i * 16:(gate_i + 1) * 16],
                in1=gains_t[:, post_k, l * 16:(l + 1) * 16], op=ALU.mult),
                reads=[modT_b, gains_b], writes=[der_b])

    def ffn_group(l, g):
        load_xg(g)
        prenorm(der_t[:, 2, :], modT[:, 48:64])
        for fc in range(NFC):
            slot, slot_b = wrot.next()
            view = load_w_cols(w_gu[LI[l]], fc * 128, 128, slot, slot_b, dst_col0=0, width=256)
            load_w_cols(w_gu[LI[l]], DFF + fc * 128, 128, slot, slot_b, dst_col0=128, width=256)
            pg, pgb = PG.next()
            pu, pub = PU.next()
            for c in range(NCH):
                P.op("pe", mm(pg[:, :], view[:, c, 0:128], hT_t[:, c, :], c == 0, c == NCH - 1),
                     reads=[slot_b, hT_b], writes=[pgb])
            for c in range(NCH):
                P.op("pe", mm(pu[:, :], view[:, c, 128:256], hT_t[:, c, :], c == 0, c == NCH - 1),
                     reads=[slot_b, hT_b], writes=[pub])
            tmp, tb = tmps.next()
            P.op("act", lambda h, pg=pg, tmp=tmp: h.activation(out=tmp[:, :], in_=pg[:, :], func=AF.Silu),
                 reads=[pgb], writes=[tb])
            P.op("dve", lambda h, pu=pu, tmp=tmp, fc=fc: h.tensor_tensor(
                out=big_t[:, fc, :], in0=tmp[:, :], in1=pu[:, :], op=ALU.mult), reads=[tb, pub], writes=[big_b])
        for dc in range(NCH):
            slot, slot_b = wrot.next()
            view = slot[:, 0:NFC * 128].rearrange("p (j o) -> p j o", o=128)
            src = w_dn[LI[l]].rearrange("(j p) o -> p j o", p=128)[:, :, dc * 128:(dc + 1) * 128]
            P.dma("pool", lambda h, view=view, src=src: h.dma_start(out=view, in_=src), writes=[slot_b])
            py, pyb = PY.next()
            for j in range(NFC):
                P.op("pe", mm(py[:, :], view[:, j, :], big_t[:, j, :], j == 0, j == NFC - 1),
                     reads=[slot_b, big_b], writes=[pyb])
            P.op("act", lambda h, py=py, dc=dc: h.activation(out=yT_t[:, dc, :], in_=py[:, :], func=AF.Copy),
                 reads=[pyb], writes=[yT_b])
        postnorm_res(der_t[:, 3, :])
        store_xg(g)

    TG2 = 1024
    assert attn_lim - xg_off >= 159744, (attn_lim, xg_off)
    XY = nc.alloc_sbuf_tensor_at("f_xy", [128, NCH, TG2], F32, offset=xg_off)
    H2 = nc.alloc_sbuf_tensor_at("f_h2", [128, NCH, TG2], BF16, offset=xg_off + 65536)
    A2 = nc.alloc_sbuf_tensor_at("f_a2", [128, 22, TG2], BF16, offset=xg_off + 98304)
    fws = [(nc.alloc_sbuf_tensor_at("f_w%d" % i, [128, 4096], BF16, offset=xg_off + 143360 + i * 8192), Buf("f_w%d" % i))
           for i in range(2)]
    fws += [(nc.alloc_sbuf_tensor_at("f_w%d" % (2 + i), [128, 4096], BF16, offset=phase_base + i * 8192), Buf("f_w%d" % (2 + i)))
            for i in range(2)]
    fwrot = Rot(fws)
    xins = Rot([(nc.alloc_sbuf_tensor_at("f_xin%d" % i, [128, 512], F32, offset=phase_base + 16384 + i * 2048), Buf("f_xin%d" % i))
                for i in range(4)])
    xouts = Rot([(nc.alloc_sbuf_tensor_at("f_xo%d" % i, [128, 512], F32, offset=phase_base + 24576 + i * 2048), Buf("f_xo%d" % i))
                 for i in range(4)])
    XY_b, H2_b, A2_b = Buf("f_xy"), Buf("f_h2"), Buf("f_a2")

    def ffn_big(l):
        acol, bcol, coef = der_t[:, 2, :], modT[:, 48:64], der_t[:, 3, :]
        Wgu = w_gu[LI[l]]
        Wdn = w_dn[LI[l]].rearrange("(j p) o -> p j o", p=128)
        for g2 in range(2):
            t0 = g2 * TG2
            gb = [2 * g2, 2 * g2 + 1]
            P.dma("sp", lambda h, t0=t0: h.dma_start(out=XY[:, :, :], in_=xT_v[:, :, t0:t0 + TG2]),
                  reads=[xT_b[gb[0]], xT_b[gb[1]], xTw_b[gb[0]], xTw_b[gb[1]]], writes=[XY_b])
            for th in range(2):
                hs = slice(th * 512, (th + 1) * 512)
                ssq_rstd(None, XY_b, src_fn=lambda c, hs=hs: XY[:, c, hs])
                for c in range(NCH):
                    tmp, tb = tmps.next()
                    P.op("dve", lambda h, tmp=tmp, c=c, hs=hs: h.scalar_tensor_tensor(
                        out=tmp[:, :], in0=XY[:, c, hs], scalar=acol[:, c:c + 1], in1=rstd_t[:, :],
                        op0=ALU.mult, op1=ALU.mult), reads=[XY_b, der_b, rstd_b], writes=[tb])
                    P.op("act", lambda h, tmp=tmp, c=c, hs=hs: h.activation(
                        out=H2[:, c, hs], in_=tmp[:, :], func=AF.Identity, bias=bcol[:, c:c + 1], scale=1.0),
                        reads=[tb, modT_b], writes=[H2_b])
            for fh in range(2):
                for fcl in range(22):
                    fc = fh * 22 + fcl
                    slot, slot_b = fwrot.next()
                    view = load_w_cols(Wgu, fc * 128, 128, slot, slot_b, dst_col0=0, width=256)
                    load_w_cols(Wgu, DFF + fc * 128, 128, slot, slot_b, dst_col0=128, width=256)
                    for th in range(2):
                        hs = slice(th * 512, (th + 1) * 512)
                        pg, pgb = PG.next()
                        pu, pub = PU.next()
                        for c in range(NCH):
                            P.op("pe", mm(pg[:, :], view[:, c, 0:128], H2[:, c, hs], c == 0, c == NCH - 1),
                                 reads=[slot_b, H2_b], writes=[pgb])
                        for c in range(NCH):
                            P.op("pe", mm(pu[:, :], view[:, c, 128:256], H2[:, c, hs], c == 0, c == NCH - 1),
                                 reads=[slot_b, H2_b], writes=[pub])
                        tmp, tb = tmps.next()
                        P.op("act", lambda h, pg=pg, tmp=tmp: h.activation(out=tmp[:, :], in_=pg[:, :], func=AF.Silu),
                             reads=[pgb], writes=[tb])
                        P.op("dve", lambda h, pu=pu, tmp=tmp, fcl=fcl, hs=hs: h.tensor_tensor(
                            out=A2[:, fcl, hs], in0=tmp[:, :], in1=pu[:, :], op=ALU.mult), reads=[tb, pub], writes=[A2_b])
                for dc in range(NCH):
                    slot, slot_b = fwrot.next()
                    view = slot[:, 0:22 * 128].rearrange("p (j o) -> p j o", o=128)
                    src = Wdn[:, fh * 22:(fh + 1) * 22, dc * 128:(dc + 1) * 128]
                    P.dma("pool", lambda h, view=view, src=src: h.dma_start(out=view, in_=src), writes=[slot_b])
                    for th in range(2):
                        hs = slice(th * 512, (th + 1) * 512)
                        py, pyb = PY.next()
                        for j in range(22):
                            P.op("pe", mm(py[:, :], view[:, j, :], A2[:, j, hs], j == 0, j == 21),
                                 reads=[slot_b, A2_b], writes=[pyb])
                        if fh == 0:
                            P.op("act", lambda h, py=py, dc=dc, hs=hs: h.activation(out=XY[:, dc, hs], in_=py[:, :], func=AF.Copy),
                                 reads=[pyb], writes=[XY_b])
                        else:
                            P.op("dve", lambda h, py=py, dc=dc, hs=hs: h.tensor_tensor(out=XY[:, dc, hs], in0=py[:, :],
                                                                                       in1=XY[:, dc, hs], op=ALU.add),
                                 reads=[pyb, XY_b], writes=[XY_b])
            for th in range(2):
                hs = slice(th * 512, (th + 1) * 512)
                gg = gb[th]
                ssq_rstd(None, XY_b, src_fn=lambda c, hs=hs: XY[:, c, hs])
                for c in range(NCH):
                    xin, xinb = xins.next()
                    xo, xob = xouts.next()
                    P.dma("sp", lambda h, xin=xin, c=c, gg=gg: h.dma_start(out=xin[:, :], in_=xT_v[:, c, gg * 512:(gg + 1) * 512]),
                          reads=[xT_b[gg]], writes=[xinb])
                    tmp, tb = tmps.next()
                    P.op("dve", lambda h, tmp=tmp, c=c, hs=hs: h.scalar_tensor_tensor(
                        out=tmp[:, :], in0=XY[:, c, hs], scalar=coef[:, c:c + 1], in1=rstd_t[:, :],
                        op0=ALU.mult, op1=ALU.mult), reads=[XY_b, der_b, rstd_b], writes=[tb])
                    P.op("pool" if c % 2 == 0 else "dve", lambda h, tmp=tmp, xin=xin, xo=xo: h.tensor_tensor(
                        out=xo[:, :], in0=xin[:, :], in1=tmp[:, :], op=ALU.add), reads=[tb, xinb], writes=[xob])
                    P.dma("sp", lambda h, xo=xo, c=c, gg=gg: h.dma_start(out=xT_v[:, c, gg * 512:(gg + 1) * 512], in_=xo[:, :]),
                          reads=[xob], writes=[xTw_b[gg]])

    def sgu_setup():
        st = {}
        st["wsT"] = sb("sgu_wsT", [128, 16, 128], BF16)
        st["bs"] = sb("sgu_bs", [128, 16, 128], F32)
        st["lng"] = sb("sgu_lng", [128, D], F32)
        st["lnb"] = sb("sgu_lnb", [128, D], F32)
        st["tri"] = sb("sgu_tri", [128, 128], F32)
        st["stat"] = sb("sgu_stat", [128, 8], F32)
        st["vtm"] = nc.alloc_sbuf_tensor_at("sgu_vtm", [128, 4, D], BF16, offset=big_off + 16 * TG * 2)
        st["b"] = {k: Buf("sgu_" + k) for k in ("wsT", "bs", "lng", "lnb", "vtm", "tri", "stat")}
        b = st["b"]
        P.dma("sp", lambda h: h.dma_start(out=st["tri"][:, :], in_=trimask[:, :]), writes=[b["tri"]])
        P.dma("sp", lambda h: h.dma_start(out=st["bs"][:, :, :].rearrange("p g t -> p (g t)"),
                                          in_=sgu_b_s[0, :].partition_broadcast(128)), writes=[b["bs"]])
        P.dma("sp", lambda h: h.dma_start(out=st["lng"][:, :], in_=sgu_ln_g[0, :].partition_broadcast(128)),
              writes=[b["lng"]])
        P.dma("sp", lambda h: h.dma_start(out=st["lnb"][:, :], in_=sgu_ln_b[0, :].partition_broadcast(128)),
              writes=[b["lnb"]])
        for gi in range(16):
            tmp, tb = tmps.next()
            P.dma("sp", lambda h, tmp=tmp, gi=gi: h.dma_start(out=tmp[:, 0:128], in_=sgu_w_s[0, gi, :, :]), writes=[tb])
            P.op("dve", lambda h, tmp=tmp: h.tensor_tensor(out=tmp[:, 128:256], in0=tmp[:, 0:128], in1=st["tri"][:, :],
                                                           op=ALU.mult), reads=[tb, b["tri"]], writes=[tb])
            pst, psb = PM
            P.op("pe", lambda h, tmp=tmp, pst=pst: h.transpose(pst[:, 0:128], tmp[:, 128:256], ident_t[:, :]),
                 reads=[tb, ident_b], writes=[psb])
            P.op("dve", lambda h, gi=gi, pst=pst: h.tensor_copy(out=st["wsT"][:, gi, :], in_=pst[:, 0:128]),
                 reads=[psb], writes=[b["wsT"]])
        return st

    def sgu_group(st, g):
        b = st["b"]
        load_xg(g)
        prenorm(der_t[:, 0, :], modT[:, 0:16])
        W = sgu_w_in[0]
        def evac_u(oc, pst, psb):
            P.op("act", lambda h: h.activation(out=big_t[:, oc, :], in_=pst[:, :], func=AF.Gelu),
                 reads=[psb], writes=[big_b])
        linear_fm(W, 0, 16, lambda c: hT_t[:, c, :], [hT_b], NCH, evac_u, ckey="sgu", g=g)
        zv4 = yT_t[:, :, :].rearrange("p c t -> p (c t)").rearrange("p (b d) -> p b d", b=4)
        for cg in range(4):
            slot, slot_b = wrot.next()
            view = load_w_cols(W, D + cg * 512, 512, slot, slot_b, ckey=("sgv", cg), g=g)
            for blk in range(4):
                pst, psb = PY.next()
                for c in range(NCH):
                    P.op("pe", mm(pst[:, :], hT_t[:, c, blk * 128:(blk + 1) * 128], view[:, c, :], c == 0, c == NCH - 1),
                         reads=[hT_b, slot_b], writes=[psb])
                P.op("act", lambda h, pst=pst, cg=cg, blk=blk: h.activation(
                    out=zv4[:, blk, cg * 512:(cg + 1) * 512], in_=pst[:, :], func=AF.Gelu), reads=[psb], writes=[yT_b])
        stat = st["stat"]
        for blk in range(4):
            zv = zv4[:, blk, :]
            P.op("dve", lambda h, zv=zv: h.tensor_reduce(out=stat[:, 0:1], in_=zv, axis=mybir.AxisListType.X, op=ALU.add),
                 reads=[yT_b], writes=[b["stat"]])
            P.op("act", lambda h, zv=zv, blk=blk: h.activation(out=st["vtm"][:, blk, :], in_=zv, func=AF.Square,
                                                               accum_out=stat[:, 1:2]),
                 reads=[yT_b], writes=[b["stat"], b["vtm"]])
            P.op("dve", lambda h: h.tensor_scalar_mul(out=stat[:, 2:3], in0=stat[:, 0:1], scalar1=1.0 / D),
                 reads=[b["stat"]], writes=[b["stat"]])
            P.op("dve", lambda h: h.tensor_tensor(out=stat[:, 3:4], in0=stat[:, 2:3], in1=stat[:, 2:3], op=ALU.mult),
                 reads=[b["stat"]], writes=[b["stat"]])
            P.op("dve", lambda h: h.scalar_tensor_tensor(out=stat[:, 4:5], in0=stat[:, 1:2], scalar=1.0 / D,
                                                         in1=stat[:, 3:4], op0=ALU.mult, op1=ALU.subtract),
                 reads=[b["stat"]], writes=[b["stat"]])
            P.op("act", lambda h: h.activation(out=stat[:, 5:6], in_=stat[:, 4:5], func=AF.Sqrt, bias=eps_t[:, 0:1],
                                               scale=1.0), reads=[b["stat"], ones_b], writes=[b["stat"]])
            P.op("dve", lambda h: h.reciprocal(out=stat[:, 6:7], in_=stat[:, 5:6]), reads=[b["stat"]], writes=[b["stat"]])
            P.op("dve", lambda h, zv=zv: h.tensor_scalar(out=zv, in0=zv, scalar1=stat[:, 2:3], scalar2=stat[:, 6:7],
                                                         op0=ALU.subtract, op1=ALU.mult), reads=[b["stat"], yT_b], writes=[yT_b])
            P.op("pool", lambda h, zv=zv: h.tensor_tensor(out=zv, in0=zv, in1=st["lng"][:, :], op=ALU.mult),
                 reads=[yT_b, b["lng"]], writes=[yT_b])
            P.op("dve", lambda h, zv=zv, blk=blk: h.tensor_tensor(out=st["vtm"][:, blk, :], in0=zv, in1=st["lnb"][:, :],
                                                                  op=ALU.add), reads=[yT_b, b["lnb"]], writes=[b["vtm"]])
        for gi in range(16):
            pst, psb = PY.next()
            for blk in range(4):
                P.op("pe", mm(pst[:, blk * 128:(blk + 1) * 128], st["vtm"][:, blk, gi * 128:(gi + 1) * 128],
                              st["wsT"][:, gi, :], True, True), reads=[b["vtm"], b["wsT"]], writes=[psb])
            tmp, tb = tmps.next()
            P.op("dve", lambda h, pst=pst, tmp=tmp, gi=gi: h.tensor_tensor(
                out=tmp[:, :].rearrange("p (b t) -> p b t", b=4), in0=pst[:, :].rearrange("p (b t) -> p b t", b=4),
                in1=st["bs"][:, gi:gi + 1, :].broadcast_to([128, 4, 128]), op=ALU.add),
                reads=[psb, b["bs"]], writes=[tb])
            P.op("pool", lambda h, tmp=tmp, gi=gi: h.tensor_tensor(out=big_t[:, gi, :], in0=big_t[:, gi, :], in1=tmp[:, :],
                                                                   op=ALU.mult), reads=[tb, big_b], writes=[big_b])
        def evac_y(oc, pst, psb):
            P.op("act", lambda h: h.activation(out=yT_t[:, oc, :], in_=pst[:, :], func=AF.Copy),
                 reads=[psb], writes=[yT_b])
        linear_fm(sgu_w_out[0], 0, 16, lambda c: big_t[:, c, :], [big_b], NCH, evac_y, ckey="sgo", g=g)
        postnorm_res(der_t[:, 1, :])
        store_xg(g)

    def at_alloc(state, name, shape, dt):
        nbytes = int(np.prod(shape[1:])) * (4 if dt in (F32, I32) else 2)
        off = (state["off"] + 31) // 32 * 32
        state["off"] = off + nbytes
        assert state["off"] <= attn_lim, (name, state["off"], attn_lim)
        state["n"] += 1
        return nc.alloc_sbuf_tensor_at("%s_%d" % (name, state["n"]), list(shape), dt, offset=off)

    def evac_to(dst_fn, dst_buf, func=None, eng="act"):
        def ev(oc, pst, psb):
            P.op("act", lambda h: h.activation(out=dst_fn(oc), in_=pst[0:dst_fn(oc).shape[0], :], func=AF.Copy),
                 reads=[psb], writes=[dst_buf])
        return ev

    def attn_out_phase(W2d, ckey=None):
        for g in range(NG):
            P.dma("sp", lambda h, g=g: h.dma_start(
                out=big_t[:, 0:16, :], in_=att_s.rearrange("(c p) t -> p c t", p=128)[:, :, g * TG:(g + 1) * TG]),
                reads=[att_b], writes=[big_b])
            load_xg(g)
            linear_fm(W2d, 0, 16, lambda c: big_t[:, c, :], [big_b], NCH,
                      evac_to(lambda oc: yT_t[:, oc, :], yT_b), ckey=ckey, g=g)
            postnorm_res(der_t[:, 1, :])
            store_xg(g)

    def all_gather(src, src_b, dst, dst_b):
        P.dma("pool", lambda h: h.collective_compute("AllGather", ALU.bypass, replica_groups=RG,
                                                     ins=[src.ap().opt()], outs=[dst.ap().opt()]),
              reads=[src_b], writes=[dst_b], inc=1)

    fox_cache = {}

    def fox_layer(l, j):
        dr = fdr
        db = dr["b"]
        W = fox_w_in[j]
        kin_v = [t.ap().rearrange("(c p) t -> p c t", p=128) for t in dr["kin"]]
        vin_v = [t.ap().rearrange("p (h i d) -> p h i d", h=2, i=16) for t in dr["vin"]]
        lin_v = dr["lin"].ap().rearrange("(i p) h -> p i h", p=128)
        ph = {"off": phase_base, "n": 100 * l}
        bf_t = sb("fox_bf%d" % l, [128, 16], F32)
        vst_t = sb("fox_vst%d" % l, [128, 4, 512], BF16)
        lf_t = sb("fox_lf%d" % l, [128, 4, 16], F32)
        bf_b, vst_b, lf_b = fox_cache.setdefault("p1", (Buf("bf"), Buf("vst"), Buf("lf")))
        P.dma("sp", lambda h: h.dma_start(out=bf_t[:, :], in_=fox_b_f[j, :].partition_broadcast(128)), writes=[bf_b])
        for g in range(NG):
            load_xg(g)
            prenorm(der_t[:, 0, :], modT[:, 0:16])
            linear_fm(W, 0, 16, lambda c: hT_t[:, c, :], [hT_b], NCH, evac_to(lambda oc: big_t[:, oc, :], big_b),
                      ckey=("fq", j), g=g)
            P.dma("sp", lambda h, g=g: h.dma_start(
                out=q_s.rearrange("(c p) t -> p c t", p=128)[:, :, g * TG:(g + 1) * TG], in_=big_t[:, 0:16, :]),
                reads=[big_b], writes=[q_b])
            linear_fm(W, D, 16, lambda c: hT_t[:, c, :], [hT_b], NCH, evac_to(lambda oc: big_t[:, 16 + oc, :], big_b),
                      ckey=("fk", j), g=g)
            for ck in range(8):
                P.dma("sp", lambda h, g=g, ck=ck: h.dma_start(out=kin_v[ck][:, :, g * TG:(g + 1) * TG],
                                                               in_=big_t[:, 16 + 2 * ck:18 + 2 * ck, :]),
                      reads=[big_b], writes=[db["kin"]])
            for cg in range(4):
                slot, slot_b = wrot.next()
                view = load_w_cols(W, 2 * D + cg * 512, 512, slot, slot_b, ckey=("fv", j, cg), g=g)
                for blk in range(4):
                    pst, psb = PY.next()
                    for c in range(NCH):
                        P.op("pe", mm(pst[:, :], hT_t[:, c, blk * 128:(blk + 1) * 128], view[:, c, :], c == 0, c == NCH - 1),
                             reads=[hT_b, slot_b], writes=[psb])
                    P.op("act", lambda h, pst=pst, blk=blk: h.activation(out=vst_t[:, blk, :], in_=pst[:, :], func=AF.Copy),
                         reads=[psb], writes=[vst_b])
                for hh in range(4):
                    P.dma("sp", lambda h, g=g, cg=cg, hh=hh: h.dma_start(
                        out=vin_v[(cg * 4 + hh) // 2][:, (cg * 4 + hh) % 2, g * 4:(g + 1) * 4, :],
                        in_=vst_t[:, :, hh * 128:(hh + 1) * 128]), reads=[vst_b], writes=[db["vin"]])
            slot, slot_b = wrot.next()
            view = load_w_cols(W, 3 * D, 16, slot, slot_b, ckey=("ffg", j), g=g)
            for blk in range(4):
                pst, psb = PY.next()
                for c in range(NCH):
                    P.op("pe", mm(pst[:, 0:16], hT_t[:, c, blk * 128:(blk + 1) * 128], view[:, c, :], c == 0, c == NCH - 1),
                         reads=[hT_b, slot_b], writes=[psb])
                P.op("dve", lambda h, pst=pst, blk=blk: h.tensor_tensor(out=lf_t[:, blk, :], in0=pst[:, 0:16], in1=bf_t[:, :],
                                                                        op=ALU.add), reads=[psb, bf_b], writes=[lf_b])
            lf2 = lf_t[:, :, :].rearrange("p b h -> p (b h)")
            P.op("act", lambda h: h.activation(out=lf2, in_=lf2, func=AF.Exp, scale=-1.0), reads=[lf_b], writes=[lf_b])
            P.op("act", lambda h: h.activation(out=lf2, in_=lf2, func=AF.Ln, bias=1.0, scale=1.0), reads=[lf_b], writes=[lf_b])
            P.op("dve", lambda h: h.tensor_scalar_mul(out=lf2, in0=lf2, scalar1=-1.0), reads=[lf_b], writes=[lf_b])
            P.dma("sp", lambda h, g=g: h.dma_start(out=lin_v[:, g * 4:(g + 1) * 4, :], in_=lf_t[:, :, :]),
                  reads=[lf_b], writes=[db["lin"]])
        all_gather(dr["lin"], db["lin"], dr["lout"], db["lout"])
        for ck in range(8):
            all_gather(dr["kin"][ck], db["kin"], dr["kout"][ck], db["kout"][ck])
            all_gather(dr["vin"][ck], db["vin"], dr["vout"][ck], db["vout"][ck])
        P.barrier(exclude=db["kout"] + db["vout"])
        A = {"off": xg_off, "n": 1000 * (l + 1)}
        lfa = at_alloc(A, "lfa", [128, 64, 16], F32)
        ftm = at_alloc(A, "ftm", [128, 64, 16], F32)
        tbc = at_alloc(A, "tbc", [128, 64, 16], F32)
        pfa = at_alloc(A, "pfa", [128, 64, 16], F32)
        pfb = at_alloc(A, "pfb", [128, 64, 16], F32)
        fref = at_alloc(A, "fref", [128, 16, 16], F32)
        triu_t = at_alloc(A, "triu", [128, 128], F32)
        oh4_t = at_alloc(A, "oh4", [128, 4], F32)
        mfox_t = at_alloc(A, "mfox", [128, 4, 128], BF16)
        zo_t = at_alloc(A, "zo", [128, 16, 64], F32)
        qh = [at_alloc(A, "qh%d" % i, [128, TOK], BF16) for i in range(2)]
        kh = [at_alloc(A, "kh%d" % i, [128, 4, TOK], BF16) for i in range(2)]
        vh = [at_alloc(A, "vh%d" % i, [128, 4, 16, 128], BF16) for i in range(2)]
        bh = [at_alloc(A, "bh%d" % i, [128, 16, 64], F32) for i in range(2)]
        ah = [at_alloc(A, "ah%d" % i, [128, TOK], BF16) for i in range(2)]
        pts = Rot([(at_alloc(A, "pt%d" % i, [128, 512], BF16), [Buf("pt%d_%d" % (i, m)) for m in range(4)]) for i in range(4)])
        rec_t = at_alloc(A, "rec", [128, 512], F32)
        B_ = fox_cache.setdefault("B_", {k: Buf("fx_" + k) for k in (
            "lfa", "ftm", "tbc", "pfa", "pfb", "fref", "triu", "oh4", "mfox", "rec",
            "qh0", "qh1", "kh0", "kh1", "vh0", "vh1", "bh0", "bh1", "ah0", "ah1")})
        P.dma("sp", lambda h: h.dma_start(out=triu_t[:, :], in_=triu[:, :]), writes=[B_["triu"]])
        P.dma("sp", lambda h: h.dma_start(out=oh4_t[:, :], in_=oh4_in[:, :]), writes=[B_["oh4"]])
        P.dma("sp", lambda h: h.dma_start(out=zo_t[:, :, :].rearrange("p a b -> p (a b)"), in_=zo_in[:, :]), writes=[B_["oh4"]])
        P.dma("pool", lambda h: h.dma_start(out=mfox_t[:, :, :].rearrange("p j q -> p (j q)"), in_=m_fox_in[:, :]),
              writes=[B_["mfox"]])
        lout_ap = dr["lout"].ap()
        for rr in range(4):
            P.dma("sp", lambda h, rr=rr: h.dma_start(
                out=lfa[:, :, :].rearrange("p (i r) h -> p r i h", r=4)[:, rr],
                in_=lout_ap[rr * TOK:(rr + 1) * TOK, :].rearrange("(i p) h -> p i h", p=128)),
                reads=[db["lout"]], writes=[B_["lfa"]])
        lfa2 = lfa[:, :, :].rearrange("p g h -> p (g h)")
        ftm2 = ftm[:, :, :].rearrange("p g h -> p (g h)")
        tbc2 = tbc[:, :, :].rearrange("p g h -> p (g h)")
        pfa2 = pfa[:, :, :].rearrange("p g h -> p (g h)")
        pfb2 = pfb[:, :, :].rearrange("p g h -> p (g h)")
        for half in range(2):
            cs = slice(half * 512, (half + 1) * 512)
            pst, psb = PY.next()
            P.op("pe", mm(pst[:, :], triu_t[:, :], lfa2[:, cs], True, True), reads=[B_["triu"], B_["lfa"]], writes=[psb])
            P.op("dve", lambda h, pst=pst, cs=cs: h.tensor_copy(out=ftm2[:, cs], in_=pst[:, :]), reads=[psb], writes=[B_["ftm"]])
            pst, psb = PY.next()
            P.op("pe", mm(pst[:, :], ones_t[:, :], lfa2[:, cs], True, True), reads=[ones_b, B_["lfa"]], writes=[psb])
            P.op("dve", lambda h, pst=pst, cs=cs: h.tensor_copy(out=tbc2[:, cs], in_=pst[:, :]), reads=[psb], writes=[B_["tbc"]])
        P.op("dve", lambda h: h.tensor_copy(out=pfa2, in_=tbc2), reads=[B_["tbc"]], writes=[B_["pfa"]])
        cur, curb, oth, othb = pfa2, B_["pfa"], pfb2, B_["pfb"]
        sh = 1
        while sh < 64:
            w = sh * 16
            P.op("dve", lambda h, cur=cur, oth=oth, w=w: h.tensor_copy(out=oth[:, 0:w], in_=cur[:, 0:w]),
                 reads=[curb], writes=[othb])
            P.op("dve", lambda h, cur=cur, oth=oth, w=w: h.tensor_tensor(out=oth[:, w:1024], in0=cur[:, w:1024],
                                                                         in1=cur[:, 0:1024 - w], op=ALU.add),
                 reads=[curb], writes=[othb])
            cur, curb, oth, othb = oth, othb, cur, curb
            sh *= 2
        P.op("dve", lambda h, cur=cur, oth=oth: h.tensor_tensor(out=oth, in0=cur, in1=tbc2, op=ALU.subtract),
             reads=[curb, B_["tbc"]], writes=[othb])
        P.op("dve", lambda h, oth=oth: h.tensor_tensor(out=ftm2, in0=ftm2, in1=oth, op=ALU.add),
             reads=[othb, B_["ftm"]], writes=[B_["ftm"]])
        P.op("dve", lambda h, cur=cur, oth=oth: h.scalar_tensor_tensor(out=oth, in0=tbc2, scalar=-0.5, in1=cur,
                                                                        op0=ALU.mult, op1=ALU.add),
             reads=[curb, B_["tbc"]], writes=[othb])
        fmid4 = (pfa if oth is pfa2 else pfb)[:, :, :].rearrange("p (i r) h -> p r i h", r=4)
        P.op("dve", lambda h: h.tensor_scalar(out=fref[:, :, :], in0=fmid4[:, 0], scalar1=oh4_t[:, 0:1], scalar2=None,
                                              op0=ALU.mult), reads=[othb, B_["oh4"]], writes=[B_["fref"]])
        for rr in range(1, 4):
            P.op("dve", lambda h, rr=rr: h.scalar_tensor_tensor(out=fref[:, :, :], in0=fmid4[:, rr], scalar=oh4_t[:, rr:rr + 1],
                                                                in1=fref[:, :, :], op0=ALU.mult, op1=ALU.add),
                 reads=[othb, B_["oh4"], B_["fref"]], writes=[B_["fref"]])
        kout_v = [t.ap().rearrange("(r c d) t -> d r c t", r=4, d=128) for t in dr["kout"]]
        vout_v = [t.ap().rearrange("(r p) (h i d) -> p r h i d", r=4, h=2, i=16) for t in dr["vout"]]
        PS_ST = Rot([psum[0], psum[1], psum[2], psum[7]])
        PS_O = Rot([psum[3], psum[4]])
        PS_D = Rot([psum[5], psum[6]])
        scale = 128.0 ** -0.5
        def head_res(hd):
            s2 = hd % 2
            return (qh[s2], kh[s2], vh[s2], bh[s2], ah[s2],
                    B_["qh%d" % s2], B_["kh%d" % s2], B_["vh%d" % s2], B_["bh%d" % s2], B_["ah%d" % s2])

        def emit_head_loads(hd):
            qt, kt, vt, bt, at, qb, kb, vb, bb, ab = head_res(hd)
            P.dma("sp", lambda h: h.dma_start(out=qt[:, :], in_=q_s[hd * 128:(hd + 1) * 128, :]), reads=[q_b], writes=[qb])
            P.dma("sp", lambda h: h.dma_start(out=kt[:, :, :], in_=kout_v[hd // 2][:, :, hd % 2, :]),
                  reads=[db["kout"][hd // 2]], writes=[kb])
            P.dma("sp", lambda h: h.dma_start(out=vt[:, :, :, :], in_=vout_v[hd // 2][:, :, hd % 2]),
                  reads=[db["vout"][hd // 2]], writes=[vb])
            P.op("dve", lambda h: h.tensor_tensor(
                out=bt[:, :, :], in0=fref[:, :, hd:hd + 1].broadcast_to([128, 16, 64]),
                in1=ftm[:, :, hd:hd + 1].rearrange("p g o -> p o g").broadcast_to([128, 16, 64]), op=ALU.subtract),
                reads=[B_["fref"], B_["ftm"]], writes=[bb])
            P.op("dve", lambda h: h.tensor_tensor(out=bt[:, :, :], in0=bt[:, :, :], in1=zo_t[:, :, :], op=ALU.add),
                 reads=[bb, B_["oh4"]], writes=[bb])

        steps = [(hd, jq, gk) for hd in range(16) for jq in range(4) for gk in range(16 * (jq + 1))]
        nst = len(steps)
        qk = {}
        acc = {}

        def emit_qk(idx):
            hd, jq, gk = steps[idx]
            qt, kt, vt, bt, at, qb, kb, vb, bb, ab = head_res(hd)
            rr, ii = gk % 4, gk // 4
            mmin = max(0, -(-(gk - 3 - 16 * jq) // 4))
            c0 = mmin * 128
            pst, psb = PS_ST.next()
            P.op("pe", mm(pst[:, c0:512], kt[:, rr, ii * 128:(ii + 1) * 128], qt[:, jq * 512 + c0:(jq + 1) * 512],
                          True, True), reads=[kb, qb], writes=[psb])
            qk[idx] = (pst, psb, mmin, c0)

        def emit_exp(idx):
            hd, jq, gk = steps[idx]
            qt, kt, vt, bt, at, qb, kb, vb, bb, ab = head_res(hd)
            pst, psb, mmin, c0 = qk[idx]
            pt, ptb = pts.next()
            for m in range(mmin, 4):
                li = 4 * jq + m
                P.op("act", lambda h, m=m, li=li: h.activation(
                    out=pt[:, m * 128:(m + 1) * 128], in_=pst[:, m * 128:(m + 1) * 128], func=AF.Exp,
                    bias=bt[:, li, gk:gk + 1], scale=scale), reads=[psb, bb], writes=[ptb[m]])
                jm = gk - 4 * li
                if 0 <= jm <= 3:
                    P.op("pool", lambda h, m=m, jm=jm: h.tensor_tensor(
                        out=pt[:, m * 128:(m + 1) * 128], in0=pt[:, m * 128:(m + 1) * 128], in1=mfox_t[:, jm, :],
                        op=ALU.mult), reads=[ptb[m], B_["mfox"]], writes=[ptb[m]])
            qk[idx] = (pst, psb, mmin, c0, pt, ptb)

        def emit_pv(idx):
            hd, jq, gk = steps[idx]
            qt, kt, vt, bt, at, qb, kb, vb, bb, ab = head_res(hd)
            pst, psb, mmin, c0, pt, ptb = qk.pop(idx)
            rr, ii = gk % 4, gk // 4
            ng = 16 * (jq + 1)
            if gk == 0:
                acc[(hd, jq)] = (PS_O.next(), PS_D.next())
            (po, pob), (pd, pdb) = acc[(hd, jq)]
            P.op("pe", mm(po[:, c0:512], vt[:, rr, ii, :], pt[:, c0:512], gk == 0, gk == ng - 1),
                 reads=[vb] + ptb[mmin:], writes=[pob])
            P.op("pe", mm(pd[:, c0:512], onesb_t[:, :], pt[:, c0:512], gk == 0, gk == ng - 1),
                 reads=[ones_b] + ptb[mmin:], writes=[pdb])
            if gk == ng - 1:
                del acc[(hd, jq)]
                P.op("dve", lambda h: h.reciprocal(out=rec_t[:, :], in_=pd[:, :]), reads=[pdb], writes=[B_["rec"]])
                P.op("dve", lambda h: h.tensor_tensor(out=at[:, jq * 512:(jq + 1) * 512], in0=po[:, :],
                                                      in1=rec_t[:, :], op=ALU.mult),
                     reads=[pob, B_["rec"]], writes=[ab])
                if jq == 3:
                    P.dma("sp", lambda h: h.dma_start(out=att_s[hd * 128:(hd + 1) * 128, :], in_=at[:, :]),
                          reads=[ab], writes=[att_b])
                    if hd + 2 < 16:
                        emit_head_loads(hd + 2)

        emit_head_loads(0)
        emit_head_loads(1)
        emit_qk(0)
        emit_qk(1)
        emit_qk(2)
        for idx in range(nst):
            emit_exp(idx)
            if idx + 3 < nst:
                emit_qk(idx + 3)
            emit_pv(idx)
        P.barrier()
        attn_out_phase(fox_w_out[j], ckey=("fo", j))
        P.barrier()

    def swa_layer(l):
        dr = swa_dr
        db = dr["b"]
        W = swa_w_in[0]
        kin_v = [t.ap().rearrange("(h d) t -> d h t", d=64) for t in dr["kin"]]
        vin_v = [t.ap().rearrange("p (i f) -> p i f", i=8) for t in dr["vin"]]
        q_v = q_s.rearrange("(h d) t -> d h t", d=64)
        att_v = att_s.rearrange("(h d) t -> d h t", d=64)
        c16 = sb("swa_c16", [16, TOK], F32)
        s16 = sb("swa_s16", [16, TOK], F32)
        pmat_t = sb("swa_pmat", [64, 16], BF16)
        invf_t = sb("swa_invf", [16, 1], F32)
        vst_t = sb("swa_vst", [128, 4, 512], BF16)
        SB_ = {k: Buf("sw_" + k) for k in ("c16", "s16", "pmat", "invf", "vst", "posi")}
        posi_t = sb("swa_posi", [16, TOK], I32)
        P.dma("sp", lambda h: h.dma_start(out=posi_t[:, :], in_=posin[0, :].partition_broadcast(16)), writes=[SB_["posi"]])
        P.dma("pool", lambda h: h.dma_start(out=pmat_t[:, :], in_=pmat_in[:, :]), writes=[SB_["pmat"]])
        P.dma("sp", lambda h: h.dma_start(out=invf_t[:, :], in_=invf_in[:, :]), writes=[SB_["invf"]])
        PI = float(np.pi)
        C1 = 6.28125
        C2 = float(2 * np.pi - 6.28125)
        ang = yT_t[0:16, 8:12, :].rearrange("p c t -> p (c t)")
        nf = yT_t[0:16, 0:4, :].rearrange("p c t -> p (c t)")
        mk = yT_t[0:16, 4:8, :].rearrange("p c t -> p (c t)")
        ys, yc = s16[:, :], c16[:, :]
        RW = dict(reads=[SB_["posi"], SB_["invf"], yT_b, SB_["s16"], SB_["c16"]],
                  writes=[yT_b, SB_["s16"], SB_["c16"], SB_["posi"]])
        P.op("dve", lambda h: h.tensor_copy(out=ang, in_=posi_t[:, :]), **RW)
        P.op("dve", lambda h: h.tensor_scalar_mul(out=ang, in0=ang, scalar1=invf_t[:, 0:1]), **RW)
        P.op("dve", lambda h: h.tensor_scalar_mul(out=nf, in0=ang, scalar1=float(1.0 / (2 * np.pi))), **RW)
        P.op("dve", lambda h: h.tensor_copy(out=posi_t[:, :], in_=nf), **RW)
        P.op("dve", lambda h: h.tensor_copy(out=nf, in_=posi_t[:, :]), **RW)
        P.op("dve", lambda h: h.scalar_tensor_tensor(out=ys, in0=nf, scalar=-C1, in1=ang, op0=ALU.mult, op1=ALU.add), **RW)
        P.op("dve", lambda h: h.scalar_tensor_tensor(out=ys, in0=nf, scalar=-C2, in1=ys, op0=ALU.mult, op1=ALU.add), **RW)
        P.op("dve", lambda h: h.tensor_single_scalar(out=mk, in_=ys, scalar=PI, op=ALU.is_gt), **RW)
        P.op("dve", lambda h: h.scalar_tensor_tensor(out=ys, in0=mk, scalar=-2 * PI, in1=ys, op0=ALU.mult, op1=ALU.add), **RW)
        P.op("dve", lambda h: h.tensor_single_scalar(out=mk, in_=ys, scalar=-PI, op=ALU.is_lt), **RW)
        P.op("dve", lambda h: h.scalar_tensor_tensor(out=ys, in0=mk, scalar=2 * PI, in1=ys, op0=ALU.mult, op1=ALU.add), **RW)
        P.op("dve", lambda h: h.tensor_scalar_add(out=yc, in0=ys, scalar1=PI / 2), **RW)
        P.op("dve", lambda h: h.tensor_single_scalar(out=mk, in_=yc, scalar=PI, op=ALU.is_gt), **RW)
        P.op("dve", lambda h: h.scalar_tensor_tensor(out=yc, in0=mk, scalar=-2 * PI, in1=yc, op0=ALU.mult, op1=ALU.add), **RW)
        P.op("act", lambda h: h.activation(out=ys, in_=ys, func=AF.Sin), reads=[SB_["s16"]], writes=[SB_["s16"]])
        P.op("act", lambda h: h.activation(out=yc, in_=yc, func=AF.Sin), reads=[SB_["c16"]], writes=[SB_["c16"]])

        def rope(tile_fn, nheads, g):
            for hh in range(nheads):
                pst, psb = PM
                P.op("pe", mm(pst[0:16, :], pmat_t[:, :], tile_fn(hh), True, True), reads=[SB_["pmat"], big_b], writes=[psb])
                t1, t1b = tmps.next()
                t2, t2b = tmps.next()
                P.op("dve", lambda h, t1=t1, hh=hh: h.tensor_tensor(out=t1[0:16, :], in0=tile_fn(hh)[0:16, :],
                                                                    in1=c16[:, g * TG:(g + 1) * TG], op=ALU.mult),
                     reads=[big_b, SB_["c16"]], writes=[t1b])
                P.op("dve", lambda h, t2=t2, pst=pst: h.tensor_tensor(out=t2[0:16, :], in0=pst[0:16, :],
                                                                      in1=s16[:, g * TG:(g + 1) * TG], op=ALU.mult),
                     reads=[psb, SB_["s16"]], writes=[t2b])
                P.op("pool", lambda h, t1=t1, t2=t2, hh=hh: h.tensor_tensor(out=tile_fn(hh)[0:16, :], in0=t1[0:16, :],
                                                                            in1=t2[0:16, :], op=ALU.add),
                     reads=[t1b, t2b], writes=[big_b])

        for g in range(NG):
            load_xg(g)
            prenorm(der_t[:, 0, :], modT[:, 0:16])
            linear_fm(W, 0, 32, lambda c: hT_t[:, c, :], [hT_b], NCH, evac_to(lambda oc: big_t[0:64, oc, :], big_b),
                      ocw=64, per_load=8, ckey="swq", g=g)
            rope(lambda hh: big_t[0:64, hh, :], 32, g)
            P.dma("sp", lambda h, g=g: h.dma_start(out=q_v[:, :, g * TG:(g + 1) * TG], in_=big_t[0:64, 0:32, :]),
                  reads=[big_b], writes=[q_b])
            linear_fm(W, D, 8, lambda c: hT_t[:, c, :], [hT_b], NCH, evac_to(lambda oc: big_t[0:64, 32 + oc, :], big_b),
                      ocw=64, per_load=8, ckey="swk", g=g)
            rope(lambda hh: big_t[0:64, 32 + hh, :], 8, g)
            for ck in range(2):
                P.dma("sp", lambda h, g=g, ck=ck: h.dma_start(out=kin_v[ck][:, :, g * TG:(g + 1) * TG],
                                                               in_=big_t[0:64, 32 + 4 * ck:36 + 4 * ck, :]),
                      reads=[big_b], writes=[db["kin"]])
            slot, slot_b = wrot.next()
            view = load_w_cols(W, D + 512, 512, slot, slot_b, ckey="swv", g=g)
            for blk in range(4):
                pst, psb = PY.next()
                for c in range(NCH):
                    P.op("pe", mm(pst[:, :], hT_t[:, c, blk * 128:(blk + 1) * 128], view[:, c, :], c == 0, c == NCH - 1),
                         reads=[hT_b, slot_b], writes=[psb])
                P.op("act", lambda h, pst=pst, blk=blk: h.activation(out=vst_t[:, blk, :], in_=pst[:, :], func=AF.Copy),
                     reads=[psb], writes=[SB_["vst"]])
            P.dma("sp", lambda h, g=g: h.dma_start(out=vin_v[g // 2][:, (g % 2) * 4:(g % 2) * 4 + 4, :], in_=vst_t[:, :, :]),
                  reads=[SB_["vst"]], writes=[db["vin"]])
        for ck in range(2):
            all_gather(dr["kin"][ck], db["kin"], dr["kout"][ck], db["kout"])
            all_gather(dr["vin"][ck], db["vin"], dr["vout"][ck], db["vout"])
        P.barrier()
        A = {"off": xg_off, "n": 5000}
        kc2 = [at_alloc(A, "kc%d" % i, [64, 8, 128], BF16) for i in range(2)]
        vc2 = [at_alloc(A, "vc%d" % i, [128, 512], BF16) for i in range(2)]
        kp2_ = [at_alloc(A, "kp%d" % i, [64, 8, 128], BF16) for i in range(2)]
        vp2_ = [at_alloc(A, "vp%d" % i, [128, 512], BF16) for i in range(2)]
        qb2 = [at_alloc(A, "qblk%d" % i, [64, 32, 128], BF16) for i in range(2)]
        ab2 = [at_alloc(A, "ablk%d" % i, [64, 32, 128], BF16) for i in range(2)]
        kcand = at_alloc(A, "kcand", [64, 5, 8, 128], BF16)
        vcand = at_alloc(A, "vcand", [128, 5, 512], BF16)
        mtri = at_alloc(A, "mtri", [128, 128], BF16)
        mprev = at_alloc(A, "mprev", [128, 128], BF16)
        mprev0 = at_alloc(A, "mprev0", [128, 128], BF16)
        sel5 = at_alloc(A, "sel5", [128, 5], F32)
        sinke = at_alloc(A, "sinke", [64, 32], F32)
        den_t = at_alloc(A, "den", [64, 512], F32)
        ptc = Rot([(at_alloc(A, "ptc%d" % i, [128, 512], BF16), Buf("ptc%d" % i)) for i in range(2)])
        ptp = Rot([(at_alloc(A, "ptp%d" % i, [128, 512], BF16), Buf("ptp%d" % i)) for i in range(2)])
        B_ = {k: Buf("sw3_" + k) for k in ("kc0", "kc1", "vc0", "vc1", "kcand", "vcand", "kp0", "kp1", "vp0", "vp1",
                                           "qblk0", "qblk1", "ablk0", "ablk1", "mtri", "mprev",
                                           "mprev0", "sel5", "sinke", "den")}
        P.dma("pool", lambda h: h.dma_start(out=mtri[:, :], in_=triu[:, :]), writes=[B_["mtri"]])
        P.dma("pool", lambda h: h.dma_start(out=mprev[:, :], in_=m_prev_in[:, :]), writes=[B_["mprev"]])
        P.dma("pool", lambda h: h.dma_start(out=mprev0[:, :], in_=m_prev0_in[:, :]), writes=[B_["mprev0"]])
        P.dma("sp", lambda h: h.dma_start(out=sel5[:, :], in_=sel5_in[:, :]), writes=[B_["sel5"]])
        P.dma("sp", lambda h: h.dma_start(out=sinke[:, :], in_=swa_sinks[0, :].partition_broadcast(64)), writes=[B_["sinke"]])
        P.op("act", lambda h: h.activation(out=sinke[:, :], in_=sinke[:, :], func=AF.Exp), reads=[B_["sinke"]], writes=[B_["sinke"]])
        kout_v = [t.ap().rearrange("(r h d) t -> d r h t", r=4, d=64) for t in dr["kout"]]
        vout_v = [t.ap().rearrange("(r p) (i f) -> p r i f", r=4, i=8) for t in dr["vout"]]
        PS_C = Rot([psum[0], psum[1]])
        PS_P = Rot([psum[2], psum[3]])
        PS_O = Rot([psum[4], psum[5]])
        PS_D = Rot([psum[6], psum[7]])
        scale = 64.0 ** -0.5

        def emit_block_loads(i):
            par = i % 2
            kc, vc, kp, vp, qblk = kc2[par], vc2[par], kp2_[par], vp2_[par], qb2[par]
            kcb, vcb, kpb, vpb, qbb = (B_["kc%d" % par], B_["vc%d" % par], B_["kp%d" % par], B_["vp%d" % par],
                                       B_["qblk%d" % par])
            ip = max(i - 1, 0)
            for ck in range(2):
                P.dma("sp", lambda h, ck=ck: h.dma_start(out=kc[:, 4 * ck:4 * ck + 4, :],
                                                         in_=kin_v[ck][:, :, i * 128:(i + 1) * 128]),
                      reads=[db["kin"]], writes=[kcb])
                for rr in range(4):
                    P.dma("sp", lambda h, ck=ck, rr=rr: h.dma_start(
                        out=kcand[:, rr, 4 * ck:4 * ck + 4, :], in_=kout_v[ck][:, rr, :, i * 128:(i + 1) * 128]),
                        reads=[db["kout"]], writes=[B_["kcand"]])
                P.dma("sp", lambda h, ck=ck: h.dma_start(
                    out=kcand[:, 4, 4 * ck:4 * ck + 4, :], in_=kout_v[ck][:, 3, :, ip * 128:(ip + 1) * 128]),
                    reads=[db["kout"]], writes=[B_["kcand"]])
            P.dma("sp", lambda h: h.dma_start(out=vc[:, :], in_=vin_v[i // 8][:, i % 8, :]), reads=[db["vin"]], writes=[vcb])
            P.dma("sp", lambda h: h.dma_start(out=vcand[:, 0:4, :], in_=vout_v[i // 8][:, :, i % 8, :]),
                  reads=[db["vout"]], writes=[B_["vcand"]])
            P.dma("sp", lambda h: h.dma_start(out=vcand[:, 4, :], in_=vout_v[ip // 8][:, 3, ip % 8, :]),
                  reads=[db["vout"]], writes=[B_["vcand"]])
            P.dma("sp", lambda h: h.dma_start(out=qblk[:, :, :], in_=q_v[:, :, i * 128:(i + 1) * 128]),
                  reads=[q_b], writes=[qbb])

        def emit_block_select(i):
            par = i % 2
            kp, vp = kp2_[par], vp2_[par]
            kpb, vpb = B_["kp%d" % par], B_["vp%d" % par]
            kpf = kp[:, :, :].rearrange("d h t -> d (h t)")
            P.op("dve", lambda h: h.tensor_scalar(out=kpf, in0=kcand[:, 0, :, :].rearrange("d h t -> d (h t)"),
                                                  scalar1=sel5[0:64, 0:1], scalar2=None, op0=ALU.mult),
                 reads=[B_["kcand"], B_["sel5"]], writes=[kpb])
            P.op("dve", lambda h: h.tensor_scalar(out=vp[:, :], in0=vcand[:, 0, :], scalar1=sel5[:, 0:1], scalar2=None,
                                                  op0=ALU.mult), reads=[B_["vcand"], B_["sel5"]], writes=[vpb])
            for cnd in range(1, 5):
                P.op("dve", lambda h, cnd=cnd: h.scalar_tensor_tensor(
                    out=kpf, in0=kcand[:, cnd, :, :].rearrange("d h t -> d (h t)"), scalar=sel5[0:64, cnd:cnd + 1], in1=kpf,
                    op0=ALU.mult, op1=ALU.add), reads=[B_["kcand"], B_["sel5"], kpb], writes=[kpb])
                P.op("dve", lambda h, cnd=cnd: h.scalar_tensor_tensor(
                    out=vp[:, :], in0=vcand[:, cnd, :], scalar=sel5[:, cnd:cnd + 1], in1=vp[:, :],
                    op0=ALU.mult, op1=ALU.add), reads=[B_["vcand"], B_["sel5"], vpb], writes=[vpb])

        sw_steps = [(i, hk) for i in range(NBLK) for hk in range(8)]
        sA = {}
        sB = {}

        def stageA(si):
            i, hk = sw_steps[si]
            par = i % 2
            qsl = qb2[par][:, hk * 4:(hk + 1) * 4, :]
            pc, pcb = PS_C.next()
            pp, ppb = PS_P.next()
            P.op("pe", mm(pc[:, :], kc2[par][:, hk, :], qsl, True, True), reads=[B_["kc%d" % par], B_["qblk%d" % par]], writes=[pcb])
            P.op("pe", mm(pp[:, :], kp2_[par][:, hk, :], qsl, True, True), reads=[B_["kp%d" % par], B_["qblk%d" % par]], writes=[ppb])
            sA[si] = (pc, pcb, pp, ppb)

        def stageB(si):
            i, hk = sw_steps[si]
            par = i % 2
            vc, vp, ablk = vc2[par], vp2_[par], ab2[par]
            vcb, vpb, abb = B_["vc%d" % par], B_["vp%d" % par], B_["ablk%d" % par]
            pc, pcb, pp, ppb = sA.pop(si)
            mpv, mpvb = (mprev0, B_["mprev0"]) if i == 0 else (mprev, B_["mprev"])
            tc_, tcb = ptc.next()
            tp_, tpb = ptp.next()
            P.op("act", lambda h: h.activation(out=tc_[:, :], in_=pc[:, :], func=AF.Exp, scale=scale), reads=[pcb], writes=[tcb])
            P.op("act", lambda h: h.activation(out=tp_[:, :], in_=pp[:, :], func=AF.Exp, scale=scale), reads=[ppb], writes=[tpb])
            P.op("pool", lambda h: h.tensor_tensor(
                out=tc_[:, :].rearrange("k (a q) -> k a q", a=4), in0=tc_[:, :].rearrange("k (a q) -> k a q", a=4),
                in1=mtri[:, :].rearrange("k (o q) -> k o q", o=1).broadcast_to([128, 4, 128]), op=ALU.mult),
                reads=[tcb, B_["mtri"]], writes=[tcb])
            P.op("dve", lambda h: h.tensor_tensor(
                out=tp_[:, :].rearrange("k (a q) -> k a q", a=4), in0=tp_[:, :].rearrange("k (a q) -> k a q", a=4),
                in1=mpv[:, :].rearrange("k (o q) -> k o q", o=1).broadcast_to([128, 4, 128]), op=ALU.mult),
                reads=[tpb, mpvb], writes=[tpb])
            po, pob = PS_O.next()
            pd, pdb = PS_D.next()
            P.op("pe", mm(po[0:64, :], vc[:, hk * 64:(hk + 1) * 64], tc_[:, :], True, False), reads=[vcb, tcb], writes=[pob])
            P.op("pe", mm(po[0:64, :], vp[:, hk * 64:(hk + 1) * 64], tp_[:, :], False, True), reads=[vpb, tpb], writes=[pob])
            P.op("pe", mm(pd[0:64, :], onesb_t[:, 0:64], tc_[:, :], True, False), reads=[ones_b, tcb], writes=[pdb])
            P.op("pe", mm(pd[0:64, :], onesb_t[:, 0:64], tp_[:, :], False, True), reads=[ones_b, tpb], writes=[pdb])
            sB[si] = (po, pob, pd, pdb)

        def stageB2(si):
            i, hk = sw_steps[si]
            par = i % 2
            ablk, abb = ab2[par], B_["ablk%d" % par]
            po, pob, pd, pdb = sB.pop(si)
            P.op("dve", lambda h: h.tensor_tensor(
                out=den_t[:, :].rearrange("d (a q) -> d a q", a=4), in0=pd[0:64, :].rearrange("d (a q) -> d a q", a=4),
                in1=sinke[:, hk * 4:(hk + 1) * 4].rearrange("d (a o) -> d a o", o=1).broadcast_to([64, 4, 128]), op=ALU.add),
                reads=[pdb, B_["sinke"]], writes=[B_["den"]])
            P.op("act", lambda h: h.activation(out=den_t[:, :], in_=den_t[:, :], func=AF.Ln), reads=[B_["den"]], writes=[B_["den"]])
            P.op("act", lambda h: h.activation(out=den_t[:, :], in_=den_t[:, :], func=AF.Exp, scale=-1.0),
                 reads=[B_["den"]], writes=[B_["den"]])
            P.op("dve", lambda h: h.tensor_tensor(
                out=ablk[:, hk * 4:(hk + 1) * 4, :], in0=po[0:64, :].rearrange("d (a q) -> d a q", a=4),
                in1=den_t[:, :].rearrange("d (a q) -> d a q", a=4), op=ALU.mult),
                reads=[pob, B_["den"]], writes=[abb])
            if hk == 7:
                P.dma("sp", lambda h: h.dma_start(out=att_v[:, :, i * 128:(i + 1) * 128], in_=ablk[:, :, :]),
                      reads=[abb], writes=[att_b])

        emit_block_loads(0)
        emit_block_select(0)
        stageA(0)
        for si in range(len(sw_steps)):
            bi, bh = sw_steps[si]
            if bh == 0 and bi + 1 < NBLK:
                emit_block_loads(bi + 1)
            if si + 1 < len(sw_steps):
                if sw_steps[si + 1][1] == 0:
                    emit_block_select(sw_steps[si + 1][0])
                stageA(si + 1)
            stageB(si)
            if si >= 1:
                stageB2(si - 1)
        stageB2(len(sw_steps) - 1)
        P.barrier()
        attn_out_phase(swa_w_out[0], ckey="swo")
        P.barrier()

    for l in layers:
        compute_mod(l)
        kind = l % 3
        if do_mixer:
            if kind == 0:
                arena["off"] = phase_base
                fox_layer(l, l // 3)
            if kind == 2:
                arena["off"] = phase_base
                swa_layer(l)
            if kind == 1:
                arena["off"] = phase_base
                st = sgu_setup()
                for g in range(NG):
                    sgu_group(st, g)
                P.barrier()
        if do_ffn:
            P.barrier()
            ffn_big(l)
            P.barrier()

    for g in range(NG):
        load_xg(g)
        stage = yT_t[:, :, :].rearrange("p c t -> p (c t)").rearrange("p (b d) -> p b d", b=4)
        for b in range(4):
            for q in range(4):
                pst, psb = PY.next()
                for j in range(4):
                    c = q * 4 + j
                    P.op("pe", lambda h, pst=pst, b=b, c=c, j=j: h.transpose(
                        pst[:, j * 128:(j + 1) * 128], xg_t[:, c, b * 128:(b + 1) * 128], ident_t[:, :]),
                        reads=[xg_b, ident_b], writes=[psb])
                if q % 2 == 0:
                    P.op("act", lambda h, pst=pst, b=b, q=q: h.activation(out=stage[:, b, q * 512:(q + 1) * 512],
                                                                          in_=pst[:, :], func=AF.Copy),
                         reads=[psb], writes=[yT_b])
                else:
                    P.op("dve", lambda h, pst=pst, b=b, q=q: h.tensor_copy(out=stage[:, b, q * 512:(q + 1) * 512],
                                                                           in_=pst[:, :]), reads=[psb], writes=[yT_b])
        P.dma("sp", lambda h, g=g: h.dma_start(
            out=yout[g * TG:(g + 1) * TG, :].rearrange("(b p) d -> p b d", p=128), in_=stage),
            reads=[yT_b], writes=[yout_b])
    P.barrier()
    P.emit(nc)
    es.close()
    return nc


_TRI = np.tril(np.ones((128, 128), np.float32))


def _prep_inputs(inp, layers):
    x = np.asarray(inp["x"], np.float32)
    maps = []
    shared = {
        "ident": np.eye(128, dtype=np.float32),
        "trimask": _TRI,
        "ffn_w_gu": np.ascontiguousarray(np.asarray(inp["ffn_w_gu"], np.float32)[list(layers)]),
        "ffn_w_down": np.ascontiguousarray(np.asarray(inp["ffn_w_down"], np.float32)[list(layers)]),
        "sgu_w_in": np.ascontiguousarray(inp["sgu_w_in"], np.float32),
        "sgu_ln_g": np.ascontiguousarray(inp["sgu_ln_g"], np.float32),
        "sgu_ln_b": np.ascontiguousarray(inp["sgu_ln_b"], np.float32),
        "sgu_w_s": np.ascontiguousarray(inp["sgu_w_s"], np.float32),
        "sgu_b_s": np.ascontiguousarray(inp["sgu_b_s"], np.float32).reshape(1, 2048),
        "sgu_w_out": np.ascontiguousarray(inp["sgu_w_out"], np.float32),
    }
    for n in ("fox_w_in", "fox_b_f", "fox_w_out", "swa_w_in", "swa_sinks", "swa_w_out"):
        shared[n] = np.ascontiguousarray(inp[n], np.float32)
    shared["triu"] = np.ascontiguousarray(_TRI.T)
    shared["m_prev"] = np.ascontiguousarray(1.0 - _TRI.T)
    pm = np.zeros((64, 16), np.float32)
    for m_ in range(8):
        pm[m_ + 8, m_] = -1.0
        pm[m_, m_ + 8] = 1.0
    shared["pmat"] = pm
    inv = (500000.0 ** (-np.arange(0, 16, 2, dtype=np.float32) / np.float32(16))).astype(np.float32)
    shared["invf"] = np.concatenate([inv, inv]).reshape(16, 1).astype(np.float32)
    for n in ("mix_pre_g", "mix_post_g", "ffn_pre_g", "ffn_post_g"):
        shared[n] = np.ascontiguousarray(inp[n], np.float32).reshape(64, 128)
    for core in range(8):
        b, r = core // 4, core % 4
        xb = x[b].reshape(16, 4, 128, D)[:, r].reshape(TOK, D)
        m = dict(shared)
        m["xs"] = np.ascontiguousarray(xb)
        m["ada_w"] = np.ascontiguousarray(np.asarray(inp["ada_w"], np.float32)[list(layers)][:, :, r * 3072:(r + 1) * 3072])
        m["ada_b"] = np.ascontiguousarray(np.asarray(inp["ada_b"], np.float32)[list(layers)][:, r * 3072:(r + 1) * 3072])
        m["cvec"] = np.ascontiguousarray(inp["c"][b], np.float32).reshape(16, 128)
        pos = np.asarray(inp["positions"])[b].astype(np.int32)
        m["posin"] = np.ascontiguousarray(pos.reshape(16, 4, 128)[:, r].reshape(1, TOK))
        mf = np.zeros((128, 4, 128), np.float32)
        for j_ in range(4):
            if j_ < r:
                mf[:, j_, :] = 1.0
            elif j_ == r:
                mf[:, j_, :] = _TRI.T
        m["m_fox"] = mf.reshape(128, 512)
        m["m_prev0"] = np.zeros((128, 128), np.float32) if r == 0 else np.ascontiguousarray(1.0 - _TRI.T)
        oh = np.zeros((128, 4), np.float32)
        oh[:, r] = 1.0
        m["oh4"] = oh
        zo = np.zeros((128, 16, 64), np.float32)
        for li_ in range(16):
            zo[:, li_, 4 * li_ + r + 1:] = -30000.0
        m["zo"] = zo.reshape(128, 1024)
        s5 = np.zeros((128, 5), np.float32)
        s5[:, (r - 1) if r > 0 else 4] = 1.0
        m["sel5"] = s5
        maps.append(m)
    return maps


def run(inp, layers=(0, 1, 2, 3), **kw):
    nc = build_program(layers=layers, **kw)
    maps = _prep_inputs(inp, layers)
    res = run_bass_kernel_spmd(nc, maps, core_ids=list(range(8)))
    out = np.empty((2, SEQ, D), np.float32)
    for core in range(8):
        b, r = core // 4, core % 4
        out[b].reshape(16, 4, 128, D)[:, r] = res.results[core]["yout"].reshape(16, 128, D)
    return out


def kernel(**inputs):
    return run(inputs)
```

```python
import numpy as np
import ml_dtypes
from contextlib import ExitStack
import concourse.bass as bass
import concourse.mybir as mybir
from concourse.bass_utils import run_bass_kernel_spmd

F32 = mybir.dt.float32
BF16 = mybir.dt.bfloat16
I32 = mybir.dt.int32
AF = mybir.ActivationFunctionType
ALU = mybir.AluOpType

D = 2048
NCH = 16
SEQ = 8192
TOK = 2048
NBLK = 16
TG = 512
NG = TOK // TG
DFF = 5632
NFC = DFF // 128
EPS = 1e-6
FOX_IN = 6160
SWA_IN = 3072
ENGS = ("pe", "act", "dve", "pool", "sp")
BLOCKNAME = {"pe": "tensor", "act": "scalar", "dve": "vector", "pool": "gpsimd", "sp": "sync"}


class Buf:
    __slots__ = ("name", "w", "rs", "dtotal")

    def __init__(self, name):
        self.name = name
        self.w = None
        self.rs = {}
        self.dtotal = 0


class Plan:
    def __init__(self):
        self.recs = {e: [] for e in ENGS}
        self.seen = {e: {} for e in ENGS}
        self.dbufs = {}

    def _deps(self, eng, reads, writes, skipkey=None):
        need = {}
        seen = self.seen[eng]

        def add(tok):
            key, val = tok
            if key == ("E", "pe") and eng == "pe":
                return
            if key == skipkey:
                return
            if seen.get(key, -1) >= val:
                return
            if need.get(key, -1) < val:
                need[key] = val

        for b in reads:
            if b.w is not None:
                add(b.w)
        for b in writes:
            if b.w is not None:
                add(b.w)
            for k, v in b.rs.items():
                add((k, v))
        for k, v in need.items():
            seen[k] = v
            if k[0] == "E":
                self.recs[k[1]][v][3] = True
        return list(need.items())

    def op(self, eng, fn, reads=(), writes=()):
        waits = self._deps(eng, reads, writes)
        idx = len(self.recs[eng])
        self.recs[eng].append([waits, fn, None, False, 0])
        key = ("E", eng)
        for b in reads:
            if b.rs.get(key, -1) < idx:
                b.rs[key] = idx
        for b in writes:
            b.w = (key, idx)
            b.rs = {}

    def dma(self, eng, fn, reads=(), writes=(), dbuf=None, inc=16):
        if dbuf is None:
            dbuf = writes[0]
        waits = self._deps(eng, reads, writes, skipkey=("D", id(dbuf)))
        self.dbufs[id(dbuf)] = dbuf
        dbuf.dtotal += inc
        key = ("D", id(dbuf))
        val = dbuf.dtotal
        self.recs[eng].append([waits, fn, id(dbuf), False, inc])
        for b in reads:
            if b.rs.get(key, -1) < val:
                b.rs[key] = val
        for b in writes:
            b.w = (key, val)
            b.rs = {}

    def barrier(self, exclude=()):
        excl = set(id(b) for b in exclude)
        for e in ENGS:
            need = []
            seen = self.seen[e]
            for e2 in ENGS:
                if e2 == e or not self.recs[e2]:
                    continue
                idx = None
                for j in range(len(self.recs[e2]) - 1, -1, -1):
                    r = self.recs[e2][j]
                    if r[1] is not None and r[2] is None:
                        idx = j
                        break
                if idx is None:
                    continue
                key = ("E", e2)
                if seen.get(key, -1) < idx:
                    seen[key] = idx
                    self.recs[e2][idx][3] = True
                    need.append((key, idx))
            for bid, b in self.dbufs.items():
                key = ("D", bid)
                if bid in excl:
                    continue
                if b.dtotal > 0 and seen.get(key, -1) < b.dtotal:
                    seen[key] = b.dtotal
                    need.append((key, b.dtotal))
            if need:
                self.recs[e].append([need, None, None, False, 0])

    def emit(self, nc):
        vals = {}
        for e in ENGS:
            cnt = 0
            v = []
            for rec in self.recs[e]:
                if rec[3]:
                    cnt += 1
                v.append(cnt)
            vals[e] = v
            assert cnt < 60000, (e, cnt)
        with ExitStack() as es:
            esem = {e: es.enter_context(nc.semaphore("sem_" + e)) for e in ENGS}
            dsem = {}
            for n, bid in enumerate(self.dbufs):
                dsem[bid] = es.enter_context(nc.semaphore("dsem%d" % n))
            block = es.enter_context(nc.Block())
            for e in ENGS:
                def body(h, e=e):
                    for waits, fn, dma, flagged, inc in self.recs[e]:
                        for key, val in waits:
                            if key[0] == "E":
                                h.wait_ge(esem[key[1]], vals[key[1]][val])
                            else:
                                h.wait_ge(dsem[key[1]], val)
                        if fn is None:
                            continue
                        ins = fn(h)
                        if dma is not None:
                            ins.then_inc(dsem[dma], inc)
                        elif flagged:
                            ins.then_inc(esem[e], 1)
                getattr(block, BLOCKNAME[e])(body)


class Rot:
    def __init__(self, items):
        self.items = items
        self.i = 0

    def next(self):
        it = self.items[self.i % len(self.items)]
        self.i += 1
        return it


def build_program(layers=(0, 1, 2, 3), do_mixer=True, do_ffn=True):
    NL = len(layers)
    LI = {l: i for i, l in enumerate(layers)}
    nc = bass.Bass("TRN2", target_bir_lowering=False)
    P = Plan()

    def din(name, shape, dt=F32):
        return nc.dram_tensor(name, list(shape), dt, kind="ExternalInput").ap()

    xs = din("xs", [TOK, D])
    cvec = din("cvec", [16, 128])
    ident = din("ident", [128, 128])
    ada_w = din("ada_w", [NL, D, 3072])
    ada_b = din("ada_b", [NL, 3072])
    gains = [din(n, [64, 128]) for n in ("mix_pre_g", "mix_post_g", "ffn_pre_g", "ffn_post_g")]
    w_gu = din("ffn_w_gu", [NL, D, 2 * DFF])
    w_dn = din("ffn_w_down", [NL, DFF, D])
    sgu_w_in = din("sgu_w_in", [1, D, 2 * D])
    sgu_ln_g = din("sgu_ln_g", [1, D])
    sgu_ln_b = din("sgu_ln_b", [1, D])
    sgu_w_s = din("sgu_w_s", [1, 16, 128, 128])
    sgu_b_s = din("sgu_b_s", [1, 16 * 128])
    sgu_w_out = din("sgu_w_out", [1, D, D])
    trimask = din("trimask", [128, 128])
    fox_w_in = din("fox_w_in", [2, D, FOX_IN])
    fox_b_f = din("fox_b_f", [2, 16])
    fox_w_out = din("fox_w_out", [2, D, D])
    swa_w_in = din("swa_w_in", [1, D, SWA_IN])
    swa_sinks = din("swa_sinks", [1, 32])
    swa_w_out = din("swa_w_out", [1, D, D])
    posin = din("posin", [1, TOK], I32)
    triu = din("triu", [128, 128])
    m_prev_in = din("m_prev", [128, 128])
    m_fox_in = din("m_fox", [128, 4 * 128])
    m_prev0_in = din("m_prev0", [128, 128])
    zo_in = din("zo", [128, 1024])
    oh4_in = din("oh4", [128, 4])
    sel5_in = din("sel5", [128, 5])
    pmat_in = din("pmat", [64, 16])
    invf_in = din("invf", [16, 1])
    yout = nc.dram_tensor("yout", [TOK, D], F32, kind="ExternalOutput").ap()

    xT_s = nc.dram_tensor("xT_s", [D, TOK], F32).ap()
    xT_v = xT_s.rearrange("(c p) t -> p c t", p=128)
    xT_b = [Buf("xT_s%d" % g) for g in range(NG)]
    xTw_b = [Buf("xTw_s%d" % g) for g in range(NG)]
    yout_b = Buf("yout")
    modin = [nc.dram_tensor("modin%d" % i, [128, 24], F32) for i in range(4)]
    modout = [nc.dram_tensor("modout%d" % i, [4 * 128, 24], F32) for i in range(4)]
    modin_b, modout_b = Buf("modin"), Buf("modout")
    q_s = nc.dram_tensor("q_s", [D, TOK], BF16).ap()
    q_b = Buf("q_s")
    att_s = nc.dram_tensor("att_s", [D, TOK], BF16).ap()
    att_b = Buf("att_s")
    RG = [[0, 1, 2, 3], [4, 5, 6, 7]]
    fdr = {}
    fdr["kin"] = [nc.dram_tensor("fkin%d" % i, [256, TOK], BF16) for i in range(8)]
    fdr["kout"] = [nc.dram_tensor("fkout%d" % i, [4 * 256, TOK], BF16) for i in range(8)]
    fdr["vin"] = [nc.dram_tensor("fvin%d" % i, [128, 2 * 16 * 128], BF16) for i in range(8)]
    fdr["vout"] = [nc.dram_tensor("fvout%d" % i, [4 * 128, 2 * 16 * 128], BF16) for i in range(8)]
    fdr["lin"] = nc.dram_tensor("flin", [TOK, 16], F32)
    fdr["lout"] = nc.dram_tensor("flout", [4 * TOK, 16], F32)
    fdr["b"] = {k: Buf("f" + k) for k in ("kin", "vin", "lin", "lout")}
    fdr["b"]["kout"] = [Buf("fkout%d" % i) for i in range(8)]
    fdr["b"]["vout"] = [Buf("fvout%d" % i) for i in range(8)]
    swa_dr = {}
    swa_dr["kin"] = [nc.dram_tensor("skin%d" % i, [256, TOK], BF16) for i in range(2)]
    swa_dr["kout"] = [nc.dram_tensor("skout%d" % i, [4 * 256, TOK], BF16) for i in range(2)]
    swa_dr["vin"] = [nc.dram_tensor("svin%d" % i, [128, 8 * 512], BF16) for i in range(2)]
    swa_dr["vout"] = [nc.dram_tensor("svout%d" % i, [4 * 128, 8 * 512], BF16) for i in range(2)]
    swa_dr["b"] = {k: Buf("s" + k) for k in ("kin", "kout", "vin", "vout")}

    arena = {"off": 16640}

    def sb(name, shape, dt):
        nbytes = int(np.prod(shape[1:])) * (4 if dt in (F32, I32) else 2)
        off = (arena["off"] + 31) // 32 * 32
        arena["off"] = off + nbytes
        assert arena["off"] <= 229344, (name, arena["off"])
        t = nc.alloc_sbuf_tensor_at(name, list(shape), dt, offset=off)
        return t

    ident_t = sb("ident_t", [128, 128], F32)
    ident_b = Buf("ident")
    ones_t = sb("ones_t", [128, 128], F32)
    ones_b = Buf("ones")
    eps_t = sb("eps_t", [128, 1], F32)
    one11 = ones_t
    cact_t = sb("cact_t", [128, 16], BF16)
    cact_b = Buf("cact")
    gains_t = sb("gains_t", [128, 4, 64], F32)
    gains_b = Buf("gains")
    modT = sb("modT", [128, 96], F32)
    modp_t = sb("modp_t", [128, 24], F32)
    modp_b = Buf("modp")
    modT_b = Buf("modT")
    der_t = sb("der_t", [128, 4, 16], F32)
    der_b = Buf("der")
    onesb_t = sb("onesb_t", [128, 128], BF16)
    cst_t = sb("cst_t", [128, 4], F32)
    xg_off = (arena["off"] + 31) // 32 * 32
    xg_t = sb("xg_t", [128, NCH, TG], F32)
    xg_b = Buf("xg")
    hT_t = sb("hT_t", [128, NCH, TG], BF16)
    hT_b = Buf("hT")
    yT_t = sb("yT_t", [128, NCH, TG], F32)
    yT_b = Buf("yT")
    big_off = (arena["off"] + 31) // 32 * 32
    big_t = sb("big_t", [128, NFC, TG], BF16)
    big_b = Buf("big")
    wsl = []
    for i in range(2):
        t = sb("wslot%d" % i, [128, 8192], BF16)
        wsl.append((t, Buf("wslot%d" % i)))
    wrot = Rot(wsl)
    attn_lim = arena["off"]
    sqs = Rot([(sb("sq%d" % i, [128, TG], BF16), Buf("sq%d" % i)) for i in range(4)])
    tmps = Rot([(sb("tmp%d" % i, [128, TG], F32), Buf("tmp%d" % i)) for i in range(2)])
    rstd_t = sb("rstd_t", [128, TG], F32)
    rstd_b = Buf("rstd")
    rt_t = sb("rt_t", [128, TG], F32)
    rt_b = Buf("rt")
    row_t = sb("row_t", [1, 512], F32)
    row_b = Buf("row")
    brow_t = sb("brow_t", [1, 512], F32)
    brow_b = Buf("brow")
    small_t = sb("small_t", [128, 64], F32)
    small_b = Buf("small")
    phase_base = arena["off"]

    es = ExitStack()
    psum = []
    for i in range(8):
        t = es.enter_context(nc.psum_tensor("ps%d" % i, [128, 512], F32))
        psum.append((t, Buf("ps%d" % i)))
    PG = Rot([psum[0], psum[2]])
    PU = Rot([psum[1], psum[3]])
    PY = Rot([psum[4], psum[5]])
    PY6 = Rot([psum[0], psum[1], psum[2], psum[3], psum[4], psum[5]])
    PSSQ = psum[6]
    PM = psum[7]

    mm = lambda out, lhsT, rhs, st, sp: (lambda h: h.matmul(out, lhsT, rhs, start=st, stop=sp))

    wcache = {}
    wcache_b = Buf("wcache")

    def load_w_cols(W2d, col0, ncols, slot, slot_b, dst_col0=0, width=None, kch=NCH, ckey=None, g=0):
        width = width or ncols
        view = slot[:, 0:kch * width].rearrange("p (c n) -> p c n", n=width)
        dst = view[:, :, dst_col0:dst_col0 + ncols]
        if ckey is not None and g > 0:
            cap = wcache[ckey]
            P.dma("pool", lambda h: h.dma_start(out=dst, in_=cap.rearrange("p (c n) -> p c n", n=ncols)),
                  reads=[wcache_b], writes=[slot_b])
            return view
        src = W2d.rearrange("(c p) n -> p c n", p=128)[:, :, col0:col0 + ncols]
        P.dma("pool", lambda h: h.dma_start(out=dst, in_=src), writes=[slot_b])
        if ckey is not None:
            cap = nc.dram_tensor("wc_%d" % len(wcache), [128, kch * ncols], BF16).ap()
            wcache[ckey] = cap
            P.dma("sp", lambda h: h.dma_start(out=cap.rearrange("p (c n) -> p c n", n=ncols), in_=dst),
                  reads=[slot_b], writes=[wcache_b])
        return view

    def ssq_rstd(src_t, src_b, src_fn=None, act_share=10):
        pst, psb = PSSQ
        if src_fn is None:
            src_fn = lambda c: src_t[:, c, :]
        for c in range(NCH):
            sq, sqb = sqs.next()
            if (c * act_share) % 16 < act_share:
                P.op("act", lambda h, sq=sq, c=c: h.activation(out=sq[:, :], in_=src_fn(c), func=AF.Square),
                     reads=[src_b], writes=[sqb])
            else:
                P.op("dve", lambda h, sq=sq, c=c: h.tensor_tensor(out=sq[:, :], in0=src_fn(c), in1=src_fn(c), op=ALU.mult),
                     reads=[src_b], writes=[sqb])
            P.op("pe", mm(pst[:, :], onesb_t[:, :], sq[:, :], c == 0, c == NCH - 1),
                 reads=[ones_b, sqb], writes=[psb])
        P.op("act", lambda h: h.activation(out=rt_t[:, :], in_=pst[:, :], func=AF.Ln,
                                           bias=eps_t[:, 0:1], scale=1.0 / D),
             reads=[psb, ones_b], writes=[rt_b])
        P.op("act", lambda h: h.activation(out=rstd_t[:, :], in_=rt_t[:, :], func=AF.Exp, scale=-0.5),
             reads=[rt_b], writes=[rstd_b])

    def prenorm(acol, bcol):
        ssq_rstd(xg_t, xg_b)
        for c in range(NCH):
            tmp, tb = tmps.next()
            P.op("dve", lambda h, tmp=tmp, c=c: h.scalar_tensor_tensor(
                out=tmp[:, :], in0=xg_t[:, c, :], scalar=acol[:, c:c + 1], in1=rstd_t[:, :],
                op0=ALU.mult, op1=ALU.mult), reads=[xg_b, der_b, rstd_b], writes=[tb])
            P.op("act", lambda h, tmp=tmp, c=c: h.activation(
                out=hT_t[:, c, :], in_=tmp[:, :], func=AF.Identity, bias=bcol[:, c:c + 1], scale=1.0),
                reads=[tb, modT_b], writes=[hT_b])

    def postnorm_res(coef):
        ssq_rstd(yT_t, yT_b, act_share=12)
        for c in range(NCH):
            tmp, tb = tmps.next()
            P.op("dve", lambda h, tmp=tmp, c=c: h.scalar_tensor_tensor(
                out=tmp[:, :], in0=yT_t[:, c, :], scalar=coef[:, c:c + 1], in1=rstd_t[:, :],
                op0=ALU.mult, op1=ALU.mult), reads=[yT_b, der_b, rstd_b], writes=[tb])
            P.op("pool" if c % 4 != 3 else "dve", lambda h, tmp=tmp, c=c: h.tensor_tensor(
                out=xg_t[:, c, :], in0=xg_t[:, c, :], in1=tmp[:, :], op=ALU.add),
                reads=[tb, xg_b], writes=[xg_b])

    def load_xg(g):
        P.dma("sp", lambda h: h.dma_start(out=xg_t[:, :, :], in_=xT_v[:, :, g * TG:(g + 1) * TG]),
              reads=[xT_b[g], xTw_b[g]], writes=[xg_b])

    def store_xg(g):
        P.dma("sp", lambda h: h.dma_start(out=xT_v[:, :, g * TG:(g + 1) * TG], in_=xg_t[:, :, :]),
              reads=[xg_b], writes=[xT_b[g]])

    def linear_fm(W2d, col0, n_oc, rhs_fn, rhs_bufs, kch, evac, ocw=128, per_load=4, ckey=None, g=0):
        oc = 0
        while oc < n_oc:
            nl = min(per_load, n_oc - oc)
            slot, slot_b = wrot.next()
            view = load_w_cols(W2d, col0 + oc * ocw, nl * ocw, slot, slot_b, kch=kch,
                               ckey=None if ckey is None else (ckey, oc), g=g)
            for j in range(nl):
                pst, psb = PY6.next()
                for c in range(kch):
                    P.op("pe", mm(pst[0:ocw, :], view[:, c, j * ocw:(j + 1) * ocw], rhs_fn(c), c == 0, c == kch - 1),
                         reads=[slot_b] + rhs_bufs, writes=[psb])
                evac(oc + j, pst, psb)
            oc += nl

    P.dma("sp", lambda h: h.dma_start(out=ident_t[:, :], in_=ident[:, :]), writes=[ident_b])
    P.op("dve", lambda h: h.memset(ones_t[:, :], 1.0), writes=[ones_b])
    P.op("dve", lambda h: h.memset(eps_t[:, :], EPS), writes=[ones_b])
    P.op("dve", lambda h: h.memset(onesb_t[:, :], 1.0), writes=[ones_b])
    P.op("dve", lambda h: h.memset(cst_t[:, 0:1], -float(np.pi)), writes=[ones_b])
    for k in range(4):
        tmp, tb = tmps.next()
        P.dma("sp", lambda h, tmp=tmp, k=k: h.dma_start(out=tmp[0:64, 0:128], in_=gains[k][:, :]), writes=[tb])
        pst, psb = PM
        P.op("pe", lambda h, tmp=tmp: h.transpose(pst[:, 0:64], tmp[0:64, 0:128], ident_t[0:64, 0:64]),
             reads=[tb, ident_b], writes=[psb])
        P.op("dve", lambda h, k=k: h.tensor_copy(out=gains_t[:, k, :], in_=pst[:, 0:64]), reads=[psb], writes=[gains_b])
    tmp, tb = tmps.next()
    P.dma("sp", lambda h, tmp=tmp: h.dma_start(out=tmp[0:16, 0:128], in_=cvec[:, :]), writes=[tb])
    pst, psb = PM
    P.op("pe", lambda h, tmp=tmp: h.transpose(pst[:, 0:16], tmp[0:16, 0:128], ident_t[0:16, 0:16]),
         reads=[tb, ident_b], writes=[psb])
    P.op("act", lambda h: h.activation(out=cact_t[:, :], in_=pst[:, 0:16], func=AF.Silu), reads=[psb], writes=[cact_b])

    xblk = sb("xblk", [128, 4, D], F32) if False else None
    for g in range(NG):
        stage = yT_t[:, :, :].rearrange("p c t -> p (c t)").rearrange("p (b d) -> p b d", b=4)
        P.dma("sp", lambda h, g=g: h.dma_start(
            out=stage, in_=xs[g * TG:(g + 1) * TG, :].rearrange("(b p) d -> p b d", p=128)), writes=[yT_b])
        for c in range(NCH):
            pst, psb = PY.next()
            for b in range(4):
                P.op("pe", lambda h, pst=pst, b=b, c=c: h.transpose(
                    pst[:, b * 128:(b + 1) * 128], stage[:, b, c * 128:(c + 1) * 128], ident_t[:, :]),
                    reads=[yT_b, ident_b], writes=[psb])
            eng = "act" if c % 2 == 0 else "dve"
            if eng == "act":
                P.op("act", lambda h, pst=pst, c=c: h.activation(out=xg_t[:, c, :], in_=pst[:, :], func=AF.Copy),
                     reads=[psb], writes=[xg_b])
            else:
                P.op("dve", lambda h, pst=pst, c=c: h.tensor_copy(out=xg_t[:, c, :], in_=pst[:, :]),
                     reads=[psb], writes=[xg_b])
        store_xg(g)

    def compute_mod(l):
        for cg in range(6):
            slot, slot_b = wrot.next()
            view = load_w_cols(ada_w[LI[l]], cg * 512, 512, slot, slot_b)
            P.dma("sp", lambda h, cg=cg: h.dma_start(out=brow_t[0:1, :], in_=ada_b[LI[l]:LI[l] + 1, cg * 512:(cg + 1) * 512]),
                  writes=[brow_b])
            pst, psb = PY.next()
            for c in range(NCH):
                P.op("pe", mm(pst[0:1, :], cact_t[:, c:c + 1], view[:, c, :], c == 0, c == NCH - 1),
                     reads=[cact_b, slot_b], writes=[psb])
            P.op("dve", lambda h, pst=pst: h.tensor_tensor(out=row_t[0:1, :], in0=pst[0:1, :], in1=brow_t[0:1, :],
                                                           op=ALU.add), reads=[psb, brow_b], writes=[row_b])
            pm, pmb = PM
            for j in range(4):
                P.op("pe", mm(pm[:, j:j + 1], row_t[0:1, j * 128:(j + 1) * 128], one11[0:1, 0:1], True, True),
                     reads=[row_b, ones_b], writes=[pmb])
            P.op("dve", lambda h, cg=cg: h.tensor_copy(out=modp_t[:, cg * 4:(cg + 1) * 4], in_=pm[:, 0:4]),
                 reads=[pmb], writes=[modp_b])
        P.dma("sp", lambda h: h.dma_start(out=modin[l].ap(), in_=modp_t[:, :]), reads=[modp_b], writes=[modin_b])
        P.dma("pool", lambda h: h.collective_compute("AllGather", ALU.bypass, replica_groups=RG,
                                                     ins=[modin[l].ap().opt()], outs=[modout[l].ap().opt()]),
              reads=[modin_b], writes=[modout_b], inc=1)
        P.dma("sp", lambda h: h.dma_start(out=modT[:, :].rearrange("p (r c) -> p r c", r=4),
                                          in_=modout[l].ap().rearrange("(r p) c -> p r c", r=4)),
              reads=[modout_b], writes=[modT_b])
        for which, (sc_i, gate_i, pre_k, post_k) in enumerate(((1, 2, 0, 1), (4, 5, 2, 3))):
            P.op("dve", lambda h, sc_i=sc_i: h.tensor_scalar_add(out=small_t[:, 0:16], in0=modT[:, sc_i * 16:(sc_i + 1) * 16],
                                                                 scalar1=1.0), reads=[modT_b], writes=[small_b])
            P.op("dve", lambda h, which=which, pre_k=pre_k: h.tensor_tensor(
                out=der_t[:, 2 * which, :], in0=small_t[:, 0:16], in1=gains_t[:, pre_k, l * 16:(l + 1) * 16], op=ALU.mult),
                reads=[small_b, gains_b], writes=[der_b])
            P.op("dve", lambda h, which=which, gate_i=gate_i, post_k=post_k: h.tensor_tensor(
                out=der_t[:, 2 * which + 1, :], in0=modT[:, gate_i * 16:(gate_i + 1) * 16],
                in1=gains_t[:, post_k, l * 16:(l + 1) * 16], op=ALU.mult),
                reads=[modT_b, gains_b], writes=[der_b])

    def ffn_group(l, g):
        load_xg(g)
        prenorm(der_t[:, 2, :], modT[:, 48:64])
        for fc in range(NFC):
            slot, slot_b = wrot.next()
            view = load_w_cols(w_gu[LI[l]], fc * 128, 128, slot, slot_b, dst_col0=0, width=256)
            load_w_cols(w_gu[LI[l]], DFF + fc * 128, 128, slot, slot_b, dst_col0=128, width=256)
            pg, pgb = PG.next()
            pu, pub = PU.next()
            for c in range(NCH):
                P.op("pe", mm(pg[:, :], view[:, c, 0:128], hT_t[:, c, :], c == 0, c == NCH - 1),
                     reads=[slot_b, hT_b], writes=[pgb])
            for c in range(NCH):
                P.op("pe", mm(pu[:, :], view[:, c, 128:256], hT_t[:, c, :], c == 0, c == NCH - 1),
                     reads=[slot_b, hT_b], writes=[pub])
            tmp, tb = tmps.next()
            P.op("act", lambda h, pg=pg, tmp=tmp: h.activation(out=tmp[:, :], in_=pg[:, :], func=AF.Silu),
                 reads=[pgb], writes=[tb])
            P.op("dve", lambda h, pu=pu, tmp=tmp, fc=fc: h.tensor_tensor(
                out=big_t[:, fc, :], in0=tmp[:, :], in1=pu[:, :], op=ALU.mult), reads=[tb, pub], writes=[big_b])
        for dc in range(NCH):
            slot, slot_b = wrot.next()
            view = slot[:, 0:NFC * 128].rearrange("p (j o) -> p j o", o=128)
            src = w_dn[LI[l]].rearrange("(j p) o -> p j o", p=128)[:, :, dc * 128:(dc + 1) * 128]
            P.dma("pool", lambda h, view=view, src=src: h.dma_start(out=view, in_=src), writes=[slot_b])
            py, pyb = PY.next()
            for j in range(NFC):
                P.op("pe", mm(py[:, :], view[:, j, :], big_t[:, j, :], j == 0, j == NFC - 1),
                     reads=[slot_b, big_b], writes=[pyb])
            P.op("act", lambda h, py=py, dc=dc: h.activation(out=yT_t[:, dc, :], in_=py[:, :], func=AF.Copy),
                 reads=[pyb], writes=[yT_b])
        postnorm_res(der_t[:, 3, :])
        store_xg(g)

    TG2 = 1024
    assert attn_lim - xg_off >= 159744, (attn_lim, xg_off)
    XY = nc.alloc_sbuf_tensor_at("f_xy", [128, NCH, TG2], F32, offset=xg_off)
    H2 = nc.alloc_sbuf_tensor_at("f_h2", [128, NCH, TG2], BF16, offset=xg_off + 65536)
    A2 = nc.alloc_sbuf_tensor_at("f_a2", [128, 22, TG2], BF16, offset=xg_off + 98304)
    fws = [(nc.alloc_sbuf_tensor_at("f_w%d" % i, [128, 4096], BF16, offset=xg_off + 143360 + i * 8192), Buf("f_w%d" % i))
           for i in range(2)]
    fws += [(nc.alloc_sbuf_tensor_at("f_w%d" % (2 + i), [128, 4096], BF16, offset=phase_base + i * 8192), Buf("f_w%d" % (2 + i)))
            for i in range(2)]
    fwrot = Rot(fws)
    xins = Rot([(nc.alloc_sbuf_tensor_at("f_xin%d" % i, [128, 512], F32, offset=phase_base + 16384 + i * 2048), Buf("f_xin%d" % i))
                for i in range(4)])
    xouts = Rot([(nc.alloc_sbuf_tensor_at("f_xo%d" % i, [128, 512], F32, offset=phase_base + 24576 + i * 2048), Buf("f_xo%d" % i))
                 for i in range(4)])
    XY_b, H2_b, A2_b = Buf("f_xy"), Buf("f_h2"), Buf("f_a2")

    def ffn_big(l):
        acol, bcol, coef = der_t[:, 2, :], modT[:, 48:64], der_t[:, 3, :]
        Wgu = w_gu[LI[l]]
        Wdn = w_dn[LI[l]].rearrange("(j p) o -> p j o", p=128)
        for g2 in range(2):
            t0 = g2 * TG2
            gb = [2 * g2, 2 * g2 + 1]
            P.dma("sp", lambda h, t0=t0: h.dma_start(out=XY[:, :, :], in_=xT_v[:, :, t0:t0 + TG2]),
                  reads=[xT_b[gb[0]], xT_b[gb[1]], xTw_b[gb[0]], xTw_b[gb[1]]], writes=[XY_b])
            for th in range(2):
                hs = slice(th * 512, (th + 1) * 512)
                ssq_rstd(None, XY_b, src_fn=lambda c, hs=hs: XY[:, c, hs])
                for c in range(NCH):
                    tmp, tb = tmps.next()
                    P.op("dve", lambda h, tmp=tmp, c=c, hs=hs: h.scalar_tensor_tensor(
                        out=tmp[:, :], in0=XY[:, c, hs], scalar=acol[:, c:c + 1], in1=rstd_t[:, :],
                        op0=ALU.mult, op1=ALU.mult), reads=[XY_b, der_b, rstd_b], writes=[tb])
                    P.op("act", lambda h, tmp=tmp, c=c, hs=hs: h.activation(
                        out=H2[:, c, hs], in_=tmp[:, :], func=AF.Identity, bias=bcol[:, c:c + 1], scale=1.0),
                        reads=[tb, modT_b], writes=[H2_b])
            for fh in range(2):
                for fcl in range(22):
                    fc = fh * 22 + fcl
                    slot, slot_b = fwrot.next()
                    view = load_w_cols(Wgu, fc * 128, 128, slot, slot_b, dst_col0=0, width=256)
                    load_w_cols(Wgu, DFF + fc * 128, 128, slot, slot_b, dst_col0=128, width=256)
                    for th in range(2):
                        hs = slice(th * 512, (th + 1) * 512)
                        pg, pgb = PG.next()
                        pu, pub = PU.next()
                        for c in range(NCH):
                            P.op("pe", mm(pg[:, :], view[:, c, 0:128], H2[:, c, hs], c == 0, c == NCH - 1),
                                 reads=[slot_b, H2_b], writes=[pgb])
                        for c in range(NCH):
                            P.op("pe", mm(pu[:, :], view[:, c, 128:256], H2[:, c, hs], c == 0, c == NCH - 1),
                                 reads=[slot_b, H2_b], writes=[pub])
                        tmp, tb = tmps.next()
                        P.op("act", lambda h, pg=pg, tmp=tmp: h.activation(out=tmp[:, :], in_=pg[:, :], func=AF.Silu),
                             reads=[pgb], writes=[tb])
                        P.op("dve", lambda h, pu=pu, tmp=tmp, fcl=fcl, hs=hs: h.tensor_tensor(
                            out=A2[:, fcl, hs], in0=tmp[:, :], in1=pu[:, :], op=ALU.mult), reads=[tb, pub], writes=[A2_b])
                for dc in range(NCH):
                    slot, slot_b = fwrot.next()
                    view = slot[:, 0:22 * 128].rearrange("p (j o) -> p j o", o=128)
                    src = Wdn[:, fh * 22:(fh + 1) * 22, dc * 128:(dc + 1) * 128]
                    P.dma("pool", lambda h, view=view, src=src: h.dma_start(out=view, in_=src), writes=[slot_b])
                    for th in range(2):
                        hs = slice(th * 512, (th + 1) * 512)
                        py, pyb = PY.next()
                        for j in range(22):
                            P.op("pe", mm(py[:, :], view[:, j, :], A2[:, j, hs], j == 0, j == 21),
                                 reads=[slot_b, A2_b], writes=[pyb])
                        if fh == 0:
                            P.op("act", lambda h, py=py, dc=dc, hs=hs: h.activation(out=XY[:, dc, hs], in_=py[:, :], func=AF.Copy),
                                 reads=[pyb], writes=[XY_b])
                        else:
                            P.op("dve", lambda h, py=py, dc=dc, hs=hs: h.tensor_tensor(out=XY[:, dc, hs], in0=py[:, :],
                                                                                       in1=XY[:, dc, hs], op=ALU.add),
                                 reads=[pyb, XY_b], writes=[XY_b])
            for th in range(2):
                hs = slice(th * 512, (th + 1) * 512)
                gg = gb[th]
                ssq_rstd(None, XY_b, src_fn=lambda c, hs=hs: XY[:, c, hs], act_share=12)
                for c in range(NCH):
                    xin, xinb = xins.next()
                    xo, xob = xouts.next()
                    P.dma("sp", lambda h, xin=xin, c=c, gg=gg: h.dma_start(out=xin[:, :], in_=xT_v[:, c, gg * 512:(gg + 1) * 512]),
                          reads=[xT_b[gg]], writes=[xinb])
                    tmp, tb = tmps.next()
                    P.op("dve", lambda h, tmp=tmp, c=c, hs=hs: h.scalar_tensor_tensor(
                        out=tmp[:, :], in0=XY[:, c, hs], scalar=coef[:, c:c + 1], in1=rstd_t[:, :],
                        op0=ALU.mult, op1=ALU.mult), reads=[XY_b, der_b, rstd_b], writes=[tb])
                    P.op("pool" if c % 4 != 3 else "dve", lambda h, tmp=tmp, xin=xin, xo=xo: h.tensor_tensor(
                        out=xo[:, :], in0=xin[:, :], in1=tmp[:, :], op=ALU.add), reads=[tb, xinb], writes=[xob])
                    P.dma("sp", lambda h, xo=xo, c=c, gg=gg: h.dma_start(out=xT_v[:, c, gg * 512:(gg + 1) * 512], in_=xo[:, :]),
                          reads=[xob], writes=[xTw_b[gg]])

    def sgu_setup():
        st = {}
        st["wsT"] = sb("sgu_wsT", [128, 16, 128], BF16)
        st["bs"] = sb("sgu_bs", [128, 16, 128], F32)
        st["lng"] = sb("sgu_lng", [128, D], F32)
        st["lnb"] = sb("sgu_lnb", [128, D], F32)
        st["tri"] = sb("sgu_tri", [128, 128], F32)
        st["stat"] = sb("sgu_stat", [128, 8], F32)
        st["vtm"] = nc.alloc_sbuf_tensor_at("sgu_vtm", [128, 4, D], BF16, offset=big_off + 16 * TG * 2)
        st["b"] = {k: Buf("sgu_" + k) for k in ("wsT", "bs", "lng", "lnb", "vtm", "tri", "stat")}
        b = st["b"]
        P.dma("sp", lambda h: h.dma_start(out=st["tri"][:, :], in_=trimask[:, :]), writes=[b["tri"]])
        P.dma("sp", lambda h: h.dma_start(out=st["bs"][:, :, :].rearrange("p g t -> p (g t)"),
                                          in_=sgu_b_s[0, :].partition_broadcast(128)), writes=[b["bs"]])
        P.dma("sp", lambda h: h.dma_start(out=st["lng"][:, :], in_=sgu_ln_g[0, :].partition_broadcast(128)),
              writes=[b["lng"]])
        P.dma("sp", lambda h: h.dma_start(out=st["lnb"][:, :], in_=sgu_ln_b[0, :].partition_broadcast(128)),
              writes=[b["lnb"]])
        for gi in range(16):
            tmp, tb = tmps.next()
            P.dma("sp", lambda h, tmp=tmp, gi=gi: h.dma_start(out=tmp[:, 0:128], in_=sgu_w_s[0, gi, :, :]), writes=[tb])
            P.op("dve", lambda h, tmp=tmp: h.tensor_tensor(out=tmp[:, 128:256], in0=tmp[:, 0:128], in1=st["tri"][:, :],
                                                           op=ALU.mult), reads=[tb, b["tri"]], writes=[tb])
            pst, psb = PM
            P.op("pe", lambda h, tmp=tmp, pst=pst: h.transpose(pst[:, 0:128], tmp[:, 128:256], ident_t[:, :]),
                 reads=[tb, ident_b], writes=[psb])
            P.op("dve", lambda h, gi=gi, pst=pst: h.tensor_copy(out=st["wsT"][:, gi, :], in_=pst[:, 0:128]),
                 reads=[psb], writes=[b["wsT"]])
        return st

    def sgu_group(st, g):
        b = st["b"]
        load_xg(g)
        prenorm(der_t[:, 0, :], modT[:, 0:16])
        W = sgu_w_in[0]
        def evac_u(oc, pst, psb):
            P.op("act", lambda h: h.activation(out=big_t[:, oc, :], in_=pst[:, :], func=AF.Gelu),
                 reads=[psb], writes=[big_b])
        linear_fm(W, 0, 16, lambda c: hT_t[:, c, :], [hT_b], NCH, evac_u, ckey="sgu", g=g)
        zv4 = yT_t[:, :, :].rearrange("p c t -> p (c t)").rearrange("p (b d) -> p b d", b=4)
        for cg in range(4):
            slot, slot_b = wrot.next()
            view = load_w_cols(W, D + cg * 512, 512, slot, slot_b, ckey=("sgv", cg), g=g)
            for blk in range(4):
                pst, psb = PY.next()
                for c in range(NCH):
                    P.op("pe", mm(pst[:, :], hT_t[:, c, blk * 128:(blk + 1) * 128], view[:, c, :], c == 0, c == NCH - 1),
                         reads=[hT_b, slot_b], writes=[psb])
                P.op("act", lambda h, pst=pst, cg=cg, blk=blk: h.activation(
                    out=zv4[:, blk, cg * 512:(cg + 1) * 512], in_=pst[:, :], func=AF.Gelu), reads=[psb], writes=[yT_b])
        stat = st["stat"]
        for blk in range(4):
            zv = zv4[:, blk, :]
            P.op("dve", lambda h, zv=zv: h.tensor_reduce(out=stat[:, 0:1], in_=zv, axis=mybir.AxisListType.X, op=ALU.add),
                 reads=[yT_b], writes=[b["stat"]])
            P.op("act", lambda h, zv=zv, blk=blk: h.activation(out=st["vtm"][:, blk, :], in_=zv, func=AF.Square,
                                                               accum_out=stat[:, 1:2]),
                 reads=[yT_b], writes=[b["stat"], b["vtm"]])
            P.op("dve", lambda h: h.tensor_scalar_mul(out=stat[:, 2:3], in0=stat[:, 0:1], scalar1=1.0 / D),
                 reads=[b["stat"]], writes=[b["stat"]])
            P.op("dve", lambda h: h.tensor_tensor(out=stat[:, 3:4], in0=stat[:, 2:3], in1=stat[:, 2:3], op=ALU.mult),
                 reads=[b["stat"]], writes=[b["stat"]])
            P.op("dve", lambda h: h.scalar_tensor_tensor(out=stat[:, 4:5], in0=stat[:, 1:2], scalar=1.0 / D,
                                                         in1=stat[:, 3:4], op0=ALU.mult, op1=ALU.subtract),
                 reads=[b["stat"]], writes=[b["stat"]])
            P.op("act", lambda h: h.activation(out=stat[:, 5:6], in_=stat[:, 4:5], func=AF.Sqrt, bias=eps_t[:, 0:1],
                                               scale=1.0), reads=[b["stat"], ones_b], writes=[b["stat"]])
            P.op("dve", lambda h: h.reciprocal(out=stat[:, 6:7], in_=stat[:, 5:6]), reads=[b["stat"]], writes=[b["stat"]])
            P.op("dve", lambda h, zv=zv: h.tensor_scalar(out=zv, in0=zv, scalar1=stat[:, 2:3], scalar2=stat[:, 6:7],
                                                         op0=ALU.subtract, op1=ALU.mult), reads=[b["stat"], yT_b], writes=[yT_b])
            P.op("pool", lambda h, zv=zv: h.tensor_tensor(out=zv, in0=zv, in1=st["lng"][:, :], op=ALU.mult),
                 reads=[yT_b, b["lng"]], writes=[yT_b])
            P.op("dve", lambda h, zv=zv, blk=blk: h.tensor_tensor(out=st["vtm"][:, blk, :], in0=zv, in1=st["lnb"][:, :],
                                                                  op=ALU.add), reads=[yT_b, b["lnb"]], writes=[b["vtm"]])
        for gi in range(16):
            pst, psb = PY.next()
            for blk in range(4):
                P.op("pe", mm(pst[:, blk * 128:(blk + 1) * 128], st["vtm"][:, blk, gi * 128:(gi + 1) * 128],
                              st["wsT"][:, gi, :], True, True), reads=[b["vtm"], b["wsT"]], writes=[psb])
            tmp, tb = tmps.next()
            P.op("dve", lambda h, pst=pst, tmp=tmp, gi=gi: h.tensor_tensor(
                out=tmp[:, :].rearrange("p (b t) -> p b t", b=4), in0=pst[:, :].rearrange("p (b t) -> p b t", b=4),
                in1=st["bs"][:, gi:gi + 1, :].broadcast_to([128, 4, 128]), op=ALU.add),
                reads=[psb, b["bs"]], writes=[tb])
            P.op("pool", lambda h, tmp=tmp, gi=gi: h.tensor_tensor(out=big_t[:, gi, :], in0=big_t[:, gi, :], in1=tmp[:, :],
                                                                   op=ALU.mult), reads=[tb, big_b], writes=[big_b])
        def evac_y(oc, pst, psb):
            P.op("act", lambda h: h.activation(out=yT_t[:, oc, :], in_=pst[:, :], func=AF.Copy),
                 reads=[psb], writes=[yT_b])
        linear_fm(sgu_w_out[0], 0, 16, lambda c: big_t[:, c, :], [big_b], NCH, evac_y, ckey="sgo", g=g)
        postnorm_res(der_t[:, 1, :])
        store_xg(g)

    def at_alloc(state, name, shape, dt):
        nbytes = int(np.prod(shape[1:])) * (4 if dt in (F32, I32) else 2)
        off = (state["off"] + 31) // 32 * 32
        state["off"] = off + nbytes
        assert state["off"] <= attn_lim, (name, state["off"], attn_lim)
        state["n"] += 1
        return nc.alloc_sbuf_tensor_at("%s_%d" % (name, state["n"]), list(shape), dt, offset=off)

    def evac_to(dst_fn, dst_buf, func=None, eng="act"):
        def ev(oc, pst, psb):
            P.op("act", lambda h: h.activation(out=dst_fn(oc), in_=pst[0:dst_fn(oc).shape[0], :], func=AF.Copy),
                 reads=[psb], writes=[dst_buf])
        return ev

    def attn_out_phase(W2d, ckey=None):
        for g in range(NG):
            P.dma("sp", lambda h, g=g: h.dma_start(
                out=big_t[:, 0:16, :], in_=att_s.rearrange("(c p) t -> p c t", p=128)[:, :, g * TG:(g + 1) * TG]),
                reads=[att_b], writes=[big_b])
            load_xg(g)
            linear_fm(W2d, 0, 16, lambda c: big_t[:, c, :], [big_b], NCH,
                      evac_to(lambda oc: yT_t[:, oc, :], yT_b), ckey=ckey, g=g)
            postnorm_res(der_t[:, 1, :])
            store_xg(g)

    def all_gather(src, src_b, dst, dst_b):
        P.dma("pool", lambda h: h.collective_compute("AllGather", ALU.bypass, replica_groups=RG,
                                                     ins=[src.ap().opt()], outs=[dst.ap().opt()]),
              reads=[src_b], writes=[dst_b], inc=1)

    fox_cache = {}

    def fox_layer(l, j):
        dr = fdr
        db = dr["b"]
        W = fox_w_in[j]
        kin_v = [t.ap().rearrange("(c p) t -> p c t", p=128) for t in dr["kin"]]
        vin_v = [t.ap().rearrange("p (h i d) -> p h i d", h=2, i=16) for t in dr["vin"]]
        lin_v = dr["lin"].ap().rearrange("(i p) h -> p i h", p=128)
        ph = {"off": phase_base, "n": 100 * l}
        bf_t = sb("fox_bf%d" % l, [128, 16], F32)
        vst_t = sb("fox_vst%d" % l, [128, 4, 512], BF16)
        lf_t = sb("fox_lf%d" % l, [128, 4, 16], F32)
        bf_b, vst_b, lf_b = fox_cache.setdefault("p1", (Buf("bf"), Buf("vst"), Buf("lf")))
        P.dma("sp", lambda h: h.dma_start(out=bf_t[:, :], in_=fox_b_f[j, :].partition_broadcast(128)), writes=[bf_b])
        for g in range(NG):
            load_xg(g)
            prenorm(der_t[:, 0, :], modT[:, 0:16])
            linear_fm(W, 0, 16, lambda c: hT_t[:, c, :], [hT_b], NCH, evac_to(lambda oc: big_t[:, oc, :], big_b),
                      ckey=("fq", j), g=g)
            P.dma("sp", lambda h, g=g: h.dma_start(
                out=q_s.rearrange("(c p) t -> p c t", p=128)[:, :, g * TG:(g + 1) * TG], in_=big_t[:, 0:16, :]),
                reads=[big_b], writes=[q_b])
            linear_fm(W, D, 16, lambda c: hT_t[:, c, :], [hT_b], NCH, evac_to(lambda oc: big_t[:, 16 + oc, :], big_b),
                      ckey=("fk", j), g=g)
            for ck in range(8):
                P.dma("sp", lambda h, g=g, ck=ck: h.dma_start(out=kin_v[ck][:, :, g * TG:(g + 1) * TG],
                                                               in_=big_t[:, 16 + 2 * ck:18 + 2 * ck, :]),
                      reads=[big_b], writes=[db["kin"]])
            for cg in range(4):
                slot, slot_b = wrot.next()
                view = load_w_cols(W, 2 * D + cg * 512, 512, slot, slot_b, ckey=("fv", j, cg), g=g)
                for blk in range(4):
                    pst, psb = PY.next()
                    for c in range(NCH):
                        P.op("pe", mm(pst[:, :], hT_t[:, c, blk * 128:(blk + 1) * 128], view[:, c, :], c == 0, c == NCH - 1),
                             reads=[hT_b, slot_b], writes=[psb])
                    P.op("act", lambda h, pst=pst, blk=blk: h.activation(out=vst_t[:, blk, :], in_=pst[:, :], func=AF.Copy),
                         reads=[psb], writes=[vst_b])
                for hh in range(4):
                    P.dma("sp", lambda h, g=g, cg=cg, hh=hh: h.dma_start(
                        out=vin_v[(cg * 4 + hh) // 2][:, (cg * 4 + hh) % 2, g * 4:(g + 1) * 4, :],
                        in_=vst_t[:, :, hh * 128:(hh + 1) * 128]), reads=[vst_b], writes=[db["vin"]])
            slot, slot_b = wrot.next()
            view = load_w_cols(W, 3 * D, 16, slot, slot_b, ckey=("ffg", j), g=g)
            for blk in range(4):
                pst, psb = PY.next()
                for c in range(NCH):
                    P.op("pe", mm(pst[:, 0:16], hT_t[:, c, blk * 128:(blk + 1) * 128], view[:, c, :], c == 0, c == NCH - 1),
                         reads=[hT_b, slot_b], writes=[psb])
                P.op("dve", lambda h, pst=pst, blk=blk: h.tensor_tensor(out=lf_t[:, blk, :], in0=pst[:, 0:16], in1=bf_t[:, :],
                                                                        op=ALU.add), reads=[psb, bf_b], writes=[lf_b])
            lf2 = lf_t[:, :, :].rearrange("p b h -> p (b h)")
            P.op("act", lambda h: h.activation(out=lf2, in_=lf2, func=AF.Exp, scale=-1.0), reads=[lf_b], writes=[lf_b])
            P.op("act", lambda h: h.activation(out=lf2, in_=lf2, func=AF.Ln, bias=1.0, scale=1.0), reads=[lf_b], writes=[lf_b])
            P.op("dve", lambda h: h.tensor_scalar_mul(out=lf2, in0=lf2, scalar1=-1.0), reads=[lf_b], writes=[lf_b])
            P.dma("sp", lambda h, g=g: h.dma_start(out=lin_v[:, g * 4:(g + 1) * 4, :], in_=lf_t[:, :, :]),
                  reads=[lf_b], writes=[db["lin"]])
        all_gather(dr["lin"], db["lin"], dr["lout"], db["lout"])
        for ck in range(8):
            all_gather(dr["kin"][ck], db["kin"], dr["kout"][ck], db["kout"][ck])
            all_gather(dr["vin"][ck], db["vin"], dr["vout"][ck], db["vout"][ck])
        P.barrier(exclude=db["kout"] + db["vout"])
        A = {"off": xg_off, "n": 1000 * (l + 1)}
        lfa = at_alloc(A, "lfa", [128, 64, 16], F32)
        ftm = at_alloc(A, "ftm", [128, 64, 16], F32)
        tbc = at_alloc(A, "tbc", [128, 64, 16], F32)
        pfa = at_alloc(A, "pfa", [128, 64, 16], F32)
        pfb = at_alloc(A, "pfb", [128, 64, 16], F32)
        fref = at_alloc(A, "fref", [128, 16, 16], F32)
        triu_t = at_alloc(A, "triu", [128, 128], F32)
        oh4_t = at_alloc(A, "oh4", [128, 4], F32)
        mfox_t = at_alloc(A, "mfox", [128, 4, 128], BF16)
        zo_t = at_alloc(A, "zo", [128, 16, 64], F32)
        qh = [at_alloc(A, "qh%d" % i, [128, TOK], BF16) for i in range(2)]
        kh = [at_alloc(A, "kh%d" % i, [128, 4, TOK], BF16) for i in range(2)]
        vh = [at_alloc(A, "vh%d" % i, [128, 4, 16, 128], BF16) for i in range(2)]
        bh = [at_alloc(A, "bh%d" % i, [128, 16, 64], F32) for i in range(2)]
        ah = [at_alloc(A, "ah%d" % i, [128, TOK], BF16) for i in range(2)]
        pts = Rot([(at_alloc(A, "pt%d" % i, [128, 512], BF16), [Buf("pt%d_%d" % (i, m)) for m in range(4)]) for i in range(4)])
        rec_t = at_alloc(A, "rec", [128, 512], F32)
        B_ = fox_cache.setdefault("B_", {k: Buf("fx_" + k) for k in (
            "lfa", "ftm", "tbc", "pfa", "pfb", "fref", "triu", "oh4", "mfox", "rec",
            "qh0", "qh1", "kh0", "kh1", "vh0", "vh1", "bh0", "bh1", "ah0", "ah1")})
        P.dma("sp", lambda h: h.dma_start(out=triu_t[:, :], in_=triu[:, :]), writes=[B_["triu"]])
        P.dma("sp", lambda h: h.dma_start(out=oh4_t[:, :], in_=oh4_in[:, :]), writes=[B_["oh4"]])
        P.dma("sp", lambda h: h.dma_start(out=zo_t[:, :, :].rearrange("p a b -> p (a b)"), in_=zo_in[:, :]), writes=[B_["oh4"]])
        P.dma("pool", lambda h: h.dma_start(out=mfox_t[:, :, :].rearrange("p j q -> p (j q)"), in_=m_fox_in[:, :]),
              writes=[B_["mfox"]])
        lout_ap = dr["lout"].ap()
        for rr in range(4):
            P.dma("sp", lambda h, rr=rr: h.dma_start(
                out=lfa[:, :, :].rearrange("p (i r) h -> p r i h", r=4)[:, rr],
                in_=lout_ap[rr * TOK:(rr + 1) * TOK, :].rearrange("(i p) h -> p i h", p=128)),
                reads=[db["lout"]], writes=[B_["lfa"]])
        lfa2 = lfa[:, :, :].rearrange("p g h -> p (g h)")
        ftm2 = ftm[:, :, :].rearrange("p g h -> p (g h)")
        tbc2 = tbc[:, :, :].rearrange("p g h -> p (g h)")
        pfa2 = pfa[:, :, :].rearrange("p g h -> p (g h)")
        pfb2 = pfb[:, :, :].rearrange("p g h -> p (g h)")
        for half in range(2):
            cs = slice(half * 512, (half + 1) * 512)
            pst, psb = PY.next()
            P.op("pe", mm(pst[:, :], triu_t[:, :], lfa2[:, cs], True, True), reads=[B_["triu"], B_["lfa"]], writes=[psb])
            P.op("dve", lambda h, pst=pst, cs=cs: h.tensor_copy(out=ftm2[:, cs], in_=pst[:, :]), reads=[psb], writes=[B_["ftm"]])
            pst, psb = PY.next()
            P.op("pe", mm(pst[:, :], ones_t[:, :], lfa2[:, cs], True, True), reads=[ones_b, B_["lfa"]], writes=[psb])
            P.op("dve", lambda h, pst=pst, cs=cs: h.tensor_copy(out=tbc2[:, cs], in_=pst[:, :]), reads=[psb], writes=[B_["tbc"]])
        P.op("dve", lambda h: h.tensor_copy(out=pfa2, in_=tbc2), reads=[B_["tbc"]], writes=[B_["pfa"]])
        cur, curb, oth, othb = pfa2, B_["pfa"], pfb2, B_["pfb"]
        sh = 1
        while sh < 64:
            w = sh * 16
            P.op("dve", lambda h, cur=cur, oth=oth, w=w: h.tensor_copy(out=oth[:, 0:w], in_=cur[:, 0:w]),
                 reads=[curb], writes=[othb])
            P.op("dve", lambda h, cur=cur, oth=oth, w=w: h.tensor_tensor(out=oth[:, w:1024], in0=cur[:, w:1024],
                                                                         in1=cur[:, 0:1024 - w], op=ALU.add),
                 reads=[curb], writes=[othb])
            cur, curb, oth, othb = oth, othb, cur, curb
            sh *= 2
        P.op("dve", lambda h, cur=cur, oth=oth: h.tensor_tensor(out=oth, in0=cur, in1=tbc2, op=ALU.subtract),
             reads=[curb, B_["tbc"]], writes=[othb])
        P.op("dve", lambda h, oth=oth: h.tensor_tensor(out=ftm2, in0=ftm2, in1=oth, op=ALU.add),
             reads=[othb, B_["ftm"]], writes=[B_["ftm"]])
        P.op("dve", lambda h, cur=cur, oth=oth: h.scalar_tensor_tensor(out=oth, in0=tbc2, scalar=-0.5, in1=cur,
                                                                        op0=ALU.mult, op1=ALU.add),
             reads=[curb, B_["tbc"]], writes=[othb])
        fmid4 = (pfa if oth is pfa2 else pfb)[:, :, :].rearrange("p (i r) h -> p r i h", r=4)
        P.op("dve", lambda h: h.tensor_scalar(out=fref[:, :, :], in0=fmid4[:, 0], scalar1=oh4_t[:, 0:1], scalar2=None,
                                              op0=ALU.mult), reads=[othb, B_["oh4"]], writes=[B_["fref"]])
        for rr in range(1, 4):
            P.op("dve", lambda h, rr=rr: h.scalar_tensor_tensor(out=fref[:, :, :], in0=fmid4[:, rr], scalar=oh4_t[:, rr:rr + 1],
                                                                in1=fref[:, :, :], op0=ALU.mult, op1=ALU.add),
                 reads=[othb, B_["oh4"], B_["fref"]], writes=[B_["fref"]])
        kout_v = [t.ap().rearrange("(r c d) t -> d r c t", r=4, d=128) for t in dr["kout"]]
        vout_v = [t.ap().rearrange("(r p) (h i d) -> p r h i d", r=4, h=2, i=16) for t in dr["vout"]]
        PS_ST = Rot([psum[0], psum[1], psum[2], psum[7]])
        PS_O = Rot([psum[3], psum[4]])
        PS_D = Rot([psum[5], psum[6]])
        scale = 128.0 ** -0.5
        def head_res(hd):
            s2 = hd % 2
            return (qh[s2], kh[s2], vh[s2], bh[s2], ah[s2],
                    B_["qh%d" % s2], B_["kh%d" % s2], B_["vh%d" % s2], B_["bh%d" % s2], B_["ah%d" % s2])

        def emit_head_loads(hd):
            qt, kt, vt, bt, at, qb, kb, vb, bb, ab = head_res(hd)
            P.dma("sp", lambda h: h.dma_start(out=qt[:, :], in_=q_s[hd * 128:(hd + 1) * 128, :]), reads=[q_b], writes=[qb])
            P.dma("sp", lambda h: h.dma_start(out=kt[:, :, :], in_=kout_v[hd // 2][:, :, hd % 2, :]),
                  reads=[db["kout"][hd // 2]], writes=[kb])
            P.dma("sp", lambda h: h.dma_start(out=vt[:, :, :, :], in_=vout_v[hd // 2][:, :, hd % 2]),
                  reads=[db["vout"][hd // 2]], writes=[vb])
            P.op("dve", lambda h: h.tensor_tensor(
                out=bt[:, :, :], in0=fref[:, :, hd:hd + 1].broadcast_to([128, 16, 64]),
                in1=ftm[:, :, hd:hd + 1].rearrange("p g o -> p o g").broadcast_to([128, 16, 64]), op=ALU.subtract),
                reads=[B_["fref"], B_["ftm"]], writes=[bb])
            P.op("dve", lambda h: h.tensor_tensor(out=bt[:, :, :], in0=bt[:, :, :], in1=zo_t[:, :, :], op=ALU.add),
                 reads=[bb, B_["oh4"]], writes=[bb])

        steps = [(hd, jq, gk) for hd in range(16) for jq in range(4) for gk in range(16 * (jq + 1))]
        nst = len(steps)
        qk = {}
        acc = {}

        def emit_qk(idx):
            hd, jq, gk = steps[idx]
            qt, kt, vt, bt, at, qb, kb, vb, bb, ab = head_res(hd)
            rr, ii = gk % 4, gk // 4
            mmin = max(0, -(-(gk - 3 - 16 * jq) // 4))
            c0 = mmin * 128
            pst, psb = PS_ST.next()
            P.op("pe", mm(pst[:, c0:512], kt[:, rr, ii * 128:(ii + 1) * 128], qt[:, jq * 512 + c0:(jq + 1) * 512],
                          True, True), reads=[kb, qb], writes=[psb])
            qk[idx] = (pst, psb, mmin, c0)

        def emit_exp(idx):
            hd, jq, gk = steps[idx]
            qt, kt, vt, bt, at, qb, kb, vb, bb, ab = head_res(hd)
            pst, psb, mmin, c0 = qk[idx]
            pt, ptb = pts.next()
            for m in range(mmin, 4):
                li = 4 * jq + m
                P.op("act", lambda h, m=m, li=li: h.activation(
                    out=pt[:, m * 128:(m + 1) * 128], in_=pst[:, m * 128:(m + 1) * 128], func=AF.Exp,
                    bias=bt[:, li, gk:gk + 1], scale=scale), reads=[psb, bb], writes=[ptb[m]])
                jm = gk - 4 * li
                if 0 <= jm <= 3:
                    P.op("pool", lambda h, m=m, jm=jm: h.tensor_tensor(
                        out=pt[:, m * 128:(m + 1) * 128], in0=pt[:, m * 128:(m + 1) * 128], in1=mfox_t[:, jm, :],
                        op=ALU.mult), reads=[ptb[m], B_["mfox"]], writes=[ptb[m]])
            qk[idx] = (pst, psb, mmin, c0, pt, ptb)

        def emit_pv(idx):
            hd, jq, gk = steps[idx]
            qt, kt, vt, bt, at, qb, kb, vb, bb, ab = head_res(hd)
            pst, psb, mmin, c0, pt, ptb = qk.pop(idx)
            rr, ii = gk % 4, gk // 4
            ng = 16 * (jq + 1)
            if gk == 0:
                acc[(hd, jq)] = (PS_O.next(), PS_D.next())
            (po, pob), (pd, pdb) = acc[(hd, jq)]
            P.op("pe", mm(po[:, c0:512], vt[:, rr, ii, :], pt[:, c0:512], gk == 0, gk == ng - 1),
                 reads=[vb] + ptb[mmin:], writes=[pob])
            P.op("pe", mm(pd[:, c0:512], onesb_t[:, :], pt[:, c0:512], gk == 0, gk == ng - 1),
                 reads=[ones_b] + ptb[mmin:], writes=[pdb])
            if gk == ng - 1:
                del acc[(hd, jq)]
                P.op("dve", lambda h: h.reciprocal(out=rec_t[:, :], in_=pd[:, :]), reads=[pdb], writes=[B_["rec"]])
                P.op("dve", lambda h: h.tensor_tensor(out=at[:, jq * 512:(jq + 1) * 512], in0=po[:, :],
                                                      in1=rec_t[:, :], op=ALU.mult),
                     reads=[pob, B_["rec"]], writes=[ab])
                if jq == 3:
                    P.dma("sp", lambda h: h.dma_start(out=att_s[hd * 128:(hd + 1) * 128, :], in_=at[:, :]),
                          reads=[ab], writes=[att_b])
                    if hd + 2 < 16:
                        emit_head_loads(hd + 2)

        emit_head_loads(0)
        emit_head_loads(1)
        emit_qk(0)
        emit_qk(1)
        emit_qk(2)
        for idx in range(nst):
            emit_exp(idx)
            if idx + 3 < nst:
                emit_qk(idx + 3)
            emit_pv(idx)
        P.barrier()
        attn_out_phase(fox_w_out[j], ckey=("fo", j))
        P.barrier()

    def swa_layer(l):
        dr = swa_dr
        db = dr["b"]
        W = swa_w_in[0]
        kin_v = [t.ap().rearrange("(h d) t -> d h t", d=64) for t in dr["kin"]]
        vin_v = [t.ap().rearrange("p (i f) -> p i f", i=8) for t in dr["vin"]]
        q_v = q_s.rearrange("(h d) t -> d h t", d=64)
        att_v = att_s.rearrange("(h d) t -> d h t", d=64)
        c16 = sb("swa_c16", [16, TOK], F32)
        s16 = sb("swa_s16", [16, TOK], F32)
        pmat_t = sb("swa_pmat", [64, 16], BF16)
        invf_t = sb("swa_invf", [16, 1], F32)
        vst_t = sb("swa_vst", [128, 4, 512], BF16)
        SB_ = {k: Buf("sw_" + k) for k in ("c16", "s16", "pmat", "invf", "vst", "posi")}
        posi_t = sb("swa_posi", [16, TOK], I32)
        P.dma("sp", lambda h: h.dma_start(out=posi_t[:, :], in_=posin[0, :].partition_broadcast(16)), writes=[SB_["posi"]])
        P.dma("pool", lambda h: h.dma_start(out=pmat_t[:, :], in_=pmat_in[:, :]), writes=[SB_["pmat"]])
        P.dma("sp", lambda h: h.dma_start(out=invf_t[:, :], in_=invf_in[:, :]), writes=[SB_["invf"]])
        PI = float(np.pi)
        C1 = 6.28125
        C2 = float(2 * np.pi - 6.28125)
        ang = yT_t[0:16, 8:12, :].rearrange("p c t -> p (c t)")
        nf = yT_t[0:16, 0:4, :].rearrange("p c t -> p (c t)")
        mk = yT_t[0:16, 4:8, :].rearrange("p c t -> p (c t)")
        ys, yc = s16[:, :], c16[:, :]
        RW = dict(reads=[SB_["posi"], SB_["invf"], yT_b, SB_["s16"], SB_["c16"]],
                  writes=[yT_b, SB_["s16"], SB_["c16"], SB_["posi"]])
        P.op("dve", lambda h: h.tensor_copy(out=ang, in_=posi_t[:, :]), **RW)
        P.op("dve", lambda h: h.tensor_scalar_mul(out=ang, in0=ang, scalar1=invf_t[:, 0:1]), **RW)
        P.op("dve", lambda h: h.tensor_scalar_mul(out=nf, in0=ang, scalar1=float(1.0 / (2 * np.pi))), **RW)
        P.op("dve", lambda h: h.tensor_copy(out=posi_t[:, :], in_=nf), **RW)
        P.op("dve", lambda h: h.tensor_copy(out=nf, in_=posi_t[:, :]), **RW)
        P.op("dve", lambda h: h.scalar_tensor_tensor(out=ys, in0=nf, scalar=-C1, in1=ang, op0=ALU.mult, op1=ALU.add), **RW)
        P.op("dve", lambda h: h.scalar_tensor_tensor(out=ys, in0=nf, scalar=-C2, in1=ys, op0=ALU.mult, op1=ALU.add), **RW)
        P.op("dve", lambda h: h.tensor_single_scalar(out=mk, in_=ys, scalar=PI, op=ALU.is_gt), **RW)
        P.op("dve", lambda h: h.scalar_tensor_tensor(out=ys, in0=mk, scalar=-2 * PI, in1=ys, op0=ALU.mult, op1=ALU.add), **RW)
        P.op("dve", lambda h: h.tensor_single_scalar(out=mk, in_=ys, scalar=-PI, op=ALU.is_lt), **RW)
        P.op("dve", lambda h: h.scalar_tensor_tensor(out=ys, in0=mk, scalar=2 * PI, in1=ys, op0=ALU.mult, op1=ALU.add), **RW)
        P.op("dve", lambda h: h.tensor_scalar_add(out=yc, in0=ys, scalar1=PI / 2), **RW)
        P.op("dve", lambda h: h.tensor_single_scalar(out=mk, in_=yc, scalar=PI, op=ALU.is_gt), **RW)
        P.op("dve", lambda h: h.scalar_tensor_tensor(out=yc, in0=mk, scalar=-2 * PI, in1=yc, op0=ALU.mult, op1=ALU.add), **RW)
        P.op("act", lambda h: h.activation(out=ys, in_=ys, func=AF.Sin), reads=[SB_["s16"]], writes=[SB_["s16"]])
        P.op("act", lambda h: h.activation(out=yc, in_=yc, func=AF.Sin), reads=[SB_["c16"]], writes=[SB_["c16"]])

        def rope(tile_fn, nheads, g):
            for hh in range(nheads):
                pst, psb = PM
                P.op("pe", mm(pst[0:16, :], pmat_t[:, :], tile_fn(hh), True, True), reads=[SB_["pmat"], big_b], writes=[psb])
                t1, t1b = tmps.next()
                t2, t2b = tmps.next()
                P.op("dve", lambda h, t1=t1, hh=hh: h.tensor_tensor(out=t1[0:16, :], in0=tile_fn(hh)[0:16, :],
                                                                    in1=c16[:, g * TG:(g + 1) * TG], op=ALU.mult),
                     reads=[big_b, SB_["c16"]], writes=[t1b])
                P.op("dve", lambda h, t2=t2, pst=pst: h.tensor_tensor(out=t2[0:16, :], in0=pst[0:16, :],
                                                                      in1=s16[:, g * TG:(g + 1) * TG], op=ALU.mult),
                     reads=[psb, SB_["s16"]], writes=[t2b])
                P.op("pool", lambda h, t1=t1, t2=t2, hh=hh: h.tensor_tensor(out=tile_fn(hh)[0:16, :], in0=t1[0:16, :],
                                                                            in1=t2[0:16, :], op=ALU.add),
                     reads=[t1b, t2b], writes=[big_b])

        for g in range(NG):
            load_xg(g)
            prenorm(der_t[:, 0, :], modT[:, 0:16])
            linear_fm(W, 0, 32, lambda c: hT_t[:, c, :], [hT_b], NCH, evac_to(lambda oc: big_t[0:64, oc, :], big_b),
                      ocw=64, per_load=8, ckey="swq", g=g)
            rope(lambda hh: big_t[0:64, hh, :], 32, g)
            P.dma("sp", lambda h, g=g: h.dma_start(out=q_v[:, :, g * TG:(g + 1) * TG], in_=big_t[0:64, 0:32, :]),
                  reads=[big_b], writes=[q_b])
            linear_fm(W, D, 8, lambda c: hT_t[:, c, :], [hT_b], NCH, evac_to(lambda oc: big_t[0:64, 32 + oc, :], big_b),
                      ocw=64, per_load=8, ckey="swk", g=g)
            rope(lambda hh: big_t[0:64, 32 + hh, :], 8, g)
            for ck in range(2):
                P.dma("sp", lambda h, g=g, ck=ck: h.dma_start(out=kin_v[ck][:, :, g * TG:(g + 1) * TG],
                                                               in_=big_t[0:64, 32 + 4 * ck:36 + 4 * ck, :]),
                      reads=[big_b], writes=[db["kin"]])
            slot, slot_b = wrot.next()
            view = load_w_cols(W, D + 512, 512, slot, slot_b, ckey="swv", g=g)
            for blk in range(4):
                pst, psb = PY.next()
                for c in range(NCH):
                    P.op("pe", mm(pst[:, :], hT_t[:, c, blk * 128:(blk + 1) * 128], view[:, c, :], c == 0, c == NCH - 1),
                         reads=[hT_b, slot_b], writes=[psb])
                P.op("act", lambda h, pst=pst, blk=blk: h.activation(out=vst_t[:, blk, :], in_=pst[:, :], func=AF.Copy),
                     reads=[psb], writes=[SB_["vst"]])
            P.dma("sp", lambda h, g=g: h.dma_start(out=vin_v[g // 2][:, (g % 2) * 4:(g % 2) * 4 + 4, :], in_=vst_t[:, :, :]),
                  reads=[SB_["vst"]], writes=[db["vin"]])
        for ck in range(2):
            all_gather(dr["kin"][ck], db["kin"], dr["kout"][ck], db["kout"])
            all_gather(dr["vin"][ck], db["vin"], dr["vout"][ck], db["vout"])
        P.barrier()
        A = {"off": xg_off, "n": 5000}
        kc2 = [at_alloc(A, "kc%d" % i, [64, 8, 128], BF16) for i in range(2)]
        vc2 = [at_alloc(A, "vc%d" % i, [128, 512], BF16) for i in range(2)]
        kp2_ = [at_alloc(A, "kp%d" % i, [64, 8, 128], BF16) for i in range(2)]
        vp2_ = [at_alloc(A, "vp%d" % i, [128, 512], BF16) for i in range(2)]
        qb2 = [at_alloc(A, "qblk%d" % i, [64, 32, 128], BF16) for i in range(2)]
        ab2 = [at_alloc(A, "ablk%d" % i, [64, 32, 128], BF16) for i in range(2)]
        kcand = at_alloc(A, "kcand", [64, 5, 8, 128], BF16)
        vcand = at_alloc(A, "vcand", [128, 5, 512], BF16)
        mtri = at_alloc(A, "mtri", [128, 128], BF16)
        mprev = at_alloc(A, "mprev", [128, 128], BF16)
        mprev0 = at_alloc(A, "mprev0", [128, 128], BF16)
        sel5 = at_alloc(A, "sel5", [128, 5], F32)
        sinke = at_alloc(A, "sinke", [64, 32], F32)
        den_t = at_alloc(A, "den", [64, 512], F32)
        ptc = Rot([(at_alloc(A, "ptc%d" % i, [128, 512], BF16), Buf("ptc%d" % i)) for i in range(2)])
        ptp = Rot([(at_alloc(A, "ptp%d" % i, [128, 512], BF16), Buf("ptp%d" % i)) for i in range(2)])
        B_ = {k: Buf("sw3_" + k) for k in ("kc0", "kc1", "vc0", "vc1", "kcand", "vcand", "kp0", "kp1", "vp0", "vp1",
                                           "qblk0", "qblk1", "ablk0", "ablk1", "mtri", "mprev",
                                           "mprev0", "sel5", "sinke", "den")}
        P.dma("pool", lambda h: h.dma_start(out=mtri[:, :], in_=triu[:, :]), writes=[B_["mtri"]])
        P.dma("pool", lambda h: h.dma_start(out=mprev[:, :], in_=m_prev_in[:, :]), writes=[B_["mprev"]])
        P.dma("pool", lambda h: h.dma_start(out=mprev0[:, :], in_=m_prev0_in[:, :]), writes=[B_["mprev0"]])
        P.dma("sp", lambda h: h.dma_start(out=sel5[:, :], in_=sel5_in[:, :]), writes=[B_["sel5"]])
        P.dma("sp", lambda h: h.dma_start(out=sinke[:, :], in_=swa_sinks[0, :].partition_broadcast(64)), writes=[B_["sinke"]])
        P.op("act", lambda h: h.activation(out=sinke[:, :], in_=sinke[:, :], func=AF.Exp), reads=[B_["sinke"]], writes=[B_["sinke"]])
        kout_v = [t.ap().rearrange("(r h d) t -> d r h t", r=4, d=64) for t in dr["kout"]]
        vout_v = [t.ap().rearrange("(r p) (i f) -> p r i f", r=4, i=8) for t in dr["vout"]]
        PS_C = Rot([psum[0], psum[1]])
        PS_P = Rot([psum[2], psum[3]])
        PS_O = Rot([psum[4], psum[5]])
        PS_D = Rot([psum[6], psum[7]])
        scale = 64.0 ** -0.5

        def emit_block_loads(i):
            par = i % 2
            kc, vc, kp, vp, qblk = kc2[par], vc2[par], kp2_[par], vp2_[par], qb2[par]
            kcb, vcb, kpb, vpb, qbb = (B_["kc%d" % par], B_["vc%d" % par], B_["kp%d" % par], B_["vp%d" % par],
                                       B_["qblk%d" % par])
            ip = max(i - 1, 0)
            for ck in range(2):
                P.dma("sp", lambda h, ck=ck: h.dma_start(out=kc[:, 4 * ck:4 * ck + 4, :],
                                                         in_=kin_v[ck][:, :, i * 128:(i + 1) * 128]),
                      reads=[db["kin"]], writes=[kcb])
                for rr in range(4):
                    P.dma("sp", lambda h, ck=ck, rr=rr: h.dma_start(
                        out=kcand[:, rr, 4 * ck:4 * ck + 4, :], in_=kout_v[ck][:, rr, :, i * 128:(i + 1) * 128]),
                        reads=[db["kout"]], writes=[B_["kcand"]])
                P.dma("sp", lambda h, ck=ck: h.dma_start(
                    out=kcand[:, 4, 4 * ck:4 * ck + 4, :], in_=kout_v[ck][:, 3, :, ip * 128:(ip + 1) * 128]),
                    reads=[db["kout"]], writes=[B_["kcand"]])
            P.dma("sp", lambda h: h.dma_start(out=vc[:, :], in_=vin_v[i // 8][:, i % 8, :]), reads=[db["vin"]], writes=[vcb])
            P.dma("sp", lambda h: h.dma_start(out=vcand[:, 0:4, :], in_=vout_v[i // 8][:, :, i % 8, :]),
                  reads=[db["vout"]], writes=[B_["vcand"]])
            P.dma("sp", lambda h: h.dma_start(out=vcand[:, 4, :], in_=vout_v[ip // 8][:, 3, ip % 8, :]),
                  reads=[db["vout"]], writes=[B_["vcand"]])
            P.dma("sp", lambda h: h.dma_start(out=qblk[:, :, :], in_=q_v[:, :, i * 128:(i + 1) * 128]),
                  reads=[q_b], writes=[qbb])

        def emit_block_select(i):
            par = i % 2
            kp, vp = kp2_[par], vp2_[par]
            kpb, vpb = B_["kp%d" % par], B_["vp%d" % par]
            kpf = kp[:, :, :].rearrange("d h t -> d (h t)")
            P.op("dve", lambda h: h.tensor_scalar(out=kpf, in0=kcand[:, 0, :, :].rearrange("d h t -> d (h t)"),
                                                  scalar1=sel5[0:64, 0:1], scalar2=None, op0=ALU.mult),
                 reads=[B_["kcand"], B_["sel5"]], writes=[kpb])
            P.op("dve", lambda h: h.tensor_scalar(out=vp[:, :], in0=vcand[:, 0, :], scalar1=sel5[:, 0:1], scalar2=None,
                                                  op0=ALU.mult), reads=[B_["vcand"], B_["sel5"]], writes=[vpb])
            for cnd in range(1, 5):
                P.op("dve", lambda h, cnd=cnd: h.scalar_tensor_tensor(
                    out=kpf, in0=kcand[:, cnd, :, :].rearrange("d h t -> d (h t)"), scalar=sel5[0:64, cnd:cnd + 1], in1=kpf,
                    op0=ALU.mult, op1=ALU.add), reads=[B_["kcand"], B_["sel5"], kpb], writes=[kpb])
                P.op("dve", lambda h, cnd=cnd: h.scalar_tensor_tensor(
                    out=vp[:, :], in0=vcand[:, cnd, :], scalar=sel5[:, cnd:cnd + 1], in1=vp[:, :],
                    op0=ALU.mult, op1=ALU.add), reads=[B_["vcand"], B_["sel5"], vpb], writes=[vpb])

        sw_steps = [(i, hk) for i in range(NBLK) for hk in range(8)]
        sA = {}
        sB = {}

        def stageA(si):
            i, hk = sw_steps[si]
            par = i % 2
            qsl = qb2[par][:, hk * 4:(hk + 1) * 4, :]
            pc, pcb = PS_C.next()
            pp, ppb = PS_P.next()
            P.op("pe", mm(pc[:, :], kc2[par][:, hk, :], qsl, True, True), reads=[B_["kc%d" % par], B_["qblk%d" % par]], writes=[pcb])
            P.op("pe", mm(pp[:, :], kp2_[par][:, hk, :], qsl, True, True), reads=[B_["kp%d" % par], B_["qblk%d" % par]], writes=[ppb])
            sA[si] = (pc, pcb, pp, ppb)

        def stageB(si):
            i, hk = sw_steps[si]
            par = i % 2
            vc, vp, ablk = vc2[par], vp2_[par], ab2[par]
            vcb, vpb, abb = B_["vc%d" % par], B_["vp%d" % par], B_["ablk%d" % par]
            pc, pcb, pp, ppb = sA.pop(si)
            mpv, mpvb = (mprev0, B_["mprev0"]) if i == 0 else (mprev, B_["mprev"])
            tc_, tcb = ptc.next()
            tp_, tpb = ptp.next()
            P.op("act", lambda h: h.activation(out=tc_[:, :], in_=pc[:, :], func=AF.Exp, scale=scale), reads=[pcb], writes=[tcb])
            P.op("act", lambda h: h.activation(out=tp_[:, :], in_=pp[:, :], func=AF.Exp, scale=scale), reads=[ppb], writes=[tpb])
            P.op("pool", lambda h: h.tensor_tensor(
                out=tc_[:, :].rearrange("k (a q) -> k a q", a=4), in0=tc_[:, :].rearrange("k (a q) -> k a q", a=4),
                in1=mtri[:, :].rearrange("k (o q) -> k o q", o=1).broadcast_to([128, 4, 128]), op=ALU.mult),
                reads=[tcb, B_["mtri"]], writes=[tcb])
            P.op("dve", lambda h: h.tensor_tensor(
                out=tp_[:, :].rearrange("k (a q) -> k a q", a=4), in0=tp_[:, :].rearrange("k (a q) -> k a q", a=4),
                in1=mpv[:, :].rearrange("k (o q) -> k o q", o=1).broadcast_to([128, 4, 128]), op=ALU.mult),
                reads=[tpb, mpvb], writes=[tpb])
            po, pob = PS_O.next()
            pd, pdb = PS_D.next()
            P.op("pe", mm(po[0:64, :], vc[:, hk * 64:(hk + 1) * 64], tc_[:, :], True, False), reads=[vcb, tcb], writes=[pob])
            P.op("pe", mm(po[0:64, :], vp[:, hk * 64:(hk + 1) * 64], tp_[:, :], False, True), reads=[vpb, tpb], writes=[pob])
            P.op("pe", mm(pd[0:64, :], onesb_t[:, 0:64], tc_[:, :], True, False), reads=[ones_b, tcb], writes=[pdb])
            P.op("pe", mm(pd[0:64, :], onesb_t[:, 0:64], tp_[:, :], False, True), reads=[ones_b, tpb], writes=[pdb])
            sB[si] = (po, pob, pd, pdb)

        def stageB2(si):
            i, hk = sw_steps[si]
            par = i % 2
            ablk, abb = ab2[par], B_["ablk%d" % par]
            po, pob, pd, pdb = sB.pop(si)
            P.op("dve", lambda h: h.tensor_tensor(
                out=den_t[:, :].rearrange("d (a q) -> d a q", a=4), in0=pd[0:64, :].rearrange("d (a q) -> d a q", a=4),
                in1=sinke[:, hk * 4:(hk + 1) * 4].rearrange("d (a o) -> d a o", o=1).broadcast_to([64, 4, 128]), op=ALU.add),
                reads=[pdb, B_["sinke"]], writes=[B_["den"]])
            P.op("act", lambda h: h.activation(out=den_t[:, :], in_=den_t[:, :], func=AF.Ln), reads=[B_["den"]], writes=[B_["den"]])
            P.op("act", lambda h: h.activation(out=den_t[:, :], in_=den_t[:, :], func=AF.Exp, scale=-1.0),
                 reads=[B_["den"]], writes=[B_["den"]])
            P.op("dve", lambda h: h.tensor_tensor(
                out=ablk[:, hk * 4:(hk + 1) * 4, :], in0=po[0:64, :].rearrange("d (a q) -> d a q", a=4),
                in1=den_t[:, :].rearrange("d (a q) -> d a q", a=4), op=ALU.mult),
                reads=[pob, B_["den"]], writes=[abb])
            if hk == 7:
                P.dma("sp", lambda h: h.dma_start(out=att_v[:, :, i * 128:(i + 1) * 128], in_=ablk[:, :, :]),
                      reads=[abb], writes=[att_b])

        emit_block_loads(0)
        emit_block_select(0)
        stageA(0)
        for si in range(len(sw_steps)):
            bi, bh = sw_steps[si]
            if bh == 0 and bi + 1 < NBLK:
                emit_block_loads(bi + 1)
            if si + 1 < len(sw_steps):
                if sw_steps[si + 1][1] == 0:
                    emit_block_select(sw_steps[si + 1][0])
                stageA(si + 1)
            stageB(si)
            if si >= 1:
                stageB2(si - 1)
        stageB2(len(sw_steps) - 1)
        P.barrier()
        attn_out_phase(swa_w_out[0], ckey="swo")
        P.barrier()

    for l in layers:
        compute_mod(l)
        kind = l % 3
        if do_mixer:
            if kind == 0:
                arena["off"] = phase_base
                fox_layer(l, l // 3)
            if kind == 2:
                arena["off"] = phase_base
                swa_layer(l)
            if kind == 1:
                arena["off"] = phase_base
                st = sgu_setup()
                for g in range(NG):
                    sgu_group(st, g)
                P.barrier()
        if do_ffn:
            P.barrier()
            ffn_big(l)
            P.barrier()

    for g in range(NG):
        load_xg(g)
        stage = yT_t[:, :, :].rearrange("p c t -> p (c t)").rearrange("p (b d) -> p b d", b=4)
        for b in range(4):
            for q in range(4):
                pst, psb = PY.next()
                for j in range(4):
                    c = q * 4 + j
                    P.op("pe", lambda h, pst=pst, b=b, c=c, j=j: h.transpose(
                        pst[:, j * 128:(j + 1) * 128], xg_t[:, c, b * 128:(b + 1) * 128], ident_t[:, :]),
                        reads=[xg_b, ident_b], writes=[psb])
                if q % 2 == 0:
                    P.op("act", lambda h, pst=pst, b=b, q=q: h.activation(out=stage[:, b, q * 512:(q + 1) * 512],
                                                                          in_=pst[:, :], func=AF.Copy),
                         reads=[psb], writes=[yT_b])
                else:
                    P.op("dve", lambda h, pst=pst, b=b, q=q: h.tensor_copy(out=stage[:, b, q * 512:(q + 1) * 512],
                                                                           in_=pst[:, :]), reads=[psb], writes=[yT_b])
        P.dma("sp", lambda h, g=g: h.dma_start(
            out=yout[g * TG:(g + 1) * TG, :].rearrange("(b p) d -> p b d", p=128), in_=stage),
            reads=[yT_b], writes=[yout_b])
    P.barrier()
    P.emit(nc)
    es.close()
    return nc


_TRI = np.tril(np.ones((128, 128), np.float32))


def _prep_inputs(inp, layers):
    x = np.asarray(inp["x"], np.float32)
    maps = []
    shared = {
        "ident": np.eye(128, dtype=np.float32),
        "trimask": _TRI,
        "ffn_w_gu": np.ascontiguousarray(np.asarray(inp["ffn_w_gu"], np.float32)[list(layers)]),
        "ffn_w_down": np.ascontiguousarray(np.asarray(inp["ffn_w_down"], np.float32)[list(layers)]),
        "sgu_w_in": np.ascontiguousarray(inp["sgu_w_in"], np.float32),
        "sgu_ln_g": np.ascontiguousarray(inp["sgu_ln_g"], np.float32),
        "sgu_ln_b": np.ascontiguousarray(inp["sgu_ln_b"], np.float32),
        "sgu_w_s": np.ascontiguousarray(inp["sgu_w_s"], np.float32),
        "sgu_b_s": np.ascontiguousarray(inp["sgu_b_s"], np.float32).reshape(1, 2048),
        "sgu_w_out": np.ascontiguousarray(inp["sgu_w_out"], np.float32),
    }
    for n in ("fox_w_in", "fox_b_f", "fox_w_out", "swa_w_in", "swa_sinks", "swa_w_out"):
        shared[n] = np.ascontiguousarray(inp[n], np.float32)
    shared["triu"] = np.ascontiguousarray(_TRI.T)
    shared["m_prev"] = np.ascontiguousarray(1.0 - _TRI.T)
    pm = np.zeros((64, 16), np.float32)
    for m_ in range(8):
        pm[m_ + 8, m_] = -1.0
        pm[m_, m_ + 8] = 1.0
    shared["pmat"] = pm
    inv = (500000.0 ** (-np.arange(0, 16, 2, dtype=np.float32) / np.float32(16))).astype(np.float32)
    shared["invf"] = np.concatenate([inv, inv]).reshape(16, 1).astype(np.float32)
    for n in ("mix_pre_g", "mix_post_g", "ffn_pre_g", "ffn_post_g"):
        shared[n] = np.ascontiguousarray(inp[n], np.float32).reshape(64, 128)
    for core in range(8):
        b, r = core // 4, core % 4
        xb = x[b].reshape(16, 4, 128, D)[:, r].reshape(TOK, D)
        m = dict(shared)
        m["xs"] = np.ascontiguousarray(xb)
        m["ada_w"] = np.ascontiguousarray(np.asarray(inp["ada_w"], np.float32)[list(layers)][:, :, r * 3072:(r + 1) * 3072])
        m["ada_b"] = np.ascontiguousarray(np.asarray(inp["ada_b"], np.float32)[list(layers)][:, r * 3072:(r + 1) * 3072])
        m["cvec"] = np.ascontiguousarray(inp["c"][b], np.float32).reshape(16, 128)
        pos = np.asarray(inp["positions"])[b].astype(np.int32)
        m["posin"] = np.ascontiguousarray(pos.reshape(16, 4, 128)[:, r].reshape(1, TOK))
        mf = np.zeros((128, 4, 128), np.float32)
        for j_ in range(4):
            if j_ < r:
                mf[:, j_, :] = 1.0
            elif j_ == r:
                mf[:, j_, :] = _TRI.T
        m["m_fox"] = mf.reshape(128, 512)
        m["m_prev0"] = np.zeros((128, 128), np.float32) if r == 0 else np.ascontiguousarray(1.0 - _TRI.T)
        oh = np.zeros((128, 4), np.float32)
        oh[:, r] = 1.0
        m["oh4"] = oh
        zo = np.zeros((128, 16, 64), np.float32)
        for li_ in range(16):
            zo[:, li_, 4 * li_ + r + 1:] = -30000.0
        m["zo"] = zo.reshape(128, 1024)
        s5 = np.zeros((128, 5), np.float32)
        s5[:, (r - 1) if r > 0 else 4] = 1.0
        m["sel5"] = s5
        maps.append(m)
    return maps


def run(inp, layers=(0, 1, 2, 3), **kw):
    nc = build_program(layers=layers, **kw)
    maps = _prep_inputs(inp, layers)
    res = run_bass_kernel_spmd(nc, maps, core_ids=list(range(8)))
    out = np.empty((2, SEQ, D), np.float32)
    for core in range(8):
        b, r = core // 4, core % 4
        out[b].reshape(16, 4, 128, D)[:, r] = res.results[core]["yout"].reshape(16, 128, D)
    return out


def kernel(**inputs):
    return run(inputs)
```
